# Optimizing a Trainium2 kernel written in Bass

```python
import math
import jax, jax.numpy as jnp
from jax import lax
import numpy as np

D_MODEL = 1024
BATCH = 16
SEQ = 4096
DEPTH = 1

GLA_HEADS = 4
GLA_K = D_MODEL // 2
GLA_V = D_MODEL
GLA_DK = GLA_K // GLA_HEADS
GLA_DV = GLA_V // GLA_HEADS
GLA_RANK = 16
GLA_GATE_NORM = 16.0
GLA_CHUNK = 64
DIFF_HEADS = 4
DIFF_HD = 128
DIFF_QK = DIFF_HEADS * 2 * DIFF_HD
DIFF_V = DIFF_HEADS * 2 * DIFF_HD
Q_BLOCK = 128
ROPE_THETA = 10000.0
FFN_HIDDEN = -(-8 * D_MODEL // (3 * 256)) * 256
NORM_EPS = 1e-6
SUBLN_EPS = 1e-5
IN_SIZES = (GLA_K, GLA_K, GLA_V, GLA_V, 2 * GLA_RANK, DIFF_QK, DIFF_QK, DIFF_V, D_MODEL, D_MODEL)
IN_WIDTH = sum(IN_SIZES)

kernel_name = "hybrid_gla_diffattn_encoder_block"


def rms_norm(x, w, eps=NORM_EPS):
    xf = x.astype(jnp.float32)
    y = xf * lax.rsqrt(jnp.mean(xf * xf, axis=-1, keepdims=True) + eps)
    return (y * w.astype(jnp.float32)).astype(x.dtype)


def rope(x, pos):
    d = x.shape[-1]
    inv_freq = 1.0 / (ROPE_THETA ** (jnp.arange(0, d, 2, dtype=jnp.float32) / d))
    freqs = pos.astype(jnp.float32)[:, None] * inv_freq[None, :]
    emb = jnp.concatenate([freqs, freqs], axis=-1)
    cos = jnp.cos(emb)[None, :, None, None, :]
    sin = jnp.sin(emb)[None, :, None, None, :]
    xf = x.astype(jnp.float32)
    x1, x2 = xf[..., : d // 2], xf[..., d // 2:]
    rot = jnp.concatenate([-x2, x1], axis=-1)
    return (xf * cos + rot * sin).astype(x.dtype)


def gla_chunked(q, k, v, g):
    B, S, H, dk = q.shape
    dv = v.shape[-1]
    C = GLA_CHUNK
    N = S // C

    def chunks(t):
        return t.reshape(B, N, C, H, t.shape[-1]).transpose(1, 0, 3, 2, 4)

    qc, kc, vc, gc = chunks(q), chunks(k), chunks(v), chunks(g)
    b = jnp.cumsum(gc, axis=3)
    b_last = b[:, :, :, -1:, :]
    q_t = qc * jnp.exp(b)
    k_t = kc * jnp.exp(-b)
    k_s = kc * jnp.exp(b_last - b)
    mask = jnp.tril(jnp.ones((C, C), dtype=bool))
    attn = jnp.where(mask, jnp.einsum('nbhid,nbhjd->nbhij', q_t, k_t), 0.0)
    o_intra = jnp.einsum('nbhij,nbhjv->nbhiv', attn, vc)

    def step(state, xs):
        q_n, k_n, v_n, decay_n = xs
        o = jnp.einsum('bhid,bhdv->bhiv', q_n, state)
        state = state * decay_n[:, :, 0, :, None] + jnp.einsum('bhjd,bhjv->bhdv', k_n, v_n)
        return state, o

    state0 = jnp.zeros((B, H, dk, dv), dtype=q.dtype)
    _, o_inter = lax.scan(step, state0, (q_t, k_s, vc, jnp.exp(b_last)))
    o = o_intra + o_inter
    return o.transpose(1, 0, 3, 2, 4).reshape(B, S, H, dv)


def diff_attention(q, k, v, lam):
    B, S, H, _, d = q.shape
    nb = S // Q_BLOCK
    qb = q.reshape(B, nb, Q_BLOCK, H, 2, d).transpose(1, 0, 2, 3, 4, 5)

    def block(q_blk):
        s = jnp.einsum('bqhcd,bkhcd->bhcqk', q_blk, k).astype(jnp.float32)
        p = jax.nn.softmax(s, axis=-1)
        w = p[:, :, 0] - lam * p[:, :, 1]
        return jnp.einsum('bhqk,bkhv->bqhv', w.astype(v.dtype), v)

    o = lax.map(block, qb)
    return o.transpose(1, 0, 2, 3, 4).reshape(B, S, H, 2 * d)


def hybrid_mixer(u, w_in, w_gk2, b_gk, gla_norm_w, lq1, lk1, lq2, lk2, subln_w, w_out, lambda_init):
    B, S, _ = u.shape
    f32 = jnp.float32
    proj = u @ w_in
    idx = []
    acc = 0
    for sz in IN_SIZES[:-1]:
        acc += sz
        idx.append(acc)
    gq, gk, gv, gr, glr, dq, dk, dv, ga, gb = jnp.split(proj, idx, axis=-1)

    q = (gq.astype(f32) * (GLA_DK ** -0.5)).reshape(B, S, GLA_HEADS, GLA_DK)
    k = gk.astype(f32).reshape(B, S, GLA_HEADS, GLA_DK)
    v = gv.astype(f32).reshape(B, S, GLA_HEADS, GLA_DV)
    glr = glr.astype(f32).reshape(B, S, 2, GLA_RANK)
    logits = jnp.einsum('bsdr,drk->bsdk', glr, w_gk2.astype(f32)) + b_gk.astype(f32)
    log_a = jax.nn.log_sigmoid(logits) / GLA_GATE_NORM
    g_f = log_a[:, :, 0].reshape(B, S, GLA_HEADS, GLA_DK)
    g_b = log_a[:, :, 1].reshape(B, S, GLA_HEADS, GLA_DK)
    o_f = gla_chunked(q, k, v, g_f)
    o_b = jnp.flip(gla_chunked(jnp.flip(q, 1), jnp.flip(k, 1), jnp.flip(v, 1), jnp.flip(g_b, 1)), 1)
    o_a = rms_norm(o_f + o_b, gla_norm_w)
    y_a = o_a.reshape(B, S, GLA_V) * jax.nn.silu(gr.astype(f32))

    pos = jnp.arange(S)
    q2 = rope(dq.reshape(B, S, DIFF_HEADS, 2, DIFF_HD), pos) * jnp.asarray(DIFF_HD ** -0.5, dtype=dq.dtype)
    k2 = rope(dk.reshape(B, S, DIFF_HEADS, 2, DIFF_HD), pos)
    v2 = dv.reshape(B, S, DIFF_HEADS, 2 * DIFF_HD)
    lam = (jnp.exp(jnp.sum(lq1.astype(f32) * lk1.astype(f32)))
           - jnp.exp(jnp.sum(lq2.astype(f32) * lk2.astype(f32))) + lambda_init)
    o2 = diff_attention(q2, k2, v2, lam)
    y_b = (rms_norm(o2, subln_w, eps=SUBLN_EPS).astype(f32) * (1.0 - lambda_init)).reshape(B, S, DIFF_V)

    merged = jax.nn.sigmoid(ga.astype(f32)) * y_a + jax.nn.sigmoid(gb.astype(f32)) * y_b
    return merged.astype(u.dtype) @ w_out


def swiglu(x, w_ffn_in, w_ffn_out):
    a = x @ w_ffn_in
    gate, up = jnp.split(a, [FFN_HIDDEN], axis=-1)
    return (jax.nn.silu(gate) * up) @ w_ffn_out


def setup_inputs(seed: int = 0) -> dict:
    key = jax.random.key(seed)
    ks = jax.random.split(key, 16)
    f = jnp.float32
    nrm = lambda k, shp, s: jax.random.normal(k, shp, dtype=f) * s
    return {
        "x": jax.random.normal(ks[0], (BATCH, SEQ, D_MODEL), dtype=f),
        "norm_mix_w": 1.0 + nrm(ks[1], (DEPTH, D_MODEL), 0.02),
        "w_in": nrm(ks[2], (DEPTH, D_MODEL, IN_WIDTH), D_MODEL ** -0.5),
        "w_gk2": nrm(ks[3], (DEPTH, 2, GLA_RANK, GLA_K), GLA_RANK ** -0.5),
        "b_gk": nrm(ks[4], (DEPTH, 2, GLA_K), 0.1),
        "gla_norm_w": 1.0 + nrm(ks[5], (DEPTH, GLA_DV), 0.02),
        "lambda_q1": nrm(ks[6], (DEPTH, DIFF_HD), 0.1),
        "lambda_k1": nrm(ks[7], (DEPTH, DIFF_HD), 0.1),
        "lambda_q2": nrm(ks[8], (DEPTH, DIFF_HD), 0.1),
        "lambda_k2": nrm(ks[9], (DEPTH, DIFF_HD), 0.1),
        "diff_subln_w": 1.0 + nrm(ks[10], (DEPTH, 2 * DIFF_HD), 0.02),
        "w_out": nrm(ks[11], (DEPTH, D_MODEL, D_MODEL), D_MODEL ** -0.5),
        "norm_ffn_w": 1.0 + nrm(ks[12], (DEPTH, D_MODEL), 0.02),
        "w_ffn_in": nrm(ks[13], (DEPTH, D_MODEL, 2 * FFN_HIDDEN), D_MODEL ** -0.5),
        "w_ffn_out": nrm(ks[14], (DEPTH, FFN_HIDDEN, D_MODEL), FFN_HIDDEN ** -0.5),
        "norm_final_w": 1.0 + nrm(ks[15], (D_MODEL,), 0.02),
    }


def reference(x, norm_mix_w, w_in, w_gk2, b_gk, gla_norm_w, lambda_q1, lambda_k1, lambda_q2,
              lambda_k2, diff_subln_w, w_out, norm_ffn_w, w_ffn_in, w_ffn_out, norm_final_w):
    h = x
    for layer in range(DEPTH):
        lambda_init = 0.8 - 0.6 * math.exp(-0.3 * layer)
        u = rms_norm(h, norm_mix_w[layer])
        h = h + hybrid_mixer(u, w_in[layer], w_gk2[layer], b_gk[layer], gla_norm_w[layer],
                             lambda_q1[layer], lambda_k1[layer], lambda_q2[layer], lambda_k2[layer],
                             diff_subln_w[layer], w_out[layer], lambda_init)
        h = h + swiglu(rms_norm(h, norm_ffn_w[layer]), w_ffn_in[layer], w_ffn_out[layer])
    return rms_norm(h, norm_final_w)
```

```python
from contextlib import ExitStack
import math
import numpy as np
import ml_dtypes
import concourse.bass as bass
import concourse.mybir as mybir
from concourse.bass_utils import run_bass_kernel_spmd

F32 = mybir.dt.float32
BF16 = mybir.dt.bfloat16
AF = mybir.ActivationFunctionType
ALU = mybir.AluOpType

D = 1024
KC = D // 128
GLA_H, GLA_DK, GLA_DV, GLA_RANK = 4, 128, 256, 16
DIFF_H, DIFF_HD = 4, 128
FFN = 2816
FC = FFN // 128
NORM_EPS = 1e-6
SUBLN_EPS = 1e-5
LAMBDA_INIT = 0.8 - 0.6 * math.exp(-0.3 * 0)
N_CORES = 8

OFF_GQ, OFF_GK, OFF_GV, OFF_GR, OFF_GLR = 0, 512, 1024, 2048, 3072
OFF_DQ, OFF_DK, OFF_DV, OFF_GA, OFF_GB = 3104, 4128, 5152, 6176, 7200


def _w1_columns():
    cols = []
    cols += list(range(OFF_GQ, OFF_GQ + 512))
    cols += list(range(OFF_GK, OFF_GK + 512))
    for off in (OFF_DQ, OFF_DK):
        for hc in range(8):
            base = off + hc * 128
            cols += list(range(base, base + 128))
            cols += list(range(base + 64, base + 128)) + list(range(base, base + 64))
    cols += list(range(OFF_GLR, OFF_GLR + 32))
    cols += list(range(OFF_GK, OFF_GK + 512))
    for off in (OFF_GV, OFF_GR, OFF_DV, OFF_GA, OFF_GB):
        cols += list(range(off, off + 1024))
    return np.array(cols, dtype=np.int64)


W1_COLS = 512 * 10 + 32 + 512 * 11
FM_BLOCKS = [("gq", 0), ("gk", 512)] + [("dq", 1024 + 512 * i) for i in range(4)] + \
            [("dk", 3072 + 512 * i) for i in range(4)]
GLR_OFF = 5120
TM_OFF = 5152
TM_BLOCKS = [("gk", 0)] + [(n, j) for n in ("gv", "gr", "dv", "ga", "gb") for j in range(2)]


class Sched:
    CE = ("pe", "act", "dve", "pool")
    ALL = ("pe", "act", "dve", "pool", "sp")

    def __init__(self, nc, es):
        self.nc = nc
        self.prog = {e: es.enter_context(nc.semaphore("prog_" + e)) for e in self.CE}
        self.cum = {e: 0 for e in self.CE}
        self.dsem = {}
        self.es = es
        self.res_w = {}
        self.res_r = {}
        self.ops = {e: [] for e in self.ALL}
        self.waited = {e: {} for e in self.ALL}
        self.sigval = {}
        self.uid = 0

    def _slot(self, slot):
        if slot not in self.dsem:
            self.dsem[slot] = [self.es.enter_context(self.nc.semaphore("d_" + slot)), 0]
        return self.dsem[slot]

    def _collect(self, eng, reads, writes, is_dma):
        deps = []
        for k in reads:
            for ev in self.res_w.get(k, ()):
                deps.append((ev, True))
        for k in writes:
            for ev in self.res_w.get(k, ()):
                deps.append((ev, False))
            for ev in self.res_r.get(k, ()):
                deps.append((ev, False))
        out = []
        for ev, raw in deps:
            if ev[0] == "eng" and ev[1] == eng and not is_dma:
                if eng == "pe":
                    continue
            out.append(ev)
        return out

    def _update(self, ev, reads, writes):
        for k in writes:
            self.res_w[k] = [ev]
            self.res_r[k] = []
        for k in reads:
            self.res_r.setdefault(k, []).append(ev)

    def op(self, eng, fn, reads=(), writes=()):
        self.uid += 1
        ev = ("eng", eng, self.uid)
        deps = self._collect(eng, reads, writes, False)
        self.ops[eng].append(dict(fn=fn, deps=deps, ev=ev, dma=None))
        self._update(ev, reads, writes)
        return ev

    def dma(self, queue, fn, slot, reads=(), writes=()):
        s = self._slot(slot)
        deps = self._collect(queue, reads, writes, True)
        if s[1] > 0:
            deps.append(("dma", slot, s[1]))
        s[1] += 1
        ev = ("dma", slot, s[1])
        self.ops[queue].append(dict(fn=fn, deps=deps, ev=ev, dma=slot))
        self._update(ev, reads, writes)
        return ev

    def flush_wait_all(self):
        deps = [("dma", slot, s[1]) for slot, s in self.dsem.items() if s[1] > 0]
        self.ops["sp"].append(dict(fn=None, deps=deps, ev=None, dma=None))

    def emit(self, block):
        done = getattr(self, "done_uid", 0)
        for e in self.ALL:
            for o in self.ops[e]:
                o["deps"] = [d for d in o["deps"] if not (d[0] == "eng" and d[2] <= done)]
        self.done_uid = self.uid
        need = set()
        for e in self.ALL:
            for o in self.ops[e]:
                for d in o["deps"]:
                    if d[0] == "eng":
                        need.add(d[2])
        for e in self.CE:
            c = self.cum[e]
            for o in self.ops[e]:
                if o["dma"] is None and o["ev"] is not None and o["ev"][2] in need:
                    c += 1
                    self.sigval[o["ev"][2]] = c
            self.cum[e] = c
        starters = dict(pe=block.tensor, act=block.scalar, dve=block.vector, pool=block.gpsimd, sp=block.sync)
        for e in self.ALL:
            ops = self.ops[e]
            if not ops:
                continue
            waited = self.waited[e]

            def body(eng, ops=ops, waited=waited, e=e):
                for o in ops:
                    for d in o["deps"]:
                        if d[0] == "eng":
                            key, sem, val = d[1], self.prog[d[1]], self.sigval[d[2]]
                        else:
                            key, sem, val = "d_" + d[1], self.dsem[d[1]][0], 16 * d[2]
                        if waited.get(key, 0) >= val:
                            continue
                        eng.wait_ge(sem, val)
                        waited[key] = val
                    if o["fn"] is None:
                        continue
                    ins = o["fn"](eng)
                    if o["dma"] is not None:
                        ins.then_inc(self.dsem[o["dma"]][0], 16)
                    elif o["ev"][2] in self.sigval:
                        ins.then_inc(self.prog[e], 1)
            starters[e](body)
        self.ops = {e: [] for e in self.ALL}


class Builder:
    def __init__(self, S, NSEQ, debug=False, phases=("p1", "gla", "attn", "ffn"), scratch_in=False, cut=99):
        self.S, self.NSEQ, self.debug, self.phases = S, NSEQ, debug, phases
        self.scratch_in, self.cut = scratch_in, cut
        self.T = S * NSEQ
        self.NT = S // 128
        self.NG = S // 512
        self.nc = bass.Bass("TRN2", target_bir_lowering=False)

    def dram(self, name, shape, dt, kind):
        return self.nc.dram_tensor(name, list(shape), dt, kind=kind).ap()

    def declare(self):
        S, T, NSEQ = self.S, self.T, self.NSEQ
        I = "ExternalInput"
        self.x = self.dram("x", [NSEQ, S, D], F32, I)
        self.w1 = self.dram("w1", [D, W1_COLS], F32, I)
        self.nw_mix = self.dram("nw_mix", [128, KC], F32, I)
        self.nw_ffn = self.dram("nw_ffn", [128, KC], F32, I)
        self.nw_fin = self.dram("nw_fin", [1, D], F32, I)
        self.cosT = self.dram("cosT", [128, S], F32, I)
        self.sinT = self.dram("sinT", [128, S], F32, I)
        self.wgk = self.dram("wgk", [2, 17, 512], F32, I)
        self.gnw = self.dram("gnw", [1, 1024], F32, I)
        self.sln = self.dram("sln", [1, 256], F32, I)
        self.lam4 = self.dram("lam4", [1, 512], F32, I)
        self.w_out = self.dram("w_out", [D, D], F32, I)
        self.w_fi = self.dram("w_fi", [D, 2 * FFN], F32, I)
        self.w_fo = self.dram("w_fo", [FFN, D], F32, I)
        self.cst = self.dram("cst", [128, 6 * 128], F32, I)
        self.msk = self.dram("msk", [128, 2 * 128], F32, I)
        self.y = self.dram("y", [NSEQ, S, D], F32, "ExternalOutput")
        K = "ExternalOutput" if self.debug else "Internal"
        K1 = K
        if self.scratch_in:
            K = "ExternalInput"
        self.gqT = self.dram("gqT", [4, 128, T], BF16, K)
        self.gkT = self.dram("gkT", [4, 128, T], BF16, K)
        self.dqT = self.dram("dqT", [8, 128, T], BF16, K)
        self.dkT = self.dram("dkT", [8, 128, T], BF16, K)
        self.glrT = self.dram("glrT", [32, T], F32, K)
        self.tm = {n: self.dram("tm_" + n, [T, (512 if n == "gk" else 1024)], BF16, K)
                   for n in ("gk", "gv", "gr", "dv", "ga", "gb")}
        K = K1
        self.ob = self.dram("ob", [T, 1024], F32, K)
        self.ya = self.dram("ya", [T, 1024], F32, K)
        self.yb = self.dram("yb", [T, 1024], F32, K)
        self.wfib = self.dram("wfib", [FC, 128, 2, KC, 128], BF16, "Internal")

    def _uniq(self, name):
        self._n = getattr(self, "_n", 0) + 1
        return f"{name}_{self._n}"

    def sb(self, es, name, shape, dt):
        return es.enter_context(self.nc.sbuf_tensor(self._uniq(name), list(shape), dt))

    def ps(self, es, name, shape, dt=F32):
        return es.enter_context(self.nc.psum_tensor(self._uniq(name), list(shape), dt))

    def phase1(self, sc, s):
        nc, S, NT, NG = self.nc, self.S, self.NT, self.NG
        t0 = s * S
        with ExitStack() as es, nc.Block() as block:
            uT = self.sb(es, "uT", [128, KC, S], BF16)
            cosT = self.sb(es, "cosT_sb", [128, S], F32)
            sinT = self.sb(es, "sinT_sb", [128, S], F32)
            nwm = self.sb(es, "nwm", [128, KC], F32)
            ident = self.sb(es, "ident", [128, 128], BF16)
            identf = self.sb(es, "identf", [128, 128], F32)
            xt = [self.sb(es, f"xt{i}", [128, D], F32) for i in range(2)]
            xn = [self.sb(es, f"xn{i}", [128, D], BF16) for i in range(2)]
            st = [self.sb(es, f"st{i}", [128, 4], F32) for i in range(2)]
            wb = [self.sb(es, f"wb{i}", [128, KC, 512], BF16) for i in range(2)]
            so = [self.sb(es, f"so{i}", [128, 512], BF16) for i in range(4)]
            sof = [self.sb(es, f"sof{i}", [32, 512], F32) for i in range(2)]
            r1 = [self.sb(es, f"r1_{i}", [128, 512], F32) for i in range(2)]
            r2 = [self.sb(es, f"r2_{i}", [128, 512], F32) for i in range(2)]
            psT = [self.ps(es, f"psT{i}", [128, 1024], BF16) for i in range(2)]
            pb = [self.ps(es, f"pb{i}", [128, 512]) for i in range(6)]

            sc.dma("sp", lambda e: e.dma_start(out=cosT[:], in_=self.cosT[:, :]), "c_cos", writes=["cosT"])
            sc.dma("sp", lambda e: e.dma_start(out=sinT[:], in_=self.sinT[:, :]), "c_sin", writes=["sinT"])
            sc.dma("sp", lambda e: e.dma_start(out=nwm[:], in_=self.nw_mix[:, :]), "c_nwm", writes=["nwm"])
            sc.dma("sp", lambda e: e.dma_start(out=identf[:], in_=self.cst[:, 0:128]), "c_id", writes=["identf"])
            sc.op("dve", lambda e: e.tensor_copy(out=ident[:], in_=identf[:]), reads=["identf"], writes=["ident"])

            for tt in range(NT):
                i = tt % 2
                sc.dma("sp", lambda e, i=i, tt=tt: e.dma_start(out=xt[i][:], in_=self.x[s, tt * 128:(tt + 1) * 128, :]),
                       f"xt{i}", writes=[f"xt{i}"])
                sc.op("act", lambda e, i=i: e.activation(out=xn[i][:], in_=xt[i][:], func=AF.Square, accum_out=st[i][:, 0:1]),
                      reads=[f"xt{i}"], writes=[f"xn{i}", f"st{i}"])
                sc.op("act", lambda e, i=i: e.activation(out=st[i][:, 1:2], in_=st[i][:, 0:1], func=AF.Ln, scale=1.0 / D, bias=self.eps_t[:, 0:1]),
                      reads=[f"st{i}"], writes=[f"st{i}"])
                sc.op("act", lambda e, i=i: e.activation(out=st[i][:, 2:3], in_=st[i][:, 1:2], func=AF.Exp, scale=-0.5),
                      reads=[f"st{i}"], writes=[f"st{i}"])
                sc.op("dve", lambda e, i=i: e.tensor_scalar(out=xn[i][:], in0=xt[i][:], scalar1=st[i][:, 2:3], scalar2=None, op0=ALU.mult),
                      reads=[f"xt{i}", f"st{i}"], writes=[f"xn{i}"])
                for kc in range(KC):
                    sc.op("pe", lambda e, i=i, kc=kc: e.transpose(psT[i][:, kc * 128:(kc + 1) * 128], xn[i][:, kc * 128:(kc + 1) * 128], ident[:]),
                          reads=[f"xn{i}", "ident"], writes=[f"psT{i}"])
                for kc in range(KC):
                    sc.op("dve", lambda e, i=i, kc=kc, tt=tt: e.tensor_scalar(
                        out=uT[:, kc, tt * 128:(tt + 1) * 128], in0=psT[i][:, kc * 128:(kc + 1) * 128],
                        scalar1=nwm[:, kc:kc + 1], scalar2=None, op0=ALU.mult),
                        reads=[f"psT{i}", "nwm"], writes=[("uT", tt)])
            uT_all = [("uT", tt) for tt in range(NT)]

            state = dict(wbi=0, pbi=0, soi=0, ri=0, sofi=0)

            def load_w(col0, ncols):
                i = state["wbi"] % 2
                state["wbi"] += 1
                src = self.w1.rearrange("(k p) c -> p k c", p=128)[:, :, col0:col0 + ncols]
                sc.dma("pool", lambda e, i=i: e.dma_start(out=wb[i][:, :, 0:ncols], in_=src), f"wb{i}", writes=[f"wb{i}"])
                return i

            def next_pb():
                j = state["pbi"] % 6
                state["pbi"] += 1
                return j

            def store(dst_ap, src_tile_key, src_ap, slotname):
                sc.dma("sp", lambda e: e.dma_start(out=dst_ap, in_=src_ap), slotname, reads=[src_tile_key])

            def fm_group(wi, c, tg, j, M=128):
                for kc in range(KC):
                    sc.op("pe", lambda e, kc=kc: e.matmul(pb[j][0:M, :], lhsT=wb[wi][:, kc, c * 128:c * 128 + M],
                                                            rhs=uT[:, kc, tg * 512:(tg + 1) * 512], start=(kc == 0), stop=(kc == KC - 1)),
                          reads=[f"wb{wi}"] + uT_all[tg * 4:(tg + 1) * 4], writes=[f"pb{j}"])

            for name, col0 in FM_BLOCKS:
                wi = load_w(col0, 512)
                bidx = (col0 - {"gq": 0, "gk": 512, "dq": 1024, "dk": 3072}[name]) // 512
                if name in ("gq", "gk"):
                    dst = self.gqT if name == "gq" else self.gkT
                    for c in range(4):
                        for tg in range(NG):
                            j = next_pb()
                            fm_group(wi, c, tg, j)
                            k = state["soi"] % 4
                            state["soi"] += 1
                            if (c + tg) % 2 == 0:
                                sc.op("act", lambda e, j=j, k=k: e.activation(out=so[k][:], in_=pb[j][:], func=AF.Copy),
                                      reads=[f"pb{j}"], writes=[f"so{k}"])
                            else:
                                sc.op("dve", lambda e, j=j, k=k: e.tensor_copy(out=so[k][:], in_=pb[j][:]),
                                      reads=[f"pb{j}"], writes=[f"so{k}"])
                            store(dst[c, :, t0 + tg * 512:t0 + (tg + 1) * 512], f"so{k}", so[k][:], f"so{k}")
                else:
                    dst = self.dqT if name == "dq" else self.dkT
                    for pr in range(2):
                        hc = bidx * 2 + pr
                        for tg in range(NG):
                            ja, jb = next_pb(), next_pb()
                            fm_group(wi, 2 * pr, tg, ja)
                            fm_group(wi, 2 * pr + 1, tg, jb)
                            ri = state["ri"] % 2
                            state["ri"] += 1
                            k = state["soi"] % 4
                            state["soi"] += 1
                            tsl = slice(tg * 512, (tg + 1) * 512)
                            sc.op("dve", lambda e, ja=ja, ri=ri, tsl=tsl: e.tensor_tensor(out=r1[ri][:], in0=pb[ja][:], in1=cosT[:, tsl], op=ALU.mult),
                                  reads=[f"pb{ja}", "cosT"], writes=[f"r1_{ri}"])
                            sc.op("dve", lambda e, jb=jb, ri=ri, tsl=tsl: e.tensor_tensor(out=r2[ri][:], in0=pb[jb][:], in1=sinT[:, tsl], op=ALU.mult),
                                  reads=[f"pb{jb}", "sinT"], writes=[f"r2_{ri}"])
                            sc.op("pool", lambda e, ri=ri, k=k: e.tensor_tensor(out=so[k][:], in0=r1[ri][:], in1=r2[ri][:], op=ALU.add),
                                  reads=[f"r1_{ri}", f"r2_{ri}"], writes=[f"so{k}"])
                            store(dst[hc, :, t0 + tg * 512:t0 + (tg + 1) * 512], f"so{k}", so[k][:], f"so{k}")
            wi = load_w(GLR_OFF, 32)
            for tg in range(NG):
                j = next_pb()
                fm_group(wi, 0, tg, j, M=32)
                k = state["sofi"] % 2
                state["sofi"] += 1
                sc.op("dve", lambda e, j=j, k=k: e.tensor_copy(out=sof[k][:], in_=pb[j][0:32, :]), reads=[f"pb{j}"], writes=[f"sof{k}"])
                store(self.glrT[:, t0 + tg * 512:t0 + (tg + 1) * 512], f"sof{k}", sof[k][:], f"sof{k}")
            for bi, (name, jj) in enumerate(TM_BLOCKS):
                wi = load_w(TM_OFF + bi * 512, 512)
                dst = self.tm[name]
                for tt in range(NT):
                    j = next_pb()
                    for kc in range(KC):
                        sc.op("pe", lambda e, kc=kc, j=j, tt=tt, wi=wi: e.matmul(pb[j][:], lhsT=uT[:, kc, tt * 128:(tt + 1) * 128], rhs=wb[wi][:, kc, :],
                                                                        start=(kc == 0), stop=(kc == KC - 1)),
                              reads=[f"wb{wi}", ("uT", tt)], writes=[f"pb{j}"])
                    k = state["soi"] % 4
                    state["soi"] += 1
                    if name == "gr":
                        sc.op("act", lambda e, j=j, k=k: e.activation(out=so[k][:], in_=pb[j][:], func=AF.Silu), reads=[f"pb{j}"], writes=[f"so{k}"])
                    elif name in ("ga", "gb"):
                        sc.op("act", lambda e, j=j, k=k: e.activation(out=so[k][:], in_=pb[j][:], func=AF.Sigmoid), reads=[f"pb{j}"], writes=[f"so{k}"])
                    elif tt % 2 == 0:
                        sc.op("act", lambda e, j=j, k=k: e.activation(out=so[k][:], in_=pb[j][:], func=AF.Copy), reads=[f"pb{j}"], writes=[f"so{k}"])
                    else:
                        sc.op("dve", lambda e, j=j, k=k: e.tensor_copy(out=so[k][:], in_=pb[j][:]), reads=[f"pb{j}"], writes=[f"so{k}"])
                    store(dst[t0 + tt * 128:t0 + (tt + 1) * 128, jj * 512:(jj + 1) * 512], f"so{k}", so[k][:], f"so{k}")
            sc.flush_wait_all()
            sc.emit(block)


    def phase_gla(self, sc, s, dirn):
        nc, S = self.nc, self.S
        t0 = s * S
        NP, NSS = S // 128, S // 512
        fwd = dirn == 0
        with ExitStack() as es, nc.Block() as block:
            Uinc = self.sb(es, "Uinc", [128, 128], F32)
            Ustr = self.sb(es, "Ustr", [128, 128], F32)
            mask4 = self.sb(es, "mask4", [128, 4, 128], F32)
            wgk = self.sb(es, "wgk_sb", [17, 512], F32)
            gnwb = self.sb(es, "gnwb", [128, 1024], F32)
            glra = [self.sb(es, f"glra{i}", [17, 512], F32) for i in range(2)]
            qT = [self.sb(es, f"gqT{i}", [128, 4, 512], BF16) for i in range(2)]
            kT = [self.sb(es, f"gkT{i}", [128, 4, 512], BF16) for i in range(2)]
            ktm = [self.sb(es, f"gktm{i}", [128, 4, 512], BF16) for i in range(2)]
            vv = [self.sb(es, f"gv{i}", [128, 4, 1024], BF16) for i in range(2)]
            obt = [self.sb(es, f"obt{i}", [128, 1024], F32) for i in range(2)]
            grt = [self.sb(es, f"grt{i}", [128, 1024], BF16) for i in range(2)]
            gat = [self.sb(es, f"gat{i}", [128, 1024], BF16) for i in range(2)]
            e1 = self.sb(es, "g_e1", [128, 512], F32)
            sp = self.sb(es, "g_sp", [128, 512], F32)
            ebl = self.sb(es, "g_ebl", [128, 512], F32)
            ks = self.sb(es, "g_ks", [128, 512], BF16)
            eb = self.sb(es, "g_eb", [128, 4, 128], F32)
            enb = self.sb(es, "g_enb", [128, 4, 128], F32)
            qt = self.sb(es, "g_qt", [128, 4, 128], BF16)
            kt = self.sb(es, "g_kt", [128, 4, 128], BF16)
            attm = self.sb(es, "g_attm", [128, 4, 128], BF16)
            S32 = [self.sb(es, f"S32_{h}", [128, 256], F32) for h in range(4)]
            Sbf = [self.sb(es, f"Sbf_{h}", [128, 256], BF16) for h in range(4)]
            osb = [self.sb(es, f"osb{i}", [128, 1024], F32) for i in range(2)]
            ysb = [self.sb(es, f"ysb{i}", [128, 1024], F32) for i in range(2)]
            g32 = self.sb(es, "g_g32", [128, 1024], F32)
            junk = self.sb(es, "g_junk", [128, 4, 256], F32)
            stt = [self.sb(es, f"g_st{i}", [128, 12], F32) for i in range(2)]
            bA = self.ps(es, "bA", [128, 512])
            bB = self.ps(es, "bB", [128, 512])
            bO = [self.ps(es, f"bO{h}", [128, 512]) for h in range(4)]
            bKV = [self.ps(es, f"bKV{i}", [128, 512]) for i in range(2)]

            co = 128 * (1 + 2 * dirn)
            sc.dma("sp", lambda e: e.dma_start(out=Uinc[:], in_=self.cst[:, co:co + 128]), "c_a", writes=["Uinc"])
            sc.dma("sp", lambda e: e.dma_start(out=Ustr[:], in_=self.cst[:, co + 128:co + 256]), "c_b", writes=["Ustr"])
            for h in range(4):
                sc.dma("sp", lambda e, h=h: e.dma_start(out=mask4[:, h, :], in_=self.msk[:, dirn * 128:(dirn + 1) * 128]), "c_c", writes=["mask4"])
            sc.dma("sp", lambda e: e.dma_start(out=wgk[:], in_=self.wgk[dirn, :, :]), "c_d", writes=["wgk"])
            sc.dma("sp", lambda e: e.dma_start(out=gnwb[:], in_=self.gnw.broadcast_to([128, 1024])), "c_e", writes=["gnwb"])
            for i in range(2):
                sc.op("dve", lambda e, i=i: e.memset(glra[i][:], 1.0), writes=[f"glra{i}"])
            for h in range(4):
                sc.op("dve", lambda e, h=h: e.memset(S32[h][:], 0.0), writes=[f"S32_{h}"])
                sc.op("dve", lambda e, h=h: e.memset(Sbf[h][:], 0.0), writes=[f"Sbf_{h}"])

            if fwd:
                r1, c1, r2, c2 = slice(0, 64), 63, slice(64, 128), 127
            else:
                r1, c1, r2, c2 = slice(64, 128), 64, slice(0, 64), 0
            qscale = float(GLA_DK ** -0.5)

            def load_ss(ss, i):
                tb = t0 + ss * 512
                sc.dma("sp", lambda e: e.dma_start(out=glra[i][0:16, :], in_=self.glrT[dirn * 16:(dirn + 1) * 16, tb:tb + 512]), f"glra{i}", writes=[f"glra{i}"])
                sc.dma("sp", lambda e: e.dma_start(out=qT[i][:], in_=self.gqT[:, :, tb:tb + 512].rearrange("h p t -> p h t")), f"gqT{i}", writes=[f"gqT{i}"])
                sc.dma("sp", lambda e: e.dma_start(out=kT[i][:], in_=self.gkT[:, :, tb:tb + 512].rearrange("h p t -> p h t")), f"gkT{i}", writes=[f"gkT{i}"])
                sc.dma("sp", lambda e: e.dma_start(out=ktm[i][:], in_=self.tm["gk"][tb:tb + 512, :].rearrange("(a p) c -> p a c", p=128)), f"gktm{i}", writes=[f"gktm{i}"])
                sc.dma("sp", lambda e: e.dma_start(out=vv[i][:], in_=self.tm["gv"][tb:tb + 512, :].rearrange("(a p) c -> p a c", p=128)), f"gv{i}", writes=[f"gv{i}"])

            order = list(range(NSS)) if fwd else list(range(NSS - 1, -1, -1))
            load_ss(order[0], 0)
            for n, ss in enumerate(order):
                i = n % 2
                if n + 1 < NSS:
                    load_ss(order[n + 1], (n + 1) % 2)
                subs = [0, 1, 2, 3] if fwd else [3, 2, 1, 0]
                for m, sub in enumerate(subs):
                    pi = (n * 4 + m) % 2
                    tok = t0 + ss * 512 + sub * 128
                    tsl = slice(sub * 128, (sub + 1) * 128)
                    if fwd:
                        sc.dma("sp", lambda e, pi=pi, tok=tok: e.dma_start(out=obt[pi][:], in_=self.ob[tok:tok + 128, :]), f"obt{pi}", writes=[f"obt{pi}"])
                        sc.dma("sp", lambda e, pi=pi, tok=tok: e.dma_start(out=grt[pi][:], in_=self.tm["gr"][tok:tok + 128, :]), f"grt{pi}", writes=[f"grt{pi}"])
                        sc.dma("sp", lambda e, pi=pi, tok=tok: e.dma_start(out=gat[pi][:], in_=self.tm["ga"][tok:tok + 128, :]), f"gat{pi}", writes=[f"gat{pi}"])
                    sc.op("pe", lambda e, i=i, tsl=tsl: e.matmul(bA[:], lhsT=glra[i][0:17, tsl], rhs=wgk[0:17, :], start=True, stop=True),
                          reads=[f"glra{i}", "wgk"], writes=["bA"])
                    sc.op("act", lambda e: e.activation(out=e1[:], in_=bA[:], func=AF.Exp, scale=-1.0), reads=["bA"], writes=["e1"])
                    sc.op("act", lambda e: e.activation(out=sp[:], in_=e1[:], func=AF.Ln, bias=self.eps_t[:, 2:3]), reads=["e1"], writes=["sp"])
                    if self.cut <= 1:
                        continue
                    sc.op("pe", lambda e: e.matmul(bA[:], lhsT=Ustr[:], rhs=sp[:], start=True, stop=True), reads=["Ustr", "sp"], writes=["bA"])
                    sc.op("act", lambda e: e.activation(out=ebl[:], in_=bA[:], func=AF.Exp), reads=["bA"], writes=["ebl"])
                    sc.op("dve", lambda e, i=i, sub=sub: e.tensor_tensor(out=ks[:], in0=ktm[i][:, sub, :], in1=ebl[:], op=ALU.mult),
                          reads=[f"gktm{i}", "ebl"], writes=["ks"])
                    if self.cut <= 2:
                        continue
                    for h in range(4):
                        sc.op("pe", lambda e, h=h: e.matmul(bB[:, h * 128:(h + 1) * 128], lhsT=sp[:, h * 128:(h + 1) * 128], rhs=Uinc[:], start=True, stop=True),
                              reads=["sp", "Uinc"], writes=["bB"])
                    sc.op("act", lambda e: e.activation(out=eb[:].rearrange("p h t -> p (h t)"), in_=bB[:], func=AF.Exp), reads=["bB"], writes=["eb"])
                    sc.op("act", lambda e: e.activation(out=enb[:].rearrange("p h t -> p (h t)"), in_=bB[:], func=AF.Exp, scale=-1.0), reads=["bB"], writes=["enb"])
                    if self.cut <= 3:
                        continue
                    sc.op("dve", lambda e, i=i, tsl=tsl: e.scalar_tensor_tensor(out=qt[:], in0=eb[:], scalar=qscale, in1=qT[i][:, :, tsl], op0=ALU.mult, op1=ALU.mult),
                          reads=["eb", f"gqT{i}"], writes=["qt"])
                    sc.op("dve", lambda e, i=i, tsl=tsl: e.tensor_tensor(out=kt[:], in0=enb[:], in1=kT[i][:, :, tsl], op=ALU.mult),
                          reads=["enb", f"gkT{i}"], writes=["kt"])
                    if self.cut <= 4:
                        continue
                    for h in range(4):
                        sc.op("pe", lambda e, h=h: e.matmul(bA[:, h * 128:(h + 1) * 128], lhsT=kt[:, h, :], rhs=qt[:, h, :], start=True, stop=True),
                              reads=["kt", "qt"], writes=["bA"])
                    sc.op("dve", lambda e: e.tensor_tensor(out=attm[:].rearrange("p h t -> p (h t)"), in0=bA[:], in1=mask4[:].rearrange("p h t -> p (h t)"), op=ALU.mult),
                          reads=["bA", "mask4"], writes=["attm"])
                    if self.cut <= 5:
                        continue
                    for h in range(4):
                        vs = slice(h * 256, (h + 1) * 256)
                        sc.op("pe", lambda e, h=h, i=i, sub=sub, vs=vs: e.matmul(bO[h][:, 0:256], lhsT=attm[:, h, :], rhs=vv[i][:, sub, vs], start=True, stop=False),
                              reads=["attm", f"gv{i}"], writes=[f"bO{h}"])
                        sc.op("pe", lambda e, h=h: e.matmul(bO[h][r1, 0:256], lhsT=qt[:, h, r1], rhs=Sbf[h][:], start=False, stop=True),
                              reads=["qt", f"Sbf_{h}"], writes=[f"bO{h}"])
                    for h in range(4):
                        vs = slice(h * 256, (h + 1) * 256)
                        kb = h % 2
                        if self.cut <= 6:
                            continue
                        sc.op("pe", lambda e, h=h, i=i, sub=sub, vs=vs, kb=kb: e.matmul(bKV[0][:, kb * 256:(kb + 1) * 256], lhsT=ks[r1, h * 128:(h + 1) * 128], rhs=vv[i][r1, sub, vs], start=True, stop=True),
                              reads=["ks", f"gv{i}"], writes=["bKV0"])
                        sc.op("pe", lambda e, h=h, i=i, sub=sub, vs=vs, kb=kb: e.matmul(bKV[1][:, kb * 256:(kb + 1) * 256], lhsT=ks[r2, h * 128:(h + 1) * 128], rhs=vv[i][r2, sub, vs], start=True, stop=True),
                              reads=["ks", f"gv{i}"], writes=["bKV1"])
                        if self.cut <= 7:
                            continue
                        sc.op("dve", lambda e, h=h, kb=kb: e.scalar_tensor_tensor(out=S32[h][:], in0=S32[h][:], scalar=eb[:, h, c1:c1 + 1], in1=bKV[0][:, kb * 256:(kb + 1) * 256], op0=ALU.mult, op1=ALU.add),
                              reads=[f"S32_{h}", "eb", "bKV0"], writes=[f"S32_{h}"])
                        if self.cut <= 8:
                            continue
                        sc.op("pool", lambda e, h=h: e.tensor_copy(out=Sbf[h][:], in_=S32[h][:]), reads=[f"S32_{h}"], writes=[f"Sbf_{h}"])
                        if self.cut <= 9:
                            continue
                        sc.op("pe", lambda e, h=h: e.matmul(bO[h][r2, 0:256], lhsT=qt[:, h, r2], rhs=Sbf[h][:], start=False, stop=True),
                              reads=["qt", f"Sbf_{h}"], writes=[f"bO{h}"])
                        sc.op("dve", lambda e, h=h, kb=kb: e.scalar_tensor_tensor(out=S32[h][:], in0=S32[h][:], scalar=eb[:, h, c2:c2 + 1], in1=bKV[1][:, kb * 256:(kb + 1) * 256], op0=ALU.mult, op1=ALU.add),
                              reads=[f"S32_{h}", "eb", "bKV1"], writes=[f"S32_{h}"])
                        sc.op("pool", lambda e, h=h: e.tensor_copy(out=Sbf[h][:], in_=S32[h][:]), reads=[f"S32_{h}"], writes=[f"Sbf_{h}"])
                        if fwd:
                            sc.op("dve", lambda e, h=h, pi=pi, vs=vs: e.tensor_tensor(out=osb[pi][:, vs], in0=bO[h][:, 0:256], in1=obt[pi][:, vs], op=ALU.add),
                                  reads=[f"bO{h}", f"obt{pi}"], writes=[f"osb{pi}"])
                        else:
                            sc.op("act", lambda e, h=h, pi=pi, vs=vs: e.activation(out=osb[pi][:, vs], in_=bO[h][:, 0:256], func=AF.Copy),
                                  reads=[f"bO{h}"], writes=[f"osb{pi}"])
                    if not fwd:
                        sc.dma("sp", lambda e, pi=pi, tok=tok: e.dma_start(out=self.ob[tok:tok + 128, :], in_=osb[pi][:]), f"osb{pi}",
                               reads=[f"osb{pi}"], writes=[("ob", tok)])
                    else:
                        for h in range(4):
                            vs = slice(h * 256, (h + 1) * 256)
                            sc.op("act", lambda e, h=h, pi=pi, vs=vs: e.activation(out=junk[:, h, :], in_=osb[pi][:, vs], func=AF.Square, accum_out=stt[pi][:, h:h + 1]),
                                  reads=[f"osb{pi}"], writes=[f"junk{h}", (f"gst{pi}", h)])
                        sc.op("act", lambda e, pi=pi: e.activation(out=stt[pi][:, 4:8], in_=stt[pi][:, 0:4], func=AF.Ln, scale=1.0 / GLA_DV, bias=self.eps_t[:, 0:1]),
                              reads=[(f"gst{pi}", h) for h in range(4)], writes=[f"gst{pi}"])
                        sc.op("act", lambda e, pi=pi: e.activation(out=stt[pi][:, 8:12], in_=stt[pi][:, 4:8], func=AF.Exp, scale=-0.5),
                              reads=[f"gst{pi}"], writes=[f"gst{pi}"])
                        sc.op("pool", lambda e, pi=pi: e.tensor_tensor(out=g32[:], in0=grt[pi][:], in1=gat[pi][:], op=ALU.mult),
                              reads=[f"grt{pi}", f"gat{pi}"], writes=["g32"])
                        sc.op("pool", lambda e: e.tensor_tensor(out=g32[:], in0=g32[:], in1=gnwb[:], op=ALU.mult), reads=["g32", "gnwb"], writes=["g32"])
                        for h in range(4):
                            vs = slice(h * 256, (h + 1) * 256)
                            sc.op("dve", lambda e, h=h, pi=pi, vs=vs: e.scalar_tensor_tensor(out=ysb[pi][:, vs], in0=osb[pi][:, vs], scalar=stt[pi][:, 8 + h:9 + h], in1=g32[:, vs], op0=ALU.mult, op1=ALU.mult),
                                  reads=[f"osb{pi}", f"gst{pi}", "g32"], writes=[f"ysb{pi}"])
                        sc.dma("sp", lambda e, pi=pi, tok=tok: e.dma_start(out=self.ya[tok:tok + 128, :], in_=ysb[pi][:]), f"ysb{pi}",
                               reads=[f"ysb{pi}"], writes=[("ya", tok)])
            sc.flush_wait_all()
            sc.emit(block)

    def phase_attn(self, sc, s):
        nc, S, NT, NG = self.nc, self.S, self.NT, self.NG
        t0 = s * S
        LAG = 2
        with ExitStack() as es, nc.Block() as block:
            KT = [self.sb(es, f"aKT{i}", [128, 2, S], BF16) for i in range(2)]
            VA = [self.sb(es, f"aVA{i}", [128, NT, 258], BF16) for i in range(2)]
            QT = [self.sb(es, f"aQT{i}", [128, 2, 512], BF16) for i in range(2)]
            PT = [self.sb(es, f"aPT{i}", [128, 512], BF16) for i in range(4)]
            A0 = self.sb(es, "aA0", [128, 4, 256], F32)
            o2 = self.sb(es, "ao2", [128, 4, 256], F32)
            gbt = [self.sb(es, f"agb{i}", [128, 4, 256], BF16) for i in range(2)]
            g32 = self.sb(es, "ag32", [128, 4, 256], F32)
            ybs = [self.sb(es, f"aybs{i}", [128, 4, 256], F32) for i in range(2)]
            slnb = self.sb(es, "aslnb", [128, 256], F32)
            lam = self.sb(es, "alam", [1, 512], F32)
            lt = self.sb(es, "alt", [1, 16], F32)
            ones1 = self.sb(es, "aones1", [1, 128], F32)
            nlam = self.sb(es, "anlam", [128, 1], F32)
            rr = self.sb(es, "arr", [128, 16], F32)
            stt = self.sb(es, "ast", [128, 12], F32)
            junk = self.sb(es, "ajunk", [128, 4, 256], F32)
            ACC = [self.ps(es, f"aACC{i}", [128, 512]) for i in range(4)]
            SB = [self.ps(es, f"aSB{i}", [128, 512]) for i in range(4)]

            sc.dma("sp", lambda e: e.dma_start(out=lam[:], in_=self.lam4[:, :]), "c_a", writes=["lam"])
            sc.dma("sp", lambda e: e.dma_start(out=slnb[:], in_=self.sln.broadcast_to([128, 256])), "c_b", writes=["slnb"])
            sc.op("dve", lambda e: e.memset(ones1[:], 1.0), writes=["ones1"])
            sc.op("dve", lambda e: e.tensor_tensor(out=lam[:, 0:128], in0=lam[:, 0:128], in1=lam[:, 128:256], op=ALU.mult), reads=["lam"], writes=["lam"])
            sc.op("dve", lambda e: e.tensor_tensor(out=lam[:, 256:384], in0=lam[:, 256:384], in1=lam[:, 384:512], op=ALU.mult), reads=["lam"], writes=["lam"])
            sc.op("act", lambda e: e.activation(out=lam[:, 128:256], in_=lam[:, 0:128], func=AF.Copy, accum_out=lt[:, 0:1]), reads=["lam"], writes=["lam", "lt"])
            sc.op("act", lambda e: e.activation(out=lam[:, 384:512], in_=lam[:, 256:384], func=AF.Copy, accum_out=lt[:, 1:2]), reads=["lam"], writes=["lam", "lt"])
            sc.op("act", lambda e: e.activation(out=lt[:, 2:4], in_=lt[:, 0:2], func=AF.Exp), reads=["lt"], writes=["lt"])
            sc.op("dve", lambda e: e.tensor_tensor(out=lt[:, 4:5], in0=lt[:, 3:4], in1=lt[:, 2:3], op=ALU.subtract), reads=["lt"], writes=["lt"])
            sc.op("dve", lambda e: e.tensor_scalar(out=lt[:, 5:6], in0=lt[:, 4:5], scalar1=-LAMBDA_INIT, scalar2=None, op0=ALU.add), reads=["lt"], writes=["lt"])
            sc.op("pe", lambda e: e.matmul(ACC[0][:, 0:1], lhsT=ones1[:], rhs=lt[:, 5:6], start=True, stop=True), reads=["ones1", "lt"], writes=["aACC0"])
            sc.op("dve", lambda e: e.tensor_copy(out=nlam[:], in_=ACC[0][:, 0:1]), reads=["aACC0"], writes=["nlam"])
            sc.op("dve", lambda e: e.tensor_scalar(out=slnb[:], in0=slnb[:], scalar1=float(1.0 - LAMBDA_INIT), scalar2=None, op0=ALU.mult), reads=["slnb"], writes=["slnb"])
            for i in range(2):
                sc.op("pool", lambda e, i=i: e.memset(VA[i][:, :, 256:258], 1.0), writes=[f"aVA{i}"])

            def load_head(h, i):
                for c in range(2):
                    sc.dma("sp", lambda e, c=c: e.dma_start(out=KT[i][:, c, :], in_=self.dkT[2 * h + c, :, t0:t0 + S]), f"aKT{i}", writes=[f"aKT{i}"])
                for a0 in range(0, NT, 8):
                    a1 = min(NT, a0 + 8)
                    sc.dma("sp", lambda e, a0=a0, a1=a1: e.dma_start(
                        out=VA[i][:, a0:a1, 0:256],
                        in_=self.tm["dv"][t0 + a0 * 128:t0 + a1 * 128, h * 256:(h + 1) * 256].rearrange("(a p) c -> p a c", p=128)),
                        f"aVA{i}", writes=[f"aVA{i}"])

            scale = float(DIFF_HD ** -0.5)
            load_head(0, 0)
            cnt = 0
            for h in range(DIFF_H):
                hi = h % 2
                if h + 1 < DIFF_H:
                    load_head(h + 1, (h + 1) % 2)
                for qg in range(NG):
                    qi = cnt % 2
                    cnt += 1
                    tb = t0 + qg * 512
                    for c in range(2):
                        sc.dma("sp", lambda e, c=c, qi=qi, tb=tb, h=h: e.dma_start(out=QT[qi][:, c, :], in_=self.dqT[2 * h + c, :, tb:tb + 512]), f"aQT{qi}", writes=[f"aQT{qi}"])
                    sc.dma("sp", lambda e, qi=qi, tb=tb, h=h: e.dma_start(out=gbt[qi][:], in_=self.tm["gb"][tb:tb + 512, h * 256:(h + 1) * 256].rearrange("(a p) c -> p a c", p=128)),
                           f"agb{qi}", writes=[f"agb{qi}"])
                    for c in range(2):
                        for step in range(NT + LAG):
                            if step < NT:
                                k_ = step
                                b_ = k_ % 4
                                sc.op("pe", lambda e, c=c, k_=k_, b_=b_, hi=hi, qi=qi: e.matmul(SB[b_][:], lhsT=KT[hi][:, c, k_ * 128:(k_ + 1) * 128], rhs=QT[qi][:, c, :], start=True, stop=True),
                                      reads=[f"aKT{hi}", f"aQT{qi}"], writes=[f"aSB{b_}"])
                                sc.op("act", lambda e, b_=b_: e.activation(out=PT[b_][:], in_=SB[b_][:], func=AF.Exp, scale=scale, bias=self.eps_t[:, 3:4]),
                                      reads=[f"aSB{b_}"], writes=[f"aPT{b_}"])
                            if step >= LAG:
                                k_ = step - LAG
                                b_ = k_ % 4
                                for qs in range(4):
                                    sc.op("pe", lambda e, k_=k_, b_=b_, qs=qs, hi=hi: e.matmul(ACC[qs][:, 0:257], lhsT=PT[b_][:, qs * 128:(qs + 1) * 128], rhs=VA[hi][:, k_, 0:257],
                                                                                              start=(k_ == 0), stop=(k_ == NT - 1)),
                                          reads=[f"aPT{b_}", f"aVA{hi}"], writes=[f"aACC{qs}"])
                        for qs in range(4):
                            col = c * 4 + qs
                            sc.op("dve", lambda e, qs=qs, col=col: e.reciprocal(out=rr[:, col:col + 1], in_=ACC[qs][:, 256:257]), reads=[f"aACC{qs}"], writes=[("rr", col)])
                            if c == 0:
                                sc.op("dve", lambda e, qs=qs, col=col: e.tensor_scalar(out=A0[:, qs, :], in0=ACC[qs][:, 0:256], scalar1=rr[:, col:col + 1], scalar2=None, op0=ALU.mult),
                                      reads=[f"aACC{qs}", ("rr", col)], writes=[("A0", qs)])
                            else:
                                sc.op("dve", lambda e, col=col: e.tensor_tensor(out=rr[:, 8 + col:9 + col], in0=rr[:, col:col + 1], in1=nlam[:], op=ALU.mult), reads=[("rr", col), "nlam"], writes=[("rr", 8 + col)])
                                sc.op("dve", lambda e, qs=qs, col=col: e.scalar_tensor_tensor(out=o2[:, qs, :], in0=ACC[qs][:, 0:256], scalar=rr[:, 8 + col:9 + col], in1=A0[:, qs, :], op0=ALU.mult, op1=ALU.add),
                                      reads=[f"aACC{qs}", ("rr", 8 + col), ("A0", qs)], writes=[("o2", qs)])
                    for qs in range(4):
                        sc.op("act", lambda e, qs=qs: e.activation(out=junk[:, qs, :], in_=o2[:, qs, :], func=AF.Square, accum_out=stt[:, qs:qs + 1]), reads=[("o2", qs)], writes=[f"ajunk{qs}", ("ast", qs)])
                    sc.op("act", lambda e: e.activation(out=stt[:, 4:8], in_=stt[:, 0:4], func=AF.Ln, scale=1.0 / 256, bias=self.eps_t[:, 1:2]), reads=[("ast", q_) for q_ in range(4)], writes=["ast"])
                    sc.op("act", lambda e: e.activation(out=stt[:, 8:12], in_=stt[:, 4:8], func=AF.Exp, scale=-0.5), reads=["ast"], writes=["ast"])
                    for qs in range(4):
                        sc.op("pool", lambda e, qs=qs, qi=qi: e.tensor_tensor(out=g32[:, qs, :], in0=gbt[qi][:, qs, :], in1=slnb[:], op=ALU.mult), reads=[f"agb{qi}", "slnb"], writes=[("ag32", qs)])
                    for qs in range(4):
                        sc.op("dve", lambda e, qs=qs, qi=qi: e.scalar_tensor_tensor(out=ybs[qi][:, qs, :], in0=o2[:, qs, :], scalar=stt[:, 8 + qs:9 + qs], in1=g32[:, qs, :], op0=ALU.mult, op1=ALU.mult),
                              reads=[("o2", qs), "ast", ("ag32", qs)], writes=[(f"aybs{qi}", qs)])
                    sc.dma("sp", lambda e, qi=qi, tb=tb, h=h: e.dma_start(out=self.yb[tb:tb + 512, h * 256:(h + 1) * 256].rearrange("(a p) c -> p a c", p=128), in_=ybs[qi][:]),
                           f"aybs{qi}", reads=[(f"aybs{qi}", q_) for q_ in range(4)], writes=[("yb", tb, h)])
            sc.flush_wait_all()
            sc.emit(block)

    def phase_wprep(self, sc):
        nc = self.nc
        with ExitStack() as es, nc.Block() as block:
            wt = self.sb(es, "wprep", [128, KC, 2 * FFN], BF16)
            src = self.w_fi.rearrange("(k p) c -> p k c", p=128)
            for kc in range(KC):
                sc.dma("pool", lambda e, kc=kc: e.dma_start(out=wt[:, kc, :], in_=src[:, kc, :]), f"wp{kc % 4}", writes=[("wprep", kc)])
            allk = [("wprep", kc) for kc in range(KC)]
            for f in range(FC):
                for g in range(2):
                    c0 = g * FFN + f * 128
                    sc.dma("sp", lambda e, f=f, g=g, c0=c0: e.dma_start(out=self.wfib[f, :, g, :, :], in_=wt[:, :, c0:c0 + 128]), f"wps{(2 * f + g) % 4}",
                           reads=allk, writes=[("wfib", f)])
            sc.flush_wait_all()
            sc.emit(block)

    def phase_ffn(self, sc):
        nc, S, T = self.nc, self.S, self.T
        NGT = T // 512
        with ExitStack() as es, nc.Block() as block:
            wo = self.sb(es, "f_wo", [128, KC, D], BF16)
            wfo = self.sb(es, "f_wfo", [128, FC, D], BF16)
            wfi = [self.sb(es, f"f_wfi{i}", [128, 2, KC, 128], BF16) for i in range(4)]
            ident = self.sb(es, "f_ident", [128, 128], BF16)
            identf = self.sb(es, "f_identf", [128, 128], F32)
            nwf = self.sb(es, "f_nwf", [128, KC], F32)
            nwfin = self.sb(es, "f_nwfin", [128, D], F32)
            la = [self.sb(es, f"f_la{i}", [128, D], F32) for i in range(2)]
            lb = [self.sb(es, f"f_lb{i}", [128, D], F32) for i in range(2)]
            lx = [self.sb(es, f"f_lx{i}", [128, D], F32) for i in range(2)]
            mb = [self.sb(es, f"f_mb{i}", [128, D], BF16) for i in range(2)]
            mT = self.sb(es, "f_mT", [128, KC, 512], BF16)
            hT = self.sb(es, "f_hT", [128, KC, 512], BF16)
            hsb = self.sb(es, "f_h", [128, 4, D], F32)
            aT = self.sb(es, "f_aT", [128, FC, 512], BF16)
            sg = [self.sb(es, f"f_sg{i}", [128, 512], F32) for i in range(2)]
            ot = [self.sb(es, f"f_ot{i}", [128, D], F32) for i in range(2)]
            stt = [self.sb(es, f"f_st{i}", [128, 4], F32) for i in range(2)]
            psT = [self.ps(es, f"f_psT{i}", [128, 1024], BF16) for i in range(2)]
            pb = [self.ps(es, f"f_pb{i}", [128, 512]) for i in range(6)]

            sc.dma("pool", lambda e: e.dma_start(out=wo[:], in_=self.w_out.rearrange("(k p) c -> p k c", p=128)), "pw_a", writes=["wo"])
            for f0 in range(0, FC, 2):
                sc.dma("pool", lambda e, f0=f0: e.dma_start(out=wfo[:, f0:f0 + 2, :], in_=self.w_fo[f0 * 128:(f0 + 2) * 128, :].rearrange("(k p) c -> p k c", p=128)),
                       f"pw_b{(f0 // 2) % 2}", writes=[("wfo", f0)])
            wfo_all = [("wfo", f0) for f0 in range(0, FC, 2)]
            sc.dma("sp", lambda e: e.dma_start(out=identf[:], in_=self.cst[:, 0:128]), "c_c", writes=["identf"])
            sc.op("dve", lambda e: e.tensor_copy(out=ident[:], in_=identf[:]), reads=["identf"], writes=["ident"])
            sc.dma("sp", lambda e: e.dma_start(out=nwf[:], in_=self.nw_ffn[:, :]), "c_d", writes=["nwf"])
            sc.dma("sp", lambda e: e.dma_start(out=nwfin[:], in_=self.nw_fin.broadcast_to([128, D])), "c_e", writes=["nwfin"])

            xf = self.x.rearrange("n s d -> (n s) d")
            yf = self.y.rearrange("n s d -> (n s) d")
            st8 = dict(pbi=0, li=0, wi=0, sgi=0, oti=0, pti=0)

            def next_pb():
                j = st8["pbi"] % 6
                st8["pbi"] += 1
                return j

            def norm_rows(src_ap, src_keys, i):
                sc.op("act", lambda e: e.activation(out=mb[i][:], in_=src_ap, func=AF.Square, accum_out=stt[i][:, 0:1]), reads=src_keys, writes=[f"mb{i}", f"fst{i}"])
                sc.op("act", lambda e: e.activation(out=stt[i][:, 1:2], in_=stt[i][:, 0:1], func=AF.Ln, scale=1.0 / D, bias=self.eps_t[:, 0:1]), reads=[f"fst{i}"], writes=[f"fst{i}"])
                sc.op("act", lambda e: e.activation(out=stt[i][:, 2:3], in_=stt[i][:, 1:2], func=AF.Exp, scale=-0.5), reads=[f"fst{i}"], writes=[f"fst{i}"])

            def transpose_to(dstT, dkey, i, tt_, scal):
                pt = st8["pti"] % 2
                st8["pti"] += 1
                for kc in range(KC):
                    sc.op("pe", lambda e, kc=kc: e.transpose(psT[pt][:, kc * 128:(kc + 1) * 128], mb[i][:, kc * 128:(kc + 1) * 128], ident[:]),
                          reads=[f"mb{i}", "ident"], writes=[f"f_psT{pt}"])
                for kc in range(KC):
                    if scal is None:
                        sc.op("dve", lambda e, kc=kc: e.tensor_copy(out=dstT[:, kc, tt_ * 128:(tt_ + 1) * 128], in_=psT[pt][:, kc * 128:(kc + 1) * 128]),
                              reads=[f"f_psT{pt}"], writes=[(dkey, tt_)])
                    else:
                        sc.op("dve", lambda e, kc=kc: e.tensor_scalar(out=dstT[:, kc, tt_ * 128:(tt_ + 1) * 128], in0=psT[pt][:, kc * 128:(kc + 1) * 128],
                                                                       scalar1=scal[:, kc:kc + 1], scalar2=None, op0=ALU.mult),
                              reads=[f"f_psT{pt}", "nwf"], writes=[(dkey, tt_)])

            for g in range(NGT):
                tb = g * 512
                for tt_ in range(4):
                    i = st8["li"] % 2
                    st8["li"] += 1
                    r0 = tb + tt_ * 128
                    sc.dma("sp", lambda e, i=i, r0=r0: e.dma_start(out=la[i][:], in_=self.ya[r0:r0 + 128, :]), f"f_la{i}", writes=[f"la{i}"])
                    sc.dma("sp", lambda e, i=i, r0=r0: e.dma_start(out=lb[i][:], in_=self.yb[r0:r0 + 128, :]), f"f_lb{i}", writes=[f"lb{i}"])
                    sc.op("pool", lambda e, i=i: e.tensor_tensor(out=mb[i][:], in0=la[i][:], in1=lb[i][:], op=ALU.add), reads=[f"la{i}", f"lb{i}"], writes=[f"mb{i}"])
                    transpose_to(mT, "mT", i, tt_, None)
                for tt_ in range(4):
                    i = st8["li"] % 2
                    st8["li"] += 1
                    r0 = tb + tt_ * 128
                    sc.dma("sp", lambda e, i=i, r0=r0: e.dma_start(out=lx[i][:], in_=xf[r0:r0 + 128, :]), f"f_lx{i}", writes=[f"lx{i}"])
                    for nh in range(2):
                        j = next_pb()
                        for kc in range(KC):
                            sc.op("pe", lambda e, kc=kc, j=j, nh=nh, tt_=tt_: e.matmul(pb[j][:], lhsT=mT[:, kc, tt_ * 128:(tt_ + 1) * 128], rhs=wo[:, kc, nh * 512:(nh + 1) * 512],
                                                                                    start=(kc == 0), stop=(kc == KC - 1)),
                                  reads=[("mT", tt_), "wo"], writes=[f"f_pb{j}"])
                        sc.op("dve", lambda e, j=j, nh=nh, tt_=tt_, i=i: e.tensor_tensor(out=hsb[:, tt_, nh * 512:(nh + 1) * 512], in0=pb[j][:], in1=lx[i][:, nh * 512:(nh + 1) * 512], op=ALU.add),
                              reads=[f"f_pb{j}", f"lx{i}"], writes=[("h", tt_)])
                    norm_rows(hsb[:, tt_, :], [("h", tt_)], i)
                    sc.op("dve", lambda e, i=i, tt_=tt_: e.tensor_scalar(out=mb[i][:], in0=hsb[:, tt_, :], scalar1=stt[i][:, 2:3], scalar2=None, op0=ALU.mult),
                          reads=[("h", tt_), f"fst{i}"], writes=[f"mb{i}"])
                    transpose_to(hT, "hT", i, tt_, nwf)
                hT_all = [("hT", k) for k in range(4)]
                for f in range(FC):
                    wi = st8["wi"] % 4
                    st8["wi"] += 1
                    sc.dma("sp", lambda e, wi=wi, f=f: e.dma_start(out=wfi[wi][:], in_=self.wfib[f, :, :, :, :]), f"f_wfi{wi}", reads=[("wfib", f)], writes=[f"wfi{wi}"])
                    jg, ju = next_pb(), next_pb()
                    for gi, j in ((0, jg), (1, ju)):
                        for kc in range(KC):
                            sc.op("pe", lambda e, kc=kc, j=j, gi=gi, wi=wi: e.matmul(pb[j][:], lhsT=wfi[wi][:, gi, kc, :], rhs=hT[:, kc, :], start=(kc == 0), stop=(kc == KC - 1)),
                                  reads=[f"wfi{wi}"] + hT_all, writes=[f"f_pb{j}"])
                    si = st8["sgi"] % 2
                    st8["sgi"] += 1
                    sc.op("act", lambda e, jg=jg, si=si: e.activation(out=sg[si][:], in_=pb[jg][:], func=AF.Silu), reads=[f"f_pb{jg}"], writes=[f"sg{si}"])
                    sc.op("dve", lambda e, ju=ju, si=si, f=f: e.tensor_tensor(out=aT[:, f, :], in0=pb[ju][:], in1=sg[si][:], op=ALU.mult),
                          reads=[f"f_pb{ju}", f"sg{si}"], writes=[("aT", f)])
                aT_all = [("aT", f) for f in range(FC)]
                for tt_ in range(4):
                    oi = st8["oti"] % 2
                    st8["oti"] += 1
                    r0 = tb + tt_ * 128
                    for nh in range(2):
                        j = next_pb()
                        for f in range(FC):
                            sc.op("pe", lambda e, f=f, j=j, nh=nh, tt_=tt_: e.matmul(pb[j][:], lhsT=aT[:, f, tt_ * 128:(tt_ + 1) * 128], rhs=wfo[:, f, nh * 512:(nh + 1) * 512],
                                                                                  start=(f == 0), stop=(f == FC - 1)),
                                  reads=aT_all + wfo_all, writes=[f"f_pb{j}"])
                        sc.op("dve", lambda e, j=j, nh=nh, tt_=tt_: e.tensor_tensor(out=hsb[:, tt_, nh * 512:(nh + 1) * 512], in0=pb[j][:], in1=hsb[:, tt_, nh * 512:(nh + 1) * 512], op=ALU.add),
                              reads=[f"f_pb{j}", ("h", tt_)], writes=[("h", tt_)])
                    norm_rows(hsb[:, tt_, :], [("h", tt_)], oi)
                    sc.op("dve", lambda e, oi=oi, tt_=tt_: e.scalar_tensor_tensor(out=ot[oi][:], in0=hsb[:, tt_, :], scalar=stt[oi][:, 2:3], in1=nwfin[:], op0=ALU.mult, op1=ALU.mult),
                          reads=[("h", tt_), f"fst{oi}", "nwfin"], writes=[f"ot{oi}"])
                    sc.dma("sp", lambda e, oi=oi, r0=r0: e.dma_start(out=yf[r0:r0 + 128, :], in_=ot[oi][:]), f"f_ot{oi}", reads=[f"ot{oi}"], writes=[("y", r0)])
            sc.flush_wait_all()
            sc.emit(block)

    def build(self):
        nc = self.nc
        self.declare()
        with ExitStack() as es:
            sc = Sched(nc, es)
            self.eps_t = self.sb(es, "eps_t", [128, 4], F32)
            with nc.Block() as block:
                sc.op("dve", lambda e: e.memset(self.eps_t[:, 0:1], NORM_EPS), writes=["eps"])
                sc.op("dve", lambda e: e.memset(self.eps_t[:, 1:2], SUBLN_EPS), writes=["eps"])
                sc.op("dve", lambda e: e.memset(self.eps_t[:, 2:3], 1.0), writes=["eps"])
                sc.op("dve", lambda e: e.memset(self.eps_t[:, 3:4], 0.0), writes=["eps"])
                sc.emit(block)
            for s in range(self.NSEQ):
                if "p1" in self.phases:
                    self.phase1(sc, s)
                if "gla" in self.phases:
                    self.phase_gla(sc, s, 1)
                    self.phase_gla(sc, s, 0)
                if "attn" in self.phases:
                    self.phase_attn(sc, s)
            if "ffn" in self.phases:
                self.phase_wprep(sc)
                self.phase_ffn(sc)
        return nc


def host_consts(S):
    inv_freq = 1.0 / (10000.0 ** (np.arange(0, DIFF_HD, 2, dtype=np.float32) / DIFF_HD))
    pos = np.arange(S, dtype=np.float32)
    fr = pos[None, :] * np.concatenate([inv_freq, inv_freq])[:, None].astype(np.float32)
    cosT = np.cos(fr).astype(np.float32)
    sgn = np.where(np.arange(128) < 64, -1.0, 1.0).astype(np.float32)[:, None]
    sinT = (np.sin(fr) * sgn).astype(np.float32)
    p = np.arange(128)
    same = (p[:, None] // 64) == (p[None, :] // 64)
    sI, tI = p[:, None], p[None, :]
    g = -1.0 / 16.0
    ident = np.eye(128, dtype=np.float32)
    uinc_f = (same & (sI <= tI)) * g
    ustr_f = (same & (sI > tI)) * g
    uinc_b = (same & (sI >= tI)) * g
    ustr_b = (same & (sI < tI)) * g
    cst = np.concatenate([ident, uinc_f, ustr_f, uinc_b, ustr_b, np.zeros((128, 128))], axis=1).astype(np.float32)
    m_f = (same & (sI <= tI)).astype(np.float32)
    m_b = (same & (sI >= tI)).astype(np.float32)
    msk = np.concatenate([m_f, m_b], axis=1).astype(np.float32)
    return cosT, sinT, cst, msk


def host_inputs(inp, S):
    c = np.ascontiguousarray
    w_in = np.asarray(inp["w_in"])[0]
    w1 = c(w_in[:, _w1_columns()])
    cosT, sinT, cst, msk = host_consts(S)
    wgk = np.concatenate([np.asarray(inp["w_gk2"])[0], np.asarray(inp["b_gk"])[0][:, None, :]], axis=1)
    shared = dict(
        w1=w1,
        nw_mix=c(np.asarray(inp["norm_mix_w"])[0].reshape(KC, 128).T),
        nw_ffn=c(np.asarray(inp["norm_ffn_w"])[0].reshape(KC, 128).T),
        nw_fin=c(np.asarray(inp["norm_final_w"]).reshape(1, D)),
        cosT=cosT, sinT=sinT, cst=cst, msk=msk,
        wgk=c(wgk.astype(np.float32)),
        gnw=c(np.tile(np.asarray(inp["gla_norm_w"])[0], 4).reshape(1, 1024)),
        sln=c(np.asarray(inp["diff_subln_w"])[0].reshape(1, 256)),
        lam4=c(np.concatenate([np.asarray(inp[k])[0] for k in ("lambda_q1", "lambda_k1", "lambda_q2", "lambda_k2")]).reshape(1, 512)),
        w_out=c(np.asarray(inp["w_out"])[0]),
        w_fi=c(np.asarray(inp["w_ffn_in"])[0]),
        w_fo=c(np.asarray(inp["w_ffn_out"])[0]),
    )
    return {k: np.asarray(v, dtype=np.float32) for k, v in shared.items()}


def kernel(**inputs):
    x = np.asarray(inputs["x"], dtype=np.float32)
    B, S, _ = x.shape
    nseq = B // N_CORES
    b = Builder(S, nseq)
    nc = b.build()
    shared = host_inputs(inputs, S)
    in_maps = [dict(shared, x=np.ascontiguousarray(x[i * nseq:(i + 1) * nseq])) for i in range(N_CORES)]
    res = run_bass_kernel_spmd(nc, in_maps, core_ids=list(range(N_CORES)))
    return np.concatenate([np.asarray(r["y"]) for r in res.results], axis=0).astype(np.float32)
```

```python
from contextlib import ExitStack
import math
import numpy as np
import ml_dtypes
import concourse.bass as bass
import concourse.mybir as mybir
from concourse.bass_utils import run_bass_kernel_spmd

F32 = mybir.dt.float32
BF16 = mybir.dt.bfloat16
AF = mybir.ActivationFunctionType
ALU = mybir.AluOpType

D = 1024
KC = D // 128
GLA_H, GLA_DK, GLA_DV, GLA_RANK = 4, 128, 256, 16
DIFF_H, DIFF_HD = 4, 128
FFN = 2816
FC = FFN // 128
NORM_EPS = 1e-6
SUBLN_EPS = 1e-5
LAMBDA_INIT = 0.8 - 0.6 * math.exp(-0.3 * 0)
N_CORES = 8

OFF_GQ, OFF_GK, OFF_GV, OFF_GR, OFF_GLR = 0, 512, 1024, 2048, 3072
OFF_DQ, OFF_DK, OFF_DV, OFF_GA, OFF_GB = 3104, 4128, 5152, 6176, 7200


def _w1_columns():
    cols = []
    cols += list(range(OFF_GQ, OFF_GQ + 512))
    cols += list(range(OFF_GK, OFF_GK + 512))
    for off in (OFF_DQ, OFF_DK):
        for hc in range(8):
            base = off + hc * 128
            cols += list(range(base, base + 128))
            cols += list(range(base + 64, base + 128)) + list(range(base, base + 64))
    cols += list(range(OFF_GLR, OFF_GLR + 32))
    cols += list(range(OFF_GK, OFF_GK + 512))
    for off in (OFF_GV, OFF_GR, OFF_DV, OFF_GA, OFF_GB):
        cols += list(range(off, off + 1024))
    return np.array(cols, dtype=np.int64)


W1_COLS = 512 * 10 + 32 + 512 * 11
FM_BLOCKS = [("gq", 0), ("gk", 512)] + [("dq", 1024 + 512 * i) for i in range(4)] + \
            [("dk", 3072 + 512 * i) for i in range(4)]
GLR_OFF = 5120
TM_OFF = 5152
TM_BLOCKS = [("gk", 0)] + [(n, j) for n in ("gv", "gr", "dv", "ga", "gb") for j in range(2)]


class Sched:
    CE = ("pe", "act", "dve", "pool")
    ALL = ("pe", "act", "dve", "pool", "sp")

    def __init__(self, nc, es):
        self.nc = nc
        self.prog = {e: es.enter_context(nc.semaphore("prog_" + e)) for e in self.CE}
        self.cum = {e: 0 for e in self.CE}
        self.dsem = {}
        self.es = es
        self.res_w = {}
        self.res_r = {}
        self.ops = {e: [] for e in self.ALL}
        self.waited = {e: {} for e in self.ALL}
        self.sigval = {}
        self.uid = 0

    def _slot(self, slot):
        if slot not in self.dsem:
            self.dsem[slot] = [self.es.enter_context(self.nc.semaphore("d_" + slot)), 0]
        return self.dsem[slot]

    def _collect(self, eng, reads, writes, is_dma):
        deps = []
        for k in reads:
            for ev in self.res_w.get(k, ()):
                deps.append((ev, True))
        for k in writes:
            for ev in self.res_w.get(k, ()):
                deps.append((ev, False))
            for ev in self.res_r.get(k, ()):
                deps.append((ev, False))
        out = []
        for ev, raw in deps:
            if ev[0] == "eng" and ev[1] == eng and not is_dma:
                if eng == "pe":
                    continue
            out.append(ev)
        return out

    def _update(self, ev, reads, writes):
        for k in writes:
            self.res_w[k] = [ev]
            self.res_r[k] = []
        for k in reads:
            self.res_r.setdefault(k, []).append(ev)

    def op(self, eng, fn, reads=(), writes=()):
        self.uid += 1
        ev = ("eng", eng, self.uid)
        deps = self._collect(eng, reads, writes, False)
        self.ops[eng].append(dict(fn=fn, deps=deps, ev=ev, dma=None))
        self._update(ev, reads, writes)
        return ev

    def dma(self, queue, fn, slot, reads=(), writes=()):
        s = self._slot(slot)
        deps = self._collect(queue, reads, writes, True)
        if s[1] > 0:
            deps.append(("dma", slot, s[1]))
        s[1] += 1
        ev = ("dma", slot, s[1])
        self.ops[queue].append(dict(fn=fn, deps=deps, ev=ev, dma=slot))
        self._update(ev, reads, writes)
        return ev

    def flush_wait_all(self):
        deps = [("dma", slot, s[1]) for slot, s in self.dsem.items() if s[1] > 0]
        self.ops["sp"].append(dict(fn=None, deps=deps, ev=None, dma=None))

    def emit(self, block):
        done = getattr(self, "done_uid", 0)
        for e in self.ALL:
            for o in self.ops[e]:
                o["deps"] = [d for d in o["deps"] if not (d[0] == "eng" and d[2] <= done)]
        self.done_uid = self.uid
        need = set()
        for e in self.ALL:
            for o in self.ops[e]:
                for d in o["deps"]:
                    if d[0] == "eng":
                        need.add(d[2])
        for e in self.CE:
            c = self.cum[e]
            for o in self.ops[e]:
                if o["dma"] is None and o["ev"] is not None and o["ev"][2] in need:
                    c += 1
                    self.sigval[o["ev"][2]] = c
            self.cum[e] = c
        starters = dict(pe=block.tensor, act=block.scalar, dve=block.vector, pool=block.gpsimd, sp=block.sync)
        for e in self.ALL:
            ops = self.ops[e]
            if not ops:
                continue
            waited = self.waited[e]

            def body(eng, ops=ops, waited=waited, e=e):
                for o in ops:
                    for d in o["deps"]:
                        if d[0] == "eng":
                            key, sem, val = d[1], self.prog[d[1]], self.sigval[d[2]]
                        else:
                            key, sem, val = "d_" + d[1], self.dsem[d[1]][0], 16 * d[2]
                        if waited.get(key, 0) >= val:
                            continue
                        eng.wait_ge(sem, val)
                        waited[key] = val
                    if o["fn"] is None:
                        continue
                    ins = o["fn"](eng)
                    if o["dma"] is not None:
                        ins.then_inc(self.dsem[o["dma"]][0], 16)
                    elif o["ev"][2] in self.sigval:
                        ins.then_inc(self.prog[e], 1)
            starters[e](body)
        self.ops = {e: [] for e in self.ALL}


class Builder:
    def __init__(self, S, NSEQ, debug=False, phases=("p1", "gla", "attn", "ffn"), scratch_in=False, cut=99):
        self.S, self.NSEQ, self.debug, self.phases = S, NSEQ, debug, phases
        self.scratch_in, self.cut = scratch_in, cut
        self.T = S * NSEQ
        self.NT = S // 128
        self.NG = S // 512
        self.nc = bass.Bass("TRN2", target_bir_lowering=False)

    def dram(self, name, shape, dt, kind):
        return self.nc.dram_tensor(name, list(shape), dt, kind=kind).ap()

    def declare(self):
        S, T, NSEQ = self.S, self.T, self.NSEQ
        I = "ExternalInput"
        self.x = self.dram("x", [NSEQ, S, D], F32, I)
        self.w1 = self.dram("w1", [D, W1_COLS], F32, I)
        self.nw_mix = self.dram("nw_mix", [128, KC], F32, I)
        self.nw_ffn = self.dram("nw_ffn", [128, KC], F32, I)
        self.nw_fin = self.dram("nw_fin", [1, D], F32, I)
        self.cosT = self.dram("cosT", [128, S], F32, I)
        self.sinT = self.dram("sinT", [128, S], F32, I)
        self.wgk = self.dram("wgk", [2, 17, 512], F32, I)
        self.gnw = self.dram("gnw", [1, 1024], F32, I)
        self.sln = self.dram("sln", [1, 256], F32, I)
        self.lam4 = self.dram("lam4", [1, 512], F32, I)
        self.w_out = self.dram("w_out", [D, D], F32, I)
        self.w_fi = self.dram("w_fi", [D, 2 * FFN], F32, I)
        self.w_fo = self.dram("w_fo", [FFN, D], F32, I)
        self.cst = self.dram("cst", [128, 6 * 128], F32, I)
        self.msk = self.dram("msk", [128, 2 * 128], F32, I)
        self.y = self.dram("y", [NSEQ, S, D], F32, "ExternalOutput")
        K = "ExternalOutput" if self.debug else "Internal"
        K1 = K
        if self.scratch_in:
            K = "ExternalInput"
        self.gqT = self.dram("gqT", [4, 128, T], BF16, K)
        self.gkT = self.dram("gkT", [4, 128, T], BF16, K)
        self.dqT = self.dram("dqT", [8, 128, T], BF16, K)
        self.dkT = self.dram("dkT", [8, 128, T], BF16, K)
        self.glrT = self.dram("glrT", [32, T], BF16, K)
        self.tm = {n: self.dram("tm_" + n, [T, (512 if n == "gk" else 1024)], BF16, K)
                   for n in ("gk", "gv", "gr", "dv", "ga", "gb")}
        K = K1
        self.ob = self.dram("ob", [T, 1024], F32, K)
        self.ya = self.dram("ya", [T, 1024], F32, K)
        self.yb = self.dram("yb", [T, 1024], F32, K)
        self.wfib = self.dram("wfib", [FC, 128, 2, KC, 128], BF16, "Internal")

    def _uniq(self, name):
        self._n = getattr(self, "_n", 0) + 1
        return f"{name}_{self._n}"

    def sb(self, es, name, shape, dt):
        return es.enter_context(self.nc.sbuf_tensor(self._uniq(name), list(shape), dt))

    def ps(self, es, name, shape, dt=F32):
        return es.enter_context(self.nc.psum_tensor(self._uniq(name), list(shape), dt))

    def phase1(self, sc, s):
        nc, S, NT, NG = self.nc, self.S, self.NT, self.NG
        t0 = s * S
        with ExitStack() as es, nc.Block() as block:
            uT = self.sb(es, "uT", [128, KC, S], BF16)
            cosT = self.sb(es, "cosT_sb", [128, S], F32)
            sinT = self.sb(es, "sinT_sb", [128, S], F32)
            nwm = self.sb(es, "nwm", [128, KC], F32)
            ident = self.sb(es, "ident", [128, 128], BF16)
            identf = self.sb(es, "identf", [128, 128], F32)
            xt = [self.sb(es, f"xt{i}", [128, D], F32) for i in range(2)]
            xn = [self.sb(es, f"xn{i}", [128, D], BF16) for i in range(2)]
            st = [self.sb(es, f"st{i}", [128, 4], F32) for i in range(2)]
            wb = [self.sb(es, f"wb{i}", [128, KC, 512], BF16) for i in range(2)]
            so = [self.sb(es, f"so{i}", [128, 512], BF16) for i in range(4)]
            sof = [self.sb(es, f"sof{i}", [32, 512], BF16) for i in range(2)]
            r1 = [self.sb(es, f"r1_{i}", [128, 512], F32) for i in range(2)]
            r2 = [self.sb(es, f"r2_{i}", [128, 512], F32) for i in range(2)]
            psT = [self.ps(es, f"psT{i}", [128, 1024], BF16) for i in range(2)]
            pb = [self.ps(es, f"pb{i}", [128, 512]) for i in range(6)]

            sc.dma("sp", lambda e: e.dma_start(out=cosT[:], in_=self.cosT[:, :]), "c_cos", writes=["cosT"])
            sc.dma("sp", lambda e: e.dma_start(out=sinT[:], in_=self.sinT[:, :]), "c_sin", writes=["sinT"])
            sc.dma("sp", lambda e: e.dma_start(out=nwm[:], in_=self.nw_mix[:, :]), "c_nwm", writes=["nwm"])
            sc.dma("sp", lambda e: e.dma_start(out=identf[:], in_=self.cst[:, 0:128]), "c_id", writes=["identf"])
            sc.op("dve", lambda e: e.tensor_copy(out=ident[:], in_=identf[:]), reads=["identf"], writes=["ident"])

            for tt in range(NT):
                i = tt % 2
                sc.dma("sp", lambda e, i=i, tt=tt: e.dma_start(out=xt[i][:], in_=self.x[s, tt * 128:(tt + 1) * 128, :]),
                       f"xt{i}", writes=[f"xt{i}"])
                sc.op("act", lambda e, i=i: e.activation(out=xn[i][:], in_=xt[i][:], func=AF.Square, accum_out=st[i][:, 0:1]),
                      reads=[f"xt{i}"], writes=[f"xn{i}", f"st{i}"])
                sc.op("act", lambda e, i=i: e.activation(out=st[i][:, 1:2], in_=st[i][:, 0:1], func=AF.Ln, scale=1.0 / D, bias=self.eps_t[:, 0:1]),
                      reads=[f"st{i}"], writes=[f"st{i}"])
                sc.op("act", lambda e, i=i: e.activation(out=st[i][:, 2:3], in_=st[i][:, 1:2], func=AF.Exp, scale=-0.5),
                      reads=[f"st{i}"], writes=[f"st{i}"])
                sc.op("dve", lambda e, i=i: e.tensor_scalar(out=xn[i][:], in0=xt[i][:], scalar1=st[i][:, 2:3], scalar2=None, op0=ALU.mult),
                      reads=[f"xt{i}", f"st{i}"], writes=[f"xn{i}"])
                for kc in range(KC):
                    sc.op("pe", lambda e, i=i, kc=kc: e.transpose(psT[i][:, kc * 128:(kc + 1) * 128], xn[i][:, kc * 128:(kc + 1) * 128], ident[:]),
                          reads=[f"xn{i}", "ident"], writes=[f"psT{i}"])
                for kc in range(KC):
                    sc.op("dve", lambda e, i=i, kc=kc, tt=tt: e.tensor_scalar(
                        out=uT[:, kc, tt * 128:(tt + 1) * 128], in0=psT[i][:, kc * 128:(kc + 1) * 128],
                        scalar1=nwm[:, kc:kc + 1], scalar2=None, op0=ALU.mult),
                        reads=[f"psT{i}", "nwm"], writes=[("uT", tt)])
            uT_all = [("uT", tt) for tt in range(NT)]

            state = dict(wbi=0, pbi=0, soi=0, ri=0, sofi=0)

            def load_w(col0, ncols):
                i = state["wbi"] % 2
                state["wbi"] += 1
                src = self.w1.rearrange("(k p) c -> p k c", p=128)[:, :, col0:col0 + ncols]
                sc.dma("pool", lambda e, i=i: e.dma_start(out=wb[i][:, :, 0:ncols], in_=src), f"wb{i}", writes=[f"wb{i}"])
                return i

            def next_pb():
                j = state["pbi"] % 6
                state["pbi"] += 1
                return j

            def store(dst_ap, src_tile_key, src_ap, slotname):
                sc.dma("sp", lambda e: e.dma_start(out=dst_ap, in_=src_ap), slotname, reads=[src_tile_key])

            def fm_group(wi, c, tg, j, M=128):
                for kc in range(KC):
                    sc.op("pe", lambda e, kc=kc: e.matmul(pb[j][0:M, :], lhsT=wb[wi][:, kc, c * 128:c * 128 + M],
                                                            rhs=uT[:, kc, tg * 512:(tg + 1) * 512], start=(kc == 0), stop=(kc == KC - 1)),
                          reads=[f"wb{wi}"] + uT_all[tg * 4:(tg + 1) * 4], writes=[f"pb{j}"])

            for name, col0 in FM_BLOCKS:
                wi = load_w(col0, 512)
                bidx = (col0 - {"gq": 0, "gk": 512, "dq": 1024, "dk": 3072}[name]) // 512
                if name in ("gq", "gk"):
                    dst = self.gqT if name == "gq" else self.gkT
                    for c in range(4):
                        for tg in range(NG):
                            j = next_pb()
                            fm_group(wi, c, tg, j)
                            k = state["soi"] % 4
                            state["soi"] += 1
                            if (c + tg) % 2 == 0:
                                sc.op("act", lambda e, j=j, k=k: e.activation(out=so[k][:], in_=pb[j][:], func=AF.Copy),
                                      reads=[f"pb{j}"], writes=[f"so{k}"])
                            else:
                                sc.op("dve", lambda e, j=j, k=k: e.tensor_copy(out=so[k][:], in_=pb[j][:]),
                                      reads=[f"pb{j}"], writes=[f"so{k}"])
                            store(dst[c, :, t0 + tg * 512:t0 + (tg + 1) * 512], f"so{k}", so[k][:], f"so{k}")
                else:
                    dst = self.dqT if name == "dq" else self.dkT
                    for pr in range(2):
                        hc = bidx * 2 + pr
                        for tg in range(NG):
                            ja, jb = next_pb(), next_pb()
                            fm_group(wi, 2 * pr, tg, ja)
                            fm_group(wi, 2 * pr + 1, tg, jb)
                            ri = state["ri"] % 2
                            state["ri"] += 1
                            k = state["soi"] % 4
                            state["soi"] += 1
                            tsl = slice(tg * 512, (tg + 1) * 512)
                            sc.op("dve", lambda e, ja=ja, ri=ri, tsl=tsl: e.tensor_tensor(out=r1[ri][:], in0=pb[ja][:], in1=cosT[:, tsl], op=ALU.mult),
                                  reads=[f"pb{ja}", "cosT"], writes=[f"r1_{ri}"])
                            sc.op("dve", lambda e, jb=jb, ri=ri, tsl=tsl: e.tensor_tensor(out=r2[ri][:], in0=pb[jb][:], in1=sinT[:, tsl], op=ALU.mult),
                                  reads=[f"pb{jb}", "sinT"], writes=[f"r2_{ri}"])
                            sc.op("pool", lambda e, ri=ri, k=k: e.tensor_tensor(out=so[k][:], in0=r1[ri][:], in1=r2[ri][:], op=ALU.add),
                                  reads=[f"r1_{ri}", f"r2_{ri}"], writes=[f"so{k}"])
                            store(dst[hc, :, t0 + tg * 512:t0 + (tg + 1) * 512], f"so{k}", so[k][:], f"so{k}")
            wi = load_w(GLR_OFF, 32)
            for tg in range(NG):
                j = next_pb()
                fm_group(wi, 0, tg, j, M=32)
                k = state["sofi"] % 2
                state["sofi"] += 1
                sc.op("dve", lambda e, j=j, k=k: e.tensor_copy(out=sof[k][:], in_=pb[j][0:32, :]), reads=[f"pb{j}"], writes=[f"sof{k}"])
                store(self.glrT[:, t0 + tg * 512:t0 + (tg + 1) * 512], f"sof{k}", sof[k][:], f"sof{k}")
            for bi, (name, jj) in enumerate(TM_BLOCKS):
                wi = load_w(TM_OFF + bi * 512, 512)
                dst = self.tm[name]
                for tt in range(NT):
                    j = next_pb()
                    for kc in range(KC):
                        sc.op("pe", lambda e, kc=kc, j=j, tt=tt, wi=wi: e.matmul(pb[j][:], lhsT=uT[:, kc, tt * 128:(tt + 1) * 128], rhs=wb[wi][:, kc, :],
                                                                        start=(kc == 0), stop=(kc == KC - 1)),
                              reads=[f"wb{wi}", ("uT", tt)], writes=[f"pb{j}"])
                    k = state["soi"] % 4
                    state["soi"] += 1
                    if name == "gr":
                        sc.op("act", lambda e, j=j, k=k: e.activation(out=so[k][:], in_=pb[j][:], func=AF.Silu), reads=[f"pb{j}"], writes=[f"so{k}"])
                    elif name in ("ga", "gb"):
                        sc.op("act", lambda e, j=j, k=k: e.activation(out=so[k][:], in_=pb[j][:], func=AF.Sigmoid), reads=[f"pb{j}"], writes=[f"so{k}"])
                    elif tt % 2 == 0:
                        sc.op("act", lambda e, j=j, k=k: e.activation(out=so[k][:], in_=pb[j][:], func=AF.Copy), reads=[f"pb{j}"], writes=[f"so{k}"])
                    else:
                        sc.op("dve", lambda e, j=j, k=k: e.tensor_copy(out=so[k][:], in_=pb[j][:]), reads=[f"pb{j}"], writes=[f"so{k}"])
                    store(dst[t0 + tt * 128:t0 + (tt + 1) * 128, jj * 512:(jj + 1) * 512], f"so{k}", so[k][:], f"so{k}")
            sc.flush_wait_all()
            sc.emit(block)


    def phase_gla(self, sc, s, dirn):
        nc, S = self.nc, self.S
        t0 = s * S
        NP, NSS = S // 128, S // 512
        fwd = dirn == 0
        with ExitStack() as es, nc.Block() as block:
            Uinc = self.sb(es, "Uinc", [128, 128], BF16)
            Ustr = self.sb(es, "Ustr", [128, 128], BF16)
            Uf = self.sb(es, "Uf", [128, 256], F32)
            wgkf = self.sb(es, "wgkf", [17, 512], F32)
            mask4 = self.sb(es, "mask4", [128, 4, 128], F32)
            wgk = self.sb(es, "wgk_sb", [17, 512], BF16)
            gnwb = self.sb(es, "gnwb", [128, 1024], F32)
            glra = [self.sb(es, f"glra{i}", [17, 512], BF16) for i in range(2)]
            qT = [self.sb(es, f"gqT{i}", [128, 4, 512], BF16) for i in range(2)]
            kT = [self.sb(es, f"gkT{i}", [128, 4, 512], BF16) for i in range(2)]
            ktm = [self.sb(es, f"gktm{i}", [128, 4, 512], BF16) for i in range(2)]
            vv = [self.sb(es, f"gv{i}", [128, 4, 1024], BF16) for i in range(2)]
            obt = [self.sb(es, f"obt{i}", [128, 1024], F32) for i in range(2)]
            grt = [self.sb(es, f"grt{i}", [128, 1024], BF16) for i in range(2)]
            gat = [self.sb(es, f"gat{i}", [128, 1024], BF16) for i in range(2)]
            e1 = self.sb(es, "g_e1", [128, 512], F32)
            sp = self.sb(es, "g_sp", [128, 512], BF16)
            ebl = self.sb(es, "g_ebl", [128, 512], F32)
            ks = self.sb(es, "g_ks", [128, 512], BF16)
            eb = self.sb(es, "g_eb", [128, 4, 128], F32)
            enb = self.sb(es, "g_enb", [128, 4, 128], F32)
            qt = self.sb(es, "g_qt", [128, 4, 128], BF16)
            kt = self.sb(es, "g_kt", [128, 4, 128], BF16)
            attm = self.sb(es, "g_attm", [128, 4, 128], BF16)
            S32 = [self.sb(es, f"S32_{h}", [128, 256], F32) for h in range(4)]
            Sbf = [self.sb(es, f"Sbf_{h}", [128, 256], BF16) for h in range(4)]
            osb = [self.sb(es, f"osb{i}", [128, 1024], F32) for i in range(2)]
            ysb = [self.sb(es, f"ysb{i}", [128, 1024], F32) for i in range(2)]
            g32 = self.sb(es, "g_g32", [128, 1024], F32)
            junk = self.sb(es, "g_junk", [128, 4, 256], F32)
            stt = [self.sb(es, f"g_st{i}", [128, 12], F32) for i in range(2)]
            bA = self.ps(es, "bA", [128, 512])
            bB = self.ps(es, "bB", [128, 512])
            bO = [self.ps(es, f"bO{h}", [128, 512]) for h in range(4)]
            bKV = [self.ps(es, f"bKV{i}", [128, 512]) for i in range(2)]

            co = 128 * (1 + 2 * dirn)
            sc.dma("sp", lambda e: e.dma_start(out=Uf[:], in_=self.cst[:, co:co + 256]), "c_a", writes=["Uf"])
            sc.op("dve", lambda e: e.tensor_copy(out=Uinc[:], in_=Uf[:, 0:128]), reads=["Uf"], writes=["Uinc"])
            sc.op("dve", lambda e: e.tensor_copy(out=Ustr[:], in_=Uf[:, 128:256]), reads=["Uf"], writes=["Ustr"])
            for h in range(4):
                sc.dma("sp", lambda e, h=h: e.dma_start(out=mask4[:, h, :], in_=self.msk[:, dirn * 128:(dirn + 1) * 128]), "c_c", writes=["mask4"])
            sc.dma("sp", lambda e: e.dma_start(out=wgkf[:], in_=self.wgk[dirn, :, :]), "c_d", writes=["wgkf"])
            sc.op("dve", lambda e: e.tensor_copy(out=wgk[:], in_=wgkf[:]), reads=["wgkf"], writes=["wgk"])
            sc.dma("sp", lambda e: e.dma_start(out=gnwb[:], in_=self.gnw.broadcast_to([128, 1024])), "c_e", writes=["gnwb"])
            for i in range(2):
                sc.op("dve", lambda e, i=i: e.memset(glra[i][:], 1.0), writes=[f"glra{i}"])
            for h in range(4):
                sc.op("dve", lambda e, h=h: e.memset(S32[h][:], 0.0), writes=[f"S32_{h}"])
                sc.op("dve", lambda e, h=h: e.memset(Sbf[h][:], 0.0), writes=[f"Sbf_{h}"])

            if fwd:
                r1, c1, r2, c2 = slice(0, 64), 63, slice(64, 128), 127
            else:
                r1, c1, r2, c2 = slice(64, 128), 64, slice(0, 64), 0
            qscale = float(GLA_DK ** -0.5)

            def load_ss(ss, i):
                tb = t0 + ss * 512
                sc.dma("sp", lambda e: e.dma_start(out=glra[i][0:16, :], in_=self.glrT[dirn * 16:(dirn + 1) * 16, tb:tb + 512]), f"glra{i}", writes=[f"glra{i}"])
                sc.dma("sp", lambda e: e.dma_start(out=qT[i][:], in_=self.gqT[:, :, tb:tb + 512].rearrange("h p t -> p h t")), f"gqT{i}", writes=[f"gqT{i}"])
                sc.dma("sp", lambda e: e.dma_start(out=kT[i][:], in_=self.gkT[:, :, tb:tb + 512].rearrange("h p t -> p h t")), f"gkT{i}", writes=[f"gkT{i}"])
                sc.dma("sp", lambda e: e.dma_start(out=ktm[i][:], in_=self.tm["gk"][tb:tb + 512, :].rearrange("(a p) c -> p a c", p=128)), f"gktm{i}", writes=[f"gktm{i}"])
                sc.dma("sp", lambda e: e.dma_start(out=vv[i][:], in_=self.tm["gv"][tb:tb + 512, :].rearrange("(a p) c -> p a c", p=128)), f"gv{i}", writes=[f"gv{i}"])

            ks2 = [ks, self.sb(es, "g_ks_b", [128, 512], BF16)]
            eb2 = [eb, self.sb(es, "g_eb_b", [128, 4, 128], F32)]
            qt2 = [qt, self.sb(es, "g_qt_b", [128, 4, 128], BF16)]
            attm2 = [attm, self.sb(es, "g_attm_b", [128, 4, 128], BF16)]

            order = list(range(NSS)) if fwd else list(range(NSS - 1, -1, -1))
            steps = []
            for n, ss in enumerate(order):
                for m, sub in enumerate([0, 1, 2, 3] if fwd else [3, 2, 1, 0]):
                    steps.append(dict(n=n, ss=ss, i=n % 2, sub=sub, pi=(n * 4 + m) % 2, q=(n * 4 + m) % 2,
                                      tok=t0 + ss * 512 + sub * 128, tsl=slice(sub * 128, (sub + 1) * 128), first=(m == 0)))

            def loads(st):
                if fwd:
                    pi, tok = st["pi"], st["tok"]
                    sc.dma("sp", lambda e: e.dma_start(out=obt[pi][:], in_=self.ob[tok:tok + 128, :]), f"obt{pi}", writes=[f"obt{pi}"])
                    sc.dma("sp", lambda e: e.dma_start(out=grt[pi][:], in_=self.tm["gr"][tok:tok + 128, :]), f"grt{pi}", writes=[f"grt{pi}"])
                    sc.dma("sp", lambda e: e.dma_start(out=gat[pi][:], in_=self.tm["ga"][tok:tok + 128, :]), f"gat{pi}", writes=[f"gat{pi}"])

            def pre_a(st):
                i, tsl = st["i"], st["tsl"]
                sc.op("pe", lambda e: e.matmul(bA[:], lhsT=glra[i][0:17, tsl], rhs=wgk[0:17, :], start=True, stop=True),
                      reads=[f"glra{i}", "wgk"], writes=["bA"])
                sc.op("act", lambda e: e.activation(out=e1[:], in_=bA[:], func=AF.Exp, scale=-1.0), reads=["bA"], writes=["e1"])
                sc.op("act", lambda e: e.activation(out=sp[:], in_=e1[:], func=AF.Ln, bias=self.eps_t[:, 2:3]), reads=["e1"], writes=["sp"])

            def pre_b(st):
                i, sub, q = st["i"], st["sub"], st["q"]
                sc.op("pe", lambda e: e.matmul(bA[:], lhsT=Ustr[:], rhs=sp[:], start=True, stop=True), reads=["Ustr", "sp"], writes=["bA"])
                for h in range(4):
                    sc.op("pe", lambda e, h=h: e.matmul(bB[:, h * 128:(h + 1) * 128], lhsT=sp[:, h * 128:(h + 1) * 128], rhs=Uinc[:], start=True, stop=True),
                          reads=["sp", "Uinc"], writes=["bB"])
                sc.op("act", lambda e: e.activation(out=ebl[:], in_=bA[:], func=AF.Exp), reads=["bA"], writes=["ebl"])
                sc.op("act", lambda e: e.activation(out=eb2[q][:].rearrange("p h t -> p (h t)"), in_=bB[:], func=AF.Exp), reads=["bB"], writes=[f"eb{q}"])
                sc.op("act", lambda e: e.activation(out=enb[:].rearrange("p h t -> p (h t)"), in_=bB[:], func=AF.Exp, scale=-1.0), reads=["bB"], writes=["enb"])

            def pre_c(st):
                i, sub, q, tsl = st["i"], st["sub"], st["q"], st["tsl"]
                sc.op("pool", lambda e: e.tensor_tensor(out=ks2[q][:], in0=ktm[i][:, sub, :], in1=ebl[:], op=ALU.mult),
                      reads=[f"gktm{i}", "ebl"], writes=[f"ks{q}"])
                sc.op("dve", lambda e: e.scalar_tensor_tensor(out=qt2[q][:], in0=eb2[q][:], scalar=qscale, in1=qT[i][:, :, tsl], op0=ALU.mult, op1=ALU.mult),
                      reads=[f"eb{q}", f"gqT{i}"], writes=[f"qt{q}"])
                sc.op("pool", lambda e: e.tensor_tensor(out=kt[:], in0=enb[:], in1=kT[i][:, :, tsl], op=ALU.mult),
                      reads=["enb", f"gkT{i}"], writes=["kt"])

            def pre_d(st):
                q = st["q"]
                for h in range(4):
                    sc.op("pe", lambda e, h=h: e.matmul(bA[:, h * 128:(h + 1) * 128], lhsT=kt[:, h, :], rhs=qt2[q][:, h, :], start=True, stop=True),
                          reads=["kt", f"qt{q}"], writes=["bA"])
                sc.op("dve", lambda e: e.tensor_tensor(out=attm2[q][:].rearrange("p h t -> p (h t)"), in0=bA[:], in1=mask4[:].rearrange("p h t -> p (h t)"), op=ALU.mult),
                      reads=["bA", "mask4"], writes=[f"attm{q}"])

            def scan_open(st):
                i, sub, q = st["i"], st["sub"], st["q"]
                for h in range(4):
                    vs = slice(h * 256, (h + 1) * 256)
                    sc.op("pe", lambda e, h=h, vs=vs: e.matmul(bO[h][:, 0:256], lhsT=attm2[q][:, h, :], rhs=vv[i][:, sub, vs], start=True, stop=False),
                          reads=[f"attm{q}", f"gv{i}"], writes=[f"bO{h}"])
                    sc.op("pe", lambda e, h=h: e.matmul(bO[h][r1, 0:256], lhsT=qt2[q][:, h, r1], rhs=Sbf[h][:], start=False, stop=True),
                          reads=[f"qt{q}", f"Sbf_{h}"], writes=[f"bO{h}"])

            def scan_head(st, h):
                i, sub, q, pi = st["i"], st["sub"], st["q"], st["pi"]
                vs = slice(h * 256, (h + 1) * 256)
                kb = h % 2
                ksl = slice(kb * 256, (kb + 1) * 256)
                hs = slice(h * 128, (h + 1) * 128)
                sc.op("pe", lambda e: e.matmul(bKV[0][:, ksl], lhsT=ks2[q][r1, hs], rhs=vv[i][r1, sub, vs], start=True, stop=True),
                      reads=[f"ks{q}", f"gv{i}"], writes=["bKV0"])
                sc.op("pe", lambda e: e.matmul(bKV[1][:, ksl], lhsT=ks2[q][r2, hs], rhs=vv[i][r2, sub, vs], start=True, stop=True),
                      reads=[f"ks{q}", f"gv{i}"], writes=["bKV1"])
                sc.op("dve", lambda e: e.scalar_tensor_tensor(out=Sbf[h][:], in0=S32[h][:], scalar=eb2[q][:, h, c1:c1 + 1], in1=bKV[0][:, ksl], op0=ALU.mult, op1=ALU.add),
                      reads=[f"S32_{h}", f"eb{q}", "bKV0"], writes=[f"Sbf_{h}"])
                sc.op("dve", lambda e: e.scalar_tensor_tensor(out=S32[h][:], in0=S32[h][:], scalar=eb2[q][:, h, c1:c1 + 1], in1=bKV[0][:, ksl], op0=ALU.mult, op1=ALU.add),
                      reads=[f"S32_{h}", f"eb{q}", "bKV0"], writes=[f"S32_{h}"])
                sc.op("pe", lambda e: e.matmul(bO[h][r2, 0:256], lhsT=qt2[q][:, h, r2], rhs=Sbf[h][:], start=False, stop=True),
                      reads=[f"qt{q}", f"Sbf_{h}"], writes=[f"bO{h}"])
                sc.op("dve", lambda e: e.scalar_tensor_tensor(out=S32[h][:], in0=S32[h][:], scalar=eb2[q][:, h, c2:c2 + 1], in1=bKV[1][:, ksl], op0=ALU.mult, op1=ALU.add),
                      reads=[f"S32_{h}", f"eb{q}", "bKV1"], writes=[f"S32_{h}"])
                sc.op("act", lambda e: e.activation(out=Sbf[h][:], in_=S32[h][:], func=AF.Copy), reads=[f"S32_{h}"], writes=[f"Sbf_{h}"])
                if fwd:
                    sc.op("dve", lambda e: e.tensor_tensor(out=osb[pi][:, vs], in0=bO[h][:, 0:256], in1=obt[pi][:, vs], op=ALU.add),
                          reads=[f"bO{h}", f"obt{pi}"], writes=[(f"osb{pi}", h)])
                else:
                    sc.op("act", lambda e: e.activation(out=osb[pi][:, vs], in_=bO[h][:, 0:256], func=AF.Copy),
                          reads=[f"bO{h}"], writes=[(f"osb{pi}", h)])

            def epilogue(st):
                pi, tok = st["pi"], st["tok"]
                okeys = [(f"osb{pi}", h) for h in range(4)]
                if not fwd:
                    sc.dma("act", lambda e: e.dma_start(out=self.ob[tok:tok + 128, :], in_=osb[pi][:]), f"osb{pi}", reads=okeys, writes=[("ob", tok)])
                    return
                for h in range(4):
                    vs = slice(h * 256, (h + 1) * 256)
                    sc.op("act", lambda e, h=h, vs=vs: e.activation(out=junk[:, h, :], in_=osb[pi][:, vs], func=AF.Square, accum_out=stt[pi][:, h:h + 1]),
                          reads=[(f"osb{pi}", h)], writes=[f"junk{h}", (f"gst{pi}", h)])
                sc.op("act", lambda e: e.activation(out=stt[pi][:, 4:8], in_=stt[pi][:, 0:4], func=AF.Ln, scale=1.0 / GLA_DV, bias=self.eps_t[:, 0:1]),
                      reads=[(f"gst{pi}", h) for h in range(4)], writes=[f"gst{pi}"])
                sc.op("act", lambda e: e.activation(out=stt[pi][:, 8:12], in_=stt[pi][:, 4:8], func=AF.Exp, scale=-0.5), reads=[f"gst{pi}"], writes=[f"gst{pi}"])
                sc.op("pool", lambda e: e.tensor_tensor(out=g32[:], in0=grt[pi][:], in1=gat[pi][:], op=ALU.mult), reads=[f"grt{pi}", f"gat{pi}"], writes=["g32"])
                sc.op("pool", lambda e: e.tensor_tensor(out=g32[:], in0=g32[:], in1=gnwb[:], op=ALU.mult), reads=["g32", "gnwb"], writes=["g32"])
                for h in range(4):
                    vs = slice(h * 256, (h + 1) * 256)
                    sc.op("dve", lambda e, h=h, vs=vs: e.scalar_tensor_tensor(out=ysb[pi][:, vs], in0=osb[pi][:, vs], scalar=stt[pi][:, 8 + h:9 + h], in1=g32[:, vs], op0=ALU.mult, op1=ALU.mult),
                          reads=[(f"osb{pi}", h), f"gst{pi}", "g32"], writes=[(f"ysb{pi}", h)])
                sc.dma("act", lambda e: e.dma_start(out=self.ya[tok:tok + 128, :], in_=ysb[pi][:]), f"ysb{pi}",
                       reads=[(f"ysb{pi}", h) for h in range(4)], writes=[("ya", tok)])

            load_ss(order[0], 0)
            if NSS > 1:
                load_ss(order[1], 1)
            loads(steps[0])
            for f_ in (pre_a, pre_b, pre_c, pre_d):
                f_(steps[0])
            for p, st in enumerate(steps):
                nx = steps[p + 1] if p + 1 < len(steps) else None
                if nx is not None:
                    loads(nx)
                scan_open(st)
                parts = (pre_a, pre_b, pre_c, pre_d)
                for h in range(4):
                    if nx is not None:
                        parts[h](nx)
                    scan_head(st, h)
                epilogue(st)
                if nx is not None and nx["first"] and nx["n"] + 1 < NSS:
                    load_ss(order[nx["n"] + 1], (nx["n"] + 1) % 2)
            sc.flush_wait_all()
            sc.emit(block)

    def phase_attn(self, sc, s):
        nc, S, NT, NG = self.nc, self.S, self.NT, self.NG
        t0 = s * S
        LAG = 2
        with ExitStack() as es, nc.Block() as block:
            KT = [self.sb(es, f"aKT{i}", [128, 2, S], BF16) for i in range(2)]
            VA = [self.sb(es, f"aVA{i}", [128, NT, 258], BF16) for i in range(2)]
            QT = [self.sb(es, f"aQT{i}", [128, 2, 512], BF16) for i in range(2)]
            PT = [self.sb(es, f"aPT{i}", [128, 1024], BF16) for i in range(2)]
            A0 = self.sb(es, "aA0", [128, 4, 256], F32)
            o2 = self.sb(es, "ao2", [128, 4, 256], F32)
            gbt = [self.sb(es, f"agb{i}", [128, 4, 256], BF16) for i in range(2)]
            g32 = self.sb(es, "ag32", [128, 4, 256], F32)
            ybs = [self.sb(es, f"aybs{i}", [128, 4, 256], F32) for i in range(2)]
            slnb = self.sb(es, "aslnb", [128, 256], F32)
            lam = self.sb(es, "alam", [1, 512], F32)
            lt = self.sb(es, "alt", [1, 16], F32)
            ones1 = self.sb(es, "aones1", [1, 128], F32)
            nlam = self.sb(es, "anlam", [128, 1], F32)
            rr = self.sb(es, "arr", [128, 16], F32)
            stt = self.sb(es, "ast", [128, 12], F32)
            junk = self.sb(es, "ajunk", [128, 4, 256], F32)
            ACC = [self.ps(es, f"aACC{i}", [128, 512]) for i in range(4)]
            SB = [self.ps(es, f"aSB{i}", [128, 1024]) for i in range(2)]

            sc.dma("sp", lambda e: e.dma_start(out=lam[:], in_=self.lam4[:, :]), "c_a", writes=["lam"])
            sc.dma("sp", lambda e: e.dma_start(out=slnb[:], in_=self.sln.broadcast_to([128, 256])), "c_b", writes=["slnb"])
            sc.op("dve", lambda e: e.memset(ones1[:], 1.0), writes=["ones1"])
            sc.op("dve", lambda e: e.tensor_tensor(out=lam[:, 0:128], in0=lam[:, 0:128], in1=lam[:, 128:256], op=ALU.mult), reads=["lam"], writes=["lam"])
            sc.op("dve", lambda e: e.tensor_tensor(out=lam[:, 256:384], in0=lam[:, 256:384], in1=lam[:, 384:512], op=ALU.mult), reads=["lam"], writes=["lam"])
            sc.op("act", lambda e: e.activation(out=lam[:, 128:256], in_=lam[:, 0:128], func=AF.Copy, accum_out=lt[:, 0:1]), reads=["lam"], writes=["lam", "lt"])
            sc.op("act", lambda e: e.activation(out=lam[:, 384:512], in_=lam[:, 256:384], func=AF.Copy, accum_out=lt[:, 1:2]), reads=["lam"], writes=["lam", "lt"])
            sc.op("act", lambda e: e.activation(out=lt[:, 2:4], in_=lt[:, 0:2], func=AF.Exp), reads=["lt"], writes=["lt"])
            sc.op("dve", lambda e: e.tensor_tensor(out=lt[:, 4:5], in0=lt[:, 3:4], in1=lt[:, 2:3], op=ALU.subtract), reads=["lt"], writes=["lt"])
            sc.op("dve", lambda e: e.tensor_scalar(out=lt[:, 5:6], in0=lt[:, 4:5], scalar1=-LAMBDA_INIT, scalar2=None, op0=ALU.add), reads=["lt"], writes=["lt"])
            sc.op("pe", lambda e: e.matmul(ACC[0][:, 0:1], lhsT=ones1[:], rhs=lt[:, 5:6], start=True, stop=True), reads=["ones1", "lt"], writes=["aACC0"])
            sc.op("dve", lambda e: e.tensor_copy(out=nlam[:], in_=ACC[0][:, 0:1]), reads=["aACC0"], writes=["nlam"])
            sc.op("dve", lambda e: e.tensor_scalar(out=slnb[:], in0=slnb[:], scalar1=float(1.0 - LAMBDA_INIT), scalar2=None, op0=ALU.mult), reads=["slnb"], writes=["slnb"])
            for i in range(2):
                sc.op("pool", lambda e, i=i: e.memset(VA[i][:, :, 256:258], 1.0), writes=[f"aVA{i}"])

            def load_head(h, i):
                for c in range(2):
                    sc.dma("sp", lambda e, c=c: e.dma_start(out=KT[i][:, c, :], in_=self.dkT[2 * h + c, :, t0:t0 + S]), f"aKT{i}", writes=[f"aKT{i}"])
                for a0 in range(0, NT, 8):
                    a1 = min(NT, a0 + 8)
                    sc.dma("sp", lambda e, a0=a0, a1=a1: e.dma_start(
                        out=VA[i][:, a0:a1, 0:256],
                        in_=self.tm["dv"][t0 + a0 * 128:t0 + a1 * 128, h * 256:(h + 1) * 256].rearrange("(a p) c -> p a c", p=128)),
                        f"aVA{i}", writes=[f"aVA{i}"])

            scale = float(DIFF_HD ** -0.5)
            load_head(0, 0)
            work = [(h, qg) for h in range(DIFF_H) for qg in range(NG)]

            def load_q(widx):
                h, qg = work[widx]
                qi, tb = widx % 2, t0 + qg * 512
                for c in range(2):
                    sc.dma("sp", lambda e, c=c: e.dma_start(out=QT[qi][:, c, :], in_=self.dqT[2 * h + c, :, tb:tb + 512]), f"aQT{qi}", writes=[f"aQT{qi}"])
                sc.dma("sp", lambda e: e.dma_start(out=gbt[qi][:], in_=self.tm["gb"][tb:tb + 512, h * 256:(h + 1) * 256].rearrange("(a p) c -> p a c", p=128)),
                       f"agb{qi}", writes=[f"agb{qi}"])

            load_q(0)
            cnt = 0
            for h in range(DIFF_H):
                hi = h % 2
                if h + 1 < DIFF_H:
                    load_head(h + 1, (h + 1) % 2)
                for qg in range(NG):
                    qi = cnt % 2
                    cnt += 1
                    tb = t0 + qg * 512
                    if cnt < len(work):
                        load_q(cnt)
                    for c in range(2):
                        NP2 = NT // 2
                        for step in range(NP2 + 1):
                            if step < NP2:
                                b_ = step % 2
                                for u in range(2):
                                    k_ = 2 * step + u
                                    sc.op("pe", lambda e, c=c, k_=k_, b_=b_, u=u, hi=hi, qi=qi: e.matmul(SB[b_][:, u * 512:(u + 1) * 512], lhsT=KT[hi][:, c, k_ * 128:(k_ + 1) * 128],
                                                                                                    rhs=QT[qi][:, c, :], start=True, stop=True),
                                          reads=[f"aKT{hi}", f"aQT{qi}"], writes=[f"aSB{b_}"])
                                sc.op("act", lambda e, b_=b_: e.activation(out=PT[b_][:], in_=SB[b_][:], func=AF.Exp, scale=scale, bias=self.eps_t[:, 3:4]),
                                      reads=[f"aSB{b_}"], writes=[f"aPT{b_}"])
                            if step >= 1:
                                b_ = (step - 1) % 2
                                for u in range(2):
                                    k_ = 2 * (step - 1) + u
                                    for qs in range(4):
                                        sc.op("pe", lambda e, k_=k_, b_=b_, u=u, qs=qs, hi=hi: e.matmul(ACC[qs][:, 0:257], lhsT=PT[b_][:, u * 512 + qs * 128:u * 512 + (qs + 1) * 128],
                                                                                                  rhs=VA[hi][:, k_, 0:257], start=(k_ == 0), stop=(k_ == NT - 1)),
                                              reads=[f"aPT{b_}", f"aVA{hi}"], writes=[f"aACC{qs}"])
                        for qs in range(4):
                            col = c * 4 + qs
                            sc.op("dve", lambda e, qs=qs, col=col: e.reciprocal(out=rr[:, col:col + 1], in_=ACC[qs][:, 256:257]), reads=[f"aACC{qs}"], writes=[("rr", col)])
                            if c == 0:
                                sc.op("dve", lambda e, qs=qs, col=col: e.tensor_scalar(out=A0[:, qs, :], in0=ACC[qs][:, 0:256], scalar1=rr[:, col:col + 1], scalar2=None, op0=ALU.mult),
                                      reads=[f"aACC{qs}", ("rr", col)], writes=[("A0", qs)])
                            else:
                                sc.op("dve", lambda e, col=col: e.tensor_tensor(out=rr[:, 8 + col:9 + col], in0=rr[:, col:col + 1], in1=nlam[:], op=ALU.mult), reads=[("rr", col), "nlam"], writes=[("rr", 8 + col)])
                                sc.op("dve", lambda e, qs=qs, col=col: e.scalar_tensor_tensor(out=o2[:, qs, :], in0=ACC[qs][:, 0:256], scalar=rr[:, 8 + col:9 + col], in1=A0[:, qs, :], op0=ALU.mult, op1=ALU.add),
                                      reads=[f"aACC{qs}", ("rr", 8 + col), ("A0", qs)], writes=[("o2", qs)])
                    for qs in range(4):
                        sc.op("act", lambda e, qs=qs: e.activation(out=junk[:, qs, :], in_=o2[:, qs, :], func=AF.Square, accum_out=stt[:, qs:qs + 1]), reads=[("o2", qs)], writes=[f"ajunk{qs}", ("ast", qs)])
                    sc.op("act", lambda e: e.activation(out=stt[:, 4:8], in_=stt[:, 0:4], func=AF.Ln, scale=1.0 / 256, bias=self.eps_t[:, 1:2]), reads=[("ast", q_) for q_ in range(4)], writes=["ast"])
                    sc.op("act", lambda e: e.activation(out=stt[:, 8:12], in_=stt[:, 4:8], func=AF.Exp, scale=-0.5), reads=["ast"], writes=["ast"])
                    for qs in range(4):
                        sc.op("pool", lambda e, qs=qs, qi=qi: e.tensor_tensor(out=g32[:, qs, :], in0=gbt[qi][:, qs, :], in1=slnb[:], op=ALU.mult), reads=[f"agb{qi}", "slnb"], writes=[("ag32", qs)])
                    for qs in range(4):
                        sc.op("dve", lambda e, qs=qs, qi=qi: e.scalar_tensor_tensor(out=ybs[qi][:, qs, :], in0=o2[:, qs, :], scalar=stt[:, 8 + qs:9 + qs], in1=g32[:, qs, :], op0=ALU.mult, op1=ALU.mult),
                              reads=[("o2", qs), "ast", ("ag32", qs)], writes=[(f"aybs{qi}", qs)])
                    sc.dma("pool", lambda e, qi=qi, tb=tb, h=h: e.dma_start(out=self.yb[tb:tb + 512, h * 256:(h + 1) * 256].rearrange("(a p) c -> p a c", p=128), in_=ybs[qi][:]),
                           f"aybs{qi}", reads=[(f"aybs{qi}", q_) for q_ in range(4)], writes=[("yb", tb, h)])
            sc.flush_wait_all()
            sc.emit(block)

    def phase_wprep(self, sc):
        nc = self.nc
        with ExitStack() as es, nc.Block() as block:
            wt = self.sb(es, "wprep", [128, KC, 2 * FFN], BF16)
            src = self.w_fi.rearrange("(k p) c -> p k c", p=128)
            for kc in range(KC):
                sc.dma("pool", lambda e, kc=kc: e.dma_start(out=wt[:, kc, :], in_=src[:, kc, :]), f"wp{kc % 4}", writes=[("wprep", kc)])
            allk = [("wprep", kc) for kc in range(KC)]
            for f in range(FC):
                for g in range(2):
                    c0 = g * FFN + f * 128
                    sc.dma("sp", lambda e, f=f, g=g, c0=c0: e.dma_start(out=self.wfib[f, :, g, :, :], in_=wt[:, :, c0:c0 + 128]), f"wps{(2 * f + g) % 4}",
                           reads=allk, writes=[("wfib", f)])
            sc.flush_wait_all()
            sc.emit(block)

    def phase_ffn(self, sc):
        nc, S, T = self.nc, self.S, self.T
        NGT = T // 512
        with ExitStack() as es, nc.Block() as block:
            wo = self.sb(es, "f_wo", [128, KC, D], BF16)
            wfo = self.sb(es, "f_wfo", [128, FC, D], BF16)
            wfi = [self.sb(es, f"f_wfi{i}", [128, 2, KC, 128], BF16) for i in range(4)]
            ident = self.sb(es, "f_ident", [128, 128], BF16)
            identf = self.sb(es, "f_identf", [128, 128], F32)
            nwf = self.sb(es, "f_nwf", [128, KC], F32)
            nwfin = self.sb(es, "f_nwfin", [128, D], F32)
            la = [self.sb(es, f"f_la{i}", [128, D], F32) for i in range(2)]
            lb = [self.sb(es, f"f_lb{i}", [128, D], F32) for i in range(2)]
            lx = [self.sb(es, f"f_lx{i}", [128, D], F32) for i in range(2)]
            mb = [self.sb(es, f"f_mb{i}", [128, D], BF16) for i in range(2)]
            mT = self.sb(es, "f_mT", [128, KC, 512], BF16)
            hT = self.sb(es, "f_hT", [128, KC, 512], BF16)
            hsb = self.sb(es, "f_h", [128, 4, D], F32)
            aT = self.sb(es, "f_aT", [128, FC, 512], BF16)
            sg = [self.sb(es, f"f_sg{i}", [128, 512], F32) for i in range(2)]
            ot = [self.sb(es, f"f_ot{i}", [128, D], F32) for i in range(2)]
            stt = [self.sb(es, f"f_st{i}", [128, 4], F32) for i in range(2)]
            psT = [self.ps(es, f"f_psT{i}", [128, 1024], BF16) for i in range(2)]
            pb = [self.ps(es, f"f_pb{i}", [128, 512]) for i in range(6)]

            sc.dma("pool", lambda e: e.dma_start(out=wo[:], in_=self.w_out.rearrange("(k p) c -> p k c", p=128)), "pw_a", writes=["wo"])
            for f0 in range(0, FC, 2):
                sc.dma("pool", lambda e, f0=f0: e.dma_start(out=wfo[:, f0:f0 + 2, :], in_=self.w_fo[f0 * 128:(f0 + 2) * 128, :].rearrange("(k p) c -> p k c", p=128)),
                       f"pw_b{(f0 // 2) % 2}", writes=[("wfo", f0)])
            wfo_all = [("wfo", f0) for f0 in range(0, FC, 2)]
            sc.dma("sp", lambda e: e.dma_start(out=identf[:], in_=self.cst[:, 0:128]), "c_c", writes=["identf"])
            sc.op("dve", lambda e: e.tensor_copy(out=ident[:], in_=identf[:]), reads=["identf"], writes=["ident"])
            sc.dma("sp", lambda e: e.dma_start(out=nwf[:], in_=self.nw_ffn[:, :]), "c_d", writes=["nwf"])
            sc.dma("sp", lambda e: e.dma_start(out=nwfin[:], in_=self.nw_fin.broadcast_to([128, D])), "c_e", writes=["nwfin"])

            xf = self.x.rearrange("n s d -> (n s) d")
            yf = self.y.rearrange("n s d -> (n s) d")
            st8 = dict(pbi=0, li=0, wi=0, sgi=0, oti=0, pti=0)

            def next_pb():
                j = st8["pbi"] % 6
                st8["pbi"] += 1
                return j

            def norm_rows(src_ap, src_keys, i):
                sc.op("act", lambda e: e.activation(out=mb[i][:], in_=src_ap, func=AF.Square, accum_out=stt[i][:, 0:1]), reads=src_keys, writes=[f"mb{i}", f"fst{i}"])
                sc.op("act", lambda e: e.activation(out=stt[i][:, 1:2], in_=stt[i][:, 0:1], func=AF.Ln, scale=1.0 / D, bias=self.eps_t[:, 0:1]), reads=[f"fst{i}"], writes=[f"fst{i}"])
                sc.op("act", lambda e: e.activation(out=stt[i][:, 2:3], in_=stt[i][:, 1:2], func=AF.Exp, scale=-0.5), reads=[f"fst{i}"], writes=[f"fst{i}"])

            def transpose_to(dstT, dkey, i, tt_, scal):
                pt = st8["pti"] % 2
                st8["pti"] += 1
                for kc in range(KC):
                    sc.op("pe", lambda e, kc=kc: e.transpose(psT[pt][:, kc * 128:(kc + 1) * 128], mb[i][:, kc * 128:(kc + 1) * 128], ident[:]),
                          reads=[f"mb{i}", "ident"], writes=[f"f_psT{pt}"])
                for kc in range(KC):
                    if scal is None:
                        sc.op("dve", lambda e, kc=kc: e.tensor_copy(out=dstT[:, kc, tt_ * 128:(tt_ + 1) * 128], in_=psT[pt][:, kc * 128:(kc + 1) * 128]),
                              reads=[f"f_psT{pt}"], writes=[(dkey, tt_)])
                    else:
                        sc.op("dve", lambda e, kc=kc: e.tensor_scalar(out=dstT[:, kc, tt_ * 128:(tt_ + 1) * 128], in0=psT[pt][:, kc * 128:(kc + 1) * 128],
                                                                       scalar1=scal[:, kc:kc + 1], scalar2=None, op0=ALU.mult),
                              reads=[f"f_psT{pt}", "nwf"], writes=[(dkey, tt_)])

            for g in range(NGT):
                tb = g * 512
                for tt_ in range(4):
                    i = st8["li"] % 2
                    st8["li"] += 1
                    r0 = tb + tt_ * 128
                    sc.dma("sp", lambda e, i=i, r0=r0: e.dma_start(out=la[i][:], in_=self.ya[r0:r0 + 128, :]), f"f_la{i}", writes=[f"la{i}"])
                    sc.dma("sp", lambda e, i=i, r0=r0: e.dma_start(out=lb[i][:], in_=self.yb[r0:r0 + 128, :]), f"f_lb{i}", writes=[f"lb{i}"])
                    sc.op("pool", lambda e, i=i: e.tensor_tensor(out=mb[i][:], in0=la[i][:], in1=lb[i][:], op=ALU.add), reads=[f"la{i}", f"lb{i}"], writes=[f"mb{i}"])
                    transpose_to(mT, "mT", i, tt_, None)
                for tt_ in range(4):
                    i = st8["li"] % 2
                    st8["li"] += 1
                    r0 = tb + tt_ * 128
                    sc.dma("sp", lambda e, i=i, r0=r0: e.dma_start(out=lx[i][:], in_=xf[r0:r0 + 128, :]), f"f_lx{i}", writes=[f"lx{i}"])
                    for nh in range(2):
                        j = next_pb()
                        for kc in range(KC):
                            sc.op("pe", lambda e, kc=kc, j=j, nh=nh, tt_=tt_: e.matmul(pb[j][:], lhsT=mT[:, kc, tt_ * 128:(tt_ + 1) * 128], rhs=wo[:, kc, nh * 512:(nh + 1) * 512],
                                                                                    start=(kc == 0), stop=(kc == KC - 1)),
                                  reads=[("mT", tt_), "wo"], writes=[f"f_pb{j}"])
                        sc.op("dve", lambda e, j=j, nh=nh, tt_=tt_, i=i: e.tensor_tensor(out=hsb[:, tt_, nh * 512:(nh + 1) * 512], in0=pb[j][:], in1=lx[i][:, nh * 512:(nh + 1) * 512], op=ALU.add),
                              reads=[f"f_pb{j}", f"lx{i}"], writes=[("h", tt_)])
                    norm_rows(hsb[:, tt_, :], [("h", tt_)], i)
                    sc.op("dve", lambda e, i=i, tt_=tt_: e.tensor_scalar(out=mb[i][:], in0=hsb[:, tt_, :], scalar1=stt[i][:, 2:3], scalar2=None, op0=ALU.mult),
                          reads=[("h", tt_), f"fst{i}"], writes=[f"mb{i}"])
                    transpose_to(hT, "hT", i, tt_, nwf)
                hT_all = [("hT", k) for k in range(4)]
                for f in range(FC):
                    wi = st8["wi"] % 4
                    st8["wi"] += 1
                    sc.dma("sp", lambda e, wi=wi, f=f: e.dma_start(out=wfi[wi][:], in_=self.wfib[f, :, :, :, :]), f"f_wfi{wi}", reads=[("wfib", f)], writes=[f"wfi{wi}"])
                    jg, ju = next_pb(), next_pb()
                    for gi, j in ((0, jg), (1, ju)):
                        for kc in range(KC):
                            sc.op("pe", lambda e, kc=kc, j=j, gi=gi, wi=wi: e.matmul(pb[j][:], lhsT=wfi[wi][:, gi, kc, :], rhs=hT[:, kc, :], start=(kc == 0), stop=(kc == KC - 1)),
                                  reads=[f"wfi{wi}"] + hT_all, writes=[f"f_pb{j}"])
                    si = st8["sgi"] % 2
                    st8["sgi"] += 1
                    sc.op("act", lambda e, jg=jg, si=si: e.activation(out=sg[si][:], in_=pb[jg][:], func=AF.Silu), reads=[f"f_pb{jg}"], writes=[f"sg{si}"])
                    sc.op("dve", lambda e, ju=ju, si=si, f=f: e.tensor_tensor(out=aT[:, f, :], in0=pb[ju][:], in1=sg[si][:], op=ALU.mult),
                          reads=[f"f_pb{ju}", f"sg{si}"], writes=[("aT", f)])
                aT_all = [("aT", f) for f in range(FC)]
                for tt_ in range(4):
                    oi = st8["oti"] % 2
                    st8["oti"] += 1
                    r0 = tb + tt_ * 128
                    for nh in range(2):
                        j = next_pb()
                        for f in range(FC):
                            sc.op("pe", lambda e, f=f, j=j, nh=nh, tt_=tt_: e.matmul(pb[j][:], lhsT=aT[:, f, tt_ * 128:(tt_ + 1) * 128], rhs=wfo[:, f, nh * 512:(nh + 1) * 512],
                                                                                  start=(f == 0), stop=(f == FC - 1)),
                                  reads=aT_all + wfo_all, writes=[f"f_pb{j}"])
                        sc.op("dve", lambda e, j=j, nh=nh, tt_=tt_: e.tensor_tensor(out=hsb[:, tt_, nh * 512:(nh + 1) * 512], in0=pb[j][:], in1=hsb[:, tt_, nh * 512:(nh + 1) * 512], op=ALU.add),
                              reads=[f"f_pb{j}", ("h", tt_)], writes=[("h", tt_)])
                    norm_rows(hsb[:, tt_, :], [("h", tt_)], oi)
                    sc.op("dve", lambda e, oi=oi, tt_=tt_: e.scalar_tensor_tensor(out=ot[oi][:], in0=hsb[:, tt_, :], scalar=stt[oi][:, 2:3], in1=nwfin[:], op0=ALU.mult, op1=ALU.mult),
                          reads=[("h", tt_), f"fst{oi}", "nwfin"], writes=[f"ot{oi}"])
                    sc.dma("pool", lambda e, oi=oi, r0=r0: e.dma_start(out=yf[r0:r0 + 128, :], in_=ot[oi][:]), f"f_ot{oi}", reads=[f"ot{oi}"], writes=[("y", r0)])
            sc.flush_wait_all()
            sc.emit(block)

    def build(self):
        nc = self.nc
        self.declare()
        with ExitStack() as es:
            sc = Sched(nc, es)
            self.eps_t = self.sb(es, "eps_t", [128, 4], F32)
            with nc.Block() as block:
                sc.op("dve", lambda e: e.memset(self.eps_t[:, 0:1], NORM_EPS), writes=["eps"])
                sc.op("dve", lambda e: e.memset(self.eps_t[:, 1:2], SUBLN_EPS), writes=["eps"])
                sc.op("dve", lambda e: e.memset(self.eps_t[:, 2:3], 1.0), writes=["eps"])
                sc.op("dve", lambda e: e.memset(self.eps_t[:, 3:4], 0.0), writes=["eps"])
                sc.emit(block)
            for s in range(self.NSEQ):
                if "p1" in self.phases:
                    self.phase1(sc, s)
                if "gla" in self.phases:
                    self.phase_gla(sc, s, 1)
                    self.phase_gla(sc, s, 0)
                if "attn" in self.phases:
                    self.phase_attn(sc, s)
            if "ffn" in self.phases:
                self.phase_wprep(sc)
                self.phase_ffn(sc)
        return nc


def host_consts(S):
    inv_freq = 1.0 / (10000.0 ** (np.arange(0, DIFF_HD, 2, dtype=np.float32) / DIFF_HD))
    pos = np.arange(S, dtype=np.float32)
    fr = pos[None, :] * np.concatenate([inv_freq, inv_freq])[:, None].astype(np.float32)
    cosT = np.cos(fr).astype(np.float32)
    sgn = np.where(np.arange(128) < 64, -1.0, 1.0).astype(np.float32)[:, None]
    sinT = (np.sin(fr) * sgn).astype(np.float32)
    p = np.arange(128)
    same = (p[:, None] // 64) == (p[None, :] // 64)
    sI, tI = p[:, None], p[None, :]
    g = -1.0 / 16.0
    ident = np.eye(128, dtype=np.float32)
    uinc_f = (same & (sI <= tI)) * g
    ustr_f = (same & (sI > tI)) * g
    uinc_b = (same & (sI >= tI)) * g
    ustr_b = (same & (sI < tI)) * g
    cst = np.concatenate([ident, uinc_f, ustr_f, uinc_b, ustr_b, np.zeros((128, 128))], axis=1).astype(np.float32)
    m_f = (same & (sI <= tI)).astype(np.float32)
    m_b = (same & (sI >= tI)).astype(np.float32)
    msk = np.concatenate([m_f, m_b], axis=1).astype(np.float32)
    return cosT, sinT, cst, msk


def host_inputs(inp, S):
    c = np.ascontiguousarray
    w_in = np.asarray(inp["w_in"])[0]
    w1 = c(w_in[:, _w1_columns()])
    cosT, sinT, cst, msk = host_consts(S)
    wgk = np.concatenate([np.asarray(inp["w_gk2"])[0], np.asarray(inp["b_gk"])[0][:, None, :]], axis=1)
    shared = dict(
        w1=w1,
        nw_mix=c(np.asarray(inp["norm_mix_w"])[0].reshape(KC, 128).T),
        nw_ffn=c(np.asarray(inp["norm_ffn_w"])[0].reshape(KC, 128).T),
        nw_fin=c(np.asarray(inp["norm_final_w"]).reshape(1, D)),
        cosT=cosT, sinT=sinT, cst=cst, msk=msk,
        wgk=c(wgk.astype(np.float32)),
        gnw=c(np.tile(np.asarray(inp["gla_norm_w"])[0], 4).reshape(1, 1024)),
        sln=c(np.asarray(inp["diff_subln_w"])[0].reshape(1, 256)),
        lam4=c(np.concatenate([np.asarray(inp[k])[0] for k in ("lambda_q1", "lambda_k1", "lambda_q2", "lambda_k2")]).reshape(1, 512)),
        w_out=c(np.asarray(inp["w_out"])[0]),
        w_fi=c(np.asarray(inp["w_ffn_in"])[0]),
        w_fo=c(np.asarray(inp["w_ffn_out"])[0]),
    )
    return {k: np.asarray(v, dtype=np.float32) for k, v in shared.items()}


def kernel(**inputs):
    x = np.asarray(inputs["x"], dtype=np.float32)
    B, S, _ = x.shape
    nseq = B // N_CORES
    b = Builder(S, nseq)
    nc = b.build()
    shared = host_inputs(inputs, S)
    in_maps = [dict(shared, x=np.ascontiguousarray(x[i * nseq:(i + 1) * nseq])) for i in range(N_CORES)]
    res = run_bass_kernel_spmd(nc, in_maps, core_ids=list(range(N_CORES)))
    return np.concatenate([np.asarray(r["y"]) for r in res.results], axis=0).astype(np.float32)
```

```python
from contextlib import ExitStack
import math
import numpy as np
import ml_dtypes
import concourse.bass as bass
import concourse.mybir as mybir
from concourse.bass_utils import run_bass_kernel_spmd

F32 = mybir.dt.float32
BF16 = mybir.dt.bfloat16
AF = mybir.ActivationFunctionType
ALU = mybir.AluOpType

D = 1024
KC = D // 128
GLA_H, GLA_DK, GLA_DV, GLA_RANK = 4, 128, 256, 16
DIFF_H, DIFF_HD = 4, 128
FFN = 2816
FC = FFN // 128
NORM_EPS = 1e-6
SUBLN_EPS = 1e-5
LAMBDA_INIT = 0.8 - 0.6 * math.exp(-0.3 * 0)
N_CORES = 8

OFF_GQ, OFF_GK, OFF_GV, OFF_GR, OFF_GLR = 0, 512, 1024, 2048, 3072
OFF_DQ, OFF_DK, OFF_DV, OFF_GA, OFF_GB = 3104, 4128, 5152, 6176, 7200


def _w1_columns():
    cols = []
    cols += list(range(OFF_GQ, OFF_GQ + 512))
    cols += list(range(OFF_GK, OFF_GK + 512))
    for off in (OFF_DQ, OFF_DK):
        for hc in range(8):
            base = off + hc * 128
            cols += list(range(base, base + 128))
            cols += list(range(base + 64, base + 128)) + list(range(base, base + 64))
    cols += list(range(OFF_GLR, OFF_GLR + 32))
    cols += list(range(OFF_GK, OFF_GK + 512))
    for off in (OFF_GV, OFF_GR, OFF_DV, OFF_GA, OFF_GB):
        cols += list(range(off, off + 1024))
    return np.array(cols, dtype=np.int64)


W1_COLS = 512 * 10 + 32 + 512 * 11
FM_BLOCKS = [("gq", 0), ("gk", 512)] + [("dq", 1024 + 512 * i) for i in range(4)] + \
            [("dk", 3072 + 512 * i) for i in range(4)]
GLR_OFF = 5120
TM_OFF = 5152
TM_BLOCKS = [("gk", 0)] + [(n, j) for n in ("gv", "gr", "dv", "ga", "gb") for j in range(2)]


class Sched:
    CE = ("pe", "act", "dve", "pool")
    ALL = ("pe", "act", "dve", "pool", "sp")

    def __init__(self, nc, es):
        self.nc = nc
        self.prog = {e: es.enter_context(nc.semaphore("prog_" + e)) for e in self.CE}
        self.cum = {e: 0 for e in self.CE}
        self.dsem = {}
        self.es = es
        self.res_w = {}
        self.res_r = {}
        self.ops = {e: [] for e in self.ALL}
        self.waited = {e: {} for e in self.ALL}
        self.sigval = {}
        self.uid = 0

    def _slot(self, slot):
        if slot not in self.dsem:
            self.dsem[slot] = [self.es.enter_context(self.nc.semaphore("d_" + slot)), 0]
        return self.dsem[slot]

    def _collect(self, eng, reads, writes, is_dma):
        deps = []
        for k in reads:
            for ev in self.res_w.get(k, ()):
                deps.append((ev, True))
        for k in writes:
            for ev in self.res_w.get(k, ()):
                deps.append((ev, False))
            for ev in self.res_r.get(k, ()):
                deps.append((ev, False))
        out = []
        for ev, raw in deps:
            if ev[0] == "eng" and ev[1] == eng and not is_dma:
                if eng == "pe":
                    continue
            out.append(ev)
        return out

    def _update(self, ev, reads, writes):
        for k in writes:
            self.res_w[k] = [ev]
            self.res_r[k] = []
        for k in reads:
            self.res_r.setdefault(k, []).append(ev)

    def op(self, eng, fn, reads=(), writes=()):
        self.uid += 1
        ev = ("eng", eng, self.uid)
        deps = self._collect(eng, reads, writes, False)
        self.ops[eng].append(dict(fn=fn, deps=deps, ev=ev, dma=None))
        self._update(ev, reads, writes)
        return ev

    def dma(self, queue, fn, slot, reads=(), writes=()):
        s = self._slot(slot)
        deps = self._collect(queue, reads, writes, True)
        if s[1] > 0:
            deps.append(("dma", slot, s[1]))
        s[1] += 1
        ev = ("dma", slot, s[1])
        self.ops[queue].append(dict(fn=fn, deps=deps, ev=ev, dma=slot))
        self._update(ev, reads, writes)
        return ev

    def flush_wait_all(self):
        deps = [("dma", slot, s[1]) for slot, s in self.dsem.items() if s[1] > 0]
        self.ops["sp"].append(dict(fn=None, deps=deps, ev=None, dma=None))

    def emit(self, block):
        done = getattr(self, "done_uid", 0)
        for e in self.ALL:
            for o in self.ops[e]:
                o["deps"] = [d for d in o["deps"] if not (d[0] == "eng" and d[2] <= done)]
        self.done_uid = self.uid
        need = set()
        for e in self.ALL:
            for o in self.ops[e]:
                for d in o["deps"]:
                    if d[0] == "eng":
                        need.add(d[2])
        for e in self.CE:
            c = self.cum[e]
            for o in self.ops[e]:
                if o["dma"] is None and o["ev"] is not None and o["ev"][2] in need:
                    c += 1
                    self.sigval[o["ev"][2]] = c
            self.cum[e] = c
        starters = dict(pe=block.tensor, act=block.scalar, dve=block.vector, pool=block.gpsimd, sp=block.sync)
        for e in self.ALL:
            ops = self.ops[e]
            if not ops:
                continue
            waited = self.waited[e]

            def body(eng, ops=ops, waited=waited, e=e):
                for o in ops:
                    for d in o["deps"]:
                        if d[0] == "eng":
                            key, sem, val = d[1], self.prog[d[1]], self.sigval[d[2]]
                        else:
                            key, sem, val = "d_" + d[1], self.dsem[d[1]][0], 16 * d[2]
                        if waited.get(key, 0) >= val:
                            continue
                        eng.wait_ge(sem, val)
                        waited[key] = val
                    if o["fn"] is None:
                        continue
                    ins = o["fn"](eng)
                    if o["dma"] is not None:
                        ins.then_inc(self.dsem[o["dma"]][0], 16)
                    elif o["ev"][2] in self.sigval:
                        ins.then_inc(self.prog[e], 1)
            starters[e](body)
        self.ops = {e: [] for e in self.ALL}


class Builder:
    def __init__(self, S, NSEQ, debug=False, phases=("p1", "gla", "attn", "ffn"), scratch_in=False, cut=99):
        self.S, self.NSEQ, self.debug, self.phases = S, NSEQ, debug, phases
        self.scratch_in, self.cut = scratch_in, cut
        self.T = S * NSEQ
        self.NT = S // 128
        self.NG = S // 512
        self.nc = bass.Bass("TRN2", target_bir_lowering=False)

    def dram(self, name, shape, dt, kind):
        return self.nc.dram_tensor(name, list(shape), dt, kind=kind).ap()

    def declare(self):
        S, T, NSEQ = self.S, self.T, self.NSEQ
        I = "ExternalInput"
        self.x = self.dram("x", [NSEQ, S, D], F32, I)
        self.w1 = self.dram("w1", [D, W1_COLS], F32, I)
        self.nw_mixr = self.dram("nw_mixr", [1, D], F32, I)
        self.nw_ffn = self.dram("nw_ffn", [128, KC], F32, I)
        self.nw_fin = self.dram("nw_fin", [1, D], F32, I)
        self.cosT = self.dram("cosT", [128, S], F32, I)
        self.sinT = self.dram("sinT", [128, S], F32, I)
        self.wgk = self.dram("wgk", [2, 17, 512], F32, I)
        self.gnw = self.dram("gnw", [1, 1024], F32, I)
        self.sln = self.dram("sln", [1, 256], F32, I)
        self.lam4 = self.dram("lam4", [1, 512], F32, I)
        self.w_out = self.dram("w_out", [D, D], F32, I)
        self.w_fi = self.dram("w_fi", [D, 2 * FFN], F32, I)
        self.w_fo = self.dram("w_fo", [FFN, D], F32, I)
        self.cst = self.dram("cst", [128, 6 * 128], F32, I)
        self.msk = self.dram("msk", [128, 2 * 128], F32, I)
        self.y = self.dram("y", [NSEQ, S, D], F32, "ExternalOutput")
        K = "ExternalOutput" if self.debug else "Internal"
        K1 = K
        if self.scratch_in:
            K = "ExternalInput"
        self.gqT = self.dram("gqT", [4, 128, T], BF16, K)
        self.gkT = self.dram("gkT", [4, 128, T], BF16, K)
        self.dqT = self.dram("dqT", [8, 128, T], BF16, K)
        self.dkT = self.dram("dkT", [8, 128, T], BF16, K)
        self.glrT = self.dram("glrT", [32, T], BF16, K)
        self.tm = {n: self.dram("tm_" + n, [T, (512 if n == "gk" else 1024)], BF16, K)
                   for n in ("gk", "gv", "gr", "dv", "ga", "gb")}
        K = K1
        self.ob = self.dram("ob", [T, 1024], F32, K)
        self.ya = self.dram("ya", [T, 1024], F32, K)
        self.yb = self.dram("yb", [T, 1024], F32, K)
        self.wfib = self.dram("wfib", [FC, 128, 2, KC, 128], BF16, "Internal")

    def _uniq(self, name):
        self._n = getattr(self, "_n", 0) + 1
        return f"{name}_{self._n}"

    def sb(self, es, name, shape, dt):
        return es.enter_context(self.nc.sbuf_tensor(self._uniq(name), list(shape), dt))

    def ps(self, es, name, shape, dt=F32):
        return es.enter_context(self.nc.psum_tensor(self._uniq(name), list(shape), dt))

    def phase1(self, sc, s):
        nc, S, NT, NG = self.nc, self.S, self.NT, self.NG
        t0 = s * S
        with ExitStack() as es, nc.Block() as block:
            uT = self.sb(es, "uT", [128, KC, S], BF16)
            cosT = self.sb(es, "cosT_sb", [128, S], F32)
            sinT = self.sb(es, "sinT_sb", [128, S], F32)
            nwb = self.sb(es, "nwb", [128, D], F32)
            ident = self.sb(es, "ident", [128, 128], BF16)
            identf = self.sb(es, "identf", [128, 128], F32)
            xt = [self.sb(es, f"xt{i}", [128, D], F32) for i in range(2)]
            xn = [self.sb(es, f"xn{i}", [128, D], BF16) for i in range(2)]
            st = [self.sb(es, f"st{i}", [128, 4], F32) for i in range(2)]
            wb = [self.sb(es, f"wb{i}", [128, KC, 512], BF16) for i in range(2)]
            so = [self.sb(es, f"so{i}", [128, 512], BF16) for i in range(4)]
            sof = [self.sb(es, f"sof{i}", [32, 512], BF16) for i in range(2)]
            r1 = [self.sb(es, f"r1_{i}", [128, 512], F32) for i in range(2)]
            r2 = [self.sb(es, f"r2_{i}", [128, 512], F32) for i in range(2)]
            psT = [self.ps(es, f"psT{i}", [128, 1024], BF16) for i in range(2)]
            pb = [self.ps(es, f"pb{i}", [128, 512]) for i in range(6)]

            sc.dma("sp", lambda e: e.dma_start(out=cosT[:], in_=self.cosT[:, :]), "c_cos", writes=["cosT"])
            sc.dma("sp", lambda e: e.dma_start(out=sinT[:], in_=self.sinT[:, :]), "c_sin", writes=["sinT"])
            sc.dma("sp", lambda e: e.dma_start(out=nwb[:], in_=self.nw_mixr.broadcast_to([128, D])), "c_nwm", writes=["nwb"])
            sc.dma("sp", lambda e: e.dma_start(out=identf[:], in_=self.cst[:, 0:128]), "c_id", writes=["identf"])
            sc.op("dve", lambda e: e.tensor_copy(out=ident[:], in_=identf[:]), reads=["identf"], writes=["ident"])

            for tt in range(NT):
                i = tt % 2
                sc.dma("sp", lambda e, i=i, tt=tt: e.dma_start(out=xt[i][:], in_=self.x[s, tt * 128:(tt + 1) * 128, :]),
                       f"xt{i}", writes=[f"xt{i}"])
                sc.op("act", lambda e, i=i: e.activation(out=xn[i][:], in_=xt[i][:], func=AF.Square, accum_out=st[i][:, 0:1]),
                      reads=[f"xt{i}"], writes=[f"xn{i}", f"st{i}"])
                sc.op("act", lambda e, i=i: e.activation(out=st[i][:, 1:2], in_=st[i][:, 0:1], func=AF.Ln, scale=1.0 / D, bias=self.eps_t[:, 0:1]),
                      reads=[f"st{i}"], writes=[f"st{i}"])
                sc.op("act", lambda e, i=i: e.activation(out=st[i][:, 2:3], in_=st[i][:, 1:2], func=AF.Exp, scale=-0.5),
                      reads=[f"st{i}"], writes=[f"st{i}"])
                sc.op("dve", lambda e, i=i: e.scalar_tensor_tensor(out=xn[i][:], in0=xt[i][:], scalar=st[i][:, 2:3], in1=nwb[:], op0=ALU.mult, op1=ALU.mult),
                      reads=[f"xt{i}", f"st{i}", "nwb"], writes=[f"xn{i}"])
                for kc in range(KC):
                    sc.op("pe", lambda e, i=i, kc=kc: e.transpose(psT[i][:, kc * 128:(kc + 1) * 128], xn[i][:, kc * 128:(kc + 1) * 128], ident[:]),
                          reads=[f"xn{i}", "ident"], writes=[f"psT{i}"])
                sc.op("act", lambda e, i=i, tt=tt: e.activation(out=uT[:, :, tt * 128:(tt + 1) * 128], in_=psT[i][:].rearrange("p (k t) -> p k t", k=KC), func=AF.Copy),
                      reads=[f"psT{i}"], writes=[("uT", tt)])
            uT_all = [("uT", tt) for tt in range(NT)]

            state = dict(wbi=0, pbi=0, soi=0, ri=0, sofi=0)

            def load_w(col0, ncols):
                i = state["wbi"] % 2
                state["wbi"] += 1
                src = self.w1.rearrange("(k p) c -> p k c", p=128)[:, :, col0:col0 + ncols]
                sc.dma("pool", lambda e, i=i: e.dma_start(out=wb[i][:, :, 0:ncols], in_=src), f"wb{i}", writes=[f"wb{i}"])
                return i

            def next_pb():
                j = state["pbi"] % 6
                state["pbi"] += 1
                return j

            def store(dst_ap, src_tile_key, src_ap, slotname):
                sc.dma("sp", lambda e: e.dma_start(out=dst_ap, in_=src_ap), slotname, reads=[src_tile_key])

            def fm_group(wi, c, tg, j, M=128):
                for kc in range(KC):
                    sc.op("pe", lambda e, kc=kc: e.matmul(pb[j][0:M, :], lhsT=wb[wi][:, kc, c * 128:c * 128 + M],
                                                            rhs=uT[:, kc, tg * 512:(tg + 1) * 512], start=(kc == 0), stop=(kc == KC - 1)),
                          reads=[f"wb{wi}"] + uT_all[tg * 4:(tg + 1) * 4], writes=[f"pb{j}"])

            for name, col0 in FM_BLOCKS:
                wi = load_w(col0, 512)
                bidx = (col0 - {"gq": 0, "gk": 512, "dq": 1024, "dk": 3072}[name]) // 512
                if name in ("gq", "gk"):
                    dst = self.gqT if name == "gq" else self.gkT
                    for c in range(4):
                        for tg in range(NG):
                            j = next_pb()
                            fm_group(wi, c, tg, j)
                            k = state["soi"] % 4
                            state["soi"] += 1
                            if (c + tg) % 2 == 0:
                                sc.op("act", lambda e, j=j, k=k: e.activation(out=so[k][:], in_=pb[j][:], func=AF.Copy),
                                      reads=[f"pb{j}"], writes=[f"so{k}"])
                            else:
                                sc.op("dve", lambda e, j=j, k=k: e.tensor_copy(out=so[k][:], in_=pb[j][:]),
                                      reads=[f"pb{j}"], writes=[f"so{k}"])
                            store(dst[c, :, t0 + tg * 512:t0 + (tg + 1) * 512], f"so{k}", so[k][:], f"so{k}")
                else:
                    dst = self.dqT if name == "dq" else self.dkT
                    for pr in range(2):
                        hc = bidx * 2 + pr
                        for tg in range(NG):
                            ja, jb = next_pb(), next_pb()
                            fm_group(wi, 2 * pr, tg, ja)
                            fm_group(wi, 2 * pr + 1, tg, jb)
                            ri = state["ri"] % 2
                            state["ri"] += 1
                            k = state["soi"] % 4
                            state["soi"] += 1
                            tsl = slice(tg * 512, (tg + 1) * 512)
                            sc.op("dve", lambda e, ja=ja, ri=ri, tsl=tsl: e.tensor_tensor(out=r1[ri][:], in0=pb[ja][:], in1=cosT[:, tsl], op=ALU.mult),
                                  reads=[f"pb{ja}", "cosT"], writes=[f"r1_{ri}"])
                            sc.op("dve", lambda e, jb=jb, ri=ri, tsl=tsl: e.tensor_tensor(out=r2[ri][:], in0=pb[jb][:], in1=sinT[:, tsl], op=ALU.mult),
                                  reads=[f"pb{jb}", "sinT"], writes=[f"r2_{ri}"])
                            sc.op("pool", lambda e, ri=ri, k=k: e.tensor_tensor(out=so[k][:], in0=r1[ri][:], in1=r2[ri][:], op=ALU.add),
                                  reads=[f"r1_{ri}", f"r2_{ri}"], writes=[f"so{k}"])
                            store(dst[hc, :, t0 + tg * 512:t0 + (tg + 1) * 512], f"so{k}", so[k][:], f"so{k}")
            wi = load_w(GLR_OFF, 32)
            for tg in range(NG):
                j = next_pb()
                fm_group(wi, 0, tg, j, M=32)
                k = state["sofi"] % 2
                state["sofi"] += 1
                sc.op("dve", lambda e, j=j, k=k: e.tensor_copy(out=sof[k][:], in_=pb[j][0:32, :]), reads=[f"pb{j}"], writes=[f"sof{k}"])
                store(self.glrT[:, t0 + tg * 512:t0 + (tg + 1) * 512], f"sof{k}", sof[k][:], f"sof{k}")
            for bi, (name, jj) in enumerate(TM_BLOCKS):
                wi = load_w(TM_OFF + bi * 512, 512)
                dst = self.tm[name]
                for tt in range(NT):
                    j = next_pb()
                    for kc in range(KC):
                        sc.op("pe", lambda e, kc=kc, j=j, tt=tt, wi=wi: e.matmul(pb[j][:], lhsT=uT[:, kc, tt * 128:(tt + 1) * 128], rhs=wb[wi][:, kc, :],
                                                                        start=(kc == 0), stop=(kc == KC - 1)),
                              reads=[f"wb{wi}", ("uT", tt)], writes=[f"pb{j}"])
                    k = state["soi"] % 4
                    state["soi"] += 1
                    if name == "gr":
                        sc.op("act", lambda e, j=j, k=k: e.activation(out=so[k][:], in_=pb[j][:], func=AF.Silu), reads=[f"pb{j}"], writes=[f"so{k}"])
                    elif name in ("ga", "gb"):
                        sc.op("act", lambda e, j=j, k=k: e.activation(out=so[k][:], in_=pb[j][:], func=AF.Sigmoid), reads=[f"pb{j}"], writes=[f"so{k}"])
                    elif tt % 2 == 0:
                        sc.op("act", lambda e, j=j, k=k: e.activation(out=so[k][:], in_=pb[j][:], func=AF.Copy), reads=[f"pb{j}"], writes=[f"so{k}"])
                    else:
                        sc.op("dve", lambda e, j=j, k=k: e.tensor_copy(out=so[k][:], in_=pb[j][:]), reads=[f"pb{j}"], writes=[f"so{k}"])
                    store(dst[t0 + tt * 128:t0 + (tt + 1) * 128, jj * 512:(jj + 1) * 512], f"so{k}", so[k][:], f"so{k}")
            sc.flush_wait_all()
            sc.emit(block)


    def phase_gla(self, sc, s, dirn):
        nc, S = self.nc, self.S
        t0 = s * S
        NP, NSS = S // 128, S // 512
        fwd = dirn == 0
        with ExitStack() as es, nc.Block() as block:
            Uinc = self.sb(es, "Uinc", [128, 128], BF16)
            Ustr = self.sb(es, "Ustr", [128, 128], BF16)
            Uf = self.sb(es, "Uf", [128, 256], F32)
            wgkf = self.sb(es, "wgkf", [17, 512], F32)
            mask4 = self.sb(es, "mask4", [128, 4, 128], F32)
            wgk = self.sb(es, "wgk_sb", [17, 512], BF16)
            gnwb = self.sb(es, "gnwb", [128, 1024], F32)
            glra = [self.sb(es, f"glra{i}", [17, 512], BF16) for i in range(2)]
            qT = [self.sb(es, f"gqT{i}", [128, 4, 512], BF16) for i in range(2)]
            kT = [self.sb(es, f"gkT{i}", [128, 4, 512], BF16) for i in range(2)]
            ktm = [self.sb(es, f"gktm{i}", [128, 4, 512], BF16) for i in range(2)]
            vv = [self.sb(es, f"gv{i}", [128, 4, 1024], BF16) for i in range(2)]
            obt = [self.sb(es, f"obt{i}", [128, 1024], F32) for i in range(2)]
            grt = [self.sb(es, f"grt{i}", [128, 1024], BF16) for i in range(2)]
            gat = [self.sb(es, f"gat{i}", [128, 1024], BF16) for i in range(2)]
            e1 = self.sb(es, "g_e1", [128, 512], F32)
            sp = self.sb(es, "g_sp", [128, 512], BF16)
            ebl = self.sb(es, "g_ebl", [128, 512], F32)
            ks = self.sb(es, "g_ks", [128, 512], BF16)
            eb = self.sb(es, "g_eb", [128, 4, 128], F32)
            enb = self.sb(es, "g_enb", [128, 4, 128], F32)
            qt = self.sb(es, "g_qt", [128, 4, 128], BF16)
            kt = self.sb(es, "g_kt", [128, 4, 128], BF16)
            attm = self.sb(es, "g_attm", [128, 4, 128], BF16)
            S32 = [self.sb(es, f"S32_{h}", [128, 256], F32) for h in range(4)]
            Sbf = [self.sb(es, f"Sbf_{h}", [128, 256], BF16) for h in range(4)]
            osb = [self.sb(es, f"osb{i}", [128, 1024], F32) for i in range(2)]
            ysb = [self.sb(es, f"ysb{i}", [128, 1024], F32) for i in range(2)]
            g32 = self.sb(es, "g_g32", [128, 1024], F32)
            junk = self.sb(es, "g_junk", [128, 4, 256], F32)
            stt = [self.sb(es, f"g_st{i}", [128, 12], F32) for i in range(2)]
            bA = self.ps(es, "bA", [128, 512])
            bB = self.ps(es, "bB", [128, 512])
            bO = [self.ps(es, f"bO{h}", [128, 512]) for h in range(4)]
            bKV = [self.ps(es, f"bKV{i}", [128, 512]) for i in range(2)]

            co = 128 * (1 + 2 * dirn)
            sc.dma("sp", lambda e: e.dma_start(out=Uf[:], in_=self.cst[:, co:co + 256]), "c_a", writes=["Uf"])
            sc.op("dve", lambda e: e.tensor_copy(out=Uinc[:], in_=Uf[:, 0:128]), reads=["Uf"], writes=["Uinc"])
            sc.op("dve", lambda e: e.tensor_copy(out=Ustr[:], in_=Uf[:, 128:256]), reads=["Uf"], writes=["Ustr"])
            for h in range(4):
                sc.dma("sp", lambda e, h=h: e.dma_start(out=mask4[:, h, :], in_=self.msk[:, dirn * 128:(dirn + 1) * 128]), "c_c", writes=["mask4"])
            sc.dma("sp", lambda e: e.dma_start(out=wgkf[:], in_=self.wgk[dirn, :, :]), "c_d", writes=["wgkf"])
            sc.op("dve", lambda e: e.tensor_copy(out=wgk[:], in_=wgkf[:]), reads=["wgkf"], writes=["wgk"])
            sc.dma("sp", lambda e: e.dma_start(out=gnwb[:], in_=self.gnw.broadcast_to([128, 1024])), "c_e", writes=["gnwb"])
            for i in range(2):
                sc.op("dve", lambda e, i=i: e.memset(glra[i][:], 1.0), writes=[f"glra{i}"])
            for h in range(4):
                sc.op("dve", lambda e, h=h: e.memset(S32[h][:], 0.0), writes=[f"S32_{h}"])
                sc.op("dve", lambda e, h=h: e.memset(Sbf[h][:], 0.0), writes=[f"Sbf_{h}"])

            if fwd:
                r1, c1, r2, c2 = slice(0, 64), 63, slice(64, 128), 127
            else:
                r1, c1, r2, c2 = slice(64, 128), 64, slice(0, 64), 0
            qscale = float(GLA_DK ** -0.5)

            def load_ss(ss, i):
                tb = t0 + ss * 512
                sc.dma("sp", lambda e: e.dma_start(out=glra[i][0:16, :], in_=self.glrT[dirn * 16:(dirn + 1) * 16, tb:tb + 512]), f"glra{i}", writes=[f"glra{i}"])
                sc.dma("sp", lambda e: e.dma_start(out=qT[i][:], in_=self.gqT[:, :, tb:tb + 512].rearrange("h p t -> p h t")), f"gqT{i}", writes=[f"gqT{i}"])
                sc.dma("sp", lambda e: e.dma_start(out=kT[i][:], in_=self.gkT[:, :, tb:tb + 512].rearrange("h p t -> p h t")), f"gkT{i}", writes=[f"gkT{i}"])
                sc.dma("sp", lambda e: e.dma_start(out=ktm[i][:], in_=self.tm["gk"][tb:tb + 512, :].rearrange("(a p) c -> p a c", p=128)), f"gktm{i}", writes=[f"gktm{i}"])
                sc.dma("sp", lambda e: e.dma_start(out=vv[i][:], in_=self.tm["gv"][tb:tb + 512, :].rearrange("(a p) c -> p a c", p=128)), f"gv{i}", writes=[f"gv{i}"])

            ks2 = [ks, self.sb(es, "g_ks_b", [128, 512], BF16)]
            eb2 = [eb, self.sb(es, "g_eb_b", [128, 4, 128], F32)]
            qt2 = [qt, self.sb(es, "g_qt_b", [128, 4, 128], BF16)]
            attm2 = [attm, self.sb(es, "g_attm_b", [128, 4, 128], BF16)]

            order = list(range(NSS)) if fwd else list(range(NSS - 1, -1, -1))
            steps = []
            for n, ss in enumerate(order):
                for m, sub in enumerate([0, 1, 2, 3] if fwd else [3, 2, 1, 0]):
                    steps.append(dict(n=n, ss=ss, i=n % 2, sub=sub, pi=(n * 4 + m) % 2, q=(n * 4 + m) % 2,
                                      tok=t0 + ss * 512 + sub * 128, tsl=slice(sub * 128, (sub + 1) * 128), first=(m == 0)))

            def loads(st):
                if fwd:
                    pi, tok = st["pi"], st["tok"]
                    sc.dma("sp", lambda e: e.dma_start(out=obt[pi][:], in_=self.ob[tok:tok + 128, :]), f"obt{pi}", writes=[f"obt{pi}"])
                    sc.dma("sp", lambda e: e.dma_start(out=grt[pi][:], in_=self.tm["gr"][tok:tok + 128, :]), f"grt{pi}", writes=[f"grt{pi}"])
                    sc.dma("sp", lambda e: e.dma_start(out=gat[pi][:], in_=self.tm["ga"][tok:tok + 128, :]), f"gat{pi}", writes=[f"gat{pi}"])

            def pre_a(st):
                i, tsl = st["i"], st["tsl"]
                sc.op("pe", lambda e: e.matmul(bA[:], lhsT=glra[i][0:17, tsl], rhs=wgk[0:17, :], start=True, stop=True),
                      reads=[f"glra{i}", "wgk"], writes=["bA"])
                sc.op("act", lambda e: e.activation(out=e1[:], in_=bA[:], func=AF.Exp, scale=-1.0), reads=["bA"], writes=["e1"])
                sc.op("act", lambda e: e.activation(out=sp[:], in_=e1[:], func=AF.Ln, bias=self.eps_t[:, 2:3]), reads=["e1"], writes=["sp"])

            def pre_b(st):
                i, sub, q = st["i"], st["sub"], st["q"]
                sc.op("pe", lambda e: e.matmul(bA[:], lhsT=Ustr[:], rhs=sp[:], start=True, stop=True), reads=["Ustr", "sp"], writes=["bA"])
                for h in range(4):
                    sc.op("pe", lambda e, h=h: e.matmul(bB[:, h * 128:(h + 1) * 128], lhsT=sp[:, h * 128:(h + 1) * 128], rhs=Uinc[:], start=True, stop=True),
                          reads=["sp", "Uinc"], writes=["bB"])
                sc.op("act", lambda e: e.activation(out=ebl[:], in_=bA[:], func=AF.Exp), reads=["bA"], writes=["ebl"])
                sc.op("act", lambda e: e.activation(out=eb2[q][:].rearrange("p h t -> p (h t)"), in_=bB[:], func=AF.Exp), reads=["bB"], writes=[f"eb{q}"])
                sc.op("act", lambda e: e.activation(out=enb[:].rearrange("p h t -> p (h t)"), in_=bB[:], func=AF.Exp, scale=-1.0), reads=["bB"], writes=["enb"])

            def pre_c(st):
                i, sub, q, tsl = st["i"], st["sub"], st["q"], st["tsl"]
                sc.op("pool", lambda e: e.tensor_tensor(out=ks2[q][:], in0=ktm[i][:, sub, :], in1=ebl[:], op=ALU.mult),
                      reads=[f"gktm{i}", "ebl"], writes=[f"ks{q}"])
                sc.op("dve", lambda e: e.scalar_tensor_tensor(out=qt2[q][:], in0=eb2[q][:], scalar=qscale, in1=qT[i][:, :, tsl], op0=ALU.mult, op1=ALU.mult),
                      reads=[f"eb{q}", f"gqT{i}"], writes=[f"qt{q}"])
                sc.op("pool", lambda e: e.tensor_tensor(out=kt[:], in0=enb[:], in1=kT[i][:, :, tsl], op=ALU.mult),
                      reads=["enb", f"gkT{i}"], writes=["kt"])

            def pre_d(st):
                q = st["q"]
                for h in range(4):
                    sc.op("pe", lambda e, h=h: e.matmul(bA[:, h * 128:(h + 1) * 128], lhsT=kt[:, h, :], rhs=qt2[q][:, h, :], start=True, stop=True),
                          reads=["kt", f"qt{q}"], writes=["bA"])
                sc.op("dve", lambda e: e.tensor_tensor(out=attm2[q][:].rearrange("p h t -> p (h t)"), in0=bA[:], in1=mask4[:].rearrange("p h t -> p (h t)"), op=ALU.mult),
                      reads=["bA", "mask4"], writes=[f"attm{q}"])

            def scan_open(st):
                i, sub, q = st["i"], st["sub"], st["q"]
                for h in range(4):
                    vs = slice(h * 256, (h + 1) * 256)
                    sc.op("pe", lambda e, h=h, vs=vs: e.matmul(bO[h][:, 0:256], lhsT=attm2[q][:, h, :], rhs=vv[i][:, sub, vs], start=True, stop=False),
                          reads=[f"attm{q}", f"gv{i}"], writes=[f"bO{h}"])
                    sc.op("pe", lambda e, h=h: e.matmul(bO[h][r1, 0:256], lhsT=qt2[q][:, h, r1], rhs=Sbf[h][:], start=False, stop=True),
                          reads=[f"qt{q}", f"Sbf_{h}"], writes=[f"bO{h}"])

            def scan_head(st, h):
                i, sub, q, pi = st["i"], st["sub"], st["q"], st["pi"]
                vs = slice(h * 256, (h + 1) * 256)
                kb = h % 2
                ksl = slice(kb * 256, (kb + 1) * 256)
                hs = slice(h * 128, (h + 1) * 128)
                sc.op("pe", lambda e: e.matmul(bKV[0][:, ksl], lhsT=ks2[q][r1, hs], rhs=vv[i][r1, sub, vs], start=True, stop=True),
                      reads=[f"ks{q}", f"gv{i}"], writes=["bKV0"])
                sc.op("pe", lambda e: e.matmul(bKV[1][:, ksl], lhsT=ks2[q][r2, hs], rhs=vv[i][r2, sub, vs], start=True, stop=True),
                      reads=[f"ks{q}", f"gv{i}"], writes=["bKV1"])
                sc.op("dve", lambda e: e.scalar_tensor_tensor(out=Sbf[h][:], in0=S32[h][:], scalar=eb2[q][:, h, c1:c1 + 1], in1=bKV[0][:, ksl], op0=ALU.mult, op1=ALU.add),
                      reads=[f"S32_{h}", f"eb{q}", "bKV0"], writes=[f"Sbf_{h}"])
                sc.op("dve", lambda e: e.scalar_tensor_tensor(out=S32[h][:], in0=S32[h][:], scalar=eb2[q][:, h, c1:c1 + 1], in1=bKV[0][:, ksl], op0=ALU.mult, op1=ALU.add),
                      reads=[f"S32_{h}", f"eb{q}", "bKV0"], writes=[f"S32_{h}"])
                sc.op("pe", lambda e: e.matmul(bO[h][r2, 0:256], lhsT=qt2[q][:, h, r2], rhs=Sbf[h][:], start=False, stop=True),
                      reads=[f"qt{q}", f"Sbf_{h}"], writes=[f"bO{h}"])
                sc.op("dve", lambda e: e.scalar_tensor_tensor(out=S32[h][:], in0=S32[h][:], scalar=eb2[q][:, h, c2:c2 + 1], in1=bKV[1][:, ksl], op0=ALU.mult, op1=ALU.add),
                      reads=[f"S32_{h}", f"eb{q}", "bKV1"], writes=[f"S32_{h}"])
                sc.op("act", lambda e: e.activation(out=Sbf[h][:], in_=S32[h][:], func=AF.Copy), reads=[f"S32_{h}"], writes=[f"Sbf_{h}"])
                if fwd:
                    sc.op("dve", lambda e: e.tensor_tensor(out=osb[pi][:, vs], in0=bO[h][:, 0:256], in1=obt[pi][:, vs], op=ALU.add),
                          reads=[f"bO{h}", f"obt{pi}"], writes=[(f"osb{pi}", h)])
                else:
                    sc.op("act", lambda e: e.activation(out=osb[pi][:, vs], in_=bO[h][:, 0:256], func=AF.Copy),
                          reads=[f"bO{h}"], writes=[(f"osb{pi}", h)])

            def epilogue(st):
                pi, tok = st["pi"], st["tok"]
                okeys = [(f"osb{pi}", h) for h in range(4)]
                if not fwd:
                    sc.dma("act", lambda e: e.dma_start(out=self.ob[tok:tok + 128, :], in_=osb[pi][:]), f"osb{pi}", reads=okeys, writes=[("ob", tok)])
                    return
                for h in range(4):
                    vs = slice(h * 256, (h + 1) * 256)
                    sc.op("act", lambda e, h=h, vs=vs: e.activation(out=junk[:, h, :], in_=osb[pi][:, vs], func=AF.Square, accum_out=stt[pi][:, h:h + 1]),
                          reads=[(f"osb{pi}", h)], writes=[f"junk{h}", (f"gst{pi}", h)])
                sc.op("act", lambda e: e.activation(out=stt[pi][:, 4:8], in_=stt[pi][:, 0:4], func=AF.Ln, scale=1.0 / GLA_DV, bias=self.eps_t[:, 0:1]),
                      reads=[(f"gst{pi}", h) for h in range(4)], writes=[f"gst{pi}"])
                sc.op("act", lambda e: e.activation(out=stt[pi][:, 8:12], in_=stt[pi][:, 4:8], func=AF.Exp, scale=-0.5), reads=[f"gst{pi}"], writes=[f"gst{pi}"])
                sc.op("pool", lambda e: e.tensor_tensor(out=g32[:], in0=grt[pi][:], in1=gat[pi][:], op=ALU.mult), reads=[f"grt{pi}", f"gat{pi}"], writes=["g32"])
                sc.op("pool", lambda e: e.tensor_tensor(out=g32[:], in0=g32[:], in1=gnwb[:], op=ALU.mult), reads=["g32", "gnwb"], writes=["g32"])
                for h in range(4):
                    vs = slice(h * 256, (h + 1) * 256)
                    sc.op("dve", lambda e, h=h, vs=vs: e.scalar_tensor_tensor(out=ysb[pi][:, vs], in0=osb[pi][:, vs], scalar=stt[pi][:, 8 + h:9 + h], in1=g32[:, vs], op0=ALU.mult, op1=ALU.mult),
                          reads=[(f"osb{pi}", h), f"gst{pi}", "g32"], writes=[(f"ysb{pi}", h)])
                sc.dma("act", lambda e: e.dma_start(out=self.ya[tok:tok + 128, :], in_=ysb[pi][:]), f"ysb{pi}",
                       reads=[(f"ysb{pi}", h) for h in range(4)], writes=[("ya", tok)])

            load_ss(order[0], 0)
            if NSS > 1:
                load_ss(order[1], 1)
            loads(steps[0])
            for f_ in (pre_a, pre_b, pre_c, pre_d):
                f_(steps[0])
            for p, st in enumerate(steps):
                nx = steps[p + 1] if p + 1 < len(steps) else None
                if nx is not None:
                    loads(nx)
                scan_open(st)
                parts = (pre_a, pre_b, pre_c, pre_d)
                for h in range(4):
                    if nx is not None:
                        parts[h](nx)
                    scan_head(st, h)
                epilogue(st)
                if nx is not None and nx["first"] and nx["n"] + 1 < NSS:
                    load_ss(order[nx["n"] + 1], (nx["n"] + 1) % 2)
            sc.flush_wait_all()
            sc.emit(block)

    def phase_attn(self, sc, s, wprep=False):
        nc, S, NT, NG = self.nc, self.S, self.NT, self.NG
        t0 = s * S
        LAG = 2
        with ExitStack() as es, nc.Block() as block:
            if wprep:
                wt = self.sb(es, "wprep", [128, KC, FFN], BF16)
                wsrc = self.w_fi.rearrange("(k p) c -> p k c", p=128)

                def wprep_load(g):
                    for kc in range(KC):
                        sc.dma("pool", lambda e, kc=kc: e.dma_start(out=wt[:, kc, :], in_=wsrc[:, kc, g * FFN:(g + 1) * FFN]), f"wp{kc % 4}", writes=[("wprep", kc)])

                def wprep_store(g):
                    allk = [("wprep", kc) for kc in range(KC)]
                    for f in range(FC):
                        sc.dma("sp", lambda e, f=f: e.dma_start(out=self.wfib[f, :, g, :, :], in_=wt[:, :, f * 128:(f + 1) * 128]), f"wps{f % 4}",
                               reads=allk, writes=[("wfib", f, g)])
                wprep_load(0)
            KT = [self.sb(es, f"aKT{i}", [128, 2, S], BF16) for i in range(2)]
            VA = [self.sb(es, f"aVA{i}", [128, NT, 258], BF16) for i in range(2)]
            QT = [self.sb(es, f"aQT{i}", [128, 2, 512], BF16) for i in range(2)]
            PT = [self.sb(es, f"aPT{i}", [128, 1024], BF16) for i in range(2)]
            A0 = self.sb(es, "aA0", [128, 4, 256], F32)
            o2 = self.sb(es, "ao2", [128, 4, 256], F32)
            gbt = [self.sb(es, f"agb{i}", [128, 4, 256], BF16) for i in range(2)]
            g32 = self.sb(es, "ag32", [128, 4, 256], F32)
            ybs = [self.sb(es, f"aybs{i}", [128, 4, 256], F32) for i in range(2)]
            slnb = self.sb(es, "aslnb", [128, 256], F32)
            lam = self.sb(es, "alam", [1, 512], F32)
            lt = self.sb(es, "alt", [1, 16], F32)
            ones1 = self.sb(es, "aones1", [1, 128], F32)
            nlam = self.sb(es, "anlam", [128, 1], F32)
            rr = self.sb(es, "arr", [128, 16], F32)
            stt = self.sb(es, "ast", [128, 12], F32)
            junk = self.sb(es, "ajunk", [128, 4, 256], F32)
            ACC = [self.ps(es, f"aACC{i}", [128, 512]) for i in range(4)]
            SB = [self.ps(es, f"aSB{i}", [128, 1024]) for i in range(2)]

            sc.dma("sp", lambda e: e.dma_start(out=lam[:], in_=self.lam4[:, :]), "c_a", writes=["lam"])
            sc.dma("sp", lambda e: e.dma_start(out=slnb[:], in_=self.sln.broadcast_to([128, 256])), "c_b", writes=["slnb"])
            sc.op("dve", lambda e: e.memset(ones1[:], 1.0), writes=["ones1"])
            sc.op("dve", lambda e: e.tensor_tensor(out=lam[:, 0:128], in0=lam[:, 0:128], in1=lam[:, 128:256], op=ALU.mult), reads=["lam"], writes=["lam"])
            sc.op("dve", lambda e: e.tensor_tensor(out=lam[:, 256:384], in0=lam[:, 256:384], in1=lam[:, 384:512], op=ALU.mult), reads=["lam"], writes=["lam"])
            sc.op("act", lambda e: e.activation(out=lam[:, 128:256], in_=lam[:, 0:128], func=AF.Copy, accum_out=lt[:, 0:1]), reads=["lam"], writes=["lam", "lt"])
            sc.op("act", lambda e: e.activation(out=lam[:, 384:512], in_=lam[:, 256:384], func=AF.Copy, accum_out=lt[:, 1:2]), reads=["lam"], writes=["lam", "lt"])
            sc.op("act", lambda e: e.activation(out=lt[:, 2:4], in_=lt[:, 0:2], func=AF.Exp), reads=["lt"], writes=["lt"])
            sc.op("dve", lambda e: e.tensor_tensor(out=lt[:, 4:5], in0=lt[:, 3:4], in1=lt[:, 2:3], op=ALU.subtract), reads=["lt"], writes=["lt"])
            sc.op("dve", lambda e: e.tensor_scalar(out=lt[:, 5:6], in0=lt[:, 4:5], scalar1=-LAMBDA_INIT, scalar2=None, op0=ALU.add), reads=["lt"], writes=["lt"])
            sc.op("pe", lambda e: e.matmul(ACC[0][:, 0:1], lhsT=ones1[:], rhs=lt[:, 5:6], start=True, stop=True), reads=["ones1", "lt"], writes=["aACC0"])
            sc.op("dve", lambda e: e.tensor_copy(out=nlam[:], in_=ACC[0][:, 0:1]), reads=["aACC0"], writes=["nlam"])
            sc.op("dve", lambda e: e.tensor_scalar(out=slnb[:], in0=slnb[:], scalar1=float(1.0 - LAMBDA_INIT), scalar2=None, op0=ALU.mult), reads=["slnb"], writes=["slnb"])
            for i in range(2):
                sc.op("pool", lambda e, i=i: e.memset(VA[i][:, :, 256:258], 1.0), writes=[f"aVA{i}"])

            def load_head(h, i):
                for c in range(2):
                    sc.dma("sp", lambda e, c=c: e.dma_start(out=KT[i][:, c, :], in_=self.dkT[2 * h + c, :, t0:t0 + S]), f"aKT{i}", writes=[f"aKT{i}"])
                for a0 in range(0, NT, 8):
                    a1 = min(NT, a0 + 8)
                    sc.dma("sp", lambda e, a0=a0, a1=a1: e.dma_start(
                        out=VA[i][:, a0:a1, 0:256],
                        in_=self.tm["dv"][t0 + a0 * 128:t0 + a1 * 128, h * 256:(h + 1) * 256].rearrange("(a p) c -> p a c", p=128)),
                        f"aVA{i}", writes=[f"aVA{i}"])

            scale = float(DIFF_HD ** -0.5)
            load_head(0, 0)
            work = [(h, qg) for h in range(DIFF_H) for qg in range(NG)]

            def load_q(widx):
                h, qg = work[widx]
                qi, tb = widx % 2, t0 + qg * 512
                for c in range(2):
                    sc.dma("sp", lambda e, c=c: e.dma_start(out=QT[qi][:, c, :], in_=self.dqT[2 * h + c, :, tb:tb + 512]), f"aQT{qi}", writes=[f"aQT{qi}"])
                sc.dma("sp", lambda e: e.dma_start(out=gbt[qi][:], in_=self.tm["gb"][tb:tb + 512, h * 256:(h + 1) * 256].rearrange("(a p) c -> p a c", p=128)),
                       f"agb{qi}", writes=[f"agb{qi}"])

            load_q(0)
            cnt = 0
            for h in range(DIFF_H):
                hi = h % 2
                if h + 1 < DIFF_H:
                    load_head(h + 1, (h + 1) % 2)
                if wprep and h == 1:
                    wprep_store(0)
                    wprep_load(1)
                if wprep and h == 3:
                    wprep_store(1)
                for qg in range(NG):
                    qi = cnt % 2
                    cnt += 1
                    tb = t0 + qg * 512
                    if cnt < len(work):
                        load_q(cnt)
                    for c in range(2):
                        NP2 = NT // 2
                        for step in range(NP2 + 1):
                            if step < NP2:
                                b_ = step % 2
                                for u in range(2):
                                    k_ = 2 * step + u
                                    sc.op("pe", lambda e, c=c, k_=k_, b_=b_, u=u, hi=hi, qi=qi: e.matmul(SB[b_][:, u * 512:(u + 1) * 512], lhsT=KT[hi][:, c, k_ * 128:(k_ + 1) * 128],
                                                                                                    rhs=QT[qi][:, c, :], start=True, stop=True),
                                          reads=[f"aKT{hi}", f"aQT{qi}"], writes=[f"aSB{b_}"])
                                sc.op("act", lambda e, b_=b_: e.activation(out=PT[b_][:], in_=SB[b_][:], func=AF.Exp, scale=scale, bias=self.eps_t[:, 3:4]),
                                      reads=[f"aSB{b_}"], writes=[f"aPT{b_}"])
                            if step >= 1:
                                b_ = (step - 1) % 2
                                for u in range(2):
                                    k_ = 2 * (step - 1) + u
                                    for qs in range(4):
                                        sc.op("pe", lambda e, k_=k_, b_=b_, u=u, qs=qs, hi=hi: e.matmul(ACC[qs][:, 0:257], lhsT=PT[b_][:, u * 512 + qs * 128:u * 512 + (qs + 1) * 128],
                                                                                                  rhs=VA[hi][:, k_, 0:257], start=(k_ == 0), stop=(k_ == NT - 1)),
                                              reads=[f"aPT{b_}", f"aVA{hi}"], writes=[f"aACC{qs}"])
                        for qs in range(4):
                            col = c * 4 + qs
                            sc.op("dve", lambda e, qs=qs, col=col: e.reciprocal(out=rr[:, col:col + 1], in_=ACC[qs][:, 256:257]), reads=[f"aACC{qs}"], writes=[("rr", col)])
                            if c == 0:
                                sc.op("dve", lambda e, qs=qs, col=col: e.tensor_scalar(out=A0[:, qs, :], in0=ACC[qs][:, 0:256], scalar1=rr[:, col:col + 1], scalar2=None, op0=ALU.mult),
                                      reads=[f"aACC{qs}", ("rr", col)], writes=[("A0", qs)])
                            else:
                                sc.op("dve", lambda e, col=col: e.tensor_tensor(out=rr[:, 8 + col:9 + col], in0=rr[:, col:col + 1], in1=nlam[:], op=ALU.mult), reads=[("rr", col), "nlam"], writes=[("rr", 8 + col)])
                                sc.op("dve", lambda e, qs=qs, col=col: e.scalar_tensor_tensor(out=o2[:, qs, :], in0=ACC[qs][:, 0:256], scalar=rr[:, 8 + col:9 + col], in1=A0[:, qs, :], op0=ALU.mult, op1=ALU.add),
                                      reads=[f"aACC{qs}", ("rr", 8 + col), ("A0", qs)], writes=[("o2", qs)])
                    for qs in range(4):
                        sc.op("act", lambda e, qs=qs: e.activation(out=junk[:, qs, :], in_=o2[:, qs, :], func=AF.Square, accum_out=stt[:, qs:qs + 1]), reads=[("o2", qs)], writes=[f"ajunk{qs}", ("ast", qs)])
                    sc.op("act", lambda e: e.activation(out=stt[:, 4:8], in_=stt[:, 0:4], func=AF.Ln, scale=1.0 / 256, bias=self.eps_t[:, 1:2]), reads=[("ast", q_) for q_ in range(4)], writes=["ast"])
                    sc.op("act", lambda e: e.activation(out=stt[:, 8:12], in_=stt[:, 4:8], func=AF.Exp, scale=-0.5), reads=["ast"], writes=["ast"])
                    for qs in range(4):
                        sc.op("pool", lambda e, qs=qs, qi=qi: e.tensor_tensor(out=g32[:, qs, :], in0=gbt[qi][:, qs, :], in1=slnb[:], op=ALU.mult), reads=[f"agb{qi}", "slnb"], writes=[("ag32", qs)])
                    for qs in range(4):
                        sc.op("dve", lambda e, qs=qs, qi=qi: e.scalar_tensor_tensor(out=ybs[qi][:, qs, :], in0=o2[:, qs, :], scalar=stt[:, 8 + qs:9 + qs], in1=g32[:, qs, :], op0=ALU.mult, op1=ALU.mult),
                              reads=[("o2", qs), "ast", ("ag32", qs)], writes=[(f"aybs{qi}", qs)])
                    sc.dma("pool", lambda e, qi=qi, tb=tb, h=h: e.dma_start(out=self.yb[tb:tb + 512, h * 256:(h + 1) * 256].rearrange("(a p) c -> p a c", p=128), in_=ybs[qi][:]),
                           f"aybs{qi}", reads=[(f"aybs{qi}", q_) for q_ in range(4)], writes=[("yb", tb, h)])
            sc.flush_wait_all()
            sc.emit(block)

    def phase_wprep(self, sc):
        nc = self.nc
        with ExitStack() as es, nc.Block() as block:
            wt = self.sb(es, "wprep", [128, KC, 2 * FFN], BF16)
            src = self.w_fi.rearrange("(k p) c -> p k c", p=128)
            for kc in range(KC):
                sc.dma("pool", lambda e, kc=kc: e.dma_start(out=wt[:, kc, :], in_=src[:, kc, :]), f"wp{kc % 4}", writes=[("wprep", kc)])
            allk = [("wprep", kc) for kc in range(KC)]
            for f in range(FC):
                for g in range(2):
                    c0 = g * FFN + f * 128
                    sc.dma("sp", lambda e, f=f, g=g, c0=c0: e.dma_start(out=self.wfib[f, :, g, :, :], in_=wt[:, :, c0:c0 + 128]), f"wps{(2 * f + g) % 4}",
                           reads=allk, writes=[("wfib", f, g)])
            sc.flush_wait_all()
            sc.emit(block)

    def phase_ffn(self, sc):
        nc, S, T = self.nc, self.S, self.T
        NGT = T // 512
        with ExitStack() as es, nc.Block() as block:
            wo = self.sb(es, "f_wo", [128, KC, D], BF16)
            wfo = self.sb(es, "f_wfo", [128, FC, D], BF16)
            wfi = [self.sb(es, f"f_wfi{i}", [128, 2, KC, 128], BF16) for i in range(4)]
            ident = self.sb(es, "f_ident", [128, 128], BF16)
            identf = self.sb(es, "f_identf", [128, 128], F32)
            nwf = self.sb(es, "f_nwf", [128, KC], F32)
            nwfin = self.sb(es, "f_nwfin", [128, D], F32)
            la = [self.sb(es, f"f_la{i}", [128, D], F32) for i in range(2)]
            lb = [self.sb(es, f"f_lb{i}", [128, D], F32) for i in range(2)]
            lx = [self.sb(es, f"f_lx{i}", [128, D], F32) for i in range(2)]
            mb = [self.sb(es, f"f_mb{i}", [128, D], BF16) for i in range(2)]
            mT = self.sb(es, "f_mT", [128, KC, 512], BF16)
            hT = self.sb(es, "f_hT", [128, KC, 512], BF16)
            hsb = self.sb(es, "f_h", [128, 4, D], F32)
            aT = self.sb(es, "f_aT", [128, FC, 512], BF16)
            sg = [self.sb(es, f"f_sg{i}", [128, 512], F32) for i in range(2)]
            ot = [self.sb(es, f"f_ot{i}", [128, D], F32) for i in range(2)]
            stt = [self.sb(es, f"f_st{i}", [128, 4], F32) for i in range(2)]
            psT = [self.ps(es, f"f_psT{i}", [128, 1024], BF16) for i in range(2)]
            pb = [self.ps(es, f"f_pb{i}", [128, 512]) for i in range(6)]

            sc.dma("pool", lambda e: e.dma_start(out=wo[:], in_=self.w_out.rearrange("(k p) c -> p k c", p=128)), "pw_a", writes=["wo"])
            for f0 in range(0, FC, 2):
                sc.dma("pool", lambda e, f0=f0: e.dma_start(out=wfo[:, f0:f0 + 2, :], in_=self.w_fo[f0 * 128:(f0 + 2) * 128, :].rearrange("(k p) c -> p k c", p=128)),
                       f"pw_b{(f0 // 2) % 2}", writes=[("wfo", f0)])
            wfo_all = [("wfo", f0) for f0 in range(0, FC, 2)]
            sc.dma("sp", lambda e: e.dma_start(out=identf[:], in_=self.cst[:, 0:128]), "c_c", writes=["identf"])
            sc.op("dve", lambda e: e.tensor_copy(out=ident[:], in_=identf[:]), reads=["identf"], writes=["ident"])
            sc.dma("sp", lambda e: e.dma_start(out=nwf[:], in_=self.nw_ffn[:, :]), "c_d", writes=["nwf"])
            sc.dma("sp", lambda e: e.dma_start(out=nwfin[:], in_=self.nw_fin.broadcast_to([128, D])), "c_e", writes=["nwfin"])

            xf = self.x.rearrange("n s d -> (n s) d")
            yf = self.y.rearrange("n s d -> (n s) d")
            st8 = dict(pbi=0, li=0, wi=0, sgi=0, oti=0, pti=0)

            def next_pb():
                j = st8["pbi"] % 6
                st8["pbi"] += 1
                return j

            def norm_rows(src_ap, src_keys, i):
                sc.op("act", lambda e: e.activation(out=mb[i][:], in_=src_ap, func=AF.Square, accum_out=stt[i][:, 0:1]), reads=src_keys, writes=[f"mb{i}", f"fst{i}"])
                sc.op("act", lambda e: e.activation(out=stt[i][:, 1:2], in_=stt[i][:, 0:1], func=AF.Ln, scale=1.0 / D, bias=self.eps_t[:, 0:1]), reads=[f"fst{i}"], writes=[f"fst{i}"])
                sc.op("act", lambda e: e.activation(out=stt[i][:, 2:3], in_=stt[i][:, 1:2], func=AF.Exp, scale=-0.5), reads=[f"fst{i}"], writes=[f"fst{i}"])

            def transpose_to(dstT, dkey, i, tt_, scal):
                pt = st8["pti"] % 2
                st8["pti"] += 1
                for kc in range(KC):
                    sc.op("pe", lambda e, kc=kc: e.transpose(psT[pt][:, kc * 128:(kc + 1) * 128], mb[i][:, kc * 128:(kc + 1) * 128], ident[:]),
                          reads=[f"mb{i}", "ident"], writes=[f"f_psT{pt}"])
                for kc in range(KC):
                    if scal is None:
                        sc.op("dve", lambda e, kc=kc: e.tensor_copy(out=dstT[:, kc, tt_ * 128:(tt_ + 1) * 128], in_=psT[pt][:, kc * 128:(kc + 1) * 128]),
                              reads=[f"f_psT{pt}"], writes=[(dkey, tt_)])
                    else:
                        sc.op("dve", lambda e, kc=kc: e.tensor_scalar(out=dstT[:, kc, tt_ * 128:(tt_ + 1) * 128], in0=psT[pt][:, kc * 128:(kc + 1) * 128],
                                                                       scalar1=scal[:, kc:kc + 1], scalar2=None, op0=ALU.mult),
                              reads=[f"f_psT{pt}", "nwf"], writes=[(dkey, tt_)])

            for g in range(NGT):
                tb = g * 512
                for tt_ in range(4):
                    i = st8["li"] % 2
                    st8["li"] += 1
                    r0 = tb + tt_ * 128
                    sc.dma("sp", lambda e, i=i, r0=r0: e.dma_start(out=la[i][:], in_=self.ya[r0:r0 + 128, :]), f"f_la{i}", writes=[f"la{i}"])
                    sc.dma("sp", lambda e, i=i, r0=r0: e.dma_start(out=lb[i][:], in_=self.yb[r0:r0 + 128, :]), f"f_lb{i}", writes=[f"lb{i}"])
                    sc.op("pool", lambda e, i=i: e.tensor_tensor(out=mb[i][:], in0=la[i][:], in1=lb[i][:], op=ALU.add), reads=[f"la{i}", f"lb{i}"], writes=[f"mb{i}"])
                    transpose_to(mT, "mT", i, tt_, None)
                for tt_ in range(4):
                    i = st8["li"] % 2
                    st8["li"] += 1
                    r0 = tb + tt_ * 128
                    sc.dma("sp", lambda e, i=i, r0=r0: e.dma_start(out=lx[i][:], in_=xf[r0:r0 + 128, :]), f"f_lx{i}", writes=[f"lx{i}"])
                    for nh in range(2):
                        j = next_pb()
                        for kc in range(KC):
                            sc.op("pe", lambda e, kc=kc, j=j, nh=nh, tt_=tt_: e.matmul(pb[j][:], lhsT=mT[:, kc, tt_ * 128:(tt_ + 1) * 128], rhs=wo[:, kc, nh * 512:(nh + 1) * 512],
                                                                                    start=(kc == 0), stop=(kc == KC - 1)),
                                  reads=[("mT", tt_), "wo"], writes=[f"f_pb{j}"])
                        sc.op("dve", lambda e, j=j, nh=nh, tt_=tt_, i=i: e.tensor_tensor(out=hsb[:, tt_, nh * 512:(nh + 1) * 512], in0=pb[j][:], in1=lx[i][:, nh * 512:(nh + 1) * 512], op=ALU.add),
                              reads=[f"f_pb{j}", f"lx{i}"], writes=[("h", tt_)])
                    norm_rows(hsb[:, tt_, :], [("h", tt_)], i)
                    sc.op("dve", lambda e, i=i, tt_=tt_: e.tensor_scalar(out=mb[i][:], in0=hsb[:, tt_, :], scalar1=stt[i][:, 2:3], scalar2=None, op0=ALU.mult),
                          reads=[("h", tt_), f"fst{i}"], writes=[f"mb{i}"])
                    transpose_to(hT, "hT", i, tt_, nwf)
                hT_all = [("hT", k) for k in range(4)]
                for f in range(FC):
                    wi = st8["wi"] % 4
                    st8["wi"] += 1
                    sc.dma("sp", lambda e, wi=wi, f=f: e.dma_start(out=wfi[wi][:], in_=self.wfib[f, :, :, :, :]), f"f_wfi{wi}", reads=[("wfib", f, 0), ("wfib", f, 1)], writes=[f"wfi{wi}"])
                    jg, ju = next_pb(), next_pb()
                    for gi, j in ((0, jg), (1, ju)):
                        for kc in range(KC):
                            sc.op("pe", lambda e, kc=kc, j=j, gi=gi, wi=wi: e.matmul(pb[j][:], lhsT=wfi[wi][:, gi, kc, :], rhs=hT[:, kc, :], start=(kc == 0), stop=(kc == KC - 1)),
                                  reads=[f"wfi{wi}"] + hT_all, writes=[f"f_pb{j}"])
                    si = st8["sgi"] % 2
                    st8["sgi"] += 1
                    sc.op("act", lambda e, jg=jg, si=si: e.activation(out=sg[si][:], in_=pb[jg][:], func=AF.Silu), reads=[f"f_pb{jg}"], writes=[f"sg{si}"])
                    sc.op("dve", lambda e, ju=ju, si=si, f=f: e.tensor_tensor(out=aT[:, f, :], in0=pb[ju][:], in1=sg[si][:], op=ALU.mult),
                          reads=[f"f_pb{ju}", f"sg{si}"], writes=[("aT", f)])
                aT_all = [("aT", f) for f in range(FC)]
                for tt_ in range(4):
                    oi = st8["oti"] % 2
                    st8["oti"] += 1
                    r0 = tb + tt_ * 128
                    for nh in range(2):
                        j = next_pb()
                        for f in range(FC):
                            sc.op("pe", lambda e, f=f, j=j, nh=nh, tt_=tt_: e.matmul(pb[j][:], lhsT=aT[:, f, tt_ * 128:(tt_ + 1) * 128], rhs=wfo[:, f, nh * 512:(nh + 1) * 512],
                                                                                  start=(f == 0), stop=(f == FC - 1)),
                                  reads=aT_all + wfo_all, writes=[f"f_pb{j}"])
                        sc.op("dve", lambda e, j=j, nh=nh, tt_=tt_: e.tensor_tensor(out=hsb[:, tt_, nh * 512:(nh + 1) * 512], in0=pb[j][:], in1=hsb[:, tt_, nh * 512:(nh + 1) * 512], op=ALU.add),
                              reads=[f"f_pb{j}", ("h", tt_)], writes=[("h", tt_)])
                    norm_rows(hsb[:, tt_, :], [("h", tt_)], oi)
                    sc.op("dve", lambda e, oi=oi, tt_=tt_: e.scalar_tensor_tensor(out=ot[oi][:], in0=hsb[:, tt_, :], scalar=stt[oi][:, 2:3], in1=nwfin[:], op0=ALU.mult, op1=ALU.mult),
                          reads=[("h", tt_), f"fst{oi}", "nwfin"], writes=[f"ot{oi}"])
                    sc.dma("pool", lambda e, oi=oi, r0=r0: e.dma_start(out=yf[r0:r0 + 128, :], in_=ot[oi][:]), f"f_ot{oi}", reads=[f"ot{oi}"], writes=[("y", r0)])
            sc.flush_wait_all()
            sc.emit(block)

    def build(self):
        nc = self.nc
        self.declare()
        with ExitStack() as es:
            sc = Sched(nc, es)
            self.eps_t = self.sb(es, "eps_t", [128, 4], F32)
            with nc.Block() as block:
                sc.op("dve", lambda e: e.memset(self.eps_t[:, 0:1], NORM_EPS), writes=["eps"])
                sc.op("dve", lambda e: e.memset(self.eps_t[:, 1:2], SUBLN_EPS), writes=["eps"])
                sc.op("dve", lambda e: e.memset(self.eps_t[:, 2:3], 1.0), writes=["eps"])
                sc.op("dve", lambda e: e.memset(self.eps_t[:, 3:4], 0.0), writes=["eps"])
                sc.emit(block)
            for s in range(self.NSEQ):
                if "p1" in self.phases:
                    self.phase1(sc, s)
                if "gla" in self.phases:
                    self.phase_gla(sc, s, 1)
                    self.phase_gla(sc, s, 0)
                if "attn" in self.phases:
                    self.phase_attn(sc, s, wprep=("ffn" in self.phases and s == self.NSEQ - 1))
            if "ffn" in self.phases:
                if "attn" not in self.phases:
                    self.phase_wprep(sc)
                self.phase_ffn(sc)
        return nc


def host_consts(S):
    inv_freq = 1.0 / (10000.0 ** (np.arange(0, DIFF_HD, 2, dtype=np.float32) / DIFF_HD))
    pos = np.arange(S, dtype=np.float32)
    fr = pos[None, :] * np.concatenate([inv_freq, inv_freq])[:, None].astype(np.float32)
    cosT = np.cos(fr).astype(np.float32)
    sgn = np.where(np.arange(128) < 64, -1.0, 1.0).astype(np.float32)[:, None]
    sinT = (np.sin(fr) * sgn).astype(np.float32)
    p = np.arange(128)
    same = (p[:, None] // 64) == (p[None, :] // 64)
    sI, tI = p[:, None], p[None, :]
    g = -1.0 / 16.0
    ident = np.eye(128, dtype=np.float32)
    uinc_f = (same & (sI <= tI)) * g
    ustr_f = (same & (sI > tI)) * g
    uinc_b = (same & (sI >= tI)) * g
    ustr_b = (same & (sI < tI)) * g
    cst = np.concatenate([ident, uinc_f, ustr_f, uinc_b, ustr_b, np.zeros((128, 128))], axis=1).astype(np.float32)
    m_f = (same & (sI <= tI)).astype(np.float32)
    m_b = (same & (sI >= tI)).astype(np.float32)
    msk = np.concatenate([m_f, m_b], axis=1).astype(np.float32)
    return cosT, sinT, cst, msk


def host_inputs(inp, S):
    c = np.ascontiguousarray
    w_in = np.asarray(inp["w_in"])[0]
    w1 = c(w_in[:, _w1_columns()])
    cosT, sinT, cst, msk = host_consts(S)
    wgk = np.concatenate([np.asarray(inp["w_gk2"])[0], np.asarray(inp["b_gk"])[0][:, None, :]], axis=1)
    shared = dict(
        w1=w1,
        nw_mixr=c(np.asarray(inp["norm_mix_w"])[0].reshape(1, D)),
        nw_ffn=c(np.asarray(inp["norm_ffn_w"])[0].reshape(KC, 128).T),
        nw_fin=c(np.asarray(inp["norm_final_w"]).reshape(1, D)),
        cosT=cosT, sinT=sinT, cst=cst, msk=msk,
        wgk=c(wgk.astype(np.float32)),
        gnw=c(np.tile(np.asarray(inp["gla_norm_w"])[0], 4).reshape(1, 1024)),
        sln=c(np.asarray(inp["diff_subln_w"])[0].reshape(1, 256)),
        lam4=c(np.concatenate([np.asarray(inp[k])[0] for k in ("lambda_q1", "lambda_k1", "lambda_q2", "lambda_k2")]).reshape(1, 512)),
        w_out=c(np.asarray(inp["w_out"])[0]),
        w_fi=c(np.asarray(inp["w_ffn_in"])[0]),
        w_fo=c(np.asarray(inp["w_ffn_out"])[0]),
    )
    return {k: np.asarray(v, dtype=np.float32) for k, v in shared.items()}


def kernel(**inputs):
    x = np.asarray(inputs["x"], dtype=np.float32)
    B, S, _ = x.shape
    nseq = B // N_CORES
    b = Builder(S, nseq)
    nc = b.build()
    shared = host_inputs(inputs, S)
    in_maps = [dict(shared, x=np.ascontiguousarray(x[i * nseq:(i + 1) * nseq])) for i in range(N_CORES)]
    res = run_bass_kernel_spmd(nc, in_maps, core_ids=list(range(N_CORES)))
    return np.concatenate([np.asarray(r["y"]) for r in res.results], axis=0).astype(np.float32)
```

```python
from contextlib import ExitStack
import math
import numpy as np
import ml_dtypes
import concourse.bass as bass
import concourse.mybir as mybir
from concourse.bass_utils import run_bass_kernel_spmd

F32 = mybir.dt.float32
BF16 = mybir.dt.bfloat16
AF = mybir.ActivationFunctionType
ALU = mybir.AluOpType

D = 1024
KC = D // 128
GLA_H, GLA_DK, GLA_DV, GLA_RANK = 4, 128, 256, 16
DIFF_H, DIFF_HD = 4, 128
FFN = 2816
FC = FFN // 128
NORM_EPS = 1e-6
SUBLN_EPS = 1e-5
LAMBDA_INIT = 0.8 - 0.6 * math.exp(-0.3 * 0)
N_CORES = 8

OFF_GQ, OFF_GK, OFF_GV, OFF_GR, OFF_GLR = 0, 512, 1024, 2048, 3072
OFF_DQ, OFF_DK, OFF_DV, OFF_GA, OFF_GB = 3104, 4128, 5152, 6176, 7200


def _w1_columns():
    cols = []
    cols += list(range(OFF_GQ, OFF_GQ + 512))
    cols += list(range(OFF_GK, OFF_GK + 512))
    cols += list(range(OFF_DQ, OFF_DQ + 1024))
    cols += list(range(OFF_DK, OFF_DK + 1024))
    cols += list(range(OFF_GLR, OFF_GLR + 32))
    cols += list(range(OFF_GK, OFF_GK + 512))
    for off in (OFF_GV, OFF_GR, OFF_DV, OFF_GA, OFF_GB):
        cols += list(range(off, off + 1024))
    return np.array(cols, dtype=np.int64)


W1_COLS = 512 * 6 + 32 + 512 * 11
FM_BLOCKS = [("gq", 0), ("gk", 512)] + [("dq", 1024 + 512 * i) for i in range(2)] + \
            [("dk", 2048 + 512 * i) for i in range(2)]
GLR_OFF = 3072
TM_OFF = 3104
TM_BLOCKS = [("gk", 0)] + [(n, j) for n in ("gv", "gr", "dv", "ga", "gb") for j in range(2)]


class Sched:
    CE = ("pe", "act", "dve", "pool")
    ALL = ("pe", "act", "dve", "pool", "sp")

    def __init__(self, nc, es):
        self.nc = nc
        self.prog = {e: es.enter_context(nc.semaphore("prog_" + e)) for e in self.CE}
        self.cum = {e: 0 for e in self.CE}
        self.dsem = {}
        self.es = es
        self.res_w = {}
        self.res_r = {}
        self.ops = {e: [] for e in self.ALL}
        self.waited = {e: {} for e in self.ALL}
        self.sigval = {}
        self.uid = 0

    def _slot(self, slot):
        if slot not in self.dsem:
            self.dsem[slot] = [self.es.enter_context(self.nc.semaphore("d_" + slot)), 0]
        return self.dsem[slot]

    def _collect(self, eng, reads, writes, is_dma):
        deps = []
        for k in reads:
            for ev in self.res_w.get(k, ()):
                deps.append((ev, True))
        for k in writes:
            for ev in self.res_w.get(k, ()):
                deps.append((ev, False))
            for ev in self.res_r.get(k, ()):
                deps.append((ev, False))
        out = []
        for ev, raw in deps:
            if ev[0] == "eng" and ev[1] == eng and not is_dma:
                if eng == "pe":
                    continue
            out.append(ev)
        return out

    def _update(self, ev, reads, writes):
        for k in writes:
            self.res_w[k] = [ev]
            self.res_r[k] = []
        for k in reads:
            self.res_r.setdefault(k, []).append(ev)

    def op(self, eng, fn, reads=(), writes=()):
        self.uid += 1
        ev = ("eng", eng, self.uid)
        deps = self._collect(eng, reads, writes, False)
        self.ops[eng].append(dict(fn=fn, deps=deps, ev=ev, dma=None))
        self._update(ev, reads, writes)
        return ev

    def dma(self, queue, fn, slot, reads=(), writes=()):
        s = self._slot(slot)
        deps = self._collect(queue, reads, writes, True)
        if s[1] > 0:
            deps.append(("dma", slot, s[1]))
        s[1] += 1
        ev = ("dma", slot, s[1])
        self.ops[queue].append(dict(fn=fn, deps=deps, ev=ev, dma=slot))
        self._update(ev, reads, writes)
        return ev

    def flush_wait_all(self):
        deps = [("dma", slot, s[1]) for slot, s in self.dsem.items() if s[1] > 0]
        self.ops["sp"].append(dict(fn=None, deps=deps, ev=None, dma=None))

    def emit(self, block):
        done = getattr(self, "done_uid", 0)
        for e in self.ALL:
            for o in self.ops[e]:
                o["deps"] = [d for d in o["deps"] if not (d[0] == "eng" and d[2] <= done)]
        self.done_uid = self.uid
        need = set()
        for e in self.ALL:
            for o in self.ops[e]:
                for d in o["deps"]:
                    if d[0] == "eng":
                        need.add(d[2])
        for e in self.CE:
            c = self.cum[e]
            for o in self.ops[e]:
                if o["dma"] is None and o["ev"] is not None and o["ev"][2] in need:
                    c += 1
                    self.sigval[o["ev"][2]] = c
            self.cum[e] = c
        starters = dict(pe=block.tensor, act=block.scalar, dve=block.vector, pool=block.gpsimd, sp=block.sync)
        for e in self.ALL:
            ops = self.ops[e]
            if not ops:
                continue
            waited = self.waited[e]

            def body(eng, ops=ops, waited=waited, e=e):
                for o in ops:
                    for d in o["deps"]:
                        if d[0] == "eng":
                            key, sem, val = d[1], self.prog[d[1]], self.sigval[d[2]]
                        else:
                            key, sem, val = "d_" + d[1], self.dsem[d[1]][0], 16 * d[2]
                        if waited.get(key, 0) >= val:
                            continue
                        eng.wait_ge(sem, val)
                        waited[key] = val
                    if o["fn"] is None:
                        continue
                    ins = o["fn"](eng)
                    if o["dma"] is not None:
                        ins.then_inc(self.dsem[o["dma"]][0], 16)
                    elif o["ev"][2] in self.sigval:
                        ins.then_inc(self.prog[e], 1)
            starters[e](body)
        self.ops = {e: [] for e in self.ALL}


class Builder:
    def __init__(self, S, NSEQ, debug=False, phases=("p1", "gla", "attn", "ffn"), scratch_in=False, cut=99):
        self.S, self.NSEQ, self.debug, self.phases = S, NSEQ, debug, phases
        self.scratch_in, self.cut = scratch_in, cut
        self.T = S * NSEQ
        self.NT = S // 128
        self.NG = S // 512
        self.nc = bass.Bass("TRN2", target_bir_lowering=False)

    def dram(self, name, shape, dt, kind):
        return self.nc.dram_tensor(name, list(shape), dt, kind=kind).ap()

    def declare(self):
        S, T, NSEQ = self.S, self.T, self.NSEQ
        I = "ExternalInput"
        self.x = self.dram("x", [NSEQ, S, D], F32, I)
        self.w1 = self.dram("w1", [D, W1_COLS], F32, I)
        self.nw_mixr = self.dram("nw_mixr", [1, D], F32, I)
        self.nw_ffn = self.dram("nw_ffn", [128, KC], F32, I)
        self.nw_fin = self.dram("nw_fin", [1, D], F32, I)
        self.cosT = self.dram("cosT", [128, S], F32, I)
        self.sinT = self.dram("sinT", [128, S], F32, I)
        self.wgk = self.dram("wgk", [2, 17, 512], F32, I)
        self.gnw = self.dram("gnw", [1, 1024], F32, I)
        self.sln = self.dram("sln", [1, 256], F32, I)
        self.lam4 = self.dram("lam4", [1, 512], F32, I)
        self.w_out = self.dram("w_out", [D, D], F32, I)
        self.w_fi = self.dram("w_fi", [D, 2 * FFN], F32, I)
        self.w_fo = self.dram("w_fo", [FFN, D], F32, I)
        self.cst = self.dram("cst", [128, 6 * 128], F32, I)
        self.msk = self.dram("msk", [128, 2 * 128], F32, I)
        self.y = self.dram("y", [NSEQ, S, D], F32, "ExternalOutput")
        K = "ExternalOutput" if self.debug else "Internal"
        K1 = K
        if self.scratch_in:
            K = "ExternalInput"
        self.gqT = self.dram("gqT", [4, 128, T], BF16, K)
        self.gkT = self.dram("gkT", [4, 128, T], BF16, K)
        self.dqT = self.dram("dqT", [8, 128, T], BF16, K)
        self.dkT = self.dram("dkT", [8, 128, T], BF16, K)
        self.glrT = self.dram("glrT", [32, T], BF16, K)
        self.tm = {n: self.dram("tm_" + n, [T, (512 if n == "gk" else 1024)], BF16, K)
                   for n in ("gk", "gv", "gr", "dv", "ga", "gb")}
        K = K1
        self.ob = self.dram("ob", [T, 1024], F32, K)
        self.ya = self.dram("ya", [T, 1024], F32, K)
        self.yb = self.dram("yb", [T, 1024], F32, K)
        self.wfib = self.dram("wfib", [FC, 128, 2, KC, 128], BF16, "Internal")

    def _uniq(self, name):
        self._n = getattr(self, "_n", 0) + 1
        return f"{name}_{self._n}"

    def sb(self, es, name, shape, dt):
        return es.enter_context(self.nc.sbuf_tensor(self._uniq(name), list(shape), dt))

    def ps(self, es, name, shape, dt=F32):
        return es.enter_context(self.nc.psum_tensor(self._uniq(name), list(shape), dt))

    def phase1(self, sc, s):
        nc, S, NT, NG = self.nc, self.S, self.NT, self.NG
        t0 = s * S
        with ExitStack() as es, nc.Block() as block:
            uT = self.sb(es, "uT", [128, KC, S], BF16)
            cosT = self.sb(es, "cosT_sb", [128, S], F32)
            sinT = self.sb(es, "sinT_sb", [128, S], F32)
            nwb = self.sb(es, "nwb", [128, D], F32)
            ident = self.sb(es, "ident", [128, 128], BF16)
            identf = self.sb(es, "identf", [128, 128], F32)
            xt = [self.sb(es, f"xt{i}", [128, D], F32) for i in range(2)]
            xn = [self.sb(es, f"xn{i}", [128, D], BF16) for i in range(2)]
            st = [self.sb(es, f"st{i}", [128, 4], F32) for i in range(2)]
            wb = [self.sb(es, f"wb{i}", [128, KC, 512], BF16) for i in range(2)]
            so = [self.sb(es, f"so{i}", [128, 512], BF16) for i in range(4)]
            sof = [self.sb(es, f"sof{i}", [32, 512], BF16) for i in range(2)]
            r1 = [self.sb(es, f"r1_{i}", [128, 512], F32) for i in range(2)]
            r2 = [self.sb(es, f"r2_{i}", [128, 512], F32) for i in range(2)]
            psT = [self.ps(es, f"psT{i}", [128, 1024], BF16) for i in range(2)]
            pb = [self.ps(es, f"pb{i}", [128, 512]) for i in range(6)]
            sc.dma("sp", lambda e: e.dma_start(out=cosT[:], in_=self.cosT[:, :]), "c_cos", writes=["cosT"])
            sc.dma("sp", lambda e: e.dma_start(out=sinT[:], in_=self.sinT[:, :]), "c_sin", writes=["sinT"])
            sc.dma("sp", lambda e: e.dma_start(out=nwb[:], in_=self.nw_mixr.broadcast_to([128, D])), "c_nwm", writes=["nwb"])
            sc.dma("sp", lambda e: e.dma_start(out=identf[:], in_=self.cst[:, 0:128]), "c_id", writes=["identf"])
            sc.op("dve", lambda e: e.tensor_copy(out=ident[:], in_=identf[:]), reads=["identf"], writes=["ident"])

            def stage_a(tt):
                i = tt % 2
                sc.dma("sp", lambda e: e.dma_start(out=xt[i][:], in_=self.x[s, tt * 128:(tt + 1) * 128, :]), f"xt{i}", writes=[f"xt{i}"])
                sc.op("act", lambda e: e.activation(out=xn[i][:], in_=xt[i][:], func=AF.Square, accum_out=st[i][:, 0:1]),
                      reads=[f"xt{i}"], writes=[f"xn{i}", f"st{i}"])
                sc.op("act", lambda e: e.activation(out=st[i][:, 1:2], in_=st[i][:, 0:1], func=AF.Ln, scale=1.0 / D, bias=self.eps_t[:, 0:1]),
                      reads=[f"st{i}"], writes=[f"st{i}"])
                sc.op("act", lambda e: e.activation(out=st[i][:, 2:3], in_=st[i][:, 1:2], func=AF.Exp, scale=-0.5),
                      reads=[f"st{i}"], writes=[f"st{i}"])
                sc.op("dve", lambda e: e.scalar_tensor_tensor(out=xn[i][:], in0=xt[i][:], scalar=st[i][:, 2:3], in1=nwb[:], op0=ALU.mult, op1=ALU.mult),
                      reads=[f"xt{i}", f"st{i}", "nwb"], writes=[f"xn{i}"])

            def stage_b(tt):
                i = tt % 2
                for kc in range(KC):
                    sc.op("pe", lambda e, kc=kc: e.transpose(psT[i][:, kc * 128:(kc + 1) * 128], xn[i][:, kc * 128:(kc + 1) * 128], ident[:]),
                          reads=[f"xn{i}", "ident"], writes=[f"psT{i}"])
                sc.op("dve", lambda e: e.tensor_copy(out=uT[:, :, tt * 128:(tt + 1) * 128], in_=psT[i][:].rearrange("p (k t) -> p k t", k=KC)),
                      reads=[f"psT{i}"], writes=[("uT", tt)])

            stage_a(0)
            for tt in range(NT):
                if tt + 1 < NT:
                    stage_a(tt + 1)
                stage_b(tt)
            uT_all = [("uT", tt) for tt in range(NT)]

            state = dict(wbi=0, pbi=0, soi=0, ri=0, sofi=0)

            def load_w(col0, ncols):
                i = state["wbi"] % 2
                state["wbi"] += 1
                src = self.w1.rearrange("(k p) c -> p k c", p=128)[:, :, col0:col0 + ncols]
                sc.dma("pool", lambda e, i=i: e.dma_start(out=wb[i][:, :, 0:ncols], in_=src), f"wb{i}", writes=[f"wb{i}"])
                return i

            def next_pb():
                j = state["pbi"] % 6
                state["pbi"] += 1
                return j

            def store(dst_ap, src_tile_key, src_ap, slotname):
                sc.dma("sp", lambda e: e.dma_start(out=dst_ap, in_=src_ap), slotname, reads=[src_tile_key])

            def fm_group(wi, c, tg, j, M=128):
                for kc in range(KC):
                    sc.op("pe", lambda e, kc=kc: e.matmul(pb[j][0:M, :], lhsT=wb[wi][:, kc, c * 128:c * 128 + M],
                                                            rhs=uT[:, kc, tg * 512:(tg + 1) * 512], start=(kc == 0), stop=(kc == KC - 1)),
                          reads=[f"wb{wi}"] + uT_all[tg * 4:(tg + 1) * 4], writes=[f"pb{j}"])

            for name, col0 in FM_BLOCKS:
                wi = load_w(col0, 512)
                bidx = (col0 - {"gq": 0, "gk": 512, "dq": 1024, "dk": 2048}[name]) // 512
                if name in ("gq", "gk"):
                    dst = self.gqT if name == "gq" else self.gkT
                    for c in range(4):
                        for tg in range(NG):
                            j = next_pb()
                            fm_group(wi, c, tg, j)
                            k = state["soi"] % 4
                            state["soi"] += 1
                            if (c + tg) % 2 == 0:
                                sc.op("act", lambda e, j=j, k=k: e.activation(out=so[k][:], in_=pb[j][:], func=AF.Copy),
                                      reads=[f"pb{j}"], writes=[f"so{k}"])
                            else:
                                sc.op("dve", lambda e, j=j, k=k: e.tensor_copy(out=so[k][:], in_=pb[j][:]),
                                      reads=[f"pb{j}"], writes=[f"so{k}"])
                            store(dst[c, :, t0 + tg * 512:t0 + (tg + 1) * 512], f"so{k}", so[k][:], f"so{k}")
                else:
                    dst = self.dqT if name == "dq" else self.dkT
                    for c in range(4):
                        hc = bidx * 4 + c
                        for tg in range(NG):
                            ja = next_pb()
                            fm_group(wi, c, tg, ja)
                            ri = state["ri"] % 2
                            state["ri"] += 1
                            k = state["soi"] % 4
                            state["soi"] += 1
                            tsl = slice(tg * 512, (tg + 1) * 512)
                            sc.op("dve", lambda e, ja=ja, ri=ri, tsl=tsl: e.tensor_tensor(out=r1[ri][:], in0=pb[ja][:], in1=cosT[:, tsl], op=ALU.mult),
                                  reads=[f"pb{ja}", "cosT"], writes=[f"r1_{ri}"])
                            sc.op("dve", lambda e, ja=ja, ri=ri, tsl=tsl: e.tensor_tensor(out=r2[ri][0:64, :], in0=pb[ja][64:128, :], in1=sinT[0:64, tsl], op=ALU.mult),
                                  reads=[f"pb{ja}", "sinT"], writes=[(f"r2_{ri}", 0)])
                            sc.op("dve", lambda e, ja=ja, ri=ri, tsl=tsl: e.tensor_tensor(out=r2[ri][64:128, :], in0=pb[ja][0:64, :], in1=sinT[64:128, tsl], op=ALU.mult),
                                  reads=[f"pb{ja}", "sinT"], writes=[(f"r2_{ri}", 1)])
                            sc.op("pool", lambda e, ri=ri, k=k: e.tensor_tensor(out=so[k][:], in0=r1[ri][:], in1=r2[ri][:], op=ALU.add),
                                  reads=[f"r1_{ri}", (f"r2_{ri}", 0), (f"r2_{ri}", 1)], writes=[f"so{k}"])
                            store(dst[hc, :, t0 + tg * 512:t0 + (tg + 1) * 512], f"so{k}", so[k][:], f"so{k}")
            wi = load_w(GLR_OFF, 32)
            for tg in range(NG):
                j = next_pb()
                fm_group(wi, 0, tg, j, M=32)
                k = state["sofi"] % 2
                state["sofi"] += 1
                sc.op("dve", lambda e, j=j, k=k: e.tensor_copy(out=sof[k][:], in_=pb[j][0:32, :]), reads=[f"pb{j}"], writes=[f"sof{k}"])
                store(self.glrT[:, t0 + tg * 512:t0 + (tg + 1) * 512], f"sof{k}", sof[k][:], f"sof{k}")
            for bi, (name, jj) in enumerate(TM_BLOCKS):
                wi = load_w(TM_OFF + bi * 512, 512)
                dst = self.tm[name]
                for tt in range(NT):
                    j = next_pb()
                    for kc in range(KC):
                        sc.op("pe", lambda e, kc=kc, j=j, tt=tt, wi=wi: e.matmul(pb[j][:], lhsT=uT[:, kc, tt * 128:(tt + 1) * 128], rhs=wb[wi][:, kc, :],
                                                                        start=(kc == 0), stop=(kc == KC - 1)),
                              reads=[f"wb{wi}", ("uT", tt)], writes=[f"pb{j}"])
                    k = state["soi"] % 4
                    state["soi"] += 1
                    if name == "gr":
                        sc.op("act", lambda e, j=j, k=k: e.activation(out=so[k][:], in_=pb[j][:], func=AF.Silu), reads=[f"pb{j}"], writes=[f"so{k}"])
                    elif name in ("ga", "gb"):
                        sc.op("act", lambda e, j=j, k=k: e.activation(out=so[k][:], in_=pb[j][:], func=AF.Sigmoid), reads=[f"pb{j}"], writes=[f"so{k}"])
                    elif tt % 2 == 0:
                        sc.op("act", lambda e, j=j, k=k: e.activation(out=so[k][:], in_=pb[j][:], func=AF.Copy), reads=[f"pb{j}"], writes=[f"so{k}"])
                    else:
                        sc.op("dve", lambda e, j=j, k=k: e.tensor_copy(out=so[k][:], in_=pb[j][:]), reads=[f"pb{j}"], writes=[f"so{k}"])
                    store(dst[t0 + tt * 128:t0 + (tt + 1) * 128, jj * 512:(jj + 1) * 512], f"so{k}", so[k][:], f"so{k}")
            sc.flush_wait_all()
            sc.emit(block)


    def phase_gla(self, sc, s, dirn):
        nc, S = self.nc, self.S
        t0 = s * S
        NP, NSS = S // 128, S // 512
        fwd = dirn == 0
        with ExitStack() as es, nc.Block() as block:
            Uinc = self.sb(es, "Uinc", [128, 128], BF16)
            Ustr = self.sb(es, "Ustr", [128, 128], BF16)
            Uf = self.sb(es, "Uf", [128, 256], F32)
            wgkf = self.sb(es, "wgkf", [17, 512], F32)
            mask4 = self.sb(es, "mask4", [128, 4, 128], F32)
            wgk = self.sb(es, "wgk_sb", [17, 512], BF16)
            gnwb = self.sb(es, "gnwb", [128, 1024], F32)
            glra = [self.sb(es, f"glra{i}", [17, 512], BF16) for i in range(2)]
            qT = [self.sb(es, f"gqT{i}", [128, 4, 512], BF16) for i in range(2)]
            kT = [self.sb(es, f"gkT{i}", [128, 4, 512], BF16) for i in range(2)]
            ktm = [self.sb(es, f"gktm{i}", [128, 4, 512], BF16) for i in range(2)]
            vv = [self.sb(es, f"gv{i}", [128, 4, 1024], BF16) for i in range(2)]
            obt = [self.sb(es, f"obt{i}", [128, 1024], F32) for i in range(2)]
            grt = [self.sb(es, f"grt{i}", [128, 1024], BF16) for i in range(2)]
            gat = [self.sb(es, f"gat{i}", [128, 1024], BF16) for i in range(2)]
            e1 = self.sb(es, "g_e1", [128, 512], F32)
            sp = self.sb(es, "g_sp", [128, 512], BF16)
            ebl = self.sb(es, "g_ebl", [128, 512], F32)
            ks = self.sb(es, "g_ks", [128, 512], BF16)
            eb = self.sb(es, "g_eb", [128, 4, 128], F32)
            enb = self.sb(es, "g_enb", [128, 4, 128], F32)
            qt = self.sb(es, "g_qt", [128, 4, 128], BF16)
            kt = self.sb(es, "g_kt", [128, 4, 128], BF16)
            attm = self.sb(es, "g_attm", [128, 4, 128], BF16)
            S32 = [self.sb(es, f"S32_{h}", [128, 256], F32) for h in range(4)]
            Sbf = [self.sb(es, f"Sbf_{h}", [128, 256], BF16) for h in range(4)]
            osb = [self.sb(es, f"osb{i}", [128, 1024], F32) for i in range(2)]
            ysb = [self.sb(es, f"ysb{i}", [128, 1024], F32) for i in range(2)]
            g32 = self.sb(es, "g_g32", [128, 1024], F32)
            junk = self.sb(es, "g_junk", [128, 4, 256], F32)
            stt = [self.sb(es, f"g_st{i}", [128, 12], F32) for i in range(2)]
            bA = self.ps(es, "bA", [128, 512])
            bB = self.ps(es, "bB", [128, 512])
            bO = [self.ps(es, f"bO{h}", [128, 512]) for h in range(4)]
            bKV = [self.ps(es, f"bKV{i}", [128, 512]) for i in range(2)]

            co = 128 * (1 + 2 * dirn)
            sc.dma("sp", lambda e: e.dma_start(out=Uf[:], in_=self.cst[:, co:co + 256]), "c_a", writes=["Uf"])
            sc.op("dve", lambda e: e.tensor_copy(out=Uinc[:], in_=Uf[:, 0:128]), reads=["Uf"], writes=["Uinc"])
            sc.op("dve", lambda e: e.tensor_copy(out=Ustr[:], in_=Uf[:, 128:256]), reads=["Uf"], writes=["Ustr"])
            for h in range(4):
                sc.dma("sp", lambda e, h=h: e.dma_start(out=mask4[:, h, :], in_=self.msk[:, dirn * 128:(dirn + 1) * 128]), "c_c", writes=["mask4"])
            sc.dma("sp", lambda e: e.dma_start(out=wgkf[:], in_=self.wgk[dirn, :, :]), "c_d", writes=["wgkf"])
            sc.op("dve", lambda e: e.tensor_copy(out=wgk[:], in_=wgkf[:]), reads=["wgkf"], writes=["wgk"])
            sc.dma("sp", lambda e: e.dma_start(out=gnwb[:], in_=self.gnw.broadcast_to([128, 1024])), "c_e", writes=["gnwb"])
            for i in range(2):
                sc.op("dve", lambda e, i=i: e.memset(glra[i][:], 1.0), writes=[f"glra{i}"])
            for h in range(4):
                sc.op("dve", lambda e, h=h: e.memset(S32[h][:], 0.0), writes=[f"S32_{h}"])
                sc.op("dve", lambda e, h=h: e.memset(Sbf[h][:], 0.0), writes=[f"Sbf_{h}"])

            if fwd:
                r1, c1, r2, c2 = slice(0, 64), 63, slice(64, 128), 127
            else:
                r1, c1, r2, c2 = slice(64, 128), 64, slice(0, 64), 0
            qscale = float(GLA_DK ** -0.5)

            def load_ss(ss, i):
                tb = t0 + ss * 512
                sc.dma("sp", lambda e: e.dma_start(out=glra[i][0:16, :], in_=self.glrT[dirn * 16:(dirn + 1) * 16, tb:tb + 512]), f"glra{i}", writes=[f"glra{i}"])
                sc.dma("sp", lambda e: e.dma_start(out=qT[i][:], in_=self.gqT[:, :, tb:tb + 512].rearrange("h p t -> p h t")), f"gqT{i}", writes=[f"gqT{i}"])
                sc.dma("sp", lambda e: e.dma_start(out=kT[i][:], in_=self.gkT[:, :, tb:tb + 512].rearrange("h p t -> p h t")), f"gkT{i}", writes=[f"gkT{i}"])
                sc.dma("sp", lambda e: e.dma_start(out=ktm[i][:], in_=self.tm["gk"][tb:tb + 512, :].rearrange("(a p) c -> p a c", p=128)), f"gktm{i}", writes=[f"gktm{i}"])
                sc.dma("sp", lambda e: e.dma_start(out=vv[i][:], in_=self.tm["gv"][tb:tb + 512, :].rearrange("(a p) c -> p a c", p=128)), f"gv{i}", writes=[f"gv{i}"])

            ks2 = [ks, self.sb(es, "g_ks_b", [128, 512], BF16)]
            eb2 = [eb, self.sb(es, "g_eb_b", [128, 4, 128], F32)]
            qt2 = [qt, self.sb(es, "g_qt_b", [128, 4, 128], BF16)]
            attm2 = [attm, self.sb(es, "g_attm_b", [128, 4, 128], BF16)]

            order = list(range(NSS)) if fwd else list(range(NSS - 1, -1, -1))
            steps = []
            for n, ss in enumerate(order):
                for m, sub in enumerate([0, 1, 2, 3] if fwd else [3, 2, 1, 0]):
                    steps.append(dict(n=n, ss=ss, i=n % 2, sub=sub, pi=(n * 4 + m) % 2, q=(n * 4 + m) % 2,
                                      tok=t0 + ss * 512 + sub * 128, tsl=slice(sub * 128, (sub + 1) * 128), first=(m == 0)))

            def loads(st):
                if fwd:
                    pi, tok = st["pi"], st["tok"]
                    sc.dma("sp", lambda e: e.dma_start(out=obt[pi][:], in_=self.ob[tok:tok + 128, :]), f"obt{pi}", writes=[f"obt{pi}"])
                    sc.dma("sp", lambda e: e.dma_start(out=grt[pi][:], in_=self.tm["gr"][tok:tok + 128, :]), f"grt{pi}", writes=[f"grt{pi}"])
                    sc.dma("sp", lambda e: e.dma_start(out=gat[pi][:], in_=self.tm["ga"][tok:tok + 128, :]), f"gat{pi}", writes=[f"gat{pi}"])

            def pre_a(st):
                i, tsl = st["i"], st["tsl"]
                sc.op("pe", lambda e: e.matmul(bA[:], lhsT=glra[i][0:17, tsl], rhs=wgk[0:17, :], start=True, stop=True),
                      reads=[f"glra{i}", "wgk"], writes=["bA"])
                sc.op("act", lambda e: e.activation(out=e1[:], in_=bA[:], func=AF.Exp, scale=-1.0), reads=["bA"], writes=["e1"])
                sc.op("act", lambda e: e.activation(out=sp[:], in_=e1[:], func=AF.Ln, bias=self.eps_t[:, 2:3]), reads=["e1"], writes=["sp"])

            def pre_b(st):
                i, sub, q = st["i"], st["sub"], st["q"]
                sc.op("pe", lambda e: e.matmul(bA[:], lhsT=Ustr[:], rhs=sp[:], start=True, stop=True), reads=["Ustr", "sp"], writes=["bA"])
                for h in range(4):
                    sc.op("pe", lambda e, h=h: e.matmul(bB[:, h * 128:(h + 1) * 128], lhsT=sp[:, h * 128:(h + 1) * 128], rhs=Uinc[:], start=True, stop=True),
                          reads=["sp", "Uinc"], writes=["bB"])
                sc.op("act", lambda e: e.activation(out=ebl[:], in_=bA[:], func=AF.Exp), reads=["bA"], writes=["ebl"])
                sc.op("act", lambda e: e.activation(out=eb2[q][:].rearrange("p h t -> p (h t)"), in_=bB[:], func=AF.Exp), reads=["bB"], writes=[f"eb{q}"])
                sc.op("act", lambda e: e.activation(out=enb[:].rearrange("p h t -> p (h t)"), in_=bB[:], func=AF.Exp, scale=-1.0), reads=["bB"], writes=["enb"])

            def pre_c(st):
                i, sub, q, tsl = st["i"], st["sub"], st["q"], st["tsl"]
                sc.op("pool", lambda e: e.tensor_tensor(out=ks2[q][:], in0=ktm[i][:, sub, :], in1=ebl[:], op=ALU.mult),
                      reads=[f"gktm{i}", "ebl"], writes=[f"ks{q}"])
                sc.op("dve", lambda e: e.scalar_tensor_tensor(out=qt2[q][:], in0=eb2[q][:], scalar=qscale, in1=qT[i][:, :, tsl], op0=ALU.mult, op1=ALU.mult),
                      reads=[f"eb{q}", f"gqT{i}"], writes=[f"qt{q}"])
                sc.op("pool", lambda e: e.tensor_tensor(out=kt[:], in0=enb[:], in1=kT[i][:, :, tsl], op=ALU.mult),
                      reads=["enb", f"gkT{i}"], writes=["kt"])

            def pre_d(st):
                q = st["q"]
                for h in range(4):
                    sc.op("pe", lambda e, h=h: e.matmul(bA[:, h * 128:(h + 1) * 128], lhsT=kt[:, h, :], rhs=qt2[q][:, h, :], start=True, stop=True),
                          reads=["kt", f"qt{q}"], writes=["bA"])
                sc.op("dve", lambda e: e.tensor_tensor(out=attm2[q][:].rearrange("p h t -> p (h t)"), in0=bA[:], in1=mask4[:].rearrange("p h t -> p (h t)"), op=ALU.mult),
                      reads=["bA", "mask4"], writes=[f"attm{q}"])

            def scan_open(st):
                i, sub, q = st["i"], st["sub"], st["q"]
                for h in range(4):
                    vs = slice(h * 256, (h + 1) * 256)
                    sc.op("pe", lambda e, h=h, vs=vs: e.matmul(bO[h][:, 0:256], lhsT=attm2[q][:, h, :], rhs=vv[i][:, sub, vs], start=True, stop=False),
                          reads=[f"attm{q}", f"gv{i}"], writes=[f"bO{h}"])
                    sc.op("pe", lambda e, h=h: e.matmul(bO[h][r1, 0:256], lhsT=qt2[q][:, h, r1], rhs=Sbf[h][:], start=False, stop=True),
                          reads=[f"qt{q}", f"Sbf_{h}"], writes=[f"bO{h}"])

            def scan_head(st, h):
                i, sub, q, pi = st["i"], st["sub"], st["q"], st["pi"]
                vs = slice(h * 256, (h + 1) * 256)
                kb = h % 2
                ksl = slice(kb * 256, (kb + 1) * 256)
                hs = slice(h * 128, (h + 1) * 128)
                sc.op("pe", lambda e: e.matmul(bKV[0][:, ksl], lhsT=ks2[q][r1, hs], rhs=vv[i][r1, sub, vs], start=True, stop=True),
                      reads=[f"ks{q}", f"gv{i}"], writes=["bKV0"])
                sc.op("pe", lambda e: e.matmul(bKV[1][:, ksl], lhsT=ks2[q][r2, hs], rhs=vv[i][r2, sub, vs], start=True, stop=True),
                      reads=[f"ks{q}", f"gv{i}"], writes=["bKV1"])
                sc.op("dve", lambda e: e.scalar_tensor_tensor(out=Sbf[h][:], in0=S32[h][:], scalar=eb2[q][:, h, c1:c1 + 1], in1=bKV[0][:, ksl], op0=ALU.mult, op1=ALU.add),
                      reads=[f"S32_{h}", f"eb{q}", "bKV0"], writes=[f"Sbf_{h}"])
                sc.op("dve", lambda e: e.scalar_tensor_tensor(out=S32[h][:], in0=S32[h][:], scalar=eb2[q][:, h, c1:c1 + 1], in1=bKV[0][:, ksl], op0=ALU.mult, op1=ALU.add),
                      reads=[f"S32_{h}", f"eb{q}", "bKV0"], writes=[f"S32_{h}"])
                sc.op("pe", lambda e: e.matmul(bO[h][r2, 0:256], lhsT=qt2[q][:, h, r2], rhs=Sbf[h][:], start=False, stop=True),
                      reads=[f"qt{q}", f"Sbf_{h}"], writes=[f"bO{h}"])
                sc.op("dve", lambda e: e.scalar_tensor_tensor(out=S32[h][:], in0=S32[h][:], scalar=eb2[q][:, h, c2:c2 + 1], in1=bKV[1][:, ksl], op0=ALU.mult, op1=ALU.add),
                      reads=[f"S32_{h}", f"eb{q}", "bKV1"], writes=[f"S32_{h}"])
                sc.op("act", lambda e: e.activation(out=Sbf[h][:], in_=S32[h][:], func=AF.Copy), reads=[f"S32_{h}"], writes=[f"Sbf_{h}"])
                if fwd:
                    sc.op("dve", lambda e: e.tensor_tensor(out=osb[pi][:, vs], in0=bO[h][:, 0:256], in1=obt[pi][:, vs], op=ALU.add),
                          reads=[f"bO{h}", f"obt{pi}"], writes=[(f"osb{pi}", h)])
                else:
                    sc.op("act", lambda e: e.activation(out=osb[pi][:, vs], in_=bO[h][:, 0:256], func=AF.Copy),
                          reads=[f"bO{h}"], writes=[(f"osb{pi}", h)])

            def epilogue(st):
                pi, tok = st["pi"], st["tok"]
                okeys = [(f"osb{pi}", h) for h in range(4)]
                if not fwd:
                    sc.dma("act", lambda e: e.dma_start(out=self.ob[tok:tok + 128, :], in_=osb[pi][:]), f"osb{pi}", reads=okeys, writes=[("ob", tok)])
                    return
                for h in range(4):
                    vs = slice(h * 256, (h + 1) * 256)
                    sc.op("act", lambda e, h=h, vs=vs: e.activation(out=junk[:, h, :], in_=osb[pi][:, vs], func=AF.Square, accum_out=stt[pi][:, h:h + 1]),
                          reads=[(f"osb{pi}", h)], writes=[f"junk{h}", (f"gst{pi}", h)])
                sc.op("act", lambda e: e.activation(out=stt[pi][:, 4:8], in_=stt[pi][:, 0:4], func=AF.Ln, scale=1.0 / GLA_DV, bias=self.eps_t[:, 0:1]),
                      reads=[(f"gst{pi}", h) for h in range(4)], writes=[f"gst{pi}"])
                sc.op("act", lambda e: e.activation(out=stt[pi][:, 8:12], in_=stt[pi][:, 4:8], func=AF.Exp, scale=-0.5), reads=[f"gst{pi}"], writes=[f"gst{pi}"])
                sc.op("pool", lambda e: e.tensor_tensor(out=g32[:], in0=grt[pi][:], in1=gat[pi][:], op=ALU.mult), reads=[f"grt{pi}", f"gat{pi}"], writes=["g32"])
                sc.op("pool", lambda e: e.tensor_tensor(out=g32[:], in0=g32[:], in1=gnwb[:], op=ALU.mult), reads=["g32", "gnwb"], writes=["g32"])
                for h in range(4):
                    vs = slice(h * 256, (h + 1) * 256)
                    sc.op("dve", lambda e, h=h, vs=vs: e.scalar_tensor_tensor(out=ysb[pi][:, vs], in0=osb[pi][:, vs], scalar=stt[pi][:, 8 + h:9 + h], in1=g32[:, vs], op0=ALU.mult, op1=ALU.mult),
                          reads=[(f"osb{pi}", h), f"gst{pi}", "g32"], writes=[(f"ysb{pi}", h)])
                sc.dma("act", lambda e: e.dma_start(out=self.ya[tok:tok + 128, :], in_=ysb[pi][:]), f"ysb{pi}",
                       reads=[(f"ysb{pi}", h) for h in range(4)], writes=[("ya", tok)])

            load_ss(order[0], 0)
            if NSS > 1:
                load_ss(order[1], 1)
            loads(steps[0])
            for f_ in (pre_a, pre_b, pre_c, pre_d):
                f_(steps[0])
            for p, st in enumerate(steps):
                nx = steps[p + 1] if p + 1 < len(steps) else None
                if nx is not None:
                    loads(nx)
                scan_open(st)
                parts = (pre_a, pre_b, pre_c, pre_d)
                for h in range(4):
                    if nx is not None:
                        parts[h](nx)
                    scan_head(st, h)
                epilogue(st)
                if nx is not None and nx["first"] and nx["n"] + 1 < NSS:
                    load_ss(order[nx["n"] + 1], (nx["n"] + 1) % 2)
            sc.flush_wait_all()
            sc.emit(block)

    def phase_attn(self, sc, s, wprep=False):
        nc, S, NT, NG = self.nc, self.S, self.NT, self.NG
        t0 = s * S
        LAG = 2
        with ExitStack() as es, nc.Block() as block:
            if wprep:
                wt = self.sb(es, "wprep", [128, KC, FFN], BF16)
                wsrc = self.w_fi.rearrange("(k p) c -> p k c", p=128)

                def wprep_load(g):
                    for kc in range(KC):
                        sc.dma("pool", lambda e, kc=kc: e.dma_start(out=wt[:, kc, :], in_=wsrc[:, kc, g * FFN:(g + 1) * FFN]), f"wp{kc % 4}", writes=[("wprep", kc)])

                def wprep_store(g):
                    allk = [("wprep", kc) for kc in range(KC)]
                    for f in range(FC):
                        sc.dma("sp", lambda e, f=f: e.dma_start(out=self.wfib[f, :, g, :, :], in_=wt[:, :, f * 128:(f + 1) * 128]), f"wps{f % 4}",
                               reads=allk, writes=[("wfib", f, g)])
                wprep_load(0)
            KT = [self.sb(es, f"aKT{i}", [128, 2, S], BF16) for i in range(2)]
            VA = [self.sb(es, f"aVA{i}", [128, NT, 258], BF16) for i in range(2)]
            QT = [self.sb(es, f"aQT{i}", [128, 2, 512], BF16) for i in range(2)]
            PT = [self.sb(es, f"aPT{i}", [128, 1024], BF16) for i in range(2)]
            A0 = self.sb(es, "aA0", [128, 4, 256], F32)
            o2 = self.sb(es, "ao2", [128, 4, 256], F32)
            gbt = [self.sb(es, f"agb{i}", [128, 4, 256], BF16) for i in range(2)]
            g32 = self.sb(es, "ag32", [128, 4, 256], F32)
            ybs = [self.sb(es, f"aybs{i}", [128, 4, 256], F32) for i in range(2)]
            slnb = self.sb(es, "aslnb", [128, 256], F32)
            lam = self.sb(es, "alam", [1, 512], F32)
            lt = self.sb(es, "alt", [1, 16], F32)
            ones1 = self.sb(es, "aones1", [1, 128], F32)
            nlam = self.sb(es, "anlam", [128, 1], F32)
            rr = self.sb(es, "arr", [128, 16], F32)
            stt = self.sb(es, "ast", [128, 12], F32)
            junk = self.sb(es, "ajunk", [128, 4, 256], F32)
            ACC = [self.ps(es, f"aACC{i}", [128, 512]) for i in range(4)]
            SB = [self.ps(es, f"aSB{i}", [128, 1024]) for i in range(2)]

            sc.dma("sp", lambda e: e.dma_start(out=lam[:], in_=self.lam4[:, :]), "c_a", writes=["lam"])
            sc.dma("sp", lambda e: e.dma_start(out=slnb[:], in_=self.sln.broadcast_to([128, 256])), "c_b", writes=["slnb"])
            sc.op("dve", lambda e: e.memset(ones1[:], 1.0), writes=["ones1"])
            sc.op("dve", lambda e: e.tensor_tensor(out=lam[:, 0:128], in0=lam[:, 0:128], in1=lam[:, 128:256], op=ALU.mult), reads=["lam"], writes=["lam"])
            sc.op("dve", lambda e: e.tensor_tensor(out=lam[:, 256:384], in0=lam[:, 256:384], in1=lam[:, 384:512], op=ALU.mult), reads=["lam"], writes=["lam"])
            sc.op("act", lambda e: e.activation(out=lam[:, 128:256], in_=lam[:, 0:128], func=AF.Copy, accum_out=lt[:, 0:1]), reads=["lam"], writes=["lam", "lt"])
            sc.op("act", lambda e: e.activation(out=lam[:, 384:512], in_=lam[:, 256:384], func=AF.Copy, accum_out=lt[:, 1:2]), reads=["lam"], writes=["lam", "lt"])
            sc.op("act", lambda e: e.activation(out=lt[:, 2:4], in_=lt[:, 0:2], func=AF.Exp), reads=["lt"], writes=["lt"])
            sc.op("dve", lambda e: e.tensor_tensor(out=lt[:, 4:5], in0=lt[:, 3:4], in1=lt[:, 2:3], op=ALU.subtract), reads=["lt"], writes=["lt"])
            sc.op("dve", lambda e: e.tensor_scalar(out=lt[:, 5:6], in0=lt[:, 4:5], scalar1=-LAMBDA_INIT, scalar2=None, op0=ALU.add), reads=["lt"], writes=["lt"])
            sc.op("pe", lambda e: e.matmul(ACC[0][:, 0:1], lhsT=ones1[:], rhs=lt[:, 5:6], start=True, stop=True), reads=["ones1", "lt"], writes=["aACC0"])
            sc.op("dve", lambda e: e.tensor_copy(out=nlam[:], in_=ACC[0][:, 0:1]), reads=["aACC0"], writes=["nlam"])
            sc.op("dve", lambda e: e.tensor_scalar(out=slnb[:], in0=slnb[:], scalar1=float(1.0 - LAMBDA_INIT), scalar2=None, op0=ALU.mult), reads=["slnb"], writes=["slnb"])
            for i in range(2):
                sc.op("pool", lambda e, i=i: e.memset(VA[i][:, :, 256:258], 1.0), writes=[f"aVA{i}"])

            def load_head(h, i):
                for c in range(2):
                    sc.dma("sp", lambda e, c=c: e.dma_start(out=KT[i][:, c, :], in_=self.dkT[2 * h + c, :, t0:t0 + S]), f"aKT{i}", writes=[f"aKT{i}"])
                for a0 in range(0, NT, 8):
                    a1 = min(NT, a0 + 8)
                    sc.dma("sp", lambda e, a0=a0, a1=a1: e.dma_start(
                        out=VA[i][:, a0:a1, 0:256],
                        in_=self.tm["dv"][t0 + a0 * 128:t0 + a1 * 128, h * 256:(h + 1) * 256].rearrange("(a p) c -> p a c", p=128)),
                        f"aVA{i}", writes=[f"aVA{i}"])

            scale = float(DIFF_HD ** -0.5)
            load_head(0, 0)
            work = [(h, qg) for h in range(DIFF_H) for qg in range(NG)]

            def load_q(widx):
                h, qg = work[widx]
                qi, tb = widx % 2, t0 + qg * 512
                for c in range(2):
                    sc.dma("sp", lambda e, c=c: e.dma_start(out=QT[qi][:, c, :], in_=self.dqT[2 * h + c, :, tb:tb + 512]), f"aQT{qi}", writes=[f"aQT{qi}"])
                sc.dma("sp", lambda e: e.dma_start(out=gbt[qi][:], in_=self.tm["gb"][tb:tb + 512, h * 256:(h + 1) * 256].rearrange("(a p) c -> p a c", p=128)),
                       f"agb{qi}", writes=[f"agb{qi}"])
            load_q(0)
            cnt = 0
            for h in range(DIFF_H):
                hi = h % 2
                if h + 1 < DIFF_H:
                    load_head(h + 1, (h + 1) % 2)
                if wprep and h == 1:
                    wprep_store(0)
                    wprep_load(1)
                if wprep and h == 3:
                    wprep_store(1)
                for qg in range(NG):
                    qi = cnt % 2
                    cnt += 1
                    tb = t0 + qg * 512
                    if cnt < len(work):
                        load_q(cnt)
                    for c in range(2):
                        NP2 = NT // 2
                        for step in range(NP2 + 1):
                            if step < NP2:
                                b_ = step % 2
                                for u in range(2):
                                    k_ = 2 * step + u
                                    sc.op("pe", lambda e, c=c, k_=k_, b_=b_, u=u, hi=hi, qi=qi: e.matmul(SB[b_][:, u * 512:(u + 1) * 512], lhsT=KT[hi][:, c, k_ * 128:(k_ + 1) * 128],
                                                                                                    rhs=QT[qi][:, c, :], start=True, stop=True),
                                          reads=[f"aKT{hi}", f"aQT{qi}"], writes=[f"aSB{b_}"])
                                sc.op("act", lambda e, b_=b_: e.activation(out=PT[b_][:], in_=SB[b_][:], func=AF.Exp, scale=scale, bias=self.eps_t[:, 3:4]),
                                      reads=[f"aSB{b_}"], writes=[f"aPT{b_}"])
                            if step >= 1:
                                b_ = (step - 1) % 2
                                for u in range(2):
                                    k_ = 2 * (step - 1) + u
                                    for qs in range(4):
                                        sc.op("pe", lambda e, k_=k_, b_=b_, u=u, qs=qs, hi=hi: e.matmul(ACC[qs][:, 0:257], lhsT=PT[b_][:, u * 512 + qs * 128:u * 512 + (qs + 1) * 128],
                                                                                                  rhs=VA[hi][:, k_, 0:257], start=(k_ == 0), stop=(k_ == NT - 1)),
                                              reads=[f"aPT{b_}", f"aVA{hi}"], writes=[f"aACC{qs}"])
                        for qs in range(4):
                            col = c * 4 + qs
                            sc.op("dve", lambda e, qs=qs, col=col: e.reciprocal(out=rr[:, col:col + 1], in_=ACC[qs][:, 256:257]), reads=[f"aACC{qs}"], writes=[("rr", col)])
                            if c == 0:
                                sc.op("dve", lambda e, qs=qs, col=col: e.tensor_scalar(out=A0[:, qs, :], in0=ACC[qs][:, 0:256], scalar1=rr[:, col:col + 1], scalar2=None, op0=ALU.mult),
                                      reads=[f"aACC{qs}", ("rr", col)], writes=[("A0", qs)])
                            else:
                                sc.op("dve", lambda e, col=col: e.tensor_tensor(out=rr[:, 8 + col:9 + col], in0=rr[:, col:col + 1], in1=nlam[:], op=ALU.mult), reads=[("rr", col), "nlam"], writes=[("rr", 8 + col)])
                                sc.op("dve", lambda e, qs=qs, col=col: e.scalar_tensor_tensor(out=o2[:, qs, :], in0=ACC[qs][:, 0:256], scalar=rr[:, 8 + col:9 + col], in1=A0[:, qs, :], op0=ALU.mult, op1=ALU.add),
                                      reads=[f"aACC{qs}", ("rr", 8 + col), ("A0", qs)], writes=[("o2", qs)])
                    for qs in range(4):
                        sc.op("act", lambda e, qs=qs: e.activation(out=junk[:, qs, :], in_=o2[:, qs, :], func=AF.Square, accum_out=stt[:, qs:qs + 1]), reads=[("o2", qs)], writes=[f"ajunk{qs}", ("ast", qs)])
                    sc.op("act", lambda e: e.activation(out=stt[:, 4:8], in_=stt[:, 0:4], func=AF.Ln, scale=1.0 / 256, bias=self.eps_t[:, 1:2]), reads=[("ast", q_) for q_ in range(4)], writes=["ast"])
                    sc.op("act", lambda e: e.activation(out=stt[:, 8:12], in_=stt[:, 4:8], func=AF.Exp, scale=-0.5), reads=["ast"], writes=["ast"])
                    for qs in range(4):
                        sc.op("pool", lambda e, qs=qs, qi=qi: e.tensor_tensor(out=g32[:, qs, :], in0=gbt[qi][:, qs, :], in1=slnb[:], op=ALU.mult), reads=[f"agb{qi}", "slnb"], writes=[("ag32", qs)])
                    for qs in range(4):
                        sc.op("dve", lambda e, qs=qs, qi=qi: e.scalar_tensor_tensor(out=ybs[qi][:, qs, :], in0=o2[:, qs, :], scalar=stt[:, 8 + qs:9 + qs], in1=g32[:, qs, :], op0=ALU.mult, op1=ALU.mult),
                              reads=[("o2", qs), "ast", ("ag32", qs)], writes=[(f"aybs{qi}", qs)])
                    sc.dma("pool", lambda e, qi=qi, tb=tb, h=h: e.dma_start(out=self.yb[tb:tb + 512, h * 256:(h + 1) * 256].rearrange("(a p) c -> p a c", p=128), in_=ybs[qi][:]),
                           f"aybs{qi}", reads=[(f"aybs{qi}", q_) for q_ in range(4)], writes=[("yb", tb, h)])
            sc.flush_wait_all()
            sc.emit(block)

    def phase_wprep(self, sc):
        nc = self.nc
        with ExitStack() as es, nc.Block() as block:
            wt = self.sb(es, "wprep", [128, KC, 2 * FFN], BF16)
            src = self.w_fi.rearrange("(k p) c -> p k c", p=128)
            for kc in range(KC):
                sc.dma("pool", lambda e, kc=kc: e.dma_start(out=wt[:, kc, :], in_=src[:, kc, :]), f"wp{kc % 4}", writes=[("wprep", kc)])
            allk = [("wprep", kc) for kc in range(KC)]
            for f in range(FC):
                for g in range(2):
                    c0 = g * FFN + f * 128
                    sc.dma("sp", lambda e, f=f, g=g, c0=c0: e.dma_start(out=self.wfib[f, :, g, :, :], in_=wt[:, :, c0:c0 + 128]), f"wps{(2 * f + g) % 4}",
                           reads=allk, writes=[("wfib", f, g)])
            sc.flush_wait_all()
            sc.emit(block)

    def phase_ffn(self, sc):
        nc, S, T = self.nc, self.S, self.T
        NGT = T // 512
        with ExitStack() as es, nc.Block() as block:
            wo = self.sb(es, "f_wo", [128, KC, D], BF16)
            wfo = self.sb(es, "f_wfo", [128, FC, D], BF16)
            wfi = [self.sb(es, f"f_wfi{i}", [128, 2, KC, 128], BF16) for i in range(4)]
            ident = self.sb(es, "f_ident", [128, 128], BF16)
            identf = self.sb(es, "f_identf", [128, 128], F32)
            nwf = self.sb(es, "f_nwf", [128, KC], F32)
            nwfin = self.sb(es, "f_nwfin", [128, D], F32)
            la = [self.sb(es, f"f_la{i}", [128, D], F32) for i in range(2)]
            lb = [self.sb(es, f"f_lb{i}", [128, D], F32) for i in range(2)]
            lx = [self.sb(es, f"f_lx{i}", [128, D], F32) for i in range(2)]
            mb = [self.sb(es, f"f_mb{i}", [128, D], BF16) for i in range(2)]
            mT = self.sb(es, "f_mT", [128, KC, 512], BF16)
            hT = self.sb(es, "f_hT", [128, KC, 512], BF16)
            hsb = self.sb(es, "f_h", [128, 4, D], F32)
            aT = self.sb(es, "f_aT", [128, FC, 512], BF16)
            sg = [self.sb(es, f"f_sg{i}", [128, 512], F32) for i in range(2)]
            ot = [self.sb(es, f"f_ot{i}", [128, D], F32) for i in range(2)]
            stt = [self.sb(es, f"f_st{i}", [128, 4], F32) for i in range(2)]
            psT = [self.ps(es, f"f_psT{i}", [128, 1024], BF16) for i in range(2)]
            pb = [self.ps(es, f"f_pb{i}", [128, 512]) for i in range(6)]

            sc.dma("pool", lambda e: e.dma_start(out=wo[:], in_=self.w_out.rearrange("(k p) c -> p k c", p=128)), "pw_a", writes=["wo"])
            for f0 in range(0, FC, 2):
                sc.dma("pool", lambda e, f0=f0: e.dma_start(out=wfo[:, f0:f0 + 2, :], in_=self.w_fo[f0 * 128:(f0 + 2) * 128, :].rearrange("(k p) c -> p k c", p=128)),
                       f"pw_b{(f0 // 2) % 2}", writes=[("wfo", f0)])
            wfo_all = [("wfo", f0) for f0 in range(0, FC, 2)]
            sc.dma("sp", lambda e: e.dma_start(out=identf[:], in_=self.cst[:, 0:128]), "c_c", writes=["identf"])
            sc.op("dve", lambda e: e.tensor_copy(out=ident[:], in_=identf[:]), reads=["identf"], writes=["ident"])
            sc.dma("sp", lambda e: e.dma_start(out=nwf[:], in_=self.nw_ffn[:, :]), "c_d", writes=["nwf"])
            sc.dma("sp", lambda e: e.dma_start(out=nwfin[:], in_=self.nw_fin.broadcast_to([128, D])), "c_e", writes=["nwfin"])

            xf = self.x.rearrange("n s d -> (n s) d")
            yf = self.y.rearrange("n s d -> (n s) d")
            st8 = dict(pbi=0, li=0, wi=0, sgi=0, oti=0, pti=0)

            def next_pb():
                j = st8["pbi"] % 6
                st8["pbi"] += 1
                return j

            def norm_rows(src_ap, src_keys, i):
                sc.op("act", lambda e: e.activation(out=mb[i][:], in_=src_ap, func=AF.Square, accum_out=stt[i][:, 0:1]), reads=src_keys, writes=[f"mb{i}", f"fst{i}"])
                sc.op("act", lambda e: e.activation(out=stt[i][:, 1:2], in_=stt[i][:, 0:1], func=AF.Ln, scale=1.0 / D, bias=self.eps_t[:, 0:1]), reads=[f"fst{i}"], writes=[f"fst{i}"])
                sc.op("act", lambda e: e.activation(out=stt[i][:, 2:3], in_=stt[i][:, 1:2], func=AF.Exp, scale=-0.5), reads=[f"fst{i}"], writes=[f"fst{i}"])

            def transpose_to(dstT, dkey, i, tt_, scal):
                pt = st8["pti"] % 2
                st8["pti"] += 1
                for kc in range(KC):
                    sc.op("pe", lambda e, kc=kc: e.transpose(psT[pt][:, kc * 128:(kc + 1) * 128], mb[i][:, kc * 128:(kc + 1) * 128], ident[:]),
                          reads=[f"mb{i}", "ident"], writes=[f"f_psT{pt}"])
                for kc in range(KC):
                    if scal is None:
                        sc.op("dve", lambda e, kc=kc: e.tensor_copy(out=dstT[:, kc, tt_ * 128:(tt_ + 1) * 128], in_=psT[pt][:, kc * 128:(kc + 1) * 128]),
                              reads=[f"f_psT{pt}"], writes=[(dkey, tt_)])
                    else:
                        sc.op("dve", lambda e, kc=kc: e.tensor_scalar(out=dstT[:, kc, tt_ * 128:(tt_ + 1) * 128], in0=psT[pt][:, kc * 128:(kc + 1) * 128],
                                                                       scalar1=scal[:, kc:kc + 1], scalar2=None, op0=ALU.mult),
                              reads=[f"f_psT{pt}", "nwf"], writes=[(dkey, tt_)])

            for g in range(NGT):
                tb = g * 512
                for tt_ in range(4):
                    i = st8["li"] % 2
                    st8["li"] += 1
                    r0 = tb + tt_ * 128
                    sc.dma("sp", lambda e, i=i, r0=r0: e.dma_start(out=la[i][:], in_=self.ya[r0:r0 + 128, :]), f"f_la{i}", writes=[f"la{i}"])
                    sc.dma("sp", lambda e, i=i, r0=r0: e.dma_start(out=lb[i][:], in_=self.yb[r0:r0 + 128, :]), f"f_lb{i}", writes=[f"lb{i}"])
                    sc.op("pool", lambda e, i=i: e.tensor_tensor(out=mb[i][:], in0=la[i][:], in1=lb[i][:], op=ALU.add), reads=[f"la{i}", f"lb{i}"], writes=[f"mb{i}"])
                    transpose_to(mT, "mT", i, tt_, None)
                for tt_ in range(4):
                    i = st8["li"] % 2
                    st8["li"] += 1
                    r0 = tb + tt_ * 128
                    sc.dma("sp", lambda e, i=i, r0=r0: e.dma_start(out=lx[i][:], in_=xf[r0:r0 + 128, :]), f"f_lx{i}", writes=[f"lx{i}"])
                    for nh in range(2):
                        j = next_pb()
                        for kc in range(KC):
                            sc.op("pe", lambda e, kc=kc, j=j, nh=nh, tt_=tt_: e.matmul(pb[j][:], lhsT=mT[:, kc, tt_ * 128:(tt_ + 1) * 128], rhs=wo[:, kc, nh * 512:(nh + 1) * 512],
                                                                                    start=(kc == 0), stop=(kc == KC - 1)),
                                  reads=[("mT", tt_), "wo"], writes=[f"f_pb{j}"])
                        sc.op("dve", lambda e, j=j, nh=nh, tt_=tt_, i=i: e.tensor_tensor(out=hsb[:, tt_, nh * 512:(nh + 1) * 512], in0=pb[j][:], in1=lx[i][:, nh * 512:(nh + 1) * 512], op=ALU.add),
                              reads=[f"f_pb{j}", f"lx{i}"], writes=[("h", tt_)])
                    norm_rows(hsb[:, tt_, :], [("h", tt_)], i)
                    sc.op("dve", lambda e, i=i, tt_=tt_: e.tensor_scalar(out=mb[i][:], in0=hsb[:, tt_, :], scalar1=stt[i][:, 2:3], scalar2=None, op0=ALU.mult),
                          reads=[("h", tt_), f"fst{i}"], writes=[f"mb{i}"])
                    transpose_to(hT, "hT", i, tt_, nwf)
                hT_all = [("hT", k) for k in range(4)]
                for f in range(FC):
                    wi = st8["wi"] % 4
                    st8["wi"] += 1
                    sc.dma("sp", lambda e, wi=wi, f=f: e.dma_start(out=wfi[wi][:], in_=self.wfib[f, :, :, :, :]), f"f_wfi{wi}", reads=[("wfib", f, 0), ("wfib", f, 1)], writes=[f"wfi{wi}"])
                    jg, ju = next_pb(), next_pb()
                    for gi, j in ((0, jg), (1, ju)):
                        for kc in range(KC):
                            sc.op("pe", lambda e, kc=kc, j=j, gi=gi, wi=wi: e.matmul(pb[j][:], lhsT=wfi[wi][:, gi, kc, :], rhs=hT[:, kc, :], start=(kc == 0), stop=(kc == KC - 1)),
                                  reads=[f"wfi{wi}"] + hT_all, writes=[f"f_pb{j}"])
                    si = st8["sgi"] % 2
                    st8["sgi"] += 1
                    sc.op("act", lambda e, jg=jg, si=si: e.activation(out=sg[si][:], in_=pb[jg][:], func=AF.Silu), reads=[f"f_pb{jg}"], writes=[f"sg{si}"])
                    sc.op("dve", lambda e, ju=ju, si=si, f=f: e.tensor_tensor(out=aT[:, f, :], in0=pb[ju][:], in1=sg[si][:], op=ALU.mult),
                          reads=[f"f_pb{ju}", f"sg{si}"], writes=[("aT", f)])
                aT_all = [("aT", f) for f in range(FC)]
                for tt_ in range(4):
                    oi = st8["oti"] % 2
                    st8["oti"] += 1
                    r0 = tb + tt_ * 128
                    for nh in range(2):
                        j = next_pb()
                        for f in range(FC):
                            sc.op("pe", lambda e, f=f, j=j, nh=nh, tt_=tt_: e.matmul(pb[j][:], lhsT=aT[:, f, tt_ * 128:(tt_ + 1) * 128], rhs=wfo[:, f, nh * 512:(nh + 1) * 512],
                                                                                  start=(f == 0), stop=(f == FC - 1)),
                                  reads=aT_all + wfo_all, writes=[f"f_pb{j}"])
                        sc.op("dve", lambda e, j=j, nh=nh, tt_=tt_: e.tensor_tensor(out=hsb[:, tt_, nh * 512:(nh + 1) * 512], in0=pb[j][:], in1=hsb[:, tt_, nh * 512:(nh + 1) * 512], op=ALU.add),
                              reads=[f"f_pb{j}", ("h", tt_)], writes=[("h", tt_)])
                    norm_rows(hsb[:, tt_, :], [("h", tt_)], oi)
                    sc.op("dve", lambda e, oi=oi, tt_=tt_: e.scalar_tensor_tensor(out=ot[oi][:], in0=hsb[:, tt_, :], scalar=stt[oi][:, 2:3], in1=nwfin[:], op0=ALU.mult, op1=ALU.mult),
                          reads=[("h", tt_), f"fst{oi}", "nwfin"], writes=[f"ot{oi}"])
                    sc.dma("pool", lambda e, oi=oi, r0=r0: e.dma_start(out=yf[r0:r0 + 128, :], in_=ot[oi][:]), f"f_ot{oi}", reads=[f"ot{oi}"], writes=[("y", r0)])
            sc.flush_wait_all()
            sc.emit(block)

    def build(self):
        nc = self.nc
        self.declare()
        with ExitStack() as es:
            sc = Sched(nc, es)
            self.eps_t = self.sb(es, "eps_t", [128, 4], F32)
            with nc.Block() as block:
                sc.op("dve", lambda e: e.memset(self.eps_t[:, 0:1], NORM_EPS), writes=["eps"])
                sc.op("dve", lambda e: e.memset(self.eps_t[:, 1:2], SUBLN_EPS), writes=["eps"])
                sc.op("dve", lambda e: e.memset(self.eps_t[:, 2:3], 1.0), writes=["eps"])
                sc.op("dve", lambda e: e.memset(self.eps_t[:, 3:4], 0.0), writes=["eps"])
                sc.emit(block)
            for s in range(self.NSEQ):
                if "p1" in self.phases:
                    self.phase1(sc, s)
                if "gla" in self.phases:
                    self.phase_gla(sc, s, 1)
                    self.phase_gla(sc, s, 0)
                if "attn" in self.phases:
                    self.phase_attn(sc, s, wprep=("ffn" in self.phases and s == self.NSEQ - 1))
            if "ffn" in self.phases:
                if "attn" not in self.phases:
                    self.phase_wprep(sc)
                self.phase_ffn(sc)
        return nc


def host_consts(S):
    inv_freq = 1.0 / (10000.0 ** (np.arange(0, DIFF_HD, 2, dtype=np.float32) / DIFF_HD))
    pos = np.arange(S, dtype=np.float32)
    fr = pos[None, :] * np.concatenate([inv_freq, inv_freq])[:, None].astype(np.float32)
    cosT = np.cos(fr).astype(np.float32)
    sgn = np.where(np.arange(128) < 64, -1.0, 1.0).astype(np.float32)[:, None]
    sinT = (np.sin(fr) * sgn).astype(np.float32)
    p = np.arange(128)
    same = (p[:, None] // 64) == (p[None, :] // 64)
    sI, tI = p[:, None], p[None, :]
    g = -1.0 / 16.0
    ident = np.eye(128, dtype=np.float32)
    uinc_f = (same & (sI <= tI)) * g
    ustr_f = (same & (sI > tI)) * g
    uinc_b = (same & (sI >= tI)) * g
    ustr_b = (same & (sI < tI)) * g
    cst = np.concatenate([ident, uinc_f, ustr_f, uinc_b, ustr_b, np.zeros((128, 128))], axis=1).astype(np.float32)
    m_f = (same & (sI <= tI)).astype(np.float32)
    m_b = (same & (sI >= tI)).astype(np.float32)
    msk = np.concatenate([m_f, m_b], axis=1).astype(np.float32)
    return cosT, sinT, cst, msk


def host_inputs(inp, S):
    c = np.ascontiguousarray
    w_in = np.asarray(inp["w_in"])[0]
    w1 = c(w_in[:, _w1_columns()])
    cosT, sinT, cst, msk = host_consts(S)
    wgk = np.concatenate([np.asarray(inp["w_gk2"])[0], np.asarray(inp["b_gk"])[0][:, None, :]], axis=1)
    shared = dict(
        w1=w1,
        nw_mixr=c(np.asarray(inp["norm_mix_w"])[0].reshape(1, D)),
        nw_ffn=c(np.asarray(inp["norm_ffn_w"])[0].reshape(KC, 128).T),
        nw_fin=c(np.asarray(inp["norm_final_w"]).reshape(1, D)),
        cosT=cosT, sinT=sinT, cst=cst, msk=msk,
        wgk=c(wgk.astype(np.float32)),
        gnw=c(np.tile(np.asarray(inp["gla_norm_w"])[0], 4).reshape(1, 1024)),
        sln=c(np.asarray(inp["diff_subln_w"])[0].reshape(1, 256)),
        lam4=c(np.concatenate([np.asarray(inp[k])[0] for k in ("lambda_q1", "lambda_k1", "lambda_q2", "lambda_k2")]).reshape(1, 512)),
        w_out=c(np.asarray(inp["w_out"])[0]),
        w_fi=c(np.asarray(inp["w_ffn_in"])[0]),
        w_fo=c(np.asarray(inp["w_ffn_out"])[0]),
    )
    return {k: np.asarray(v, dtype=np.float32) for k, v in shared.items()}


def kernel(**inputs):
    x = np.asarray(inputs["x"], dtype=np.float32)
    B, S, _ = x.shape
    nseq = B // N_CORES
    b = Builder(S, nseq)
    nc = b.build()
    shared = host_inputs(inputs, S)
    in_maps = [dict(shared, x=np.ascontiguousarray(x[i * nseq:(i + 1) * nseq])) for i in range(N_CORES)]
    res = run_bass_kernel_spmd(nc, in_maps, core_ids=list(range(N_CORES)))
    return np.concatenate([np.asarray(r["y"]) for r in res.results], axis=0).astype(np.float32)
```

```python
from contextlib import ExitStack
import math
import numpy as np
import ml_dtypes
import concourse.bass as bass
import concourse.mybir as mybir
from concourse.bass_utils import run_bass_kernel_spmd

F32 = mybir.dt.float32
BF16 = mybir.dt.bfloat16
AF = mybir.ActivationFunctionType
ALU = mybir.AluOpType

D = 1024
KC = D // 128
GLA_H, GLA_DK, GLA_DV, GLA_RANK = 4, 128, 256, 16
DIFF_H, DIFF_HD = 4, 128
FFN = 2816
FC = FFN // 128
NORM_EPS = 1e-6
SUBLN_EPS = 1e-5
LAMBDA_INIT = 0.8 - 0.6 * math.exp(-0.3 * 0)
N_CORES = 8

OFF_GQ, OFF_GK, OFF_GV, OFF_GR, OFF_GLR = 0, 512, 1024, 2048, 3072
OFF_DQ, OFF_DK, OFF_DV, OFF_GA, OFF_GB = 3104, 4128, 5152, 6176, 7200


def _w1_columns():
    cols = []
    cols += list(range(OFF_GQ, OFF_GQ + 512))
    cols += list(range(OFF_GK, OFF_GK + 512))
    cols += list(range(OFF_DQ, OFF_DQ + 1024))
    cols += list(range(OFF_DK, OFF_DK + 1024))
    cols += list(range(OFF_GLR, OFF_GLR + 32))
    cols += list(range(OFF_GK, OFF_GK + 512))
    for off in (OFF_GV, OFF_GR, OFF_DV, OFF_GA, OFF_GB):
        cols += list(range(off, off + 1024))
    return np.array(cols, dtype=np.int64)


W1_COLS = 512 * 6 + 32 + 512 * 11
FM_BLOCKS = [("gq", 0), ("gk", 512)] + [("dq", 1024 + 512 * i) for i in range(2)] + \
            [("dk", 2048 + 512 * i) for i in range(2)]
GLR_OFF = 3072
TM_OFF = 3104
TM_BLOCKS = [("gk", 0)] + [(n, j) for n in ("gv", "gr", "dv", "ga", "gb") for j in range(2)]


class Sched:
    CE = ("pe", "act", "dve", "pool")
    ALL = ("pe", "act", "dve", "pool", "sp")

    def __init__(self, nc, es):
        self.nc = nc
        self.prog = {e: es.enter_context(nc.semaphore("prog_" + e)) for e in self.CE}
        self.cum = {e: 0 for e in self.CE}
        self.dsem = {}
        self.es = es
        self.res_w = {}
        self.res_r = {}
        self.ops = {e: [] for e in self.ALL}
        self.waited = {e: {} for e in self.ALL}
        self.sigval = {}
        self.uid = 0

    def _slot(self, slot):
        if slot not in self.dsem:
            self.dsem[slot] = [self.es.enter_context(self.nc.semaphore("d_" + slot)), 0]
        return self.dsem[slot]

    def _collect(self, eng, reads, writes, is_dma):
        deps = []
        for k in reads:
            for ev in self.res_w.get(k, ()):
                deps.append((ev, True))
        for k in writes:
            for ev in self.res_w.get(k, ()):
                deps.append((ev, False))
            for ev in self.res_r.get(k, ()):
                deps.append((ev, False))
        out = []
        for ev, raw in deps:
            if ev[0] == "eng" and ev[1] == eng and not is_dma:
                if eng == "pe":
                    continue
            out.append(ev)
        return out

    def _update(self, ev, reads, writes):
        for k in writes:
            self.res_w[k] = [ev]
            self.res_r[k] = []
        for k in reads:
            self.res_r.setdefault(k, []).append(ev)

    def op(self, eng, fn, reads=(), writes=()):
        self.uid += 1
        ev = ("eng", eng, self.uid)
        deps = self._collect(eng, reads, writes, False)
        self.ops[eng].append(dict(fn=fn, deps=deps, ev=ev, dma=None))
        self._update(ev, reads, writes)
        return ev

    def dma(self, queue, fn, slot, reads=(), writes=()):
        s = self._slot(slot)
        deps = self._collect(queue, reads, writes, True)
        if s[1] > 0:
            deps.append(("dma", slot, s[1]))
        s[1] += 1
        ev = ("dma", slot, s[1])
        self.ops[queue].append(dict(fn=fn, deps=deps, ev=ev, dma=slot))
        self._update(ev, reads, writes)
        return ev

    def flush_wait_all(self):
        deps = [("dma", slot, s[1]) for slot, s in self.dsem.items() if s[1] > 0]
        self.ops["sp"].append(dict(fn=None, deps=deps, ev=None, dma=None))

    def emit(self, block):
        done = getattr(self, "done_uid", 0)
        for e in self.ALL:
            for o in self.ops[e]:
                o["deps"] = [d for d in o["deps"] if not (d[0] == "eng" and d[2] <= done)]
        self.done_uid = self.uid
        need = set()
        for e in self.ALL:
            for o in self.ops[e]:
                for d in o["deps"]:
                    if d[0] == "eng":
                        need.add(d[2])
        for e in self.CE:
            c = self.cum[e]
            for o in self.ops[e]:
                if o["dma"] is None and o["ev"] is not None and o["ev"][2] in need:
                    c += 1
                    self.sigval[o["ev"][2]] = c
            self.cum[e] = c
        starters = dict(pe=block.tensor, act=block.scalar, dve=block.vector, pool=block.gpsimd, sp=block.sync)
        for e in self.ALL:
            ops = self.ops[e]
            if not ops:
                continue
            waited = self.waited[e]

            def body(eng, ops=ops, waited=waited, e=e):
                for o in ops:
                    for d in o["deps"]:
                        if d[0] == "eng":
                            key, sem, val = d[1], self.prog[d[1]], self.sigval[d[2]]
                        else:
                            key, sem, val = "d_" + d[1], self.dsem[d[1]][0], 16 * d[2]
                        if waited.get(key, 0) >= val:
                            continue
                        eng.wait_ge(sem, val)
                        waited[key] = val
                    if o["fn"] is None:
                        continue
                    ins = o["fn"](eng)
                    if o["dma"] is not None:
                        ins.then_inc(self.dsem[o["dma"]][0], 16)
                    elif o["ev"][2] in self.sigval:
                        ins.then_inc(self.prog[e], 1)
            starters[e](body)
        self.ops = {e: [] for e in self.ALL}


class Builder:
    def __init__(self, S, NSEQ, debug=False, phases=("p1", "gla", "attn", "ffn"), scratch_in=False, cut=99):
        self.S, self.NSEQ, self.debug, self.phases = S, NSEQ, debug, phases
        self.scratch_in, self.cut = scratch_in, cut
        self.T = S * NSEQ
        self.NT = S // 128
        self.NG = S // 512
        self.nc = bass.Bass("TRN2", target_bir_lowering=False)

    def dram(self, name, shape, dt, kind):
        return self.nc.dram_tensor(name, list(shape), dt, kind=kind).ap()

    def declare(self):
        S, T, NSEQ = self.S, self.T, self.NSEQ
        I = "ExternalInput"
        self.x = self.dram("x", [NSEQ, S, D], F32, I)
        self.w1 = self.dram("w1", [D, W1_COLS], F32, I)
        self.nw_mixr = self.dram("nw_mixr", [1, D], F32, I)
        self.nw_ffnr = self.dram("nw_ffnr", [1, D], F32, I)
        self.nw_fin = self.dram("nw_fin", [1, D], F32, I)
        self.cosT = self.dram("cosT", [128, S], F32, I)
        self.sinT = self.dram("sinT", [128, S], F32, I)
        self.wgk = self.dram("wgk", [2, 17, 512], F32, I)
        self.gnw = self.dram("gnw", [1, 1024], F32, I)
        self.sln = self.dram("sln", [1, 256], F32, I)
        self.lam4 = self.dram("lam4", [1, 512], F32, I)
        self.w_out = self.dram("w_out", [D, D], F32, I)
        self.w_fi = self.dram("w_fi", [D, 2 * FFN], F32, I)
        self.w_fo = self.dram("w_fo", [FFN, D], F32, I)
        self.cst = self.dram("cst", [128, 6 * 128], F32, I)
        self.msk = self.dram("msk", [128, 2 * 128], F32, I)
        self.y = self.dram("y", [NSEQ, S, D], F32, "ExternalOutput")
        K = "ExternalOutput" if self.debug else "Internal"
        K1 = K
        if self.scratch_in:
            K = "ExternalInput"
        self.gqT = self.dram("gqT", [4, 128, T], BF16, K)
        self.gkT = self.dram("gkT", [4, 128, T], BF16, K)
        self.dqT = self.dram("dqT", [8, 128, T], BF16, K)
        self.dkT = self.dram("dkT", [8, 128, T], BF16, K)
        self.glrT = self.dram("glrT", [32, T], BF16, K)
        self.tm = {n: self.dram("tm_" + n, [T, (512 if n == "gk" else 1024)], BF16, K)
                   for n in ("gk", "gv", "gr", "dv", "ga", "gb")}
        K = K1
        self.ob = self.dram("ob", [T, 1024], F32, K)
        self.ya = self.dram("ya", [T, 1024], F32, K)
        self.yb = self.dram("yb", [T, 1024], F32, K)
        self.wfib = self.dram("wfib", [FC, 128, 2, KC, 128], BF16, "Internal")

    def _uniq(self, name):
        self._n = getattr(self, "_n", 0) + 1
        return f"{name}_{self._n}"

    def sb(self, es, name, shape, dt):
        return es.enter_context(self.nc.sbuf_tensor(self._uniq(name), list(shape), dt))

    def ps(self, es, name, shape, dt=F32):
        return es.enter_context(self.nc.psum_tensor(self._uniq(name), list(shape), dt))

    def phase1(self, sc, s):
        nc, S, NT, NG = self.nc, self.S, self.NT, self.NG
        t0 = s * S
        with ExitStack() as es, nc.Block() as block:
            uT = self.sb(es, "uT", [128, KC, S], BF16)
            cosT = self.sb(es, "cosT_sb", [128, S], F32)
            sinT = self.sb(es, "sinT_sb", [128, S], F32)
            nwb = self.sb(es, "nwb", [128, D], F32)
            ident = self.sb(es, "ident", [128, 128], BF16)
            identf = self.sb(es, "identf", [128, 128], F32)
            xt = [self.sb(es, f"xt{i}", [128, D], F32) for i in range(2)]
            xn = [self.sb(es, f"xn{i}", [128, D], BF16) for i in range(2)]
            st = [self.sb(es, f"st{i}", [128, 4], F32) for i in range(2)]
            wb = [self.sb(es, f"wb{i}", [128, KC, 512], BF16) for i in range(2)]
            so = [self.sb(es, f"so{i}", [128, 512], BF16) for i in range(4)]
            sof = [self.sb(es, f"sof{i}", [32, 512], BF16) for i in range(2)]
            r1 = [self.sb(es, f"r1_{i}", [128, 512], F32) for i in range(2)]
            r2 = [self.sb(es, f"r2_{i}", [128, 512], F32) for i in range(2)]
            psT = [self.ps(es, f"psT{i}", [128, 1024], BF16) for i in range(2)]
            pb = [self.ps(es, f"pb{i}", [128, 512]) for i in range(6)]
            sc.dma("sp", lambda e: e.dma_start(out=cosT[:], in_=self.cosT[:, :]), "c_cos", writes=["cosT"])
            sc.dma("sp", lambda e: e.dma_start(out=sinT[:], in_=self.sinT[:, :]), "c_sin", writes=["sinT"])
            sc.dma("sp", lambda e: e.dma_start(out=nwb[:], in_=self.nw_mixr.broadcast_to([128, D])), "c_nwm", writes=["nwb"])
            sc.dma("sp", lambda e: e.dma_start(out=identf[:], in_=self.cst[:, 0:128]), "c_id", writes=["identf"])
            sc.op("dve", lambda e: e.tensor_copy(out=ident[:], in_=identf[:]), reads=["identf"], writes=["ident"])

            def stage_a(tt):
                i = tt % 2
                sc.dma("sp", lambda e: e.dma_start(out=xt[i][:], in_=self.x[s, tt * 128:(tt + 1) * 128, :]), f"xt{i}", writes=[f"xt{i}"])
                sc.op("act", lambda e: e.activation(out=xn[i][:], in_=xt[i][:], func=AF.Square, accum_out=st[i][:, 0:1]),
                      reads=[f"xt{i}"], writes=[f"xn{i}", f"st{i}"])
                sc.op("act", lambda e: e.activation(out=st[i][:, 1:2], in_=st[i][:, 0:1], func=AF.Ln, scale=1.0 / D, bias=self.eps_t[:, 0:1]),
                      reads=[f"st{i}"], writes=[f"st{i}"])
                sc.op("act", lambda e: e.activation(out=st[i][:, 2:3], in_=st[i][:, 1:2], func=AF.Exp, scale=-0.5),
                      reads=[f"st{i}"], writes=[f"st{i}"])
                sc.op("dve", lambda e: e.scalar_tensor_tensor(out=xn[i][:], in0=xt[i][:], scalar=st[i][:, 2:3], in1=nwb[:], op0=ALU.mult, op1=ALU.mult),
                      reads=[f"xt{i}", f"st{i}", "nwb"], writes=[f"xn{i}"])

            def stage_b(tt):
                i = tt % 2
                for kc in range(KC):
                    sc.op("pe", lambda e, kc=kc: e.transpose(psT[i][:, kc * 128:(kc + 1) * 128], xn[i][:, kc * 128:(kc + 1) * 128], ident[:]),
                          reads=[f"xn{i}", "ident"], writes=[f"psT{i}"])
                sc.op("dve", lambda e: e.tensor_copy(out=uT[:, :, tt * 128:(tt + 1) * 128], in_=psT[i][:].rearrange("p (k t) -> p k t", k=KC)),
                      reads=[f"psT{i}"], writes=[("uT", tt)])

            stage_a(0)
            for tt in range(NT):
                if tt + 1 < NT:
                    stage_a(tt + 1)
                stage_b(tt)
            uT_all = [("uT", tt) for tt in range(NT)]

            state = dict(wbi=0, pbi=0, soi=0, ri=0, sofi=0)

            def load_w(col0, ncols):
                i = state["wbi"] % 2
                state["wbi"] += 1
                src = self.w1.rearrange("(k p) c -> p k c", p=128)[:, :, col0:col0 + ncols]
                sc.dma("pool", lambda e, i=i: e.dma_start(out=wb[i][:, :, 0:ncols], in_=src), f"wb{i}", writes=[f"wb{i}"])
                return i

            def next_pb():
                j = state["pbi"] % 6
                state["pbi"] += 1
                return j

            def store(dst_ap, src_tile_key, src_ap, slotname):
                sc.dma("sp", lambda e: e.dma_start(out=dst_ap, in_=src_ap), slotname, reads=[src_tile_key])

            def fm_group(wi, c, tg, j, M=128):
                for kc in range(KC):
                    sc.op("pe", lambda e, kc=kc: e.matmul(pb[j][0:M, :], lhsT=wb[wi][:, kc, c * 128:c * 128 + M],
                                                            rhs=uT[:, kc, tg * 512:(tg + 1) * 512], start=(kc == 0), stop=(kc == KC - 1)),
                          reads=[f"wb{wi}"] + uT_all[tg * 4:(tg + 1) * 4], writes=[f"pb{j}"])

            for name, col0 in FM_BLOCKS:
                wi = load_w(col0, 512)
                bidx = (col0 - {"gq": 0, "gk": 512, "dq": 1024, "dk": 2048}[name]) // 512
                if name in ("gq", "gk"):
                    dst = self.gqT if name == "gq" else self.gkT
                    for c in range(4):
                        for tg in range(NG):
                            j = next_pb()
                            fm_group(wi, c, tg, j)
                            k = state["soi"] % 4
                            state["soi"] += 1
                            if (c + tg) % 2 == 0:
                                sc.op("act", lambda e, j=j, k=k: e.activation(out=so[k][:], in_=pb[j][:], func=AF.Copy),
                                      reads=[f"pb{j}"], writes=[f"so{k}"])
                            else:
                                sc.op("dve", lambda e, j=j, k=k: e.tensor_copy(out=so[k][:], in_=pb[j][:]),
                                      reads=[f"pb{j}"], writes=[f"so{k}"])
                            store(dst[c, :, t0 + tg * 512:t0 + (tg + 1) * 512], f"so{k}", so[k][:], f"so{k}")
                else:
                    dst = self.dqT if name == "dq" else self.dkT
                    for c in range(4):
                        hc = bidx * 4 + c
                        for tg in range(NG):
                            ja = next_pb()
                            fm_group(wi, c, tg, ja)
                            ri = state["ri"] % 2
                            state["ri"] += 1
                            k = state["soi"] % 4
                            state["soi"] += 1
                            tsl = slice(tg * 512, (tg + 1) * 512)
                            sc.op("dve", lambda e, ja=ja, ri=ri, tsl=tsl: e.tensor_tensor(out=r1[ri][:], in0=pb[ja][:], in1=cosT[:, tsl], op=ALU.mult),
                                  reads=[f"pb{ja}", "cosT"], writes=[f"r1_{ri}"])
                            sc.op("dve", lambda e, ja=ja, ri=ri, tsl=tsl: e.tensor_tensor(out=r2[ri][0:64, :], in0=pb[ja][64:128, :], in1=sinT[0:64, tsl], op=ALU.mult),
                                  reads=[f"pb{ja}", "sinT"], writes=[(f"r2_{ri}", 0)])
                            sc.op("dve", lambda e, ja=ja, ri=ri, tsl=tsl: e.tensor_tensor(out=r2[ri][64:128, :], in0=pb[ja][0:64, :], in1=sinT[64:128, tsl], op=ALU.mult),
                                  reads=[f"pb{ja}", "sinT"], writes=[(f"r2_{ri}", 1)])
                            sc.op("pool", lambda e, ri=ri, k=k: e.tensor_tensor(out=so[k][:], in0=r1[ri][:], in1=r2[ri][:], op=ALU.add),
                                  reads=[f"r1_{ri}", (f"r2_{ri}", 0), (f"r2_{ri}", 1)], writes=[f"so{k}"])
                            store(dst[hc, :, t0 + tg * 512:t0 + (tg + 1) * 512], f"so{k}", so[k][:], f"so{k}")
            wi = load_w(GLR_OFF, 32)
            for tg in range(NG):
                j = next_pb()
                fm_group(wi, 0, tg, j, M=32)
                k = state["sofi"] % 2
                state["sofi"] += 1
                sc.op("dve", lambda e, j=j, k=k: e.tensor_copy(out=sof[k][:], in_=pb[j][0:32, :]), reads=[f"pb{j}"], writes=[f"sof{k}"])
                store(self.glrT[:, t0 + tg * 512:t0 + (tg + 1) * 512], f"sof{k}", sof[k][:], f"sof{k}")
            for bi, (name, jj) in enumerate(TM_BLOCKS):
                wi = load_w(TM_OFF + bi * 512, 512)
                dst = self.tm[name]
                for tt in range(NT):
                    j = next_pb()
                    for kc in range(KC):
                        sc.op("pe", lambda e, kc=kc, j=j, tt=tt, wi=wi: e.matmul(pb[j][:], lhsT=uT[:, kc, tt * 128:(tt + 1) * 128], rhs=wb[wi][:, kc, :],
                                                                        start=(kc == 0), stop=(kc == KC - 1)),
                              reads=[f"wb{wi}", ("uT", tt)], writes=[f"pb{j}"])
                    k = state["soi"] % 4
                    state["soi"] += 1
                    if name == "gr":
                        sc.op("act", lambda e, j=j, k=k: e.activation(out=so[k][:], in_=pb[j][:], func=AF.Silu), reads=[f"pb{j}"], writes=[f"so{k}"])
                    elif name in ("ga", "gb"):
                        sc.op("act", lambda e, j=j, k=k: e.activation(out=so[k][:], in_=pb[j][:], func=AF.Sigmoid), reads=[f"pb{j}"], writes=[f"so{k}"])
                    elif tt % 2 == 0:
                        sc.op("act", lambda e, j=j, k=k: e.activation(out=so[k][:], in_=pb[j][:], func=AF.Copy), reads=[f"pb{j}"], writes=[f"so{k}"])
                    else:
                        sc.op("dve", lambda e, j=j, k=k: e.tensor_copy(out=so[k][:], in_=pb[j][:]), reads=[f"pb{j}"], writes=[f"so{k}"])
                    store(dst[t0 + tt * 128:t0 + (tt + 1) * 128, jj * 512:(jj + 1) * 512], f"so{k}", so[k][:], f"so{k}")
            sc.flush_wait_all()
            sc.emit(block)


    def phase_gla(self, sc, s, dirn):
        nc, S = self.nc, self.S
        t0 = s * S
        NP, NSS = S // 128, S // 512
        fwd = dirn == 0
        with ExitStack() as es, nc.Block() as block:
            Uinc = self.sb(es, "Uinc", [128, 128], BF16)
            Ustr = self.sb(es, "Ustr", [128, 128], BF16)
            Uf = self.sb(es, "Uf", [128, 256], F32)
            wgkf = self.sb(es, "wgkf", [17, 512], F32)
            mask4 = self.sb(es, "mask4", [128, 4, 128], F32)
            wgk = self.sb(es, "wgk_sb", [17, 512], BF16)
            gnwb = self.sb(es, "gnwb", [128, 1024], F32)
            glra = [self.sb(es, f"glra{i}", [17, 512], BF16) for i in range(2)]
            qT = [self.sb(es, f"gqT{i}", [128, 4, 512], BF16) for i in range(2)]
            kT = [self.sb(es, f"gkT{i}", [128, 4, 512], BF16) for i in range(2)]
            ktm = [self.sb(es, f"gktm{i}", [128, 4, 512], BF16) for i in range(2)]
            vv = [self.sb(es, f"gv{i}", [128, 4, 1024], BF16) for i in range(2)]
            obt = [self.sb(es, f"obt{i}", [128, 1024], F32) for i in range(2)]
            grt = [self.sb(es, f"grt{i}", [128, 1024], BF16) for i in range(2)]
            gat = [self.sb(es, f"gat{i}", [128, 1024], BF16) for i in range(2)]
            e1 = self.sb(es, "g_e1", [128, 512], F32)
            sp = self.sb(es, "g_sp", [128, 512], BF16)
            ebl = self.sb(es, "g_ebl", [128, 512], F32)
            ks = self.sb(es, "g_ks", [128, 512], BF16)
            eb = self.sb(es, "g_eb", [128, 4, 128], F32)
            enb = self.sb(es, "g_enb", [128, 4, 128], F32)
            qt = self.sb(es, "g_qt", [128, 4, 128], BF16)
            kt = self.sb(es, "g_kt", [128, 4, 128], BF16)
            attm = self.sb(es, "g_attm", [128, 4, 128], BF16)
            S32 = [self.sb(es, f"S32_{h}", [128, 256], F32) for h in range(4)]
            Sbf = [self.sb(es, f"Sbf_{h}", [128, 256], BF16) for h in range(4)]
            osb = [self.sb(es, f"osb{i}", [128, 1024], F32) for i in range(2)]
            ysb = [self.sb(es, f"ysb{i}", [128, 1024], F32) for i in range(2)]
            g32 = self.sb(es, "g_g32", [128, 1024], F32)
            junk = self.sb(es, "g_junk", [128, 4, 256], F32)
            stt = [self.sb(es, f"g_st{i}", [128, 12], F32) for i in range(2)]
            bA = self.ps(es, "bA", [128, 512])
            bB = self.ps(es, "bB", [128, 512])
            bO = [self.ps(es, f"bO{h}", [128, 512]) for h in range(4)]
            bKV = [self.ps(es, f"bKV{i}", [128, 512]) for i in range(2)]

            co = 128 * (1 + 2 * dirn)
            sc.dma("sp", lambda e: e.dma_start(out=Uf[:], in_=self.cst[:, co:co + 256]), "c_a", writes=["Uf"])
            sc.op("dve", lambda e: e.tensor_copy(out=Uinc[:], in_=Uf[:, 0:128]), reads=["Uf"], writes=["Uinc"])
            sc.op("dve", lambda e: e.tensor_copy(out=Ustr[:], in_=Uf[:, 128:256]), reads=["Uf"], writes=["Ustr"])
            for h in range(4):
                sc.dma("sp", lambda e, h=h: e.dma_start(out=mask4[:, h, :], in_=self.msk[:, dirn * 128:(dirn + 1) * 128]), "c_c", writes=["mask4"])
            sc.dma("sp", lambda e: e.dma_start(out=wgkf[:], in_=self.wgk[dirn, :, :]), "c_d", writes=["wgkf"])
            sc.op("dve", lambda e: e.tensor_copy(out=wgk[:], in_=wgkf[:]), reads=["wgkf"], writes=["wgk"])
            sc.dma("sp", lambda e: e.dma_start(out=gnwb[:], in_=self.gnw.broadcast_to([128, 1024])), "c_e", writes=["gnwb"])
            for i in range(2):
                sc.op("dve", lambda e, i=i: e.memset(glra[i][:], 1.0), writes=[f"glra{i}"])
            for h in range(4):
                sc.op("dve", lambda e, h=h: e.memset(S32[h][:], 0.0), writes=[f"S32_{h}"])
                sc.op("dve", lambda e, h=h: e.memset(Sbf[h][:], 0.0), writes=[f"Sbf_{h}"])

            if fwd:
                r1, c1, r2, c2 = slice(0, 64), 63, slice(64, 128), 127
            else:
                r1, c1, r2, c2 = slice(64, 128), 64, slice(0, 64), 0
            qscale = float(GLA_DK ** -0.5)

            def load_ss(ss, i):
                tb = t0 + ss * 512
                sc.dma("sp", lambda e: e.dma_start(out=glra[i][0:16, :], in_=self.glrT[dirn * 16:(dirn + 1) * 16, tb:tb + 512]), f"glra{i}", writes=[f"glra{i}"])
                sc.dma("sp", lambda e: e.dma_start(out=qT[i][:], in_=self.gqT[:, :, tb:tb + 512].rearrange("h p t -> p h t")), f"gqT{i}", writes=[f"gqT{i}"])
                sc.dma("sp", lambda e: e.dma_start(out=kT[i][:], in_=self.gkT[:, :, tb:tb + 512].rearrange("h p t -> p h t")), f"gkT{i}", writes=[f"gkT{i}"])
                sc.dma("sp", lambda e: e.dma_start(out=ktm[i][:], in_=self.tm["gk"][tb:tb + 512, :].rearrange("(a p) c -> p a c", p=128)), f"gktm{i}", writes=[f"gktm{i}"])
                sc.dma("sp", lambda e: e.dma_start(out=vv[i][:], in_=self.tm["gv"][tb:tb + 512, :].rearrange("(a p) c -> p a c", p=128)), f"gv{i}", writes=[f"gv{i}"])

            ks2 = [ks, self.sb(es, "g_ks_b", [128, 512], BF16)]
            eb2 = [eb, self.sb(es, "g_eb_b", [128, 4, 128], F32)]
            qt2 = [qt, self.sb(es, "g_qt_b", [128, 4, 128], BF16)]
            attm2 = [attm, self.sb(es, "g_attm_b", [128, 4, 128], BF16)]

            order = list(range(NSS)) if fwd else list(range(NSS - 1, -1, -1))
            steps = []
            for n, ss in enumerate(order):
                for m, sub in enumerate([0, 1, 2, 3] if fwd else [3, 2, 1, 0]):
                    steps.append(dict(n=n, ss=ss, i=n % 2, sub=sub, pi=(n * 4 + m) % 2, q=(n * 4 + m) % 2,
                                      tok=t0 + ss * 512 + sub * 128, tsl=slice(sub * 128, (sub + 1) * 128), first=(m == 0)))

            def loads(st):
                if fwd:
                    pi, tok = st["pi"], st["tok"]
                    sc.dma("sp", lambda e: e.dma_start(out=obt[pi][:], in_=self.ob[tok:tok + 128, :]), f"obt{pi}", writes=[f"obt{pi}"])
                    sc.dma("sp", lambda e: e.dma_start(out=grt[pi][:], in_=self.tm["gr"][tok:tok + 128, :]), f"grt{pi}", writes=[f"grt{pi}"])
                    sc.dma("sp", lambda e: e.dma_start(out=gat[pi][:], in_=self.tm["ga"][tok:tok + 128, :]), f"gat{pi}", writes=[f"gat{pi}"])

            def pre_a(st):
                i, tsl = st["i"], st["tsl"]
                sc.op("pe", lambda e: e.matmul(bA[:], lhsT=glra[i][0:17, tsl], rhs=wgk[0:17, :], start=True, stop=True),
                      reads=[f"glra{i}", "wgk"], writes=["bA"])
                sc.op("act", lambda e: e.activation(out=e1[:], in_=bA[:], func=AF.Exp, scale=-1.0), reads=["bA"], writes=["e1"])
                sc.op("act", lambda e: e.activation(out=sp[:], in_=e1[:], func=AF.Ln, bias=self.eps_t[:, 2:3]), reads=["e1"], writes=["sp"])

            def pre_b(st):
                i, sub, q = st["i"], st["sub"], st["q"]
                sc.op("pe", lambda e: e.matmul(bA[:], lhsT=Ustr[:], rhs=sp[:], start=True, stop=True), reads=["Ustr", "sp"], writes=["bA"])
                for h in range(4):
                    sc.op("pe", lambda e, h=h: e.matmul(bB[:, h * 128:(h + 1) * 128], lhsT=sp[:, h * 128:(h + 1) * 128], rhs=Uinc[:], start=True, stop=True),
                          reads=["sp", "Uinc"], writes=["bB"])
                sc.op("act", lambda e: e.activation(out=ebl[:], in_=bA[:], func=AF.Exp), reads=["bA"], writes=["ebl"])
                sc.op("act", lambda e: e.activation(out=eb2[q][:].rearrange("p h t -> p (h t)"), in_=bB[:], func=AF.Exp), reads=["bB"], writes=[f"eb{q}"])
                sc.op("act", lambda e: e.activation(out=enb[:].rearrange("p h t -> p (h t)"), in_=bB[:], func=AF.Exp, scale=-1.0), reads=["bB"], writes=["enb"])

            def pre_c(st):
                i, sub, q, tsl = st["i"], st["sub"], st["q"], st["tsl"]
                sc.op("pool", lambda e: e.tensor_tensor(out=ks2[q][:], in0=ktm[i][:, sub, :], in1=ebl[:], op=ALU.mult),
                      reads=[f"gktm{i}", "ebl"], writes=[f"ks{q}"])
                sc.op("dve", lambda e: e.scalar_tensor_tensor(out=qt2[q][:], in0=eb2[q][:], scalar=qscale, in1=qT[i][:, :, tsl], op0=ALU.mult, op1=ALU.mult),
                      reads=[f"eb{q}", f"gqT{i}"], writes=[f"qt{q}"])
                sc.op("pool", lambda e: e.tensor_tensor(out=kt[:], in0=enb[:], in1=kT[i][:, :, tsl], op=ALU.mult),
                      reads=["enb", f"gkT{i}"], writes=["kt"])

            def pre_d(st):
                q = st["q"]
                for h in range(4):
                    sc.op("pe", lambda e, h=h: e.matmul(bA[:, h * 128:(h + 1) * 128], lhsT=kt[:, h, :], rhs=qt2[q][:, h, :], start=True, stop=True),
                          reads=["kt", f"qt{q}"], writes=["bA"])
                sc.op("dve", lambda e: e.tensor_tensor(out=attm2[q][:].rearrange("p h t -> p (h t)"), in0=bA[:], in1=mask4[:].rearrange("p h t -> p (h t)"), op=ALU.mult),
                      reads=["bA", "mask4"], writes=[f"attm{q}"])

            def scan_open(st):
                i, sub, q = st["i"], st["sub"], st["q"]
                for h in range(4):
                    vs = slice(h * 256, (h + 1) * 256)
                    sc.op("pe", lambda e, h=h, vs=vs: e.matmul(bO[h][:, 0:256], lhsT=attm2[q][:, h, :], rhs=vv[i][:, sub, vs], start=True, stop=False),
                          reads=[f"attm{q}", f"gv{i}"], writes=[f"bO{h}"])
                    sc.op("pe", lambda e, h=h: e.matmul(bO[h][r1, 0:256], lhsT=qt2[q][:, h, r1], rhs=Sbf[h][:], start=False, stop=True),
                          reads=[f"qt{q}", f"Sbf_{h}"], writes=[f"bO{h}"])

            def scan_head(st, h):
                i, sub, q, pi = st["i"], st["sub"], st["q"], st["pi"]
                vs = slice(h * 256, (h + 1) * 256)
                kb = h % 2
                ksl = slice(kb * 256, (kb + 1) * 256)
                hs = slice(h * 128, (h + 1) * 128)
                sc.op("pe", lambda e: e.matmul(bKV[0][:, ksl], lhsT=ks2[q][r1, hs], rhs=vv[i][r1, sub, vs], start=True, stop=True),
                      reads=[f"ks{q}", f"gv{i}"], writes=["bKV0"])
                sc.op("pe", lambda e: e.matmul(bKV[1][:, ksl], lhsT=ks2[q][r2, hs], rhs=vv[i][r2, sub, vs], start=True, stop=True),
                      reads=[f"ks{q}", f"gv{i}"], writes=["bKV1"])
                sc.op("dve", lambda e: e.scalar_tensor_tensor(out=Sbf[h][:], in0=S32[h][:], scalar=eb2[q][:, h, c1:c1 + 1], in1=bKV[0][:, ksl], op0=ALU.mult, op1=ALU.add),
                      reads=[f"S32_{h}", f"eb{q}", "bKV0"], writes=[f"Sbf_{h}"])
                sc.op("dve", lambda e: e.scalar_tensor_tensor(out=S32[h][:], in0=S32[h][:], scalar=eb2[q][:, h, c1:c1 + 1], in1=bKV[0][:, ksl], op0=ALU.mult, op1=ALU.add),
                      reads=[f"S32_{h}", f"eb{q}", "bKV0"], writes=[f"S32_{h}"])
                sc.op("pe", lambda e: e.matmul(bO[h][r2, 0:256], lhsT=qt2[q][:, h, r2], rhs=Sbf[h][:], start=False, stop=True),
                      reads=[f"qt{q}", f"Sbf_{h}"], writes=[f"bO{h}"])
                sc.op("dve", lambda e: e.scalar_tensor_tensor(out=S32[h][:], in0=S32[h][:], scalar=eb2[q][:, h, c2:c2 + 1], in1=bKV[1][:, ksl], op0=ALU.mult, op1=ALU.add),
                      reads=[f"S32_{h}", f"eb{q}", "bKV1"], writes=[f"S32_{h}"])
                sc.op("act", lambda e: e.activation(out=Sbf[h][:], in_=S32[h][:], func=AF.Copy), reads=[f"S32_{h}"], writes=[f"Sbf_{h}"])
                if fwd:
                    sc.op("dve", lambda e: e.tensor_tensor(out=osb[pi][:, vs], in0=bO[h][:, 0:256], in1=obt[pi][:, vs], op=ALU.add),
                          reads=[f"bO{h}", f"obt{pi}"], writes=[(f"osb{pi}", h)])
                else:
                    sc.op("act", lambda e: e.activation(out=osb[pi][:, vs], in_=bO[h][:, 0:256], func=AF.Copy),
                          reads=[f"bO{h}"], writes=[(f"osb{pi}", h)])

            def epilogue(st):
                pi, tok = st["pi"], st["tok"]
                okeys = [(f"osb{pi}", h) for h in range(4)]
                if not fwd:
                    sc.dma("act", lambda e: e.dma_start(out=self.ob[tok:tok + 128, :], in_=osb[pi][:]), f"osb{pi}", reads=okeys, writes=[("ob", tok)])
                    return
                for h in range(4):
                    vs = slice(h * 256, (h + 1) * 256)
                    sc.op("act", lambda e, h=h, vs=vs: e.activation(out=junk[:, h, :], in_=osb[pi][:, vs], func=AF.Square, accum_out=stt[pi][:, h:h + 1]),
                          reads=[(f"osb{pi}", h)], writes=[f"junk{h}", (f"gst{pi}", h)])
                sc.op("act", lambda e: e.activation(out=stt[pi][:, 4:8], in_=stt[pi][:, 0:4], func=AF.Ln, scale=1.0 / GLA_DV, bias=self.eps_t[:, 0:1]),
                      reads=[(f"gst{pi}", h) for h in range(4)], writes=[f"gst{pi}"])
                sc.op("act", lambda e: e.activation(out=stt[pi][:, 8:12], in_=stt[pi][:, 4:8], func=AF.Exp, scale=-0.5), reads=[f"gst{pi}"], writes=[f"gst{pi}"])
                sc.op("pool", lambda e: e.tensor_tensor(out=g32[:], in0=grt[pi][:], in1=gat[pi][:], op=ALU.mult), reads=[f"grt{pi}", f"gat{pi}"], writes=["g32"])
                sc.op("pool", lambda e: e.tensor_tensor(out=g32[:], in0=g32[:], in1=gnwb[:], op=ALU.mult), reads=["g32", "gnwb"], writes=["g32"])
                for h in range(4):
                    vs = slice(h * 256, (h + 1) * 256)
                    sc.op("dve", lambda e, h=h, vs=vs: e.scalar_tensor_tensor(out=ysb[pi][:, vs], in0=osb[pi][:, vs], scalar=stt[pi][:, 8 + h:9 + h], in1=g32[:, vs], op0=ALU.mult, op1=ALU.mult),
                          reads=[(f"osb{pi}", h), f"gst{pi}", "g32"], writes=[(f"ysb{pi}", h)])
                sc.dma("act", lambda e: e.dma_start(out=self.ya[tok:tok + 128, :], in_=ysb[pi][:]), f"ysb{pi}",
                       reads=[(f"ysb{pi}", h) for h in range(4)], writes=[("ya", tok)])

            load_ss(order[0], 0)
            if NSS > 1:
                load_ss(order[1], 1)
            loads(steps[0])
            for f_ in (pre_a, pre_b, pre_c, pre_d):
                f_(steps[0])
            for p, st in enumerate(steps):
                nx = steps[p + 1] if p + 1 < len(steps) else None
                if nx is not None:
                    loads(nx)
                scan_open(st)
                parts = (pre_a, pre_b, pre_c, pre_d)
                for h in range(4):
                    if nx is not None:
                        parts[h](nx)
                    scan_head(st, h)
                epilogue(st)
                if nx is not None and nx["first"] and nx["n"] + 1 < NSS:
                    load_ss(order[nx["n"] + 1], (nx["n"] + 1) % 2)
            sc.flush_wait_all()
            sc.emit(block)

    def phase_attn(self, sc, s, wprep=False):
        nc, S, NT, NG = self.nc, self.S, self.NT, self.NG
        t0 = s * S
        LAG = 2
        with ExitStack() as es, nc.Block() as block:
            if wprep:
                wt = self.sb(es, "wprep", [128, KC, FFN], BF16)
                wsrc = self.w_fi.rearrange("(k p) c -> p k c", p=128)

                def wprep_load(g):
                    for kc in range(KC):
                        sc.dma("pool", lambda e, kc=kc: e.dma_start(out=wt[:, kc, :], in_=wsrc[:, kc, g * FFN:(g + 1) * FFN]), f"wp{kc % 4}", writes=[("wprep", kc)])

                def wprep_store(g):
                    allk = [("wprep", kc) for kc in range(KC)]
                    for f in range(FC):
                        sc.dma("sp", lambda e, f=f: e.dma_start(out=self.wfib[f, :, g, :, :], in_=wt[:, :, f * 128:(f + 1) * 128]), f"wps{f % 4}",
                               reads=allk, writes=[("wfib", f, g)])
                wprep_load(0)
            KT = [self.sb(es, f"aKT{i}", [128, 2, S], BF16) for i in range(2)]
            VA = [self.sb(es, f"aVA{i}", [128, NT, 258], BF16) for i in range(2)]
            QT = [self.sb(es, f"aQT{i}", [128, 2, 512], BF16) for i in range(2)]
            PT = [self.sb(es, f"aPT{i}", [128, 1024], BF16) for i in range(2)]
            A0 = self.sb(es, "aA0", [128, 4, 256], F32)
            o2 = self.sb(es, "ao2", [128, 4, 256], F32)
            gbt = [self.sb(es, f"agb{i}", [128, 4, 256], BF16) for i in range(2)]
            g32 = self.sb(es, "ag32", [128, 4, 256], F32)
            ybs = [self.sb(es, f"aybs{i}", [128, 4, 256], F32) for i in range(2)]
            slnb = self.sb(es, "aslnb", [128, 256], F32)
            lam = self.sb(es, "alam", [1, 512], F32)
            lt = self.sb(es, "alt", [1, 16], F32)
            ones1 = self.sb(es, "aones1", [1, 128], F32)
            nlam = self.sb(es, "anlam", [128, 1], F32)
            rr = self.sb(es, "arr", [128, 16], F32)
            stt = self.sb(es, "ast", [128, 12], F32)
            junk = self.sb(es, "ajunk", [128, 4, 256], F32)
            ACCall = self.ps(es, "aACC", [128, 4, 512])
            ACC = [ACCall[:, i, :] for i in range(4)]
            raw = [self.sb(es, f"araw{i}", [128, 4, 258], F32) for i in range(2)]
            SB = [self.ps(es, f"aSB{i}", [128, 1024]) for i in range(2)]

            sc.dma("sp", lambda e: e.dma_start(out=lam[:], in_=self.lam4[:, :]), "c_a", writes=["lam"])
            sc.dma("sp", lambda e: e.dma_start(out=slnb[:], in_=self.sln.broadcast_to([128, 256])), "c_b", writes=["slnb"])
            sc.op("dve", lambda e: e.memset(ones1[:], 1.0), writes=["ones1"])
            sc.op("dve", lambda e: e.tensor_tensor(out=lam[:, 0:128], in0=lam[:, 0:128], in1=lam[:, 128:256], op=ALU.mult), reads=["lam"], writes=["lam"])
            sc.op("dve", lambda e: e.tensor_tensor(out=lam[:, 256:384], in0=lam[:, 256:384], in1=lam[:, 384:512], op=ALU.mult), reads=["lam"], writes=["lam"])
            sc.op("act", lambda e: e.activation(out=lam[:, 128:256], in_=lam[:, 0:128], func=AF.Copy, accum_out=lt[:, 0:1]), reads=["lam"], writes=["lam", "lt"])
            sc.op("act", lambda e: e.activation(out=lam[:, 384:512], in_=lam[:, 256:384], func=AF.Copy, accum_out=lt[:, 1:2]), reads=["lam"], writes=["lam", "lt"])
            sc.op("act", lambda e: e.activation(out=lt[:, 2:4], in_=lt[:, 0:2], func=AF.Exp), reads=["lt"], writes=["lt"])
            sc.op("dve", lambda e: e.tensor_tensor(out=lt[:, 4:5], in0=lt[:, 3:4], in1=lt[:, 2:3], op=ALU.subtract), reads=["lt"], writes=["lt"])
            sc.op("dve", lambda e: e.tensor_scalar(out=lt[:, 5:6], in0=lt[:, 4:5], scalar1=-LAMBDA_INIT, scalar2=None, op0=ALU.add), reads=["lt"], writes=["lt"])
            sc.op("pe", lambda e: e.matmul(ACC[0][:, 0:1], lhsT=ones1[:], rhs=lt[:, 5:6], start=True, stop=True), reads=["ones1", "lt"], writes=["aACC0"])
            sc.op("dve", lambda e: e.tensor_copy(out=nlam[:], in_=ACC[0][:, 0:1]), reads=["aACC0"], writes=["nlam"])
            sc.op("dve", lambda e: e.tensor_scalar(out=slnb[:], in0=slnb[:], scalar1=float(1.0 - LAMBDA_INIT), scalar2=None, op0=ALU.mult), reads=["slnb"], writes=["slnb"])
            for i in range(2):
                sc.op("pool", lambda e, i=i: e.memset(VA[i][:, :, 256:258], 1.0), writes=[f"aVA{i}"])

            def load_head(h, i):
                for c in range(2):
                    sc.dma("sp", lambda e, c=c: e.dma_start(out=KT[i][:, c, :], in_=self.dkT[2 * h + c, :, t0:t0 + S]), f"aKT{i}", writes=[f"aKT{i}"])
                for a0 in range(0, NT, 8):
                    a1 = min(NT, a0 + 8)
                    sc.dma("sp", lambda e, a0=a0, a1=a1: e.dma_start(
                        out=VA[i][:, a0:a1, 0:256],
                        in_=self.tm["dv"][t0 + a0 * 128:t0 + a1 * 128, h * 256:(h + 1) * 256].rearrange("(a p) c -> p a c", p=128)),
                        f"aVA{i}", writes=[f"aVA{i}"])

            scale = float(DIFF_HD ** -0.5)
            load_head(0, 0)
            work = [(h, qg) for h in range(DIFF_H) for qg in range(NG)]

            def load_q(widx):
                h, qg = work[widx]
                qi, tb = widx % 2, t0 + qg * 512
                for c in range(2):
                    sc.dma("sp", lambda e, c=c: e.dma_start(out=QT[qi][:, c, :], in_=self.dqT[2 * h + c, :, tb:tb + 512]), f"aQT{qi}", writes=[f"aQT{qi}"])
                sc.dma("sp", lambda e: e.dma_start(out=gbt[qi][:], in_=self.tm["gb"][tb:tb + 512, h * 256:(h + 1) * 256].rearrange("(a p) c -> p a c", p=128)),
                       f"agb{qi}", writes=[f"agb{qi}"])
            load_q(0)
            cnt = 0
            for h in range(DIFF_H):
                hi = h % 2
                if h + 1 < DIFF_H:
                    load_head(h + 1, (h + 1) % 2)
                if wprep and h == 1:
                    wprep_store(0)
                    wprep_load(1)
                if wprep and h == 3:
                    wprep_store(1)
                for qg in range(NG):
                    qi = cnt % 2
                    cnt += 1
                    tb = t0 + qg * 512
                    if cnt < len(work):
                        load_q(cnt)
                    for c in range(2):
                        NP2 = NT // 2
                        for step in range(NP2 + 1):
                            if step < NP2:
                                b_ = step % 2
                                for u in range(2):
                                    k_ = 2 * step + u
                                    sc.op("pe", lambda e, c=c, k_=k_, b_=b_, u=u, hi=hi, qi=qi: e.matmul(SB[b_][:, u * 512:(u + 1) * 512], lhsT=KT[hi][:, c, k_ * 128:(k_ + 1) * 128],
                                                                                                    rhs=QT[qi][:, c, :], start=True, stop=True),
                                          reads=[f"aKT{hi}", f"aQT{qi}"], writes=[f"aSB{b_}"])
                                sc.op("act", lambda e, b_=b_: e.activation(out=PT[b_][:], in_=SB[b_][:], func=AF.Exp, scale=scale, bias=self.eps_t[:, 3:4]),
                                      reads=[f"aSB{b_}"], writes=[f"aPT{b_}"])
                            if step >= 1:
                                b_ = (step - 1) % 2
                                for u in range(2):
                                    k_ = 2 * (step - 1) + u
                                    for qs in range(4):
                                        sc.op("pe", lambda e, k_=k_, b_=b_, u=u, qs=qs, hi=hi: e.matmul(ACC[qs][:, 0:257], lhsT=PT[b_][:, u * 512 + qs * 128:u * 512 + (qs + 1) * 128],
                                                                                                  rhs=VA[hi][:, k_, 0:257], start=(k_ == 0), stop=(k_ == NT - 1)),
                                              reads=[f"aPT{b_}", f"aVA{hi}"], writes=[f"aACC{qs}"])
                        sc.op("dve", lambda e, c=c: e.tensor_copy(out=raw[c][:, :, 0:257], in_=ACCall[:, :, 0:257]),
                              reads=[f"aACC{q_}" for q_ in range(4)], writes=[f"araw{c}"])
                        sc.op("dve", lambda e, c=c: e.reciprocal(out=rr[:, c * 4:c * 4 + 4], in_=raw[c][:, :, 256]), reads=[f"araw{c}"], writes=[("rr", c)])
                        if c == 1:
                            sc.op("dve", lambda e: e.tensor_scalar(out=rr[:, 12:16], in0=rr[:, 4:8], scalar1=nlam[:, 0:1], scalar2=None, op0=ALU.mult),
                                  reads=[("rr", 1), "nlam"], writes=[("rr", 2)])
                        for qs in range(4):
                            if c == 0:
                                sc.op("dve", lambda e, qs=qs: e.tensor_scalar(out=A0[:, qs, :], in0=raw[0][:, qs, 0:256], scalar1=rr[:, qs:qs + 1], scalar2=None, op0=ALU.mult),
                                      reads=["araw0", ("rr", 0)], writes=[("A0", qs)])
                            else:
                                sc.op("dve", lambda e, qs=qs: e.scalar_tensor_tensor(out=o2[:, qs, :], in0=raw[1][:, qs, 0:256], scalar=rr[:, 12 + qs:13 + qs], in1=A0[:, qs, :], op0=ALU.mult, op1=ALU.add),
                                      reads=["araw1", ("rr", 2), ("A0", qs)], writes=[("o2", qs)])
                    for qs in range(4):
                        sc.op("act", lambda e, qs=qs: e.activation(out=junk[:, qs, :], in_=o2[:, qs, :], func=AF.Square, accum_out=stt[:, qs:qs + 1]), reads=[("o2", qs)], writes=[f"ajunk{qs}", ("ast", qs)])
                    sc.op("act", lambda e: e.activation(out=stt[:, 4:8], in_=stt[:, 0:4], func=AF.Ln, scale=1.0 / 256, bias=self.eps_t[:, 1:2]), reads=[("ast", q_) for q_ in range(4)], writes=["ast"])
                    sc.op("act", lambda e: e.activation(out=stt[:, 8:12], in_=stt[:, 4:8], func=AF.Exp, scale=-0.5), reads=["ast"], writes=["ast"])
                    for qs in range(4):
                        sc.op("pool", lambda e, qs=qs, qi=qi: e.tensor_tensor(out=g32[:, qs, :], in0=gbt[qi][:, qs, :], in1=slnb[:], op=ALU.mult), reads=[f"agb{qi}", "slnb"], writes=[("ag32", qs)])
                    for qs in range(4):
                        sc.op("dve", lambda e, qs=qs, qi=qi: e.scalar_tensor_tensor(out=ybs[qi][:, qs, :], in0=o2[:, qs, :], scalar=stt[:, 8 + qs:9 + qs], in1=g32[:, qs, :], op0=ALU.mult, op1=ALU.mult),
                              reads=[("o2", qs), "ast", ("ag32", qs)], writes=[(f"aybs{qi}", qs)])
                    sc.dma("pool", lambda e, qi=qi, tb=tb, h=h: e.dma_start(out=self.yb[tb:tb + 512, h * 256:(h + 1) * 256].rearrange("(a p) c -> p a c", p=128), in_=ybs[qi][:]),
                           f"aybs{qi}", reads=[(f"aybs{qi}", q_) for q_ in range(4)], writes=[("yb", tb, h)])
            sc.flush_wait_all()
            sc.emit(block)

    def phase_wprep(self, sc):
        nc = self.nc
        with ExitStack() as es, nc.Block() as block:
            wt = self.sb(es, "wprep", [128, KC, 2 * FFN], BF16)
            src = self.w_fi.rearrange("(k p) c -> p k c", p=128)
            for kc in range(KC):
                sc.dma("pool", lambda e, kc=kc: e.dma_start(out=wt[:, kc, :], in_=src[:, kc, :]), f"wp{kc % 4}", writes=[("wprep", kc)])
            allk = [("wprep", kc) for kc in range(KC)]
            for f in range(FC):
                for g in range(2):
                    c0 = g * FFN + f * 128
                    sc.dma("sp", lambda e, f=f, g=g, c0=c0: e.dma_start(out=self.wfib[f, :, g, :, :], in_=wt[:, :, c0:c0 + 128]), f"wps{(2 * f + g) % 4}",
                           reads=allk, writes=[("wfib", f, g)])
            sc.flush_wait_all()
            sc.emit(block)

    def phase_ffn(self, sc):
        nc, S, T = self.nc, self.S, self.T
        NGT = T // 512
        with ExitStack() as es, nc.Block() as block:
            wo = self.sb(es, "f_wo", [128, KC, D], BF16)
            wfo = self.sb(es, "f_wfo", [128, FC, D], BF16)
            wfi = [self.sb(es, f"f_wfi{i}", [128, 2, KC, 128], BF16) for i in range(4)]
            ident = self.sb(es, "f_ident", [128, 128], BF16)
            identf = self.sb(es, "f_identf", [128, 128], F32)
            nwfb = self.sb(es, "f_nwfb", [128, D], F32)
            nwfin = self.sb(es, "f_nwfin", [128, D], F32)
            la = [self.sb(es, f"f_la{i}", [128, D], F32) for i in range(2)]
            lb = [self.sb(es, f"f_lb{i}", [128, D], F32) for i in range(2)]
            lx = [self.sb(es, f"f_lx{i}", [128, D], F32) for i in range(2)]
            mb = [self.sb(es, f"f_mb{i}", [128, D], BF16) for i in range(2)]
            mT = self.sb(es, "f_mT", [128, KC, 512], BF16)
            hT = self.sb(es, "f_hT", [128, KC, 512], BF16)
            hsb = self.sb(es, "f_h", [128, 4, D], F32)
            aT = self.sb(es, "f_aT", [128, FC, 512], BF16)
            sg = [self.sb(es, f"f_sg{i}", [128, 512], F32) for i in range(2)]
            ot = [self.sb(es, f"f_ot{i}", [128, D], F32) for i in range(2)]
            stt = [self.sb(es, f"f_st{i}", [128, 4], F32) for i in range(2)]
            psT = [self.ps(es, f"f_psT{i}", [128, 1024], BF16) for i in range(2)]
            pb = [self.ps(es, f"f_pb{i}", [128, 512]) for i in range(6)]

            sc.dma("pool", lambda e: e.dma_start(out=wo[:], in_=self.w_out.rearrange("(k p) c -> p k c", p=128)), "pw_a", writes=["wo"])
            for f0 in range(0, FC, 2):
                sc.dma("pool", lambda e, f0=f0: e.dma_start(out=wfo[:, f0:f0 + 2, :], in_=self.w_fo[f0 * 128:(f0 + 2) * 128, :].rearrange("(k p) c -> p k c", p=128)),
                       f"pw_b{(f0 // 2) % 2}", writes=[("wfo", f0)])
            wfo_all = [("wfo", f0) for f0 in range(0, FC, 2)]
            sc.dma("sp", lambda e: e.dma_start(out=identf[:], in_=self.cst[:, 0:128]), "c_c", writes=["identf"])
            sc.op("dve", lambda e: e.tensor_copy(out=ident[:], in_=identf[:]), reads=["identf"], writes=["ident"])
            sc.dma("sp", lambda e: e.dma_start(out=nwfb[:], in_=self.nw_ffnr.broadcast_to([128, D])), "c_d", writes=["nwfb"])
            sc.dma("sp", lambda e: e.dma_start(out=nwfin[:], in_=self.nw_fin.broadcast_to([128, D])), "c_e", writes=["nwfin"])

            xf = self.x.rearrange("n s d -> (n s) d")
            yf = self.y.rearrange("n s d -> (n s) d")
            st8 = dict(pbi=0, li=0, wi=0, sgi=0, oti=0, pti=0)

            def next_pb():
                j = st8["pbi"] % 6
                st8["pbi"] += 1
                return j

            def norm_rows(src_ap, src_keys, i):
                sc.op("act", lambda e: e.activation(out=mb[i][:], in_=src_ap, func=AF.Square, accum_out=stt[i][:, 0:1]), reads=src_keys, writes=[f"mb{i}", f"fst{i}"])
                sc.op("act", lambda e: e.activation(out=stt[i][:, 1:2], in_=stt[i][:, 0:1], func=AF.Ln, scale=1.0 / D, bias=self.eps_t[:, 0:1]), reads=[f"fst{i}"], writes=[f"fst{i}"])
                sc.op("act", lambda e: e.activation(out=stt[i][:, 2:3], in_=stt[i][:, 1:2], func=AF.Exp, scale=-0.5), reads=[f"fst{i}"], writes=[f"fst{i}"])

            def transpose_to(dstT, dkey, i, tt_, on_act):
                pt = st8["pti"] % 2
                st8["pti"] += 1
                for kc in range(KC):
                    sc.op("pe", lambda e, kc=kc: e.transpose(psT[pt][:, kc * 128:(kc + 1) * 128], mb[i][:, kc * 128:(kc + 1) * 128], ident[:]),
                          reads=[f"mb{i}", "ident"], writes=[f"f_psT{pt}"])
                src = psT[pt][:].rearrange("p (k t) -> p k t", k=KC)
                if on_act:
                    sc.op("act", lambda e: e.activation(out=dstT[:, :, tt_ * 128:(tt_ + 1) * 128], in_=src, func=AF.Copy), reads=[f"f_psT{pt}"], writes=[(dkey, tt_)])
                else:
                    sc.op("dve", lambda e: e.tensor_copy(out=dstT[:, :, tt_ * 128:(tt_ + 1) * 128], in_=src), reads=[f"f_psT{pt}"], writes=[(dkey, tt_)])

            for g in range(NGT):
                tb = g * 512

                def s1a(tt_):
                    i = tt_ % 2
                    r0 = tb + tt_ * 128
                    sc.dma("sp", lambda e: e.dma_start(out=la[i][:], in_=self.ya[r0:r0 + 128, :]), f"f_la{i}", writes=[f"la{i}"])
                    sc.dma("sp", lambda e: e.dma_start(out=lb[i][:], in_=self.yb[r0:r0 + 128, :]), f"f_lb{i}", writes=[f"lb{i}"])
                    sc.op("dve", lambda e: e.tensor_tensor(out=mb[i][:], in0=la[i][:], in1=lb[i][:], op=ALU.add), reads=[f"la{i}", f"lb{i}"], writes=[f"mb{i}"])

                s1a(0)
                for tt_ in range(4):
                    if tt_ + 1 < 4:
                        s1a(tt_ + 1)
                    transpose_to(mT, "mT", tt_ % 2, tt_, on_act=True)

                def s2a(tt_):
                    i = tt_ % 2
                    r0 = tb + tt_ * 128
                    sc.dma("sp", lambda e: e.dma_start(out=lx[i][:], in_=xf[r0:r0 + 128, :]), f"f_lx{i}", writes=[f"lx{i}"])
                    for nh in range(2):
                        j = next_pb()
                        for kc in range(KC):
                            sc.op("pe", lambda e, kc=kc, j=j, nh=nh: e.matmul(pb[j][:], lhsT=mT[:, kc, tt_ * 128:(tt_ + 1) * 128], rhs=wo[:, kc, nh * 512:(nh + 1) * 512],
                                                                          start=(kc == 0), stop=(kc == KC - 1)),
                                  reads=[("mT", tt_), "wo"], writes=[f"f_pb{j}"])
                        sc.op("dve", lambda e, j=j, nh=nh: e.tensor_tensor(out=hsb[:, tt_, nh * 512:(nh + 1) * 512], in0=pb[j][:], in1=lx[i][:, nh * 512:(nh + 1) * 512], op=ALU.add),
                              reads=[f"f_pb{j}", f"lx{i}"], writes=[("h", tt_)])
                    norm_rows(hsb[:, tt_, :], [("h", tt_)], i)
                    sc.op("dve", lambda e: e.scalar_tensor_tensor(out=mb[i][:], in0=hsb[:, tt_, :], scalar=stt[i][:, 2:3], in1=nwfb[:], op0=ALU.mult, op1=ALU.mult),
                          reads=[("h", tt_), f"fst{i}", "nwfb"], writes=[f"mb{i}"])

                s2a(0)
                for tt_ in range(4):
                    if tt_ + 1 < 4:
                        s2a(tt_ + 1)
                    transpose_to(hT, "hT", tt_ % 2, tt_, on_act=False)
                hT_all = [("hT", k) for k in range(4)]
                for f in range(FC):
                    wi = st8["wi"] % 4
                    st8["wi"] += 1
                    sc.dma("sp", lambda e, wi=wi, f=f: e.dma_start(out=wfi[wi][:], in_=self.wfib[f, :, :, :, :]), f"f_wfi{wi}", reads=[("wfib", f, 0), ("wfib", f, 1)], writes=[f"wfi{wi}"])
                    jg, ju = next_pb(), next_pb()
                    for gi, j in ((0, jg), (1, ju)):
                        for kc in range(KC):
                            sc.op("pe", lambda e, kc=kc, j=j, gi=gi, wi=wi: e.matmul(pb[j][:], lhsT=wfi[wi][:, gi, kc, :], rhs=hT[:, kc, :], start=(kc == 0), stop=(kc == KC - 1)),
                                  reads=[f"wfi{wi}"] + hT_all, writes=[f"f_pb{j}"])
                    si = st8["sgi"] % 2
                    st8["sgi"] += 1
                    sc.op("act", lambda e, jg=jg, si=si: e.activation(out=sg[si][:], in_=pb[jg][:], func=AF.Silu), reads=[f"f_pb{jg}"], writes=[f"sg{si}"])
                    sc.op("dve", lambda e, ju=ju, si=si, f=f: e.tensor_tensor(out=aT[:, f, :], in0=pb[ju][:], in1=sg[si][:], op=ALU.mult),
                          reads=[f"f_pb{ju}", f"sg{si}"], writes=[("aT", f)])
                aT_all = [("aT", f) for f in range(FC)]
                for tt_ in range(4):
                    oi = st8["oti"] % 2
                    st8["oti"] += 1
                    r0 = tb + tt_ * 128
                    for nh in range(2):
                        j = next_pb()
                        for f in range(FC):
                            sc.op("pe", lambda e, f=f, j=j, nh=nh, tt_=tt_: e.matmul(pb[j][:], lhsT=aT[:, f, tt_ * 128:(tt_ + 1) * 128], rhs=wfo[:, f, nh * 512:(nh + 1) * 512],
                                                                                  start=(f == 0), stop=(f == FC - 1)),
                                  reads=aT_all + wfo_all, writes=[f"f_pb{j}"])
                        sc.op("dve", lambda e, j=j, nh=nh, tt_=tt_: e.tensor_tensor(out=hsb[:, tt_, nh * 512:(nh + 1) * 512], in0=pb[j][:], in1=hsb[:, tt_, nh * 512:(nh + 1) * 512], op=ALU.add),
                              reads=[f"f_pb{j}", ("h", tt_)], writes=[("h", tt_)])
                    norm_rows(hsb[:, tt_, :], [("h", tt_)], oi)
                    sc.op("dve", lambda e, oi=oi, tt_=tt_: e.scalar_tensor_tensor(out=ot[oi][:], in0=hsb[:, tt_, :], scalar=stt[oi][:, 2:3], in1=nwfin[:], op0=ALU.mult, op1=ALU.mult),
                          reads=[("h", tt_), f"fst{oi}", "nwfin"], writes=[f"ot{oi}"])
                    sc.dma("pool", lambda e, oi=oi, r0=r0: e.dma_start(out=yf[r0:r0 + 128, :], in_=ot[oi][:]), f"f_ot{oi}", reads=[f"ot{oi}"], writes=[("y", r0)])
            sc.flush_wait_all()
            sc.emit(block)

    def build(self):
        nc = self.nc
        self.declare()
        with ExitStack() as es:
            sc = Sched(nc, es)
            self.eps_t = self.sb(es, "eps_t", [128, 4], F32)
            with nc.Block() as block:
                sc.op("dve", lambda e: e.memset(self.eps_t[:, 0:1], NORM_EPS), writes=["eps"])
                sc.op("dve", lambda e: e.memset(self.eps_t[:, 1:2], SUBLN_EPS), writes=["eps"])
                sc.op("dve", lambda e: e.memset(self.eps_t[:, 2:3], 1.0), writes=["eps"])
                sc.op("dve", lambda e: e.memset(self.eps_t[:, 3:4], 0.0), writes=["eps"])
                sc.emit(block)
            for s in range(self.NSEQ):
                if "p1" in self.phases:
                    self.phase1(sc, s)
                if "gla" in self.phases:
                    self.phase_gla(sc, s, 1)
                    self.phase_gla(sc, s, 0)
                if "attn" in self.phases:
                    self.phase_attn(sc, s, wprep=("ffn" in self.phases and s == self.NSEQ - 1))
            if "ffn" in self.phases:
                if "attn" not in self.phases:
                    self.phase_wprep(sc)
                self.phase_ffn(sc)
        return nc


def host_consts(S):
    inv_freq = 1.0 / (10000.0 ** (np.arange(0, DIFF_HD, 2, dtype=np.float32) / DIFF_HD))
    pos = np.arange(S, dtype=np.float32)
    fr = pos[None, :] * np.concatenate([inv_freq, inv_freq])[:, None].astype(np.float32)
    cosT = np.cos(fr).astype(np.float32)
    sgn = np.where(np.arange(128) < 64, -1.0, 1.0).astype(np.float32)[:, None]
    sinT = (np.sin(fr) * sgn).astype(np.float32)
    p = np.arange(128)
    same = (p[:, None] // 64) == (p[None, :] // 64)
    sI, tI = p[:, None], p[None, :]
    g = -1.0 / 16.0
    ident = np.eye(128, dtype=np.float32)
    uinc_f = (same & (sI <= tI)) * g
    ustr_f = (same & (sI > tI)) * g
    uinc_b = (same & (sI >= tI)) * g
    ustr_b = (same & (sI < tI)) * g
    cst = np.concatenate([ident, uinc_f, ustr_f, uinc_b, ustr_b, np.zeros((128, 128))], axis=1).astype(np.float32)
    m_f = (same & (sI <= tI)).astype(np.float32)
    m_b = (same & (sI >= tI)).astype(np.float32)
    msk = np.concatenate([m_f, m_b], axis=1).astype(np.float32)
    return cosT, sinT, cst, msk


def host_inputs(inp, S):
    c = np.ascontiguousarray
    w_in = np.asarray(inp["w_in"])[0]
    w1 = c(w_in[:, _w1_columns()])
    cosT, sinT, cst, msk = host_consts(S)
    wgk = np.concatenate([np.asarray(inp["w_gk2"])[0], np.asarray(inp["b_gk"])[0][:, None, :]], axis=1)
    shared = dict(
        w1=w1,
        nw_mixr=c(np.asarray(inp["norm_mix_w"])[0].reshape(1, D)),
        nw_ffnr=c(np.asarray(inp["norm_ffn_w"])[0].reshape(1, D)),
        nw_fin=c(np.asarray(inp["norm_final_w"]).reshape(1, D)),
        cosT=cosT, sinT=sinT, cst=cst, msk=msk,
        wgk=c(wgk.astype(np.float32)),
        gnw=c(np.tile(np.asarray(inp["gla_norm_w"])[0], 4).reshape(1, 1024)),
        sln=c(np.asarray(inp["diff_subln_w"])[0].reshape(1, 256)),
        lam4=c(np.concatenate([np.asarray(inp[k])[0] for k in ("lambda_q1", "lambda_k1", "lambda_q2", "lambda_k2")]).reshape(1, 512)),
        w_out=c(np.asarray(inp["w_out"])[0]),
        w_fi=c(np.asarray(inp["w_ffn_in"])[0]),
        w_fo=c(np.asarray(inp["w_ffn_out"])[0]),
    )
    return {k: np.asarray(v, dtype=np.float32) for k, v in shared.items()}


def kernel(**inputs):
    x = np.asarray(inputs["x"], dtype=np.float32)
    B, S, _ = x.shape
    nseq = B // N_CORES
    b = Builder(S, nseq)
    nc = b.build()
    shared = host_inputs(inputs, S)
    in_maps = [dict(shared, x=np.ascontiguousarray(x[i * nseq:(i + 1) * nseq])) for i in range(N_CORES)]
    res = run_bass_kernel_spmd(nc, in_maps, core_ids=list(range(N_CORES)))
    return np.concatenate([np.asarray(r["y"]) for r in res.results], axis=0).astype(np.float32)
```

```python
from contextlib import ExitStack
import math
import numpy as np
import ml_dtypes
import concourse.bass as bass
import concourse.mybir as mybir
from concourse.bass_utils import run_bass_kernel_spmd

F32 = mybir.dt.float32
BF16 = mybir.dt.bfloat16
AF = mybir.ActivationFunctionType
ALU = mybir.AluOpType

D = 1024
KC = D // 128
GLA_H, GLA_DK, GLA_DV, GLA_RANK = 4, 128, 256, 16
DIFF_H, DIFF_HD = 4, 128
FFN = 2816
FC = FFN // 128
NORM_EPS = 1e-6
SUBLN_EPS = 1e-5
LAMBDA_INIT = 0.8 - 0.6 * math.exp(-0.3 * 0)
N_CORES = 8

OFF_GQ, OFF_GK, OFF_GV, OFF_GR, OFF_GLR = 0, 512, 1024, 2048, 3072
OFF_DQ, OFF_DK, OFF_DV, OFF_GA, OFF_GB = 3104, 4128, 5152, 6176, 7200


def _w1_columns():
    cols = []
    cols += list(range(OFF_GQ, OFF_GQ + 512))
    cols += list(range(OFF_GK, OFF_GK + 512))
    cols += list(range(OFF_DQ, OFF_DQ + 1024))
    cols += list(range(OFF_DK, OFF_DK + 1024))
    cols += list(range(OFF_GLR, OFF_GLR + 32))
    cols += list(range(OFF_GK, OFF_GK + 512))
    for off in (OFF_GV, OFF_GR, OFF_DV, OFF_GA, OFF_GB):
        cols += list(range(off, off + 1024))
    return np.array(cols, dtype=np.int64)


W1_COLS = 512 * 6 + 32 + 512 * 11
FM_BLOCKS = [("gq", 0), ("gk", 512)] + [("dq", 1024 + 512 * i) for i in range(2)] + \
            [("dk", 2048 + 512 * i) for i in range(2)]
GLR_OFF = 3072
TM_OFF = 3104
TM_BLOCKS = [("gk", 0)] + [(n, j) for n in ("gv", "gr", "dv", "ga", "gb") for j in range(2)]


class Sched:
    CE = ("pe", "act", "dve", "pool")
    ALL = ("pe", "act", "dve", "pool", "sp")

    def __init__(self, nc, es):
        self.nc = nc
        self.prog = {e: es.enter_context(nc.semaphore("prog_" + e)) for e in self.CE}
        self.cum = {e: 0 for e in self.CE}
        self.dsem = {}
        self.es = es
        self.res_w = {}
        self.res_r = {}
        self.ops = {e: [] for e in self.ALL}
        self.waited = {e: {} for e in self.ALL}
        self.sigval = {}
        self.uid = 0

    def _slot(self, slot):
        if slot not in self.dsem:
            self.dsem[slot] = [self.es.enter_context(self.nc.semaphore("d_" + slot)), 0]
        return self.dsem[slot]

    def _collect(self, eng, reads, writes, is_dma):
        deps = []
        for k in reads:
            for ev in self.res_w.get(k, ()):
                deps.append((ev, True))
        for k in writes:
            for ev in self.res_w.get(k, ()):
                deps.append((ev, False))
            for ev in self.res_r.get(k, ()):
                deps.append((ev, False))
        out = []
        for ev, raw in deps:
            if ev[0] == "eng" and ev[1] == eng and not is_dma:
                if eng == "pe":
                    continue
            out.append(ev)
        return out

    def _update(self, ev, reads, writes):
        for k in writes:
            self.res_w[k] = [ev]
            self.res_r[k] = []
        for k in reads:
            self.res_r.setdefault(k, []).append(ev)

    def op(self, eng, fn, reads=(), writes=()):
        self.uid += 1
        ev = ("eng", eng, self.uid)
        deps = self._collect(eng, reads, writes, False)
        self.ops[eng].append(dict(fn=fn, deps=deps, ev=ev, dma=None))
        self._update(ev, reads, writes)
        return ev

    def dma(self, queue, fn, slot, reads=(), writes=()):
        s = self._slot(slot)
        deps = self._collect(queue, reads, writes, True)
        if s[1] > 0:
            deps.append(("dma", slot, s[1]))
        s[1] += 1
        ev = ("dma", slot, s[1])
        self.ops[queue].append(dict(fn=fn, deps=deps, ev=ev, dma=slot))
        self._update(ev, reads, writes)
        return ev

    def flush_wait_all(self):
        deps = [("dma", slot, s[1]) for slot, s in self.dsem.items() if s[1] > 0]
        self.ops["sp"].append(dict(fn=None, deps=deps, ev=None, dma=None))

    def emit(self, block):
        done = getattr(self, "done_uid", 0)
        for e in self.ALL:
            for o in self.ops[e]:
                o["deps"] = [d for d in o["deps"] if not (d[0] == "eng" and d[2] <= done)]
        self.done_uid = self.uid
        need = set()
        for e in self.ALL:
            for o in self.ops[e]:
                for d in o["deps"]:
                    if d[0] == "eng":
                        need.add(d[2])
        for e in self.CE:
            c = self.cum[e]
            for o in self.ops[e]:
                if o["dma"] is None and o["ev"] is not None and o["ev"][2] in need:
                    c += 1
                    self.sigval[o["ev"][2]] = c
            self.cum[e] = c
        starters = dict(pe=block.tensor, act=block.scalar, dve=block.vector, pool=block.gpsimd, sp=block.sync)
        for e in self.ALL:
            ops = self.ops[e]
            if not ops:
                continue
            waited = self.waited[e]

            def body(eng, ops=ops, waited=waited, e=e):
                for o in ops:
                    for d in o["deps"]:
                        if d[0] == "eng":
                            key, sem, val = d[1], self.prog[d[1]], self.sigval[d[2]]
                        else:
                            key, sem, val = "d_" + d[1], self.dsem[d[1]][0], 16 * d[2]
                        if waited.get(key, 0) >= val:
                            continue
                        eng.wait_ge(sem, val)
                        waited[key] = val
                    if o["fn"] is None:
                        continue
                    ins = o["fn"](eng)
                    if o["dma"] is not None:
                        ins.then_inc(self.dsem[o["dma"]][0], 16)
                    elif o["ev"][2] in self.sigval:
                        ins.then_inc(self.prog[e], 1)
            starters[e](body)
        self.ops = {e: [] for e in self.ALL}


class Builder:
    def __init__(self, S, NSEQ, debug=False, phases=("p1", "gla", "attn", "ffn"), scratch_in=False, cut=99):
        self.S, self.NSEQ, self.debug, self.phases = S, NSEQ, debug, phases
        self.scratch_in, self.cut = scratch_in, cut
        self.T = S * NSEQ
        self.NT = S // 128
        self.NG = S // 512
        self.nc = bass.Bass("TRN2", target_bir_lowering=False)

    def dram(self, name, shape, dt, kind):
        return self.nc.dram_tensor(name, list(shape), dt, kind=kind).ap()

    def declare(self):
        S, T, NSEQ = self.S, self.T, self.NSEQ
        I = "ExternalInput"
        self.x = self.dram("x", [NSEQ, S, D], F32, I)
        self.w1 = self.dram("w1", [D, W1_COLS], F32, I)
        self.nw_mixr = self.dram("nw_mixr", [1, D], F32, I)
        self.nw_ffnr = self.dram("nw_ffnr", [1, D], F32, I)
        self.nw_fin = self.dram("nw_fin", [1, D], F32, I)
        self.cosT = self.dram("cosT", [128, S], F32, I)
        self.sinT = self.dram("sinT", [128, S], F32, I)
        self.wgk = self.dram("wgk", [2, 17, 512], F32, I)
        self.gnw = self.dram("gnw", [1, 1024], F32, I)
        self.sln = self.dram("sln", [1, 256], F32, I)
        self.lam4 = self.dram("lam4", [1, 512], F32, I)
        self.w_out = self.dram("w_out", [D, D], F32, I)
        self.w_fi = self.dram("w_fi", [D, 2 * FFN], F32, I)
        self.w_fo = self.dram("w_fo", [FFN, D], F32, I)
        self.cst = self.dram("cst", [128, 6 * 128], F32, I)
        self.msk = self.dram("msk", [128, 2 * 128], F32, I)
        self.y = self.dram("y", [NSEQ, S, D], F32, "ExternalOutput")
        K = "ExternalOutput" if self.debug else "Internal"
        K1 = K
        if self.scratch_in:
            K = "ExternalInput"
        self.gqT = self.dram("gqT", [4, 128, T], BF16, K)
        self.gkT = self.dram("gkT", [4, 128, T], BF16, K)
        self.dqT = self.dram("dqT", [8, 128, T], BF16, K)
        self.dkT = self.dram("dkT", [8, 128, T], BF16, K)
        self.glrT = self.dram("glrT", [32, T], BF16, K)
        self.tm = {n: self.dram("tm_" + n, [T, (512 if n == "gk" else 1024)], BF16, K)
                   for n in ("gk", "gv", "gr", "dv", "ga", "gb")}
        K = K1
        self.ob = self.dram("ob", [T, 1024], F32, K)
        self.ya = self.dram("ya", [T, 1024], F32, K)
        self.yb = self.dram("yb", [T, 1024], F32, K)
        self.wfib = self.dram("wfib", [FC, 128, 2, KC, 128], BF16, "Internal")

    def _uniq(self, name):
        self._n = getattr(self, "_n", 0) + 1
        return f"{name}_{self._n}"

    def sb(self, es, name, shape, dt):
        return es.enter_context(self.nc.sbuf_tensor(self._uniq(name), list(shape), dt))

    def ps(self, es, name, shape, dt=F32):
        return es.enter_context(self.nc.psum_tensor(self._uniq(name), list(shape), dt))

    def phase1(self, sc, s):
        nc, S, NT, NG = self.nc, self.S, self.NT, self.NG
        t0 = s * S
        with ExitStack() as es, nc.Block() as block:
            uT = self.sb(es, "uT", [128, KC, S], BF16)
            cosT = self.sb(es, "cosT_sb", [128, S], F32)
            sinT = self.sb(es, "sinT_sb", [128, S], F32)
            nwb = self.sb(es, "nwb", [128, D], F32)
            ident = self.sb(es, "ident", [128, 128], BF16)
            identf = self.sb(es, "identf", [128, 128], F32)
            xt = [self.sb(es, f"xt{i}", [128, D], F32) for i in range(4)]
            xn = [self.sb(es, f"xn{i}", [128, D], BF16) for i in range(2)]
            st = [self.sb(es, f"st{i}", [128, 4], F32) for i in range(2)]
            wb = [self.sb(es, f"wb{i}", [128, KC, 512], BF16) for i in range(2)]
            so = [self.sb(es, f"so{i}", [128, 512], BF16) for i in range(4)]
            sof = [self.sb(es, f"sof{i}", [32, 512], BF16) for i in range(2)]
            r1 = [self.sb(es, f"r1_{i}", [128, 512], F32) for i in range(2)]
            r2 = [self.sb(es, f"r2_{i}", [128, 512], F32) for i in range(2)]
            psT = [self.ps(es, f"psT{i}", [128, 1024], BF16) for i in range(2)]
            pb = [self.ps(es, f"pb{i}", [128, 512]) for i in range(6)]
            sc.dma("sp", lambda e: e.dma_start(out=nwb[:], in_=self.nw_mixr.broadcast_to([128, D])), "c_nwm", writes=["nwb"])
            sc.dma("sp", lambda e: e.dma_start(out=identf[:], in_=self.cst[:, 0:128]), "c_id", writes=["identf"])
            sc.op("dve", lambda e: e.tensor_copy(out=ident[:], in_=identf[:]), reads=["identf"], writes=["ident"])

            def load_x(tt):
                j = tt % 4
                sc.dma("sp", lambda e: e.dma_start(out=xt[j][:], in_=self.x[s, tt * 128:(tt + 1) * 128, :]), f"xt{j}", writes=[f"xt{j}"])

            def stage_a(tt):
                i = tt % 2
                j = tt % 4
                if tt + 3 < NT:
                    load_x(tt + 3)
                sc.op("act", lambda e: e.activation(out=xn[i][:], in_=xt[j][:], func=AF.Square, accum_out=st[i][:, 0:1]),
                      reads=[f"xt{j}"], writes=[f"xn{i}", f"st{i}"])
                sc.op("act", lambda e: e.activation(out=st[i][:, 1:2], in_=st[i][:, 0:1], func=AF.Ln, scale=1.0 / D, bias=self.eps_t[:, 0:1]),
                      reads=[f"st{i}"], writes=[f"st{i}"])
                sc.op("act", lambda e: e.activation(out=st[i][:, 2:3], in_=st[i][:, 1:2], func=AF.Exp, scale=-0.5),
                      reads=[f"st{i}"], writes=[f"st{i}"])
                sc.op("dve", lambda e: e.scalar_tensor_tensor(out=xn[i][:], in0=xt[j][:], scalar=st[i][:, 2:3], in1=nwb[:], op0=ALU.mult, op1=ALU.mult),
                      reads=[f"xt{j}", f"st{i}", "nwb"], writes=[f"xn{i}"])

            def stage_b(tt):
                i = tt % 2
                for kc in range(KC):
                    sc.op("pe", lambda e, kc=kc: e.transpose(psT[i][:, kc * 128:(kc + 1) * 128], xn[i][:, kc * 128:(kc + 1) * 128], ident[:]),
                          reads=[f"xn{i}", "ident"], writes=[f"psT{i}"])
                sc.op("dve", lambda e: e.tensor_copy(out=uT[:, :, tt * 128:(tt + 1) * 128], in_=psT[i][:].rearrange("p (k t) -> p k t", k=KC)),
                      reads=[f"psT{i}"], writes=[("uT", tt)])

            for tt in range(min(3, NT)):
                load_x(tt)
            stage_a(0)
            for tt in range(NT):
                if tt + 1 < NT:
                    stage_a(tt + 1)
                stage_b(tt)
            for q_ in range((S + 1023) // 1024):
                c0_, c1_ = q_ * 1024, min(S, (q_ + 1) * 1024)
                sc.dma("sp", lambda e, c0_=c0_, c1_=c1_: e.dma_start(out=cosT[:, c0_:c1_], in_=self.cosT[:, c0_:c1_]), f"c_cos{q_ % 2}", writes=[("cosT", q_)])
                sc.dma("sp", lambda e, c0_=c0_, c1_=c1_: e.dma_start(out=sinT[:, c0_:c1_], in_=self.sinT[:, c0_:c1_]), f"c_sin{q_ % 2}", writes=[("sinT", q_)])
            uT_all = [("uT", tt) for tt in range(NT)]

            state = dict(wbi=0, pbi=0, soi=0, ri=0, sofi=0)

            def load_w(col0, ncols):
                i = state["wbi"] % 2
                state["wbi"] += 1
                src = self.w1.rearrange("(k p) c -> p k c", p=128)[:, :, col0:col0 + ncols]
                sc.dma("pool", lambda e, i=i: e.dma_start(out=wb[i][:, :, 0:ncols], in_=src), f"wb{i}", writes=[f"wb{i}"])
                return i

            def next_pb():
                j = state["pbi"] % 6
                state["pbi"] += 1
                return j

            def store(dst_ap, src_tile_key, src_ap, slotname):
                sc.dma("sp", lambda e: e.dma_start(out=dst_ap, in_=src_ap), slotname, reads=[src_tile_key])

            def fm_group(wi, c, tg, j, M=128):
                for kc in range(KC):
                    sc.op("pe", lambda e, kc=kc: e.matmul(pb[j][0:M, :], lhsT=wb[wi][:, kc, c * 128:c * 128 + M],
                                                            rhs=uT[:, kc, tg * 512:(tg + 1) * 512], start=(kc == 0), stop=(kc == KC - 1)),
                          reads=[f"wb{wi}"] + uT_all[tg * 4:(tg + 1) * 4], writes=[f"pb{j}"])

            for name, col0 in FM_BLOCKS:
                wi = load_w(col0, 512)
                bidx = (col0 - {"gq": 0, "gk": 512, "dq": 1024, "dk": 2048}[name]) // 512
                if name in ("gq", "gk"):
                    dst = self.gqT if name == "gq" else self.gkT
                    for c in range(4):
                        for tg in range(NG):
                            j = next_pb()
                            fm_group(wi, c, tg, j)
                            k = state["soi"] % 4
                            state["soi"] += 1
                            if (c + tg) % 2 == 0:
                                sc.op("act", lambda e, j=j, k=k: e.activation(out=so[k][:], in_=pb[j][:], func=AF.Copy),
                                      reads=[f"pb{j}"], writes=[f"so{k}"])
                            else:
                                sc.op("dve", lambda e, j=j, k=k: e.tensor_copy(out=so[k][:], in_=pb[j][:]),
                                      reads=[f"pb{j}"], writes=[f"so{k}"])
                            store(dst[c, :, t0 + tg * 512:t0 + (tg + 1) * 512], f"so{k}", so[k][:], f"so{k}")
                else:
                    dst = self.dqT if name == "dq" else self.dkT
                    for c in range(4):
                        hc = bidx * 4 + c
                        for tg in range(NG):
                            ja = next_pb()
                            fm_group(wi, c, tg, ja)
                            ri = state["ri"] % 2
                            state["ri"] += 1
                            k = state["soi"] % 4
                            state["soi"] += 1
                            tsl = slice(tg * 512, (tg + 1) * 512)
                            sc.op("dve", lambda e, ja=ja, ri=ri, tsl=tsl: e.tensor_tensor(out=r1[ri][:], in0=pb[ja][:], in1=cosT[:, tsl], op=ALU.mult),
                                  reads=[f"pb{ja}", ("cosT", tg // 2)], writes=[f"r1_{ri}"])
                            sc.op("dve", lambda e, ja=ja, ri=ri, tsl=tsl: e.tensor_tensor(out=r2[ri][0:64, :], in0=pb[ja][64:128, :], in1=sinT[0:64, tsl], op=ALU.mult),
                                  reads=[f"pb{ja}", ("sinT", tg // 2)], writes=[(f"r2_{ri}", 0)])
                            sc.op("dve", lambda e, ja=ja, ri=ri, tsl=tsl: e.tensor_tensor(out=r2[ri][64:128, :], in0=pb[ja][0:64, :], in1=sinT[64:128, tsl], op=ALU.mult),
                                  reads=[f"pb{ja}", ("sinT", tg // 2)], writes=[(f"r2_{ri}", 1)])
                            sc.op("pool", lambda e, ri=ri, k=k: e.tensor_tensor(out=so[k][:], in0=r1[ri][:], in1=r2[ri][:], op=ALU.add),
                                  reads=[f"r1_{ri}", (f"r2_{ri}", 0), (f"r2_{ri}", 1)], writes=[f"so{k}"])
                            store(dst[hc, :, t0 + tg * 512:t0 + (tg + 1) * 512], f"so{k}", so[k][:], f"so{k}")
            wi = load_w(GLR_OFF, 32)
            for tg in range(NG):
                j = next_pb()
                fm_group(wi, 0, tg, j, M=32)
                k = state["sofi"] % 2
                state["sofi"] += 1
                sc.op("dve", lambda e, j=j, k=k: e.tensor_copy(out=sof[k][:], in_=pb[j][0:32, :]), reads=[f"pb{j}"], writes=[f"sof{k}"])
                store(self.glrT[:, t0 + tg * 512:t0 + (tg + 1) * 512], f"sof{k}", sof[k][:], f"sof{k}")
            for bi, (name, jj) in enumerate(TM_BLOCKS):
                wi = load_w(TM_OFF + bi * 512, 512)
                dst = self.tm[name]
                for tt in range(NT):
                    j = next_pb()
                    for kc in range(KC):
                        sc.op("pe", lambda e, kc=kc, j=j, tt=tt, wi=wi: e.matmul(pb[j][:], lhsT=uT[:, kc, tt * 128:(tt + 1) * 128], rhs=wb[wi][:, kc, :],
                                                                        start=(kc == 0), stop=(kc == KC - 1)),
                              reads=[f"wb{wi}", ("uT", tt)], writes=[f"pb{j}"])
                    k = state["soi"] % 4
                    state["soi"] += 1
                    if name == "gr":
                        sc.op("act", lambda e, j=j, k=k: e.activation(out=so[k][:], in_=pb[j][:], func=AF.Silu), reads=[f"pb{j}"], writes=[f"so{k}"])
                    elif name in ("ga", "gb"):
                        sc.op("act", lambda e, j=j, k=k: e.activation(out=so[k][:], in_=pb[j][:], func=AF.Sigmoid), reads=[f"pb{j}"], writes=[f"so{k}"])
                    elif tt % 2 == 0:
                        sc.op("act", lambda e, j=j, k=k: e.activation(out=so[k][:], in_=pb[j][:], func=AF.Copy), reads=[f"pb{j}"], writes=[f"so{k}"])
                    else:
                        sc.op("dve", lambda e, j=j, k=k: e.tensor_copy(out=so[k][:], in_=pb[j][:]), reads=[f"pb{j}"], writes=[f"so{k}"])
                    store(dst[t0 + tt * 128:t0 + (tt + 1) * 128, jj * 512:(jj + 1) * 512], f"so{k}", so[k][:], f"so{k}")
            sc.flush_wait_all()
            sc.emit(block)


    def phase_gla(self, sc, s, dirn):
        nc, S = self.nc, self.S
        t0 = s * S
        NP, NSS = S // 128, S // 512
        fwd = dirn == 0
        with ExitStack() as es, nc.Block() as block:
            Uinc = self.sb(es, "Uinc", [128, 128], BF16)
            Ustr = self.sb(es, "Ustr", [128, 128], BF16)
            Uf = self.sb(es, "Uf", [128, 256], F32)
            wgkf = self.sb(es, "wgkf", [17, 512], F32)
            mask4 = self.sb(es, "mask4", [128, 4, 128], F32)
            wgk = self.sb(es, "wgk_sb", [17, 512], BF16)
            gnwb = self.sb(es, "gnwb", [128, 1024], F32)
            glra = [self.sb(es, f"glra{i}", [17, 512], BF16) for i in range(2)]
            qT = [self.sb(es, f"gqT{i}", [128, 4, 512], BF16) for i in range(2)]
            kT = [self.sb(es, f"gkT{i}", [128, 4, 512], BF16) for i in range(2)]
            ktm = [self.sb(es, f"gktm{i}", [128, 4, 512], BF16) for i in range(2)]
            vv = [self.sb(es, f"gv{i}", [128, 4, 1024], BF16) for i in range(2)]
            obt = [self.sb(es, f"obt{i}", [128, 1024], F32) for i in range(2)]
            grt = [self.sb(es, f"grt{i}", [128, 1024], BF16) for i in range(2)]
            gat = [self.sb(es, f"gat{i}", [128, 1024], BF16) for i in range(2)]
            e1 = self.sb(es, "g_e1", [128, 512], F32)
            sp = self.sb(es, "g_sp", [128, 512], BF16)
            ebl = self.sb(es, "g_ebl", [128, 512], F32)
            ks = self.sb(es, "g_ks", [128, 512], BF16)
            eb = self.sb(es, "g_eb", [128, 4, 128], F32)
            enb = self.sb(es, "g_enb", [128, 4, 128], F32)
            qt = self.sb(es, "g_qt", [128, 4, 128], BF16)
            kt = self.sb(es, "g_kt", [128, 4, 128], BF16)
            attm = self.sb(es, "g_attm", [128, 4, 128], BF16)
            S32 = [self.sb(es, f"S32_{h}", [128, 256], F32) for h in range(4)]
            Sbf = [self.sb(es, f"Sbf_{h}", [128, 256], BF16) for h in range(4)]
            osb = [self.sb(es, f"osb{i}", [128, 1024], F32) for i in range(2)]
            ysb = [self.sb(es, f"ysb{i}", [128, 1024], F32) for i in range(2)]
            g32 = self.sb(es, "g_g32", [128, 1024], F32)
            junk = self.sb(es, "g_junk", [128, 4, 256], F32)
            stt = [self.sb(es, f"g_st{i}", [128, 12], F32) for i in range(2)]
            bA = self.ps(es, "bA", [128, 512])
            bB = self.ps(es, "bB", [128, 512])
            bO = [self.ps(es, f"bO{h}", [128, 512]) for h in range(4)]
            bKV = [self.ps(es, f"bKV{i}", [128, 512]) for i in range(2)]

            co = 128 * (1 + 2 * dirn)
            sc.dma("sp", lambda e: e.dma_start(out=Uf[:], in_=self.cst[:, co:co + 256]), "c_a", writes=["Uf"])
            sc.op("dve", lambda e: e.tensor_copy(out=Uinc[:], in_=Uf[:, 0:128]), reads=["Uf"], writes=["Uinc"])
            sc.op("dve", lambda e: e.tensor_copy(out=Ustr[:], in_=Uf[:, 128:256]), reads=["Uf"], writes=["Ustr"])
            for h in range(4):
                sc.dma("sp", lambda e, h=h: e.dma_start(out=mask4[:, h, :], in_=self.msk[:, dirn * 128:(dirn + 1) * 128]), "c_c", writes=["mask4"])
            sc.dma("sp", lambda e: e.dma_start(out=wgkf[:], in_=self.wgk[dirn, :, :]), "c_d", writes=["wgkf"])
            sc.op("dve", lambda e: e.tensor_copy(out=wgk[:], in_=wgkf[:]), reads=["wgkf"], writes=["wgk"])
            sc.dma("sp", lambda e: e.dma_start(out=gnwb[:], in_=self.gnw.broadcast_to([128, 1024])), "c_e", writes=["gnwb"])
            for i in range(2):
                sc.op("dve", lambda e, i=i: e.memset(glra[i][:], 1.0), writes=[f"glra{i}"])
            for h in range(4):
                sc.op("dve", lambda e, h=h: e.memset(S32[h][:], 0.0), writes=[f"S32_{h}"])
                sc.op("dve", lambda e, h=h: e.memset(Sbf[h][:], 0.0), writes=[f"Sbf_{h}"])

            if fwd:
                r1, c1, r2, c2 = slice(0, 64), 63, slice(64, 128), 127
            else:
                r1, c1, r2, c2 = slice(64, 128), 64, slice(0, 64), 0
            qscale = float(GLA_DK ** -0.5)

            def load_ss(ss, i):
                tb = t0 + ss * 512
                sc.dma("sp", lambda e: e.dma_start(out=glra[i][0:16, :], in_=self.glrT[dirn * 16:(dirn + 1) * 16, tb:tb + 512]), f"glra{i}", writes=[f"glra{i}"])
                sc.dma("sp", lambda e: e.dma_start(out=qT[i][:], in_=self.gqT[:, :, tb:tb + 512].rearrange("h p t -> p h t")), f"gqT{i}", writes=[f"gqT{i}"])
                sc.dma("sp", lambda e: e.dma_start(out=kT[i][:], in_=self.gkT[:, :, tb:tb + 512].rearrange("h p t -> p h t")), f"gkT{i}", writes=[f"gkT{i}"])
                sc.dma("sp", lambda e: e.dma_start(out=ktm[i][:], in_=self.tm["gk"][tb:tb + 512, :].rearrange("(a p) c -> p a c", p=128)), f"gktm{i}", writes=[f"gktm{i}"])
                sc.dma("sp", lambda e: e.dma_start(out=vv[i][:], in_=self.tm["gv"][tb:tb + 512, :].rearrange("(a p) c -> p a c", p=128)), f"gv{i}", writes=[f"gv{i}"])

            ks2 = [ks, self.sb(es, "g_ks_b", [128, 512], BF16)]
            eb2 = [eb, self.sb(es, "g_eb_b", [128, 4, 128], F32)]
            qt2 = [qt, self.sb(es, "g_qt_b", [128, 4, 128], BF16)]
            attm2 = [attm, self.sb(es, "g_attm_b", [128, 4, 128], BF16)]

            order = list(range(NSS)) if fwd else list(range(NSS - 1, -1, -1))
            steps = []
            for n, ss in enumerate(order):
                for m, sub in enumerate([0, 1, 2, 3] if fwd else [3, 2, 1, 0]):
                    steps.append(dict(n=n, ss=ss, i=n % 2, sub=sub, pi=(n * 4 + m) % 2, q=(n * 4 + m) % 2,
                                      tok=t0 + ss * 512 + sub * 128, tsl=slice(sub * 128, (sub + 1) * 128), first=(m == 0)))

            def loads(st):
                if fwd:
                    pi, tok = st["pi"], st["tok"]
                    sc.dma("sp", lambda e: e.dma_start(out=obt[pi][:], in_=self.ob[tok:tok + 128, :]), f"obt{pi}", writes=[f"obt{pi}"])
                    sc.dma("sp", lambda e: e.dma_start(out=grt[pi][:], in_=self.tm["gr"][tok:tok + 128, :]), f"grt{pi}", writes=[f"grt{pi}"])
                    sc.dma("sp", lambda e: e.dma_start(out=gat[pi][:], in_=self.tm["ga"][tok:tok + 128, :]), f"gat{pi}", writes=[f"gat{pi}"])

            def pre_a(st):
                i, tsl = st["i"], st["tsl"]
                sc.op("pe", lambda e: e.matmul(bA[:], lhsT=glra[i][0:17, tsl], rhs=wgk[0:17, :], start=True, stop=True),
                      reads=[f"glra{i}", "wgk"], writes=["bA"])
                sc.op("act", lambda e: e.activation(out=e1[:], in_=bA[:], func=AF.Exp, scale=-1.0), reads=["bA"], writes=["e1"])
                sc.op("act", lambda e: e.activation(out=sp[:], in_=e1[:], func=AF.Ln, bias=self.eps_t[:, 2:3]), reads=["e1"], writes=["sp"])

            def pre_b(st):
                i, sub, q = st["i"], st["sub"], st["q"]
                sc.op("pe", lambda e: e.matmul(bA[:], lhsT=Ustr[:], rhs=sp[:], start=True, stop=True), reads=["Ustr", "sp"], writes=["bA"])
                for h in range(4):
                    sc.op("pe", lambda e, h=h: e.matmul(bB[:, h * 128:(h + 1) * 128], lhsT=sp[:, h * 128:(h + 1) * 128], rhs=Uinc[:], start=True, stop=True),
                          reads=["sp", "Uinc"], writes=["bB"])
                sc.op("act", lambda e: e.activation(out=ebl[:], in_=bA[:], func=AF.Exp), reads=["bA"], writes=["ebl"])
                sc.op("act", lambda e: e.activation(out=eb2[q][:].rearrange("p h t -> p (h t)"), in_=bB[:], func=AF.Exp), reads=["bB"], writes=[f"eb{q}"])
                sc.op("act", lambda e: e.activation(out=enb[:].rearrange("p h t -> p (h t)"), in_=bB[:], func=AF.Exp, scale=-1.0), reads=["bB"], writes=["enb"])

            def pre_c(st):
                i, sub, q, tsl = st["i"], st["sub"], st["q"], st["tsl"]
                sc.op("pool", lambda e: e.tensor_tensor(out=ks2[q][:], in0=ktm[i][:, sub, :], in1=ebl[:], op=ALU.mult),
                      reads=[f"gktm{i}", "ebl"], writes=[f"ks{q}"])
                sc.op("dve", lambda e: e.scalar_tensor_tensor(out=qt2[q][:], in0=eb2[q][:], scalar=qscale, in1=qT[i][:, :, tsl], op0=ALU.mult, op1=ALU.mult),
                      reads=[f"eb{q}", f"gqT{i}"], writes=[f"qt{q}"])
                sc.op("pool", lambda e: e.tensor_tensor(out=kt[:], in0=enb[:], in1=kT[i][:, :, tsl], op=ALU.mult),
                      reads=["enb", f"gkT{i}"], writes=["kt"])

            def pre_d(st):
                q = st["q"]
                for h in range(4):
                    sc.op("pe", lambda e, h=h: e.matmul(bA[:, h * 128:(h + 1) * 128], lhsT=kt[:, h, :], rhs=qt2[q][:, h, :], start=True, stop=True),
                          reads=["kt", f"qt{q}"], writes=["bA"])
                sc.op("dve", lambda e: e.tensor_tensor(out=attm2[q][:].rearrange("p h t -> p (h t)"), in0=bA[:], in1=mask4[:].rearrange("p h t -> p (h t)"), op=ALU.mult),
                      reads=["bA", "mask4"], writes=[f"attm{q}"])

            def scan_open(st):
                i, sub, q = st["i"], st["sub"], st["q"]
                for h in range(4):
                    vs = slice(h * 256, (h + 1) * 256)
                    sc.op("pe", lambda e, h=h, vs=vs: e.matmul(bO[h][:, 0:256], lhsT=attm2[q][:, h, :], rhs=vv[i][:, sub, vs], start=True, stop=False),
                          reads=[f"attm{q}", f"gv{i}"], writes=[f"bO{h}"])
                    sc.op("pe", lambda e, h=h: e.matmul(bO[h][r1, 0:256], lhsT=qt2[q][:, h, r1], rhs=Sbf[h][:], start=False, stop=True),
                          reads=[f"qt{q}", f"Sbf_{h}"], writes=[f"bO{h}"])

            def scan_head(st, h):
                i, sub, q, pi = st["i"], st["sub"], st["q"], st["pi"]
                vs = slice(h * 256, (h + 1) * 256)
                kb = h % 2
                ksl = slice(kb * 256, (kb + 1) * 256)
                hs = slice(h * 128, (h + 1) * 128)
                sc.op("pe", lambda e: e.matmul(bKV[0][:, ksl], lhsT=ks2[q][r1, hs], rhs=vv[i][r1, sub, vs], start=True, stop=True),
                      reads=[f"ks{q}", f"gv{i}"], writes=["bKV0"])
                sc.op("pe", lambda e: e.matmul(bKV[1][:, ksl], lhsT=ks2[q][r2, hs], rhs=vv[i][r2, sub, vs], start=True, stop=True),
                      reads=[f"ks{q}", f"gv{i}"], writes=["bKV1"])
                sc.op("dve", lambda e: e.scalar_tensor_tensor(out=Sbf[h][:], in0=S32[h][:], scalar=eb2[q][:, h, c1:c1 + 1], in1=bKV[0][:, ksl], op0=ALU.mult, op1=ALU.add),
                      reads=[f"S32_{h}", f"eb{q}", "bKV0"], writes=[f"Sbf_{h}"])
                sc.op("dve", lambda e: e.scalar_tensor_tensor(out=S32[h][:], in0=S32[h][:], scalar=eb2[q][:, h, c1:c1 + 1], in1=bKV[0][:, ksl], op0=ALU.mult, op1=ALU.add),
                      reads=[f"S32_{h}", f"eb{q}", "bKV0"], writes=[f"S32_{h}"])
                sc.op("pe", lambda e: e.matmul(bO[h][r2, 0:256], lhsT=qt2[q][:, h, r2], rhs=Sbf[h][:], start=False, stop=True),
                      reads=[f"qt{q}", f"Sbf_{h}"], writes=[f"bO{h}"])
                sc.op("dve", lambda e: e.scalar_tensor_tensor(out=S32[h][:], in0=S32[h][:], scalar=eb2[q][:, h, c2:c2 + 1], in1=bKV[1][:, ksl], op0=ALU.mult, op1=ALU.add),
                      reads=[f"S32_{h}", f"eb{q}", "bKV1"], writes=[f"S32_{h}"])
                sc.op("act", lambda e: e.activation(out=Sbf[h][:], in_=S32[h][:], func=AF.Copy), reads=[f"S32_{h}"], writes=[f"Sbf_{h}"])
                if fwd:
                    sc.op("dve", lambda e: e.tensor_tensor(out=osb[pi][:, vs], in0=bO[h][:, 0:256], in1=obt[pi][:, vs], op=ALU.add),
                          reads=[f"bO{h}", f"obt{pi}"], writes=[(f"osb{pi}", h)])
                else:
                    sc.op("act", lambda e: e.activation(out=osb[pi][:, vs], in_=bO[h][:, 0:256], func=AF.Copy),
                          reads=[f"bO{h}"], writes=[(f"osb{pi}", h)])

            def epilogue(st):
                pi, tok = st["pi"], st["tok"]
                okeys = [(f"osb{pi}", h) for h in range(4)]
                if not fwd:
                    sc.dma("act", lambda e: e.dma_start(out=self.ob[tok:tok + 128, :], in_=osb[pi][:]), f"osb{pi}", reads=okeys, writes=[("ob", tok)])
                    return
                for h in range(4):
                    vs = slice(h * 256, (h + 1) * 256)
                    sc.op("act", lambda e, h=h, vs=vs: e.activation(out=junk[:, h, :], in_=osb[pi][:, vs], func=AF.Square, accum_out=stt[pi][:, h:h + 1]),
                          reads=[(f"osb{pi}", h)], writes=[f"junk{h}", (f"gst{pi}", h)])
                sc.op("act", lambda e: e.activation(out=stt[pi][:, 4:8], in_=stt[pi][:, 0:4], func=AF.Ln, scale=1.0 / GLA_DV, bias=self.eps_t[:, 0:1]),
                      reads=[(f"gst{pi}", h) for h in range(4)], writes=[f"gst{pi}"])
                sc.op("act", lambda e: e.activation(out=stt[pi][:, 8:12], in_=stt[pi][:, 4:8], func=AF.Exp, scale=-0.5), reads=[f"gst{pi}"], writes=[f"gst{pi}"])
                sc.op("pool", lambda e: e.tensor_tensor(out=g32[:], in0=grt[pi][:], in1=gat[pi][:], op=ALU.mult), reads=[f"grt{pi}", f"gat{pi}"], writes=["g32"])
                sc.op("pool", lambda e: e.tensor_tensor(out=g32[:], in0=g32[:], in1=gnwb[:], op=ALU.mult), reads=["g32", "gnwb"], writes=["g32"])
                for h in range(4):
                    vs = slice(h * 256, (h + 1) * 256)
                    sc.op("dve", lambda e, h=h, vs=vs: e.scalar_tensor_tensor(out=ysb[pi][:, vs], in0=osb[pi][:, vs], scalar=stt[pi][:, 8 + h:9 + h], in1=g32[:, vs], op0=ALU.mult, op1=ALU.mult),
                          reads=[(f"osb{pi}", h), f"gst{pi}", "g32"], writes=[(f"ysb{pi}", h)])
                sc.dma("act", lambda e: e.dma_start(out=self.ya[tok:tok + 128, :], in_=ysb[pi][:]), f"ysb{pi}",
                       reads=[(f"ysb{pi}", h) for h in range(4)], writes=[("ya", tok)])

            load_ss(order[0], 0)
            if NSS > 1:
                load_ss(order[1], 1)
            loads(steps[0])
            for f_ in (pre_a, pre_b, pre_c, pre_d):
                f_(steps[0])
            for p, st in enumerate(steps):
                nx = steps[p + 1] if p + 1 < len(steps) else None
                if nx is not None:
                    loads(nx)
                scan_open(st)
                parts = (pre_a, pre_b, pre_c, pre_d)
                for h in range(4):
                    if nx is not None:
                        parts[h](nx)
                    scan_head(st, h)
                epilogue(st)
                if nx is not None and nx["first"] and nx["n"] + 1 < NSS:
                    load_ss(order[nx["n"] + 1], (nx["n"] + 1) % 2)
            sc.flush_wait_all()
            sc.emit(block)

    def phase_attn(self, sc, s, wprep=False):
        nc, S, NT, NG = self.nc, self.S, self.NT, self.NG
        t0 = s * S
        LAG = 2
        with ExitStack() as es, nc.Block() as block:
            if wprep:
                wt = self.sb(es, "wprep", [128, KC, FFN], BF16)
                wsrc = self.w_fi.rearrange("(k p) c -> p k c", p=128)

                def wprep_load(g):
                    for kc in range(KC):
                        sc.dma("pool", lambda e, kc=kc: e.dma_start(out=wt[:, kc, :], in_=wsrc[:, kc, g * FFN:(g + 1) * FFN]), f"wp{kc % 4}", writes=[("wprep", kc)])

                def wprep_store(g):
                    allk = [("wprep", kc) for kc in range(KC)]
                    for f in range(FC):
                        sc.dma("sp", lambda e, f=f: e.dma_start(out=self.wfib[f, :, g, :, :], in_=wt[:, :, f * 128:(f + 1) * 128]), f"wps{f % 4}",
                               reads=allk, writes=[("wfib", f, g)])
                wprep_load(0)
            KT = [self.sb(es, f"aKT{i}", [128, 2, S], BF16) for i in range(2)]
            VA = [self.sb(es, f"aVA{i}", [128, NT, 258], BF16) for i in range(2)]
            QT = [self.sb(es, f"aQT{i}", [128, 2, 512], BF16) for i in range(2)]
            PT = [self.sb(es, f"aPT{i}", [128, 1024], BF16) for i in range(2)]
            A0 = self.sb(es, "aA0", [128, 4, 256], F32)
            o2 = self.sb(es, "ao2", [128, 4, 256], F32)
            gbt = [self.sb(es, f"agb{i}", [128, 4, 256], BF16) for i in range(2)]
            g32 = self.sb(es, "ag32", [128, 4, 256], F32)
            ybs = [self.sb(es, f"aybs{i}", [128, 4, 256], F32) for i in range(2)]
            slnb = self.sb(es, "aslnb", [128, 256], F32)
            lam = self.sb(es, "alam", [1, 512], F32)
            lt = self.sb(es, "alt", [1, 16], F32)
            ones1 = self.sb(es, "aones1", [1, 128], F32)
            nlam = self.sb(es, "anlam", [128, 1], F32)
            rr = self.sb(es, "arr", [128, 16], F32)
            stt = self.sb(es, "ast", [128, 12], F32)
            junk = self.sb(es, "ajunk", [128, 4, 256], F32)
            ACCall = self.ps(es, "aACC", [128, 4, 512])
            ACC = [ACCall[:, i, :] for i in range(4)]
            raw = [self.sb(es, f"araw{i}", [128, 4, 258], F32) for i in range(2)]
            SB = [self.ps(es, f"aSB{i}", [128, 1024]) for i in range(2)]

            sc.dma("sp", lambda e: e.dma_start(out=lam[:], in_=self.lam4[:, :]), "c_a", writes=["lam"])
            sc.dma("sp", lambda e: e.dma_start(out=slnb[:], in_=self.sln.broadcast_to([128, 256])), "c_b", writes=["slnb"])
            sc.op("dve", lambda e: e.memset(ones1[:], 1.0), writes=["ones1"])
            sc.op("dve", lambda e: e.tensor_tensor(out=lam[:, 0:128], in0=lam[:, 0:128], in1=lam[:, 128:256], op=ALU.mult), reads=["lam"], writes=["lam"])
            sc.op("dve", lambda e: e.tensor_tensor(out=lam[:, 256:384], in0=lam[:, 256:384], in1=lam[:, 384:512], op=ALU.mult), reads=["lam"], writes=["lam"])
            sc.op("act", lambda e: e.activation(out=lam[:, 128:256], in_=lam[:, 0:128], func=AF.Copy, accum_out=lt[:, 0:1]), reads=["lam"], writes=["lam", "lt"])
            sc.op("act", lambda e: e.activation(out=lam[:, 384:512], in_=lam[:, 256:384], func=AF.Copy, accum_out=lt[:, 1:2]), reads=["lam"], writes=["lam", "lt"])
            sc.op("act", lambda e: e.activation(out=lt[:, 2:4], in_=lt[:, 0:2], func=AF.Exp), reads=["lt"], writes=["lt"])
            sc.op("dve", lambda e: e.tensor_tensor(out=lt[:, 4:5], in0=lt[:, 3:4], in1=lt[:, 2:3], op=ALU.subtract), reads=["lt"], writes=["lt"])
            sc.op("dve", lambda e: e.tensor_scalar(out=lt[:, 5:6], in0=lt[:, 4:5], scalar1=-LAMBDA_INIT, scalar2=None, op0=ALU.add), reads=["lt"], writes=["lt"])
            sc.op("pe", lambda e: e.matmul(ACC[0][:, 0:1], lhsT=ones1[:], rhs=lt[:, 5:6], start=True, stop=True), reads=["ones1", "lt"], writes=["aACC0"])
            sc.op("dve", lambda e: e.tensor_copy(out=nlam[:], in_=ACC[0][:, 0:1]), reads=["aACC0"], writes=["nlam"])
            sc.op("dve", lambda e: e.tensor_scalar(out=slnb[:], in0=slnb[:], scalar1=float(1.0 - LAMBDA_INIT), scalar2=None, op0=ALU.mult), reads=["slnb"], writes=["slnb"])
            for i in range(2):
                sc.op("pool", lambda e, i=i: e.memset(VA[i][:, :, 256:258], 1.0), writes=[f"aVA{i}"])

            def load_head(h, i):
                for c in range(2):
                    sc.dma("sp", lambda e, c=c: e.dma_start(out=KT[i][:, c, :], in_=self.dkT[2 * h + c, :, t0:t0 + S]), f"aKT{i}", writes=[f"aKT{i}"])
                for a0 in range(0, NT, 8):
                    a1 = min(NT, a0 + 8)
                    sc.dma("sp", lambda e, a0=a0, a1=a1: e.dma_start(
                        out=VA[i][:, a0:a1, 0:256],
                        in_=self.tm["dv"][t0 + a0 * 128:t0 + a1 * 128, h * 256:(h + 1) * 256].rearrange("(a p) c -> p a c", p=128)),
                        f"aVA{i}", writes=[f"aVA{i}"])

            scale = float(DIFF_HD ** -0.5)
            load_head(0, 0)
            work = [(h, qg) for h in range(DIFF_H) for qg in range(NG)]

            def load_q(widx):
                h, qg = work[widx]
                qi, tb = widx % 2, t0 + qg * 512
                for c in range(2):
                    sc.dma("sp", lambda e, c=c: e.dma_start(out=QT[qi][:, c, :], in_=self.dqT[2 * h + c, :, tb:tb + 512]), f"aQT{qi}", writes=[f"aQT{qi}"])
                sc.dma("sp", lambda e: e.dma_start(out=gbt[qi][:], in_=self.tm["gb"][tb:tb + 512, h * 256:(h + 1) * 256].rearrange("(a p) c -> p a c", p=128)),
                       f"agb{qi}", writes=[f"agb{qi}"])
            load_q(0)
            cnt = 0
            for h in range(DIFF_H):
                hi = h % 2
                if h + 1 < DIFF_H:
                    load_head(h + 1, (h + 1) % 2)
                if wprep and h == 1:
                    wprep_store(0)
                    wprep_load(1)
                if wprep and h == 3:
                    wprep_store(1)
                for qg in range(NG):
                    qi = cnt % 2
                    cnt += 1
                    tb = t0 + qg * 512
                    if cnt < len(work):
                        load_q(cnt)
                    for c in range(2):
                        NP2 = NT // 2
                        for step in range(NP2 + 1):
                            if step < NP2:
                                b_ = step % 2
                                for u in range(2):
                                    k_ = 2 * step + u
                                    sc.op("pe", lambda e, c=c, k_=k_, b_=b_, u=u, hi=hi, qi=qi: e.matmul(SB[b_][:, u * 512:(u + 1) * 512], lhsT=KT[hi][:, c, k_ * 128:(k_ + 1) * 128],
                                                                                                    rhs=QT[qi][:, c, :], start=True, stop=True),
                                          reads=[f"aKT{hi}", f"aQT{qi}"], writes=[f"aSB{b_}"])
                                sc.op("act", lambda e, b_=b_: e.activation(out=PT[b_][:], in_=SB[b_][:], func=AF.Exp, scale=scale, bias=self.eps_t[:, 3:4]),
                                      reads=[f"aSB{b_}"], writes=[f"aPT{b_}"])
                            if step >= 1:
                                b_ = (step - 1) % 2
                                for u in range(2):
                                    k_ = 2 * (step - 1) + u
                                    for qs in range(4):
                                        sc.op("pe", lambda e, k_=k_, b_=b_, u=u, qs=qs, hi=hi: e.matmul(ACC[qs][:, 0:257], lhsT=PT[b_][:, u * 512 + qs * 128:u * 512 + (qs + 1) * 128],
                                                                                                  rhs=VA[hi][:, k_, 0:257], start=(k_ == 0), stop=(k_ == NT - 1)),
                                              reads=[f"aPT{b_}", f"aVA{hi}"], writes=[f"aACC{qs}"])
                        sc.op("dve", lambda e, c=c: e.tensor_copy(out=raw[c][:, :, 0:257], in_=ACCall[:, :, 0:257]),
                              reads=[f"aACC{q_}" for q_ in range(4)], writes=[f"araw{c}"])
                        sc.op("dve", lambda e, c=c: e.reciprocal(out=rr[:, c * 4:c * 4 + 4], in_=raw[c][:, :, 256]), reads=[f"araw{c}"], writes=[("rr", c)])
                        if c == 1:
                            sc.op("dve", lambda e: e.tensor_scalar(out=rr[:, 12:16], in0=rr[:, 4:8], scalar1=nlam[:, 0:1], scalar2=None, op0=ALU.mult),
                                  reads=[("rr", 1), "nlam"], writes=[("rr", 2)])
                        for qs in range(4):
                            if c == 0:
                                sc.op("dve", lambda e, qs=qs: e.tensor_scalar(out=A0[:, qs, :], in0=raw[0][:, qs, 0:256], scalar1=rr[:, qs:qs + 1], scalar2=None, op0=ALU.mult),
                                      reads=["araw0", ("rr", 0)], writes=[("A0", qs)])
                            else:
                                sc.op("dve", lambda e, qs=qs: e.scalar_tensor_tensor(out=o2[:, qs, :], in0=raw[1][:, qs, 0:256], scalar=rr[:, 12 + qs:13 + qs], in1=A0[:, qs, :], op0=ALU.mult, op1=ALU.add),
                                      reads=["araw1", ("rr", 2), ("A0", qs)], writes=[("o2", qs)])
                    for qs in range(4):
                        sc.op("act", lambda e, qs=qs: e.activation(out=junk[:, qs, :], in_=o2[:, qs, :], func=AF.Square, accum_out=stt[:, qs:qs + 1]), reads=[("o2", qs)], writes=[f"ajunk{qs}", ("ast", qs)])
                    sc.op("act", lambda e: e.activation(out=stt[:, 4:8], in_=stt[:, 0:4], func=AF.Ln, scale=1.0 / 256, bias=self.eps_t[:, 1:2]), reads=[("ast", q_) for q_ in range(4)], writes=["ast"])
                    sc.op("act", lambda e: e.activation(out=stt[:, 8:12], in_=stt[:, 4:8], func=AF.Exp, scale=-0.5), reads=["ast"], writes=["ast"])
                    for qs in range(4):
                        sc.op("pool", lambda e, qs=qs, qi=qi: e.tensor_tensor(out=g32[:, qs, :], in0=gbt[qi][:, qs, :], in1=slnb[:], op=ALU.mult), reads=[f"agb{qi}", "slnb"], writes=[("ag32", qs)])
                    for qs in range(4):
                        sc.op("dve", lambda e, qs=qs, qi=qi: e.scalar_tensor_tensor(out=ybs[qi][:, qs, :], in0=o2[:, qs, :], scalar=stt[:, 8 + qs:9 + qs], in1=g32[:, qs, :], op0=ALU.mult, op1=ALU.mult),
                              reads=[("o2", qs), "ast", ("ag32", qs)], writes=[(f"aybs{qi}", qs)])
                    sc.dma("pool", lambda e, qi=qi, tb=tb, h=h: e.dma_start(out=self.yb[tb:tb + 512, h * 256:(h + 1) * 256].rearrange("(a p) c -> p a c", p=128), in_=ybs[qi][:]),
                           f"aybs{qi}", reads=[(f"aybs{qi}", q_) for q_ in range(4)], writes=[("yb", tb, h)])
            sc.flush_wait_all()
            sc.emit(block)

    def phase_wprep(self, sc):
        nc = self.nc
        with ExitStack() as es, nc.Block() as block:
            wt = self.sb(es, "wprep", [128, KC, 2 * FFN], BF16)
            src = self.w_fi.rearrange("(k p) c -> p k c", p=128)
            for kc in range(KC):
                sc.dma("pool", lambda e, kc=kc: e.dma_start(out=wt[:, kc, :], in_=src[:, kc, :]), f"wp{kc % 4}", writes=[("wprep", kc)])
            allk = [("wprep", kc) for kc in range(KC)]
            for f in range(FC):
                for g in range(2):
                    c0 = g * FFN + f * 128
                    sc.dma("sp", lambda e, f=f, g=g, c0=c0: e.dma_start(out=self.wfib[f, :, g, :, :], in_=wt[:, :, c0:c0 + 128]), f"wps{(2 * f + g) % 4}",
                           reads=allk, writes=[("wfib", f, g)])
            sc.flush_wait_all()
            sc.emit(block)

    def phase_ffn(self, sc):
        nc, S, T = self.nc, self.S, self.T
        NGT = T // 512
        with ExitStack() as es, nc.Block() as block:
            wo = self.sb(es, "f_wo", [128, KC, D], BF16)
            wfo = self.sb(es, "f_wfo", [128, FC, D], BF16)
            wfi = [self.sb(es, f"f_wfi{i}", [128, 2, KC, 128], BF16) for i in range(4)]
            ident = self.sb(es, "f_ident", [128, 128], BF16)
            identf = self.sb(es, "f_identf", [128, 128], F32)
            nwfb = self.sb(es, "f_nwfb", [128, D], F32)
            nwfin = self.sb(es, "f_nwfin", [128, D], F32)
            la = [self.sb(es, f"f_la{i}", [128, D], F32) for i in range(2)]
            lb = [self.sb(es, f"f_lb{i}", [128, D], F32) for i in range(2)]
            lx = [self.sb(es, f"f_lx{i}", [128, D], F32) for i in range(2)]
            mb = [self.sb(es, f"f_mb{i}", [128, D], BF16) for i in range(2)]
            mT = self.sb(es, "f_mT", [128, KC, 512], BF16)
            hT = self.sb(es, "f_hT", [128, KC, 512], BF16)
            hsb = self.sb(es, "f_h", [128, 4, D], F32)
            aT = self.sb(es, "f_aT", [128, FC, 512], BF16)
            sg = [self.sb(es, f"f_sg{i}", [128, 512], F32) for i in range(2)]
            ot = [self.sb(es, f"f_ot{i}", [128, D], F32) for i in range(2)]
            stt = [self.sb(es, f"f_st{i}", [128, 4], F32) for i in range(2)]
            psT = [self.ps(es, f"f_psT{i}", [128, 1024], BF16) for i in range(2)]
            pb = [self.ps(es, f"f_pb{i}", [128, 512]) for i in range(6)]

            sc.dma("pool", lambda e: e.dma_start(out=wo[:], in_=self.w_out.rearrange("(k p) c -> p k c", p=128)), "pw_a", writes=["wo"])
            for f0 in range(0, FC, 2):
                sc.dma("pool", lambda e, f0=f0: e.dma_start(out=wfo[:, f0:f0 + 2, :], in_=self.w_fo[f0 * 128:(f0 + 2) * 128, :].rearrange("(k p) c -> p k c", p=128)),
                       f"pw_b{(f0 // 2) % 2}", writes=[("wfo", f0)])
            wfo_all = [("wfo", f0) for f0 in range(0, FC, 2)]
            sc.dma("sp", lambda e: e.dma_start(out=identf[:], in_=self.cst[:, 0:128]), "c_c", writes=["identf"])
            sc.op("dve", lambda e: e.tensor_copy(out=ident[:], in_=identf[:]), reads=["identf"], writes=["ident"])
            sc.dma("sp", lambda e: e.dma_start(out=nwfb[:], in_=self.nw_ffnr.broadcast_to([128, D])), "c_d", writes=["nwfb"])
            sc.dma("sp", lambda e: e.dma_start(out=nwfin[:], in_=self.nw_fin.broadcast_to([128, D])), "c_e", writes=["nwfin"])

            xf = self.x.rearrange("n s d -> (n s) d")
            yf = self.y.rearrange("n s d -> (n s) d")
            st8 = dict(pbi=0, li=0, wi=0, sgi=0, oti=0, pti=0)

            def next_pb():
                j = st8["pbi"] % 6
                st8["pbi"] += 1
                return j

            def norm_rows(src_ap, src_keys, i):
                sc.op("act", lambda e: e.activation(out=mb[i][:], in_=src_ap, func=AF.Square, accum_out=stt[i][:, 0:1]), reads=src_keys, writes=[f"mb{i}", f"fst{i}"])
                sc.op("act", lambda e: e.activation(out=stt[i][:, 1:2], in_=stt[i][:, 0:1], func=AF.Ln, scale=1.0 / D, bias=self.eps_t[:, 0:1]), reads=[f"fst{i}"], writes=[f"fst{i}"])
                sc.op("act", lambda e: e.activation(out=stt[i][:, 2:3], in_=stt[i][:, 1:2], func=AF.Exp, scale=-0.5), reads=[f"fst{i}"], writes=[f"fst{i}"])

            def transpose_to(dstT, dkey, i, tt_, on_act):
                pt = st8["pti"] % 2
                st8["pti"] += 1
                for kc in range(KC):
                    sc.op("pe", lambda e, kc=kc: e.transpose(psT[pt][:, kc * 128:(kc + 1) * 128], mb[i][:, kc * 128:(kc + 1) * 128], ident[:]),
                          reads=[f"mb{i}", "ident"], writes=[f"f_psT{pt}"])
                src = psT[pt][:].rearrange("p (k t) -> p k t", k=KC)
                if on_act:
                    sc.op("act", lambda e: e.activation(out=dstT[:, :, tt_ * 128:(tt_ + 1) * 128], in_=src, func=AF.Copy), reads=[f"f_psT{pt}"], writes=[(dkey, tt_)])
                else:
                    sc.op("dve", lambda e: e.tensor_copy(out=dstT[:, :, tt_ * 128:(tt_ + 1) * 128], in_=src), reads=[f"f_psT{pt}"], writes=[(dkey, tt_)])

            for g in range(NGT):
                tb = g * 512

                def s1a(tt_):
                    i = tt_ % 2
                    r0 = tb + tt_ * 128
                    sc.dma("sp", lambda e: e.dma_start(out=la[i][:], in_=self.ya[r0:r0 + 128, :]), f"f_la{i}", writes=[f"la{i}"])
                    sc.dma("sp", lambda e: e.dma_start(out=lb[i][:], in_=self.yb[r0:r0 + 128, :]), f"f_lb{i}", writes=[f"lb{i}"])
                    sc.op("dve", lambda e: e.tensor_tensor(out=mb[i][:], in0=la[i][:], in1=lb[i][:], op=ALU.add), reads=[f"la{i}", f"lb{i}"], writes=[f"mb{i}"])

                s1a(0)
                for tt_ in range(4):
                    if tt_ + 1 < 4:
                        s1a(tt_ + 1)
                    transpose_to(mT, "mT", tt_ % 2, tt_, on_act=True)

                def s2a(tt_):
                    i = tt_ % 2
                    r0 = tb + tt_ * 128
                    sc.dma("sp", lambda e: e.dma_start(out=lx[i][:], in_=xf[r0:r0 + 128, :]), f"f_lx{i}", writes=[f"lx{i}"])
                    for nh in range(2):
                        j = next_pb()
                        for kc in range(KC):
                            sc.op("pe", lambda e, kc=kc, j=j, nh=nh: e.matmul(pb[j][:], lhsT=mT[:, kc, tt_ * 128:(tt_ + 1) * 128], rhs=wo[:, kc, nh * 512:(nh + 1) * 512],
                                                                          start=(kc == 0), stop=(kc == KC - 1)),
                                  reads=[("mT", tt_), "wo"], writes=[f"f_pb{j}"])
                        sc.op("dve", lambda e, j=j, nh=nh: e.tensor_tensor(out=hsb[:, tt_, nh * 512:(nh + 1) * 512], in0=pb[j][:], in1=lx[i][:, nh * 512:(nh + 1) * 512], op=ALU.add),
                              reads=[f"f_pb{j}", f"lx{i}"], writes=[("h", tt_)])
                    norm_rows(hsb[:, tt_, :], [("h", tt_)], i)
                    sc.op("dve", lambda e: e.scalar_tensor_tensor(out=mb[i][:], in0=hsb[:, tt_, :], scalar=stt[i][:, 2:3], in1=nwfb[:], op0=ALU.mult, op1=ALU.mult),
                          reads=[("h", tt_), f"fst{i}", "nwfb"], writes=[f"mb{i}"])

                s2a(0)
                for tt_ in range(4):
                    if tt_ + 1 < 4:
                        s2a(tt_ + 1)
                    transpose_to(hT, "hT", tt_ % 2, tt_, on_act=False)
                hT_all = [("hT", k) for k in range(4)]
                for f in range(FC):
                    wi = st8["wi"] % 4
                    st8["wi"] += 1
                    sc.dma("sp", lambda e, wi=wi, f=f: e.dma_start(out=wfi[wi][:], in_=self.wfib[f, :, :, :, :]), f"f_wfi{wi}", reads=[("wfib", f, 0), ("wfib", f, 1)], writes=[f"wfi{wi}"])
                    jg, ju = next_pb(), next_pb()
                    for gi, j in ((0, jg), (1, ju)):
                        for kc in range(KC):
                            sc.op("pe", lambda e, kc=kc, j=j, gi=gi, wi=wi: e.matmul(pb[j][:], lhsT=wfi[wi][:, gi, kc, :], rhs=hT[:, kc, :], start=(kc == 0), stop=(kc == KC - 1)),
                                  reads=[f"wfi{wi}"] + hT_all, writes=[f"f_pb{j}"])
                    si = st8["sgi"] % 2
                    st8["sgi"] += 1
                    sc.op("act", lambda e, jg=jg, si=si: e.activation(out=sg[si][:], in_=pb[jg][:], func=AF.Silu), reads=[f"f_pb{jg}"], writes=[f"sg{si}"])
                    sc.op("dve", lambda e, ju=ju, si=si, f=f: e.tensor_tensor(out=aT[:, f, :], in0=pb[ju][:], in1=sg[si][:], op=ALU.mult),
                          reads=[f"f_pb{ju}", f"sg{si}"], writes=[("aT", f)])
                aT_all = [("aT", f) for f in range(FC)]
                for tt_ in range(4):
                    oi = st8["oti"] % 2
                    st8["oti"] += 1
                    r0 = tb + tt_ * 128
                    for nh in range(2):
                        j = next_pb()
                        for f in range(FC):
                            sc.op("pe", lambda e, f=f, j=j, nh=nh, tt_=tt_: e.matmul(pb[j][:], lhsT=aT[:, f, tt_ * 128:(tt_ + 1) * 128], rhs=wfo[:, f, nh * 512:(nh + 1) * 512],
                                                                                  start=(f == 0), stop=(f == FC - 1)),
                                  reads=aT_all + wfo_all, writes=[f"f_pb{j}"])
                        sc.op("dve", lambda e, j=j, nh=nh, tt_=tt_: e.tensor_tensor(out=hsb[:, tt_, nh * 512:(nh + 1) * 512], in0=pb[j][:], in1=hsb[:, tt_, nh * 512:(nh + 1) * 512], op=ALU.add),
                              reads=[f"f_pb{j}", ("h", tt_)], writes=[("h", tt_)])
                    norm_rows(hsb[:, tt_, :], [("h", tt_)], oi)
                    sc.op("dve", lambda e, oi=oi, tt_=tt_: e.scalar_tensor_tensor(out=ot[oi][:], in0=hsb[:, tt_, :], scalar=stt[oi][:, 2:3], in1=nwfin[:], op0=ALU.mult, op1=ALU.mult),
                          reads=[("h", tt_), f"fst{oi}", "nwfin"], writes=[f"ot{oi}"])
                    sc.dma("pool", lambda e, oi=oi, r0=r0: e.dma_start(out=yf[r0:r0 + 128, :], in_=ot[oi][:]), f"f_ot{oi}", reads=[f"ot{oi}"], writes=[("y", r0)])
            sc.flush_wait_all()
            sc.emit(block)

    def build(self):
        nc = self.nc
        self.declare()
        with ExitStack() as es:
            sc = Sched(nc, es)
            self.eps_t = self.sb(es, "eps_t", [128, 4], F32)
            with nc.Block() as block:
                sc.op("dve", lambda e: e.memset(self.eps_t[:, 0:1], NORM_EPS), writes=["eps"])
                sc.op("dve", lambda e: e.memset(self.eps_t[:, 1:2], SUBLN_EPS), writes=["eps"])
                sc.op("dve", lambda e: e.memset(self.eps_t[:, 2:3], 1.0), writes=["eps"])
                sc.op("dve", lambda e: e.memset(self.eps_t[:, 3:4], 0.0), writes=["eps"])
                sc.emit(block)
            for s in range(self.NSEQ):
                if "p1" in self.phases:
                    self.phase1(sc, s)
                if "gla" in self.phases:
                    self.phase_gla(sc, s, 1)
                    self.phase_gla(sc, s, 0)
                if "attn" in self.phases:
                    self.phase_attn(sc, s, wprep=("ffn" in self.phases and s == self.NSEQ - 1))
            if "ffn" in self.phases:
                if "attn" not in self.phases:
                    self.phase_wprep(sc)
                self.phase_ffn(sc)
        return nc


def host_consts(S):
    inv_freq = 1.0 / (10000.0 ** (np.arange(0, DIFF_HD, 2, dtype=np.float32) / DIFF_HD))
    pos = np.arange(S, dtype=np.float32)
    fr = pos[None, :] * np.concatenate([inv_freq, inv_freq])[:, None].astype(np.float32)
    cosT = np.cos(fr).astype(np.float32)
    sgn = np.where(np.arange(128) < 64, -1.0, 1.0).astype(np.float32)[:, None]
    sinT = (np.sin(fr) * sgn).astype(np.float32)
    p = np.arange(128)
    same = (p[:, None] // 64) == (p[None, :] // 64)
    sI, tI = p[:, None], p[None, :]
    g = -1.0 / 16.0
    ident = np.eye(128, dtype=np.float32)
    uinc_f = (same & (sI <= tI)) * g
    ustr_f = (same & (sI > tI)) * g
    uinc_b = (same & (sI >= tI)) * g
    ustr_b = (same & (sI < tI)) * g
    cst = np.concatenate([ident, uinc_f, ustr_f, uinc_b, ustr_b, np.zeros((128, 128))], axis=1).astype(np.float32)
    m_f = (same & (sI <= tI)).astype(np.float32)
    m_b = (same & (sI >= tI)).astype(np.float32)
    msk = np.concatenate([m_f, m_b], axis=1).astype(np.float32)
    return cosT, sinT, cst, msk


def host_inputs(inp, S):
    c = np.ascontiguousarray
    w_in = np.asarray(inp["w_in"])[0]
    w1 = c(w_in[:, _w1_columns()])
    cosT, sinT, cst, msk = host_consts(S)
    wgk = np.concatenate([np.asarray(inp["w_gk2"])[0], np.asarray(inp["b_gk"])[0][:, None, :]], axis=1)
    shared = dict(
        w1=w1,
        nw_mixr=c(np.asarray(inp["norm_mix_w"])[0].reshape(1, D)),
        nw_ffnr=c(np.asarray(inp["norm_ffn_w"])[0].reshape(1, D)),
        nw_fin=c(np.asarray(inp["norm_final_w"]).reshape(1, D)),
        cosT=cosT, sinT=sinT, cst=cst, msk=msk,
        wgk=c(wgk.astype(np.float32)),
        gnw=c(np.tile(np.asarray(inp["gla_norm_w"])[0], 4).reshape(1, 1024)),
        sln=c(np.asarray(inp["diff_subln_w"])[0].reshape(1, 256)),
        lam4=c(np.concatenate([np.asarray(inp[k])[0] for k in ("lambda_q1", "lambda_k1", "lambda_q2", "lambda_k2")]).reshape(1, 512)),
        w_out=c(np.asarray(inp["w_out"])[0]),
        w_fi=c(np.asarray(inp["w_ffn_in"])[0]),
        w_fo=c(np.asarray(inp["w_ffn_out"])[0]),
    )
    return {k: np.asarray(v, dtype=np.float32) for k, v in shared.items()}


def kernel(**inputs):
    x = np.asarray(inputs["x"], dtype=np.float32)
    B, S, _ = x.shape
    nseq = B // N_CORES
    b = Builder(S, nseq)
    nc = b.build()
    shared = host_inputs(inputs, S)
    in_maps = [dict(shared, x=np.ascontiguousarray(x[i * nseq:(i + 1) * nseq])) for i in range(N_CORES)]
    res = run_bass_kernel_spmd(nc, in_maps, core_ids=list(range(N_CORES)))
    return np.concatenate([np.asarray(r["y"]) for r in res.results], axis=0).astype(np.float32)
```

```python
from contextlib import ExitStack
import math
import numpy as np
import ml_dtypes
import concourse.bass as bass
import concourse.mybir as mybir
from concourse.bass_utils import run_bass_kernel_spmd

F32 = mybir.dt.float32
BF16 = mybir.dt.bfloat16
AF = mybir.ActivationFunctionType
ALU = mybir.AluOpType

D = 1024
KC = D // 128
GLA_H, GLA_DK, GLA_DV, GLA_RANK = 4, 128, 256, 16
DIFF_H, DIFF_HD = 4, 128
FFN = 2816
FC = FFN // 128
NORM_EPS = 1e-6
SUBLN_EPS = 1e-5
LAMBDA_INIT = 0.8 - 0.6 * math.exp(-0.3 * 0)
N_CORES = 8

OFF_GQ, OFF_GK, OFF_GV, OFF_GR, OFF_GLR = 0, 512, 1024, 2048, 3072
OFF_DQ, OFF_DK, OFF_DV, OFF_GA, OFF_GB = 3104, 4128, 5152, 6176, 7200


def _w1_columns():
    cols = []
    cols += list(range(OFF_GQ, OFF_GQ + 512))
    cols += list(range(OFF_GK, OFF_GK + 512))
    cols += list(range(OFF_DQ, OFF_DQ + 1024))
    cols += list(range(OFF_DK, OFF_DK + 1024))
    cols += list(range(OFF_GLR, OFF_GLR + 32))
    cols += list(range(OFF_GK, OFF_GK + 512))
    for off in (OFF_GV, OFF_GR, OFF_DV, OFF_GA, OFF_GB):
        cols += list(range(off, off + 1024))
    return np.array(cols, dtype=np.int64)


W1_COLS = 512 * 6 + 32 + 512 * 11
FM_BLOCKS = [("gq", 0), ("gk", 512)] + [("dq", 1024 + 512 * i) for i in range(2)] + \
            [("dk", 2048 + 512 * i) for i in range(2)]
GLR_OFF = 3072
TM_OFF = 3104
TM_BLOCKS = [("gk", 0)] + [(n, j) for n in ("gv", "gr", "dv", "ga", "gb") for j in range(2)]


class Sched:
    CE = ("pe", "act", "dve", "pool")
    ALL = ("pe", "act", "dve", "pool", "sp")

    def __init__(self, nc, es):
        self.nc = nc
        self.prog = {e: es.enter_context(nc.semaphore("prog_" + e)) for e in self.CE}
        self.cum = {e: 0 for e in self.CE}
        self.dsem = {}
        self.es = es
        self.res_w = {}
        self.res_r = {}
        self.ops = {e: [] for e in self.ALL}
        self.waited = {e: {} for e in self.ALL}
        self.sigval = {}
        self.uid = 0

    def _slot(self, slot):
        if slot not in self.dsem:
            self.dsem[slot] = [self.es.enter_context(self.nc.semaphore("d_" + slot)), 0]
        return self.dsem[slot]

    def _collect(self, eng, reads, writes, is_dma):
        deps = []
        for k in reads:
            for ev in self.res_w.get(k, ()):
                deps.append((ev, True))
        for k in writes:
            for ev in self.res_w.get(k, ()):
                deps.append((ev, False))
            for ev in self.res_r.get(k, ()):
                deps.append((ev, False))
        out = []
        for ev, raw in deps:
            if ev[0] == "eng" and ev[1] == eng and not is_dma:
                if eng == "pe":
                    continue
            out.append(ev)
        return out

    def _update(self, ev, reads, writes):
        for k in writes:
            self.res_w[k] = [ev]
            self.res_r[k] = []
        for k in reads:
            self.res_r.setdefault(k, []).append(ev)

    def op(self, eng, fn, reads=(), writes=()):
        self.uid += 1
        ev = ("eng", eng, self.uid)
        deps = self._collect(eng, reads, writes, False)
        self.ops[eng].append(dict(fn=fn, deps=deps, ev=ev, dma=None))
        self._update(ev, reads, writes)
        return ev

    def dma(self, queue, fn, slot, reads=(), writes=()):
        s = self._slot(slot)
        deps = self._collect(queue, reads, writes, True)
        if s[1] > 0:
            deps.append(("dma", slot, s[1]))
        s[1] += 1
        ev = ("dma", slot, s[1])
        self.ops[queue].append(dict(fn=fn, deps=deps, ev=ev, dma=slot))
        self._update(ev, reads, writes)
        return ev

    def flush_wait_all(self):
        deps = [("dma", slot, s[1]) for slot, s in self.dsem.items() if s[1] > 0]
        self.ops["sp"].append(dict(fn=None, deps=deps, ev=None, dma=None))

    def emit(self, block):
        done = getattr(self, "done_uid", 0)
        for e in self.ALL:
            for o in self.ops[e]:
                o["deps"] = [d for d in o["deps"] if not (d[0] == "eng" and d[2] <= done)]
        self.done_uid = self.uid
        need = set()
        for e in self.ALL:
            for o in self.ops[e]:
                for d in o["deps"]:
                    if d[0] == "eng":
                        need.add(d[2])
        for e in self.CE:
            c = self.cum[e]
            for o in self.ops[e]:
                if o["dma"] is None and o["ev"] is not None and o["ev"][2] in need:
                    c += 1
                    self.sigval[o["ev"][2]] = c
            self.cum[e] = c
        starters = dict(pe=block.tensor, act=block.scalar, dve=block.vector, pool=block.gpsimd, sp=block.sync)
        for e in self.ALL:
            ops = self.ops[e]
            if not ops:
                continue
            waited = self.waited[e]

            def body(eng, ops=ops, waited=waited, e=e):
                for o in ops:
                    for d in o["deps"]:
                        if d[0] == "eng":
                            key, sem, val = d[1], self.prog[d[1]], self.sigval[d[2]]
                        else:
                            key, sem, val = "d_" + d[1], self.dsem[d[1]][0], 16 * d[2]
                        if waited.get(key, 0) >= val:
                            continue
                        eng.wait_ge(sem, val)
                        waited[key] = val
                    if o["fn"] is None:
                        continue
                    ins = o["fn"](eng)
                    if o["dma"] is not None:
                        ins.then_inc(self.dsem[o["dma"]][0], 16)
                    elif o["ev"][2] in self.sigval:
                        ins.then_inc(self.prog[e], 1)
            starters[e](body)
        self.ops = {e: [] for e in self.ALL}


class Builder:
    def __init__(self, S, NSEQ, debug=False, phases=("p1", "gla", "attn", "ffn"), scratch_in=False, cut=99):
        self.S, self.NSEQ, self.debug, self.phases = S, NSEQ, debug, phases
        self.scratch_in, self.cut = scratch_in, cut
        self.T = S * NSEQ
        self.NT = S // 128
        self.NG = S // 512
        self.nc = bass.Bass("TRN2", target_bir_lowering=False)

    def dram(self, name, shape, dt, kind):
        return self.nc.dram_tensor(name, list(shape), dt, kind=kind).ap()

    def declare(self):
        S, T, NSEQ = self.S, self.T, self.NSEQ
        I = "ExternalInput"
        self.x = self.dram("x", [NSEQ, S, D], F32, I)
        self.w1 = self.dram("w1", [D, W1_COLS], F32, I)
        self.nw_mixr = self.dram("nw_mixr", [1, D], F32, I)
        self.nw_ffnr = self.dram("nw_ffnr", [1, D], F32, I)
        self.nw_fin = self.dram("nw_fin", [1, D], F32, I)
        self.cosT = self.dram("cosT", [128, S], F32, I)
        self.sinT = self.dram("sinT", [128, S], F32, I)
        self.wgk = self.dram("wgk", [2, 17, 512], F32, I)
        self.gnw = self.dram("gnw", [1, 1024], F32, I)
        self.sln = self.dram("sln", [1, 256], F32, I)
        self.lam4 = self.dram("lam4", [1, 512], F32, I)
        self.w_out = self.dram("w_out", [D, D], F32, I)
        self.w_fi = self.dram("w_fi", [D, 2 * FFN], F32, I)
        self.w_fo = self.dram("w_fo", [FFN, D], F32, I)
        self.cst = self.dram("cst", [128, 6 * 128], F32, I)
        self.msk = self.dram("msk", [128, 2 * 128], F32, I)
        self.y = self.dram("y", [NSEQ, S, D], F32, "ExternalOutput")
        K = "ExternalOutput" if self.debug else "Internal"
        K1 = K
        if self.scratch_in:
            K = "ExternalInput"
        self.gqT = self.dram("gqT", [4, 128, T], BF16, K)
        self.gkT = self.dram("gkT", [4, 128, T], BF16, K)
        self.dqT = self.dram("dqT", [8, 128, T], BF16, K)
        self.dkT = self.dram("dkT", [8, 128, T], BF16, K)
        self.glrT = self.dram("glrT", [32, T], BF16, K)
        self.tm = {n: self.dram("tm_" + n, [T, (512 if n == "gk" else 1024)], BF16, K)
                   for n in ("gk", "gv", "gr", "dv", "ga", "gb")}
        K = K1
        self.ob = self.dram("ob", [T, 1024], F32, K)
        self.ya = self.dram("ya", [T, 1024], F32, K)
        self.yb = self.dram("yb", [T, 1024], F32, K)
        self.wfib = self.dram("wfib", [FC, 128, 2, KC, 128], BF16, "Internal")

    def _uniq(self, name):
        self._n = getattr(self, "_n", 0) + 1
        return f"{name}_{self._n}"

    def sb(self, es, name, shape, dt):
        return es.enter_context(self.nc.sbuf_tensor(self._uniq(name), list(shape), dt))

    def ps(self, es, name, shape, dt=F32):
        return es.enter_context(self.nc.psum_tensor(self._uniq(name), list(shape), dt))

    def phase1(self, sc, s):
        nc, S, NT, NG = self.nc, self.S, self.NT, self.NG
        t0 = s * S
        with ExitStack() as es, nc.Block() as block:
            uT = self.sb(es, "uT", [128, KC, S], BF16)
            cosT = self.sb(es, "cosT_sb", [128, S], F32)
            sinT = self.sb(es, "sinT_sb", [128, S], F32)
            nwb = self.sb(es, "nwb", [128, D], F32)
            ident = self.sb(es, "ident", [128, 128], BF16)
            identf = self.sb(es, "identf", [128, 128], F32)
            xt = [self.sb(es, f"xt{i}", [128, D], F32) for i in range(4)]
            xn = [self.sb(es, f"xn{i}", [128, D], BF16) for i in range(2)]
            st = [self.sb(es, f"st{i}", [128, 4], F32) for i in range(2)]
            wb = [self.sb(es, f"wb{i}", [128, KC, 512], BF16) for i in range(2)]
            so = [self.sb(es, f"so{i}", [128, 512], BF16) for i in range(4)]
            sof = [self.sb(es, f"sof{i}", [32, 512], BF16) for i in range(2)]
            r1 = [self.sb(es, f"r1_{i}", [128, 512], F32) for i in range(2)]
            r2 = [self.sb(es, f"r2_{i}", [128, 512], F32) for i in range(2)]
            psT = [self.ps(es, f"psT{i}", [128, 1024], BF16) for i in range(2)]
            pb = [self.ps(es, f"pb{i}", [128, 512]) for i in range(6)]
            sc.dma("sp", lambda e: e.dma_start(out=nwb[:], in_=self.nw_mixr.broadcast_to([128, D])), "c_nwm", writes=["nwb"])
            sc.dma("sp", lambda e: e.dma_start(out=identf[:], in_=self.cst[:, 0:128]), "c_id", writes=["identf"])
            sc.op("dve", lambda e: e.tensor_copy(out=ident[:], in_=identf[:]), reads=["identf"], writes=["ident"])

            def load_x(tt):
                j = tt % 4
                sc.dma("sp", lambda e: e.dma_start(out=xt[j][:], in_=self.x[s, tt * 128:(tt + 1) * 128, :]), f"xt{j}", writes=[f"xt{j}"])

            def stage_a(tt):
                i = tt % 2
                j = tt % 4
                if tt + 3 < NT:
                    load_x(tt + 3)
                sc.op("act", lambda e: e.activation(out=xn[i][:], in_=xt[j][:], func=AF.Square, accum_out=st[i][:, 0:1]),
                      reads=[f"xt{j}"], writes=[f"xn{i}", f"st{i}"])
                sc.op("act", lambda e: e.activation(out=st[i][:, 1:2], in_=st[i][:, 0:1], func=AF.Ln, scale=1.0 / D, bias=self.eps_t[:, 0:1]),
                      reads=[f"st{i}"], writes=[f"st{i}"])
                sc.op("act", lambda e: e.activation(out=st[i][:, 2:3], in_=st[i][:, 1:2], func=AF.Exp, scale=-0.5),
                      reads=[f"st{i}"], writes=[f"st{i}"])
                sc.op("dve", lambda e: e.scalar_tensor_tensor(out=xn[i][:], in0=xt[j][:], scalar=st[i][:, 2:3], in1=nwb[:], op0=ALU.mult, op1=ALU.mult),
                      reads=[f"xt{j}", f"st{i}", "nwb"], writes=[f"xn{i}"])

            def stage_b(tt):
                i = tt % 2
                for kc in range(KC):
                    sc.op("pe", lambda e, kc=kc: e.transpose(psT[i][:, kc * 128:(kc + 1) * 128], xn[i][:, kc * 128:(kc + 1) * 128], ident[:]),
                          reads=[f"xn{i}", "ident"], writes=[f"psT{i}"])
                sc.op("dve", lambda e: e.tensor_copy(out=uT[:, :, tt * 128:(tt + 1) * 128], in_=psT[i][:].rearrange("p (k t) -> p k t", k=KC)),
                      reads=[f"psT{i}"], writes=[("uT", tt)])

            for tt in range(min(3, NT)):
                load_x(tt)
            stage_a(0)
            for tt in range(NT):
                if tt + 1 < NT:
                    stage_a(tt + 1)
                stage_b(tt)
            for q_ in range((S + 1023) // 1024):
                c0_, c1_ = q_ * 1024, min(S, (q_ + 1) * 1024)
                sc.dma("sp", lambda e, c0_=c0_, c1_=c1_: e.dma_start(out=cosT[:, c0_:c1_], in_=self.cosT[:, c0_:c1_]), f"c_cos{q_ % 2}", writes=[("cosT", q_)])
                sc.dma("sp", lambda e, c0_=c0_, c1_=c1_: e.dma_start(out=sinT[:, c0_:c1_], in_=self.sinT[:, c0_:c1_]), f"c_sin{q_ % 2}", writes=[("sinT", q_)])
            uT_all = [("uT", tt) for tt in range(NT)]

            state = dict(wbi=0, pbi=0, soi=0, ri=0, sofi=0)

            def load_w(col0, ncols):
                i = state["wbi"] % 2
                state["wbi"] += 1
                src = self.w1.rearrange("(k p) c -> p k c", p=128)[:, :, col0:col0 + ncols]
                sc.dma("pool", lambda e, i=i: e.dma_start(out=wb[i][:, :, 0:ncols], in_=src), f"wb{i}", writes=[f"wb{i}"])
                return i

            def next_pb():
                j = state["pbi"] % 6
                state["pbi"] += 1
                return j

            def store(dst_ap, src_tile_key, src_ap, slotname):
                sc.dma("sp", lambda e: e.dma_start(out=dst_ap, in_=src_ap), slotname, reads=[src_tile_key])

            def fm_group(wi, c, tg, j, M=128):
                for kc in range(KC):
                    sc.op("pe", lambda e, kc=kc: e.matmul(pb[j][0:M, :], lhsT=wb[wi][:, kc, c * 128:c * 128 + M],
                                                            rhs=uT[:, kc, tg * 512:(tg + 1) * 512], start=(kc == 0), stop=(kc == KC - 1)),
                          reads=[f"wb{wi}"] + uT_all[tg * 4:(tg + 1) * 4], writes=[f"pb{j}"])

            for name, col0 in FM_BLOCKS:
                wi = load_w(col0, 512)
                bidx = (col0 - {"gq": 0, "gk": 512, "dq": 1024, "dk": 2048}[name]) // 512
                if name in ("gq", "gk"):
                    dst = self.gqT if name == "gq" else self.gkT
                    for c in range(4):
                        for tg in range(NG):
                            j = next_pb()
                            fm_group(wi, c, tg, j)
                            k = state["soi"] % 4
                            state["soi"] += 1
                            if (c + tg) % 2 == 0:
                                sc.op("act", lambda e, j=j, k=k: e.activation(out=so[k][:], in_=pb[j][:], func=AF.Copy),
                                      reads=[f"pb{j}"], writes=[f"so{k}"])
                            else:
                                sc.op("dve", lambda e, j=j, k=k: e.tensor_copy(out=so[k][:], in_=pb[j][:]),
                                      reads=[f"pb{j}"], writes=[f"so{k}"])
                            store(dst[c, :, t0 + tg * 512:t0 + (tg + 1) * 512], f"so{k}", so[k][:], f"so{k}")
                else:
                    dst = self.dqT if name == "dq" else self.dkT
                    for c in range(4):
                        hc = bidx * 4 + c
                        for tg in range(NG):
                            ja = next_pb()
                            fm_group(wi, c, tg, ja)
                            ri = state["ri"] % 2
                            state["ri"] += 1
                            k = state["soi"] % 4
                            state["soi"] += 1
                            tsl = slice(tg * 512, (tg + 1) * 512)
                            sc.op("dve", lambda e, ja=ja, ri=ri, tsl=tsl: e.tensor_tensor(out=r1[ri][:], in0=pb[ja][:], in1=cosT[:, tsl], op=ALU.mult),
                                  reads=[f"pb{ja}", ("cosT", tg // 2)], writes=[f"r1_{ri}"])
                            sc.op("dve", lambda e, ja=ja, ri=ri, tsl=tsl: e.tensor_tensor(out=r2[ri][0:64, :], in0=pb[ja][64:128, :], in1=sinT[0:64, tsl], op=ALU.mult),
                                  reads=[f"pb{ja}", ("sinT", tg // 2)], writes=[(f"r2_{ri}", 0)])
                            sc.op("dve", lambda e, ja=ja, ri=ri, tsl=tsl: e.tensor_tensor(out=r2[ri][64:128, :], in0=pb[ja][0:64, :], in1=sinT[64:128, tsl], op=ALU.mult),
                                  reads=[f"pb{ja}", ("sinT", tg // 2)], writes=[(f"r2_{ri}", 1)])
                            sc.op("pool", lambda e, ri=ri, k=k: e.tensor_tensor(out=so[k][:], in0=r1[ri][:], in1=r2[ri][:], op=ALU.add),
                                  reads=[f"r1_{ri}", (f"r2_{ri}", 0), (f"r2_{ri}", 1)], writes=[f"so{k}"])
                            store(dst[hc, :, t0 + tg * 512:t0 + (tg + 1) * 512], f"so{k}", so[k][:], f"so{k}")
            wi = load_w(GLR_OFF, 32)
            for tg in range(NG):
                j = next_pb()
                fm_group(wi, 0, tg, j, M=32)
                k = state["sofi"] % 2
                state["sofi"] += 1
                sc.op("dve", lambda e, j=j, k=k: e.tensor_copy(out=sof[k][:], in_=pb[j][0:32, :]), reads=[f"pb{j}"], writes=[f"sof{k}"])
                store(self.glrT[:, t0 + tg * 512:t0 + (tg + 1) * 512], f"sof{k}", sof[k][:], f"sof{k}")
            for bi, (name, jj) in enumerate(TM_BLOCKS):
                wi = load_w(TM_OFF + bi * 512, 512)
                dst = self.tm[name]
                for tt in range(NT):
                    j = next_pb()
                    for kc in range(KC):
                        sc.op("pe", lambda e, kc=kc, j=j, tt=tt, wi=wi: e.matmul(pb[j][:], lhsT=uT[:, kc, tt * 128:(tt + 1) * 128], rhs=wb[wi][:, kc, :],
                                                                        start=(kc == 0), stop=(kc == KC - 1)),
                              reads=[f"wb{wi}", ("uT", tt)], writes=[f"pb{j}"])
                    k = state["soi"] % 4
                    state["soi"] += 1
                    if name == "gr":
                        sc.op("act", lambda e, j=j, k=k: e.activation(out=so[k][:], in_=pb[j][:], func=AF.Silu), reads=[f"pb{j}"], writes=[f"so{k}"])
                    elif name in ("ga", "gb"):
                        sc.op("act", lambda e, j=j, k=k: e.activation(out=so[k][:], in_=pb[j][:], func=AF.Sigmoid), reads=[f"pb{j}"], writes=[f"so{k}"])
                    elif tt % 2 == 0:
                        sc.op("act", lambda e, j=j, k=k: e.activation(out=so[k][:], in_=pb[j][:], func=AF.Copy), reads=[f"pb{j}"], writes=[f"so{k}"])
                    else:
                        sc.op("dve", lambda e, j=j, k=k: e.tensor_copy(out=so[k][:], in_=pb[j][:]), reads=[f"pb{j}"], writes=[f"so{k}"])
                    store(dst[t0 + tt * 128:t0 + (tt + 1) * 128, jj * 512:(jj + 1) * 512], f"so{k}", so[k][:], f"so{k}")
            sc.flush_wait_all()
            sc.emit(block)


    def phase_gla(self, sc, s, dirn):
        nc, S = self.nc, self.S
        t0 = s * S
        NP, NSS = S // 128, S // 512
        fwd = dirn == 0
        with ExitStack() as es, nc.Block() as block:
            Uinc = self.sb(es, "Uinc", [128, 128], BF16)
            Ustr = self.sb(es, "Ustr", [128, 128], BF16)
            Uf = self.sb(es, "Uf", [128, 256], F32)
            wgkf = self.sb(es, "wgkf", [17, 512], F32)
            mask4 = self.sb(es, "mask4", [128, 4, 128], F32)
            wgk = self.sb(es, "wgk_sb", [17, 512], BF16)
            gnwb = self.sb(es, "gnwb", [128, 1024], F32)
            glra = [self.sb(es, f"glra{i}", [17, 512], BF16) for i in range(2)]
            qT = [self.sb(es, f"gqT{i}", [128, 4, 512], BF16) for i in range(2)]
            kT = [self.sb(es, f"gkT{i}", [128, 4, 512], BF16) for i in range(2)]
            ktm = [self.sb(es, f"gktm{i}", [128, 4, 512], BF16) for i in range(2)]
            vv = [self.sb(es, f"gv{i}", [128, 4, 1024], BF16) for i in range(2)]
            obt = [self.sb(es, f"obt{i}", [128, 1024], F32) for i in range(2)]
            grt = [self.sb(es, f"grt{i}", [128, 1024], BF16) for i in range(2)]
            gat = [self.sb(es, f"gat{i}", [128, 1024], BF16) for i in range(2)]
            e1 = self.sb(es, "g_e1", [128, 512], F32)
            sp = self.sb(es, "g_sp", [128, 512], BF16)
            ebl = self.sb(es, "g_ebl", [128, 512], F32)
            ks = self.sb(es, "g_ks", [128, 512], BF16)
            eb = self.sb(es, "g_eb", [128, 4, 128], F32)
            enb = self.sb(es, "g_enb", [128, 4, 128], F32)
            qt = self.sb(es, "g_qt", [128, 4, 128], BF16)
            kt = self.sb(es, "g_kt", [128, 4, 128], BF16)
            attm = self.sb(es, "g_attm", [128, 4, 128], BF16)
            S32 = [self.sb(es, f"S32_{h}", [128, 256], F32) for h in range(4)]
            Sbf = [self.sb(es, f"Sbf_{h}", [128, 256], BF16) for h in range(4)]
            osb = [self.sb(es, f"osb{i}", [128, 1024], F32) for i in range(2)]
            ysb = [self.sb(es, f"ysb{i}", [128, 1024], F32) for i in range(2)]
            g32 = self.sb(es, "g_g32", [128, 1024], F32)
            junk = self.sb(es, "g_junk", [128, 4, 256], F32)
            stt = [self.sb(es, f"g_st{i}", [128, 12], F32) for i in range(2)]
            bA = self.ps(es, "bA", [128, 512])
            bB = self.ps(es, "bB", [128, 512])
            bO = [self.ps(es, f"bO{h}", [128, 512]) for h in range(4)]
            bKV = [self.ps(es, f"bKV{i}", [128, 512]) for i in range(2)]

            co = 128 * (1 + 2 * dirn)
            sc.dma("sp", lambda e: e.dma_start(out=Uf[:], in_=self.cst[:, co:co + 256]), "c_a", writes=["Uf"])
            sc.op("dve", lambda e: e.tensor_copy(out=Uinc[:], in_=Uf[:, 0:128]), reads=["Uf"], writes=["Uinc"])
            sc.op("dve", lambda e: e.tensor_copy(out=Ustr[:], in_=Uf[:, 128:256]), reads=["Uf"], writes=["Ustr"])
            for h in range(4):
                sc.dma("sp", lambda e, h=h: e.dma_start(out=mask4[:, h, :], in_=self.msk[:, dirn * 128:(dirn + 1) * 128]), "c_c", writes=["mask4"])
            sc.dma("sp", lambda e: e.dma_start(out=wgkf[:], in_=self.wgk[dirn, :, :]), "c_d", writes=["wgkf"])
            sc.op("dve", lambda e: e.tensor_copy(out=wgk[:], in_=wgkf[:]), reads=["wgkf"], writes=["wgk"])
            sc.dma("sp", lambda e: e.dma_start(out=gnwb[:], in_=self.gnw.broadcast_to([128, 1024])), "c_e", writes=["gnwb"])
            for i in range(2):
                sc.op("dve", lambda e, i=i: e.memset(glra[i][:], 1.0), writes=[f"glra{i}"])
            for h in range(4):
                sc.op("dve", lambda e, h=h: e.memset(S32[h][:], 0.0), writes=[f"S32_{h}"])
                sc.op("dve", lambda e, h=h: e.memset(Sbf[h][:], 0.0), writes=[f"Sbf_{h}"])

            if fwd:
                r1, c1, r2, c2 = slice(0, 64), 63, slice(64, 128), 127
            else:
                r1, c1, r2, c2 = slice(64, 128), 64, slice(0, 64), 0
            qscale = float(GLA_DK ** -0.5)

            def load_ss(ss, i):
                tb = t0 + ss * 512
                sc.dma("sp", lambda e: e.dma_start(out=glra[i][0:16, :], in_=self.glrT[dirn * 16:(dirn + 1) * 16, tb:tb + 512]), f"glra{i}", writes=[f"glra{i}"])
                sc.dma("sp", lambda e: e.dma_start(out=qT[i][:], in_=self.gqT[:, :, tb:tb + 512].rearrange("h p t -> p h t")), f"gqT{i}", writes=[f"gqT{i}"])
                sc.dma("sp", lambda e: e.dma_start(out=kT[i][:], in_=self.gkT[:, :, tb:tb + 512].rearrange("h p t -> p h t")), f"gkT{i}", writes=[f"gkT{i}"])
                sc.dma("sp", lambda e: e.dma_start(out=ktm[i][:], in_=self.tm["gk"][tb:tb + 512, :].rearrange("(a p) c -> p a c", p=128)), f"gktm{i}", writes=[f"gktm{i}"])
                sc.dma("sp", lambda e: e.dma_start(out=vv[i][:], in_=self.tm["gv"][tb:tb + 512, :].rearrange("(a p) c -> p a c", p=128)), f"gv{i}", writes=[f"gv{i}"])

            ks2 = [ks, self.sb(es, "g_ks_b", [128, 512], BF16)]
            eb2 = [eb, self.sb(es, "g_eb_b", [128, 4, 128], F32)]
            qt2 = [qt, self.sb(es, "g_qt_b", [128, 4, 128], BF16)]
            attm2 = [attm, self.sb(es, "g_attm_b", [128, 4, 128], BF16)]

            order = list(range(NSS)) if fwd else list(range(NSS - 1, -1, -1))
            steps = []
            for n, ss in enumerate(order):
                for m, sub in enumerate([0, 1, 2, 3] if fwd else [3, 2, 1, 0]):
                    steps.append(dict(n=n, ss=ss, i=n % 2, sub=sub, pi=(n * 4 + m) % 2, q=(n * 4 + m) % 2,
                                      tok=t0 + ss * 512 + sub * 128, tsl=slice(sub * 128, (sub + 1) * 128), first=(m == 0)))

            def loads(st):
                if fwd:
                    pi, tok = st["pi"], st["tok"]
                    sc.dma("sp", lambda e: e.dma_start(out=obt[pi][:], in_=self.ob[tok:tok + 128, :]), f"obt{pi}", writes=[f"obt{pi}"])
                    sc.dma("sp", lambda e: e.dma_start(out=grt[pi][:], in_=self.tm["gr"][tok:tok + 128, :]), f"grt{pi}", writes=[f"grt{pi}"])
                    sc.dma("sp", lambda e: e.dma_start(out=gat[pi][:], in_=self.tm["ga"][tok:tok + 128, :]), f"gat{pi}", writes=[f"gat{pi}"])

            def pre_a(st):
                i, tsl = st["i"], st["tsl"]
                sc.op("pe", lambda e: e.matmul(bA[:], lhsT=glra[i][0:17, tsl], rhs=wgk[0:17, :], start=True, stop=True),
                      reads=[f"glra{i}", "wgk"], writes=["bA"])
                sc.op("act", lambda e: e.activation(out=e1[:], in_=bA[:], func=AF.Exp, scale=-1.0), reads=["bA"], writes=["e1"])
                sc.op("act", lambda e: e.activation(out=sp[:], in_=e1[:], func=AF.Ln, bias=self.eps_t[:, 2:3]), reads=["e1"], writes=["sp"])

            def pre_b(st):
                i, sub, q = st["i"], st["sub"], st["q"]
                sc.op("pe", lambda e: e.matmul(bA[:], lhsT=Ustr[:], rhs=sp[:], start=True, stop=True), reads=["Ustr", "sp"], writes=["bA"])
                for h in range(4):
                    sc.op("pe", lambda e, h=h: e.matmul(bB[:, h * 128:(h + 1) * 128], lhsT=sp[:, h * 128:(h + 1) * 128], rhs=Uinc[:], start=True, stop=True),
                          reads=["sp", "Uinc"], writes=["bB"])
                sc.op("act", lambda e: e.activation(out=ebl[:], in_=bA[:], func=AF.Exp), reads=["bA"], writes=["ebl"])
                sc.op("act", lambda e: e.activation(out=eb2[q][:].rearrange("p h t -> p (h t)"), in_=bB[:], func=AF.Exp), reads=["bB"], writes=[f"eb{q}"])
                sc.op("act", lambda e: e.activation(out=enb[:].rearrange("p h t -> p (h t)"), in_=bB[:], func=AF.Exp, scale=-1.0), reads=["bB"], writes=["enb"])

            def pre_c(st):
                i, sub, q, tsl = st["i"], st["sub"], st["q"], st["tsl"]
                sc.op("pool", lambda e: e.tensor_tensor(out=ks2[q][:], in0=ktm[i][:, sub, :], in1=ebl[:], op=ALU.mult),
                      reads=[f"gktm{i}", "ebl"], writes=[f"ks{q}"])
                sc.op("dve", lambda e: e.scalar_tensor_tensor(out=qt2[q][:], in0=eb2[q][:], scalar=qscale, in1=qT[i][:, :, tsl], op0=ALU.mult, op1=ALU.mult),
                      reads=[f"eb{q}", f"gqT{i}"], writes=[f"qt{q}"])
                sc.op("pool", lambda e: e.tensor_tensor(out=kt[:], in0=enb[:], in1=kT[i][:, :, tsl], op=ALU.mult),
                      reads=["enb", f"gkT{i}"], writes=["kt"])

            def pre_d(st):
                q = st["q"]
                for h in range(4):
                    sc.op("pe", lambda e, h=h: e.matmul(bA[:, h * 128:(h + 1) * 128], lhsT=kt[:, h, :], rhs=qt2[q][:, h, :], start=True, stop=True),
                          reads=["kt", f"qt{q}"], writes=["bA"])
                sc.op("dve", lambda e: e.tensor_tensor(out=attm2[q][:].rearrange("p h t -> p (h t)"), in0=bA[:], in1=mask4[:].rearrange("p h t -> p (h t)"), op=ALU.mult),
                      reads=["bA", "mask4"], writes=[f"attm{q}"])

            def scan_open(st):
                i, sub, q = st["i"], st["sub"], st["q"]
                for h in range(4):
                    vs = slice(h * 256, (h + 1) * 256)
                    sc.op("pe", lambda e, h=h, vs=vs: e.matmul(bO[h][:, 0:256], lhsT=attm2[q][:, h, :], rhs=vv[i][:, sub, vs], start=True, stop=False),
                          reads=[f"attm{q}", f"gv{i}"], writes=[f"bO{h}"])
                    sc.op("pe", lambda e, h=h: e.matmul(bO[h][r1, 0:256], lhsT=qt2[q][:, h, r1], rhs=Sbf[h][:], start=False, stop=True),
                          reads=[f"qt{q}", f"Sbf_{h}"], writes=[f"bO{h}"])

            def scan_head(st, h):
                i, sub, q, pi = st["i"], st["sub"], st["q"], st["pi"]
                vs = slice(h * 256, (h + 1) * 256)
                kb = h % 2
                ksl = slice(kb * 256, (kb + 1) * 256)
                hs = slice(h * 128, (h + 1) * 128)
                sc.op("pe", lambda e: e.matmul(bKV[0][:, ksl], lhsT=ks2[q][r1, hs], rhs=vv[i][r1, sub, vs], start=True, stop=True),
                      reads=[f"ks{q}", f"gv{i}"], writes=["bKV0"])
                sc.op("pe", lambda e: e.matmul(bKV[1][:, ksl], lhsT=ks2[q][r2, hs], rhs=vv[i][r2, sub, vs], start=True, stop=True),
                      reads=[f"ks{q}", f"gv{i}"], writes=["bKV1"])
                sc.op("dve", lambda e: e.scalar_tensor_tensor(out=Sbf[h][:], in0=S32[h][:], scalar=eb2[q][:, h, c1:c1 + 1], in1=bKV[0][:, ksl], op0=ALU.mult, op1=ALU.add),
                      reads=[f"S32_{h}", f"eb{q}", "bKV0"], writes=[f"Sbf_{h}"])
                sc.op("dve", lambda e: e.scalar_tensor_tensor(out=S32[h][:], in0=S32[h][:], scalar=eb2[q][:, h, c1:c1 + 1], in1=bKV[0][:, ksl], op0=ALU.mult, op1=ALU.add),
                      reads=[f"S32_{h}", f"eb{q}", "bKV0"], writes=[f"S32_{h}"])
                sc.op("pe", lambda e: e.matmul(bO[h][r2, 0:256], lhsT=qt2[q][:, h, r2], rhs=Sbf[h][:], start=False, stop=True),
                      reads=[f"qt{q}", f"Sbf_{h}"], writes=[f"bO{h}"])
                sc.op("dve", lambda e: e.scalar_tensor_tensor(out=S32[h][:], in0=S32[h][:], scalar=eb2[q][:, h, c2:c2 + 1], in1=bKV[1][:, ksl], op0=ALU.mult, op1=ALU.add),
                      reads=[f"S32_{h}", f"eb{q}", "bKV1"], writes=[f"S32_{h}"])
                sc.op("act", lambda e: e.activation(out=Sbf[h][:], in_=S32[h][:], func=AF.Copy), reads=[f"S32_{h}"], writes=[f"Sbf_{h}"])
                if fwd:
                    sc.op("dve", lambda e: e.tensor_tensor(out=osb[pi][:, vs], in0=bO[h][:, 0:256], in1=obt[pi][:, vs], op=ALU.add),
                          reads=[f"bO{h}", f"obt{pi}"], writes=[(f"osb{pi}", h)])
                else:
                    sc.op("act", lambda e: e.activation(out=osb[pi][:, vs], in_=bO[h][:, 0:256], func=AF.Copy),
                          reads=[f"bO{h}"], writes=[(f"osb{pi}", h)])

            def epilogue(st):
                pi, tok = st["pi"], st["tok"]
                okeys = [(f"osb{pi}", h) for h in range(4)]
                if not fwd:
                    sc.dma("act", lambda e: e.dma_start(out=self.ob[tok:tok + 128, :], in_=osb[pi][:]), f"osb{pi}", reads=okeys, writes=[("ob", tok)])
                    return
                for h in range(4):
                    vs = slice(h * 256, (h + 1) * 256)
                    sc.op("act", lambda e, h=h, vs=vs: e.activation(out=junk[:, h, :], in_=osb[pi][:, vs], func=AF.Square, accum_out=stt[pi][:, h:h + 1]),
                          reads=[(f"osb{pi}", h)], writes=[f"junk{h}", (f"gst{pi}", h)])
                sc.op("act", lambda e: e.activation(out=stt[pi][:, 4:8], in_=stt[pi][:, 0:4], func=AF.Ln, scale=1.0 / GLA_DV, bias=self.eps_t[:, 0:1]),
                      reads=[(f"gst{pi}", h) for h in range(4)], writes=[f"gst{pi}"])
                sc.op("act", lambda e: e.activation(out=stt[pi][:, 8:12], in_=stt[pi][:, 4:8], func=AF.Exp, scale=-0.5), reads=[f"gst{pi}"], writes=[f"gst{pi}"])
                sc.op("pool", lambda e: e.tensor_tensor(out=g32[:], in0=grt[pi][:], in1=gat[pi][:], op=ALU.mult), reads=[f"grt{pi}", f"gat{pi}"], writes=["g32"])
                sc.op("pool", lambda e: e.tensor_tensor(out=g32[:], in0=g32[:], in1=gnwb[:], op=ALU.mult), reads=["g32", "gnwb"], writes=["g32"])
                for h in range(4):
                    vs = slice(h * 256, (h + 1) * 256)
                    sc.op("dve", lambda e, h=h, vs=vs: e.scalar_tensor_tensor(out=ysb[pi][:, vs], in0=osb[pi][:, vs], scalar=stt[pi][:, 8 + h:9 + h], in1=g32[:, vs], op0=ALU.mult, op1=ALU.mult),
                          reads=[(f"osb{pi}", h), f"gst{pi}", "g32"], writes=[(f"ysb{pi}", h)])
                sc.dma("act", lambda e: e.dma_start(out=self.ya[tok:tok + 128, :], in_=ysb[pi][:]), f"ysb{pi}",
                       reads=[(f"ysb{pi}", h) for h in range(4)], writes=[("ya", tok)])

            load_ss(order[0], 0)
            if NSS > 1:
                load_ss(order[1], 1)
            loads(steps[0])
            for f_ in (pre_a, pre_b, pre_c, pre_d):
                f_(steps[0])
            for p, st in enumerate(steps):
                nx = steps[p + 1] if p + 1 < len(steps) else None
                if nx is not None:
                    loads(nx)
                scan_open(st)
                parts = (pre_a, pre_b, pre_c, pre_d)
                for h in range(4):
                    if nx is not None:
                        parts[h](nx)
                    scan_head(st, h)
                epilogue(st)
                if nx is not None and nx["first"] and nx["n"] + 1 < NSS:
                    load_ss(order[nx["n"] + 1], (nx["n"] + 1) % 2)
            sc.flush_wait_all()
            sc.emit(block)

    def phase_attn(self, sc, s, wprep=False):
        nc, S, NT, NG = self.nc, self.S, self.NT, self.NG
        t0 = s * S
        LAG = 2
        with ExitStack() as es, nc.Block() as block:
            if wprep:
                wt = self.sb(es, "wprep", [128, KC, FFN], BF16)
                wsrc = self.w_fi.rearrange("(k p) c -> p k c", p=128)

                def wprep_load(g):
                    for kc in range(KC):
                        sc.dma("pool", lambda e, kc=kc: e.dma_start(out=wt[:, kc, :], in_=wsrc[:, kc, g * FFN:(g + 1) * FFN]), f"wp{kc % 4}", writes=[("wprep", kc)])

                def wprep_store(g):
                    allk = [("wprep", kc) for kc in range(KC)]
                    for f in range(FC):
                        sc.dma("sp", lambda e, f=f: e.dma_start(out=self.wfib[f, :, g, :, :], in_=wt[:, :, f * 128:(f + 1) * 128]), f"wps{f % 4}",
                               reads=allk, writes=[("wfib", f, g)])
                wprep_load(0)
            KT = [self.sb(es, f"aKT{i}", [128, 2, S], BF16) for i in range(2)]
            VA = [self.sb(es, f"aVA{i}", [128, NT, 258], BF16) for i in range(2)]
            QT = [self.sb(es, f"aQT{i}", [128, 2, 512], BF16) for i in range(2)]
            PT = [self.sb(es, f"aPT{i}", [128, 1024], BF16) for i in range(2)]
            A0 = self.sb(es, "aA0", [128, 4, 256], F32)
            o2 = self.sb(es, "ao2", [128, 4, 256], F32)
            gbt = [self.sb(es, f"agb{i}", [128, 4, 256], BF16) for i in range(2)]
            g32 = self.sb(es, "ag32", [128, 4, 256], F32)
            ybs = [self.sb(es, f"aybs{i}", [128, 4, 256], F32) for i in range(2)]
            slnb = self.sb(es, "aslnb", [128, 256], F32)
            lam = self.sb(es, "alam", [1, 512], F32)
            lt = self.sb(es, "alt", [1, 16], F32)
            ones1 = self.sb(es, "aones1", [1, 128], F32)
            nlam = self.sb(es, "anlam", [128, 1], F32)
            rr = self.sb(es, "arr", [128, 16], F32)
            stt = self.sb(es, "ast", [128, 12], F32)
            junk = self.sb(es, "ajunk", [128, 4, 256], F32)
            ACCall = self.ps(es, "aACC", [128, 4, 512])
            ACC = [ACCall[:, i, :] for i in range(4)]
            raw = [self.sb(es, f"araw{i}", [128, 4, 258], F32) for i in range(2)]
            SB = [self.ps(es, f"aSB{i}", [128, 1024]) for i in range(2)]

            sc.dma("sp", lambda e: e.dma_start(out=lam[:], in_=self.lam4[:, :]), "c_a", writes=["lam"])
            sc.dma("sp", lambda e: e.dma_start(out=slnb[:], in_=self.sln.broadcast_to([128, 256])), "c_b", writes=["slnb"])
            sc.op("dve", lambda e: e.memset(ones1[:], 1.0), writes=["ones1"])
            sc.op("dve", lambda e: e.tensor_tensor(out=lam[:, 0:128], in0=lam[:, 0:128], in1=lam[:, 128:256], op=ALU.mult), reads=["lam"], writes=["lam"])
            sc.op("dve", lambda e: e.tensor_tensor(out=lam[:, 256:384], in0=lam[:, 256:384], in1=lam[:, 384:512], op=ALU.mult), reads=["lam"], writes=["lam"])
            sc.op("act", lambda e: e.activation(out=lam[:, 128:256], in_=lam[:, 0:128], func=AF.Copy, accum_out=lt[:, 0:1]), reads=["lam"], writes=["lam", "lt"])
            sc.op("act", lambda e: e.activation(out=lam[:, 384:512], in_=lam[:, 256:384], func=AF.Copy, accum_out=lt[:, 1:2]), reads=["lam"], writes=["lam", "lt"])
            sc.op("act", lambda e: e.activation(out=lt[:, 2:4], in_=lt[:, 0:2], func=AF.Exp), reads=["lt"], writes=["lt"])
            sc.op("dve", lambda e: e.tensor_tensor(out=lt[:, 4:5], in0=lt[:, 3:4], in1=lt[:, 2:3], op=ALU.subtract), reads=["lt"], writes=["lt"])
            sc.op("dve", lambda e: e.tensor_scalar(out=lt[:, 5:6], in0=lt[:, 4:5], scalar1=-LAMBDA_INIT, scalar2=None, op0=ALU.add), reads=["lt"], writes=["lt"])
            sc.op("pe", lambda e: e.matmul(ACC[0][:, 0:1], lhsT=ones1[:], rhs=lt[:, 5:6], start=True, stop=True), reads=["ones1", "lt"], writes=["aACC0"])
            sc.op("dve", lambda e: e.tensor_copy(out=nlam[:], in_=ACC[0][:, 0:1]), reads=["aACC0"], writes=["nlam"])
            sc.op("dve", lambda e: e.tensor_scalar(out=slnb[:], in0=slnb[:], scalar1=float(1.0 - LAMBDA_INIT), scalar2=None, op0=ALU.mult), reads=["slnb"], writes=["slnb"])
            for i in range(2):
                sc.op("pool", lambda e, i=i: e.memset(VA[i][:, :, 256:258], 1.0), writes=[f"aVA{i}"])

            def load_head(h, i):
                for c in range(2):
                    sc.dma("sp", lambda e, c=c: e.dma_start(out=KT[i][:, c, :], in_=self.dkT[2 * h + c, :, t0:t0 + S]), f"aKT{i}", writes=[f"aKT{i}"])
                for a0 in range(0, NT, 8):
                    a1 = min(NT, a0 + 8)
                    sc.dma("sp", lambda e, a0=a0, a1=a1: e.dma_start(
                        out=VA[i][:, a0:a1, 0:256],
                        in_=self.tm["dv"][t0 + a0 * 128:t0 + a1 * 128, h * 256:(h + 1) * 256].rearrange("(a p) c -> p a c", p=128)),
                        f"aVA{i}", writes=[f"aVA{i}"])

            scale = float(DIFF_HD ** -0.5)
            load_head(0, 0)
            work = [(h, qg) for h in range(DIFF_H) for qg in range(NG)]

            def load_q(widx):
                h, qg = work[widx]
                qi, tb = widx % 2, t0 + qg * 512
                for c in range(2):
                    sc.dma("sp", lambda e, c=c: e.dma_start(out=QT[qi][:, c, :], in_=self.dqT[2 * h + c, :, tb:tb + 512]), f"aQT{qi}", writes=[f"aQT{qi}"])
                sc.dma("sp", lambda e: e.dma_start(out=gbt[qi][:], in_=self.tm["gb"][tb:tb + 512, h * 256:(h + 1) * 256].rearrange("(a p) c -> p a c", p=128)),
                       f"agb{qi}", writes=[f"agb{qi}"])
            load_q(0)
            NP2 = NT // 2
            pairs = [(widx, h, qg, c, p) for widx, (h, qg) in enumerate(work) for c in range(2) for p in range(NP2)]

            def emit_scores(g):
                widx, h, qg, c, p = pairs[g]
                b_, qi, hi = g % 2, widx % 2, h % 2
                for u in range(2):
                    k_ = 2 * p + u
                    sc.op("pe", lambda e, u=u, k_=k_: e.matmul(SB[b_][:, u * 512:(u + 1) * 512], lhsT=KT[hi][:, c, k_ * 128:(k_ + 1) * 128], rhs=QT[qi][:, c, :], start=True, stop=True),
                          reads=[f"aKT{hi}", f"aQT{qi}"], writes=[f"aSB{b_}"])
                sc.op("act", lambda e: e.activation(out=PT[b_][:], in_=SB[b_][:], func=AF.Exp, scale=scale, bias=self.eps_t[:, 3:4]),
                      reads=[f"aSB{b_}"], writes=[f"aPT{b_}"])

            def emit_pv(g):
                widx, h, qg, c, p = pairs[g]
                b_, hi = g % 2, h % 2
                for u in range(2):
                    k_ = 2 * p + u
                    for qs in range(4):
                        sc.op("pe", lambda e, u=u, k_=k_, qs=qs: e.matmul(ACC[qs][:, 0:257], lhsT=PT[b_][:, u * 512 + qs * 128:u * 512 + (qs + 1) * 128], rhs=VA[hi][:, k_, 0:257],
                                                                   start=(k_ == 0), stop=(k_ == NT - 1)),
                              reads=[f"aPT{b_}", f"aVA{hi}"], writes=[f"aACC{qs}"])

            def emit_drain(g):
                widx, h, qg, c, p = pairs[g]
                qi, tb = widx % 2, t0 + qg * 512
                sc.op("dve", lambda e: e.tensor_copy(out=raw[c][:, :, 0:257], in_=ACCall[:, :, 0:257]),
                      reads=[f"aACC{q_}" for q_ in range(4)], writes=[f"araw{c}"])
                sc.op("dve", lambda e: e.reciprocal(out=rr[:, c * 4:c * 4 + 4], in_=raw[c][:, :, 256]), reads=[f"araw{c}"], writes=[("rr", c)])
                if c == 1:
                    sc.op("dve", lambda e: e.tensor_scalar(out=rr[:, 12:16], in0=rr[:, 4:8], scalar1=nlam[:, 0:1], scalar2=None, op0=ALU.mult),
                          reads=[("rr", 1), "nlam"], writes=[("rr", 2)])
                for qs in range(4):
                    if c == 0:
                        sc.op("dve", lambda e, qs=qs: e.tensor_scalar(out=A0[:, qs, :], in0=raw[0][:, qs, 0:256], scalar1=rr[:, qs:qs + 1], scalar2=None, op0=ALU.mult),
                              reads=["araw0", ("rr", 0)], writes=[("A0", qs)])
                    else:
                        sc.op("dve", lambda e, qs=qs: e.scalar_tensor_tensor(out=o2[:, qs, :], in0=raw[1][:, qs, 0:256], scalar=rr[:, 12 + qs:13 + qs], in1=A0[:, qs, :], op0=ALU.mult, op1=ALU.add),
                              reads=["araw1", ("rr", 2), ("A0", qs)], writes=[("o2", qs)])
                if c == 0:
                    return
                for qs in range(4):
                    sc.op("act", lambda e, qs=qs: e.activation(out=junk[:, qs, :], in_=o2[:, qs, :], func=AF.Square, accum_out=stt[:, qs:qs + 1]), reads=[("o2", qs)], writes=[f"ajunk{qs}", ("ast", qs)])
                sc.op("act", lambda e: e.activation(out=stt[:, 4:8], in_=stt[:, 0:4], func=AF.Ln, scale=1.0 / 256, bias=self.eps_t[:, 1:2]), reads=[("ast", q_) for q_ in range(4)], writes=["ast"])
                sc.op("act", lambda e: e.activation(out=stt[:, 8:12], in_=stt[:, 4:8], func=AF.Exp, scale=-0.5), reads=["ast"], writes=["ast"])
                for qs in range(4):
                    sc.op("pool", lambda e, qs=qs: e.tensor_tensor(out=g32[:, qs, :], in0=gbt[qi][:, qs, :], in1=slnb[:], op=ALU.mult), reads=[f"agb{qi}", "slnb"], writes=[("ag32", qs)])
                for qs in range(4):
                    sc.op("dve", lambda e, qs=qs: e.scalar_tensor_tensor(out=ybs[qi][:, qs, :], in0=o2[:, qs, :], scalar=stt[:, 8 + qs:9 + qs], in1=g32[:, qs, :], op0=ALU.mult, op1=ALU.mult),
                          reads=[("o2", qs), "ast", ("ag32", qs)], writes=[(f"aybs{qi}", qs)])
                sc.dma("pool", lambda e: e.dma_start(out=self.yb[tb:tb + 512, h * 256:(h + 1) * 256].rearrange("(a p) c -> p a c", p=128), in_=ybs[qi][:]),
                       f"aybs{qi}", reads=[(f"aybs{qi}", q_) for q_ in range(4)], writes=[("yb", tb, h)])

            total = len(pairs)
            for g in range(total + 1):
                deferred = []
                if g < total:
                    widx, h, qg, c, p = pairs[g]
                    if p == 0 and c == 0:
                        if widx + 1 < len(work):
                            deferred.append(lambda widx=widx: load_q(widx + 1))
                        if qg == 0:
                            if h + 1 < DIFF_H:
                                deferred.append(lambda h=h: load_head(h + 1, (h + 1) % 2))
                            if wprep and h == 1:
                                deferred.append(lambda: (wprep_store(0), wprep_load(1)))
                            if wprep and h == 3:
                                deferred.append(lambda: wprep_store(1))
                    emit_scores(g)
                if g >= 1:
                    emit_pv(g - 1)
                    if pairs[g - 1][4] == NP2 - 1:
                        emit_drain(g - 1)
                for d in deferred:
                    d()
            sc.flush_wait_all()
            sc.emit(block)

    def phase_wprep(self, sc):
        nc = self.nc
        with ExitStack() as es, nc.Block() as block:
            wt = self.sb(es, "wprep", [128, KC, 2 * FFN], BF16)
            src = self.w_fi.rearrange("(k p) c -> p k c", p=128)
            for kc in range(KC):
                sc.dma("pool", lambda e, kc=kc: e.dma_start(out=wt[:, kc, :], in_=src[:, kc, :]), f"wp{kc % 4}", writes=[("wprep", kc)])
            allk = [("wprep", kc) for kc in range(KC)]
            for f in range(FC):
                for g in range(2):
                    c0 = g * FFN + f * 128
                    sc.dma("sp", lambda e, f=f, g=g, c0=c0: e.dma_start(out=self.wfib[f, :, g, :, :], in_=wt[:, :, c0:c0 + 128]), f"wps{(2 * f + g) % 4}",
                           reads=allk, writes=[("wfib", f, g)])
            sc.flush_wait_all()
            sc.emit(block)

    def phase_ffn(self, sc):
        nc, S, T = self.nc, self.S, self.T
        NGT = T // 512
        with ExitStack() as es, nc.Block() as block:
            wo = self.sb(es, "f_wo", [128, KC, D], BF16)
            wfo = self.sb(es, "f_wfo", [128, FC, D], BF16)
            wfi = [self.sb(es, f"f_wfi{i}", [128, 2, KC, 128], BF16) for i in range(4)]
            ident = self.sb(es, "f_ident", [128, 128], BF16)
            identf = self.sb(es, "f_identf", [128, 128], F32)
            nwfb = self.sb(es, "f_nwfb", [128, D], F32)
            nwfin = self.sb(es, "f_nwfin", [128, D], F32)
            la = [self.sb(es, f"f_la{i}", [128, D], F32) for i in range(2)]
            lb = [self.sb(es, f"f_lb{i}", [128, D], F32) for i in range(2)]
            lx = [self.sb(es, f"f_lx{i}", [128, D], F32) for i in range(2)]
            mb = [self.sb(es, f"f_mb{i}", [128, D], BF16) for i in range(2)]
            mT = self.sb(es, "f_mT", [128, KC, 512], BF16)
            hT = self.sb(es, "f_hT", [128, KC, 512], BF16)
            hsb = self.sb(es, "f_h", [128, 4, D], F32)
            aT = self.sb(es, "f_aT", [128, FC, 512], BF16)
            sg = [self.sb(es, f"f_sg{i}", [128, 512], F32) for i in range(2)]
            ot = [self.sb(es, f"f_ot{i}", [128, D], F32) for i in range(2)]
            stt = [self.sb(es, f"f_st{i}", [128, 4], F32) for i in range(2)]
            psT = [self.ps(es, f"f_psT{i}", [128, 1024], BF16) for i in range(2)]
            pb = [self.ps(es, f"f_pb{i}", [128, 512]) for i in range(6)]

            sc.dma("pool", lambda e: e.dma_start(out=wo[:], in_=self.w_out.rearrange("(k p) c -> p k c", p=128)), "pw_a", writes=["wo"])
            for f0 in range(0, FC, 2):
                sc.dma("pool", lambda e, f0=f0: e.dma_start(out=wfo[:, f0:f0 + 2, :], in_=self.w_fo[f0 * 128:(f0 + 2) * 128, :].rearrange("(k p) c -> p k c", p=128)),
                       f"pw_b{(f0 // 2) % 2}", writes=[("wfo", f0)])
            wfo_all = [("wfo", f0) for f0 in range(0, FC, 2)]
            sc.dma("sp", lambda e: e.dma_start(out=identf[:], in_=self.cst[:, 0:128]), "c_c", writes=["identf"])
            sc.op("dve", lambda e: e.tensor_copy(out=ident[:], in_=identf[:]), reads=["identf"], writes=["ident"])
            sc.dma("sp", lambda e: e.dma_start(out=nwfb[:], in_=self.nw_ffnr.broadcast_to([128, D])), "c_d", writes=["nwfb"])
            sc.dma("sp", lambda e: e.dma_start(out=nwfin[:], in_=self.nw_fin.broadcast_to([128, D])), "c_e", writes=["nwfin"])

            xf = self.x.rearrange("n s d -> (n s) d")
            yf = self.y.rearrange("n s d -> (n s) d")
            st8 = dict(pbi=0, li=0, wi=0, sgi=0, oti=0, pti=0)

            def next_pb():
                j = st8["pbi"] % 6
                st8["pbi"] += 1
                return j

            def norm_rows(src_ap, src_keys, i):
                sc.op("act", lambda e: e.activation(out=mb[i][:], in_=src_ap, func=AF.Square, accum_out=stt[i][:, 0:1]), reads=src_keys, writes=[f"mb{i}", f"fst{i}"])
                sc.op("act", lambda e: e.activation(out=stt[i][:, 1:2], in_=stt[i][:, 0:1], func=AF.Ln, scale=1.0 / D, bias=self.eps_t[:, 0:1]), reads=[f"fst{i}"], writes=[f"fst{i}"])
                sc.op("act", lambda e: e.activation(out=stt[i][:, 2:3], in_=stt[i][:, 1:2], func=AF.Exp, scale=-0.5), reads=[f"fst{i}"], writes=[f"fst{i}"])

            def transpose_to(dstT, dkey, i, tt_, on_act):
                pt = st8["pti"] % 2
                st8["pti"] += 1
                for kc in range(KC):
                    sc.op("pe", lambda e, kc=kc: e.transpose(psT[pt][:, kc * 128:(kc + 1) * 128], mb[i][:, kc * 128:(kc + 1) * 128], ident[:]),
                          reads=[f"mb{i}", "ident"], writes=[f"f_psT{pt}"])
                src = psT[pt][:].rearrange("p (k t) -> p k t", k=KC)
                if on_act:
                    sc.op("act", lambda e: e.activation(out=dstT[:, :, tt_ * 128:(tt_ + 1) * 128], in_=src, func=AF.Copy), reads=[f"f_psT{pt}"], writes=[(dkey, tt_)])
                else:
                    sc.op("dve", lambda e: e.tensor_copy(out=dstT[:, :, tt_ * 128:(tt_ + 1) * 128], in_=src), reads=[f"f_psT{pt}"], writes=[(dkey, tt_)])

            for g in range(NGT):
                tb = g * 512

                def s1a(tt_):
                    i = tt_ % 2
                    r0 = tb + tt_ * 128
                    sc.dma("sp", lambda e: e.dma_start(out=la[i][:], in_=self.ya[r0:r0 + 128, :]), f"f_la{i}", writes=[f"la{i}"])
                    sc.dma("sp", lambda e: e.dma_start(out=lb[i][:], in_=self.yb[r0:r0 + 128, :]), f"f_lb{i}", writes=[f"lb{i}"])
                    sc.op("dve", lambda e: e.tensor_tensor(out=mb[i][:], in0=la[i][:], in1=lb[i][:], op=ALU.add), reads=[f"la{i}", f"lb{i}"], writes=[f"mb{i}"])

                s1a(0)
                for tt_ in range(4):
                    if tt_ + 1 < 4:
                        s1a(tt_ + 1)
                    transpose_to(mT, "mT", tt_ % 2, tt_, on_act=True)

                def s2a(tt_):
                    i = tt_ % 2
                    r0 = tb + tt_ * 128
                    sc.dma("sp", lambda e: e.dma_start(out=lx[i][:], in_=xf[r0:r0 + 128, :]), f"f_lx{i}", writes=[f"lx{i}"])
                    for nh in range(2):
                        j = next_pb()
                        for kc in range(KC):
                            sc.op("pe", lambda e, kc=kc, j=j, nh=nh: e.matmul(pb[j][:], lhsT=mT[:, kc, tt_ * 128:(tt_ + 1) * 128], rhs=wo[:, kc, nh * 512:(nh + 1) * 512],
                                                                          start=(kc == 0), stop=(kc == KC - 1)),
                                  reads=[("mT", tt_), "wo"], writes=[f"f_pb{j}"])
                        sc.op("dve", lambda e, j=j, nh=nh: e.tensor_tensor(out=hsb[:, tt_, nh * 512:(nh + 1) * 512], in0=pb[j][:], in1=lx[i][:, nh * 512:(nh + 1) * 512], op=ALU.add),
                              reads=[f"f_pb{j}", f"lx{i}"], writes=[("h", tt_)])
                    norm_rows(hsb[:, tt_, :], [("h", tt_)], i)
                    sc.op("dve", lambda e: e.scalar_tensor_tensor(out=mb[i][:], in0=hsb[:, tt_, :], scalar=stt[i][:, 2:3], in1=nwfb[:], op0=ALU.mult, op1=ALU.mult),
                          reads=[("h", tt_), f"fst{i}", "nwfb"], writes=[f"mb{i}"])

                s2a(0)
                for tt_ in range(4):
                    if tt_ + 1 < 4:
                        s2a(tt_ + 1)
                    transpose_to(hT, "hT", tt_ % 2, tt_, on_act=False)
                hT_all = [("hT", k) for k in range(4)]
                for f in range(FC):
                    wi = st8["wi"] % 4
                    st8["wi"] += 1
                    sc.dma("sp", lambda e, wi=wi, f=f: e.dma_start(out=wfi[wi][:], in_=self.wfib[f, :, :, :, :]), f"f_wfi{wi}", reads=[("wfib", f, 0), ("wfib", f, 1)], writes=[f"wfi{wi}"])
                    jg, ju = next_pb(), next_pb()
                    for gi, j in ((0, jg), (1, ju)):
                        for kc in range(KC):
                            sc.op("pe", lambda e, kc=kc, j=j, gi=gi, wi=wi: e.matmul(pb[j][:], lhsT=wfi[wi][:, gi, kc, :], rhs=hT[:, kc, :], start=(kc == 0), stop=(kc == KC - 1)),
                                  reads=[f"wfi{wi}"] + hT_all, writes=[f"f_pb{j}"])
                    si = st8["sgi"] % 2
                    st8["sgi"] += 1
                    sc.op("act", lambda e, jg=jg, si=si: e.activation(out=sg[si][:], in_=pb[jg][:], func=AF.Silu), reads=[f"f_pb{jg}"], writes=[f"sg{si}"])
                    sc.op("dve", lambda e, ju=ju, si=si, f=f: e.tensor_tensor(out=aT[:, f, :], in0=pb[ju][:], in1=sg[si][:], op=ALU.mult),
                          reads=[f"f_pb{ju}", f"sg{si}"], writes=[("aT", f)])
                aT_all = [("aT", f) for f in range(FC)]
                for tt_ in range(4):
                    oi = st8["oti"] % 2
                    st8["oti"] += 1
                    r0 = tb + tt_ * 128
                    for nh in range(2):
                        j = next_pb()
                        for f in range(FC):
                            sc.op("pe", lambda e, f=f, j=j, nh=nh, tt_=tt_: e.matmul(pb[j][:], lhsT=aT[:, f, tt_ * 128:(tt_ + 1) * 128], rhs=wfo[:, f, nh * 512:(nh + 1) * 512],
                                                                                  start=(f == 0), stop=(f == FC - 1)),
                                  reads=aT_all + wfo_all, writes=[f"f_pb{j}"])
                        sc.op("dve", lambda e, j=j, nh=nh, tt_=tt_: e.tensor_tensor(out=hsb[:, tt_, nh * 512:(nh + 1) * 512], in0=pb[j][:], in1=hsb[:, tt_, nh * 512:(nh + 1) * 512], op=ALU.add),
                              reads=[f"f_pb{j}", ("h", tt_)], writes=[("h", tt_)])
                    norm_rows(hsb[:, tt_, :], [("h", tt_)], oi)
                    sc.op("dve", lambda e, oi=oi, tt_=tt_: e.scalar_tensor_tensor(out=ot[oi][:], in0=hsb[:, tt_, :], scalar=stt[oi][:, 2:3], in1=nwfin[:], op0=ALU.mult, op1=ALU.mult),
                          reads=[("h", tt_), f"fst{oi}", "nwfin"], writes=[f"ot{oi}"])
                    sc.dma("pool", lambda e, oi=oi, r0=r0: e.dma_start(out=yf[r0:r0 + 128, :], in_=ot[oi][:]), f"f_ot{oi}", reads=[f"ot{oi}"], writes=[("y", r0)])
            sc.flush_wait_all()
            sc.emit(block)

    def build(self):
        nc = self.nc
        self.declare()
        with ExitStack() as es:
            sc = Sched(nc, es)
            self.eps_t = self.sb(es, "eps_t", [128, 4], F32)
            with nc.Block() as block:
                sc.op("dve", lambda e: e.memset(self.eps_t[:, 0:1], NORM_EPS), writes=["eps"])
                sc.op("dve", lambda e: e.memset(self.eps_t[:, 1:2], SUBLN_EPS), writes=["eps"])
                sc.op("dve", lambda e: e.memset(self.eps_t[:, 2:3], 1.0), writes=["eps"])
                sc.op("dve", lambda e: e.memset(self.eps_t[:, 3:4], 0.0), writes=["eps"])
                sc.emit(block)
            for s in range(self.NSEQ):
                if "p1" in self.phases:
                    self.phase1(sc, s)
                if "gla" in self.phases:
                    self.phase_gla(sc, s, 1)
                    self.phase_gla(sc, s, 0)
                if "attn" in self.phases:
                    self.phase_attn(sc, s, wprep=("ffn" in self.phases and s == self.NSEQ - 1))
            if "ffn" in self.phases:
                if "attn" not in self.phases:
                    self.phase_wprep(sc)
                self.phase_ffn(sc)
        return nc


def host_consts(S):
    inv_freq = 1.0 / (10000.0 ** (np.arange(0, DIFF_HD, 2, dtype=np.float32) / DIFF_HD))
    pos = np.arange(S, dtype=np.float32)
    fr = pos[None, :] * np.concatenate([inv_freq, inv_freq])[:, None].astype(np.float32)
    cosT = np.cos(fr).astype(np.float32)
    sgn = np.where(np.arange(128) < 64, -1.0, 1.0).astype(np.float32)[:, None]
    sinT = (np.sin(fr) * sgn).astype(np.float32)
    p = np.arange(128)
    same = (p[:, None] // 64) == (p[None, :] // 64)
    sI, tI = p[:, None], p[None, :]
    g = -1.0 / 16.0
    ident = np.eye(128, dtype=np.float32)
    uinc_f = (same & (sI <= tI)) * g
    ustr_f = (same & (sI > tI)) * g
    uinc_b = (same & (sI >= tI)) * g
    ustr_b = (same & (sI < tI)) * g
    cst = np.concatenate([ident, uinc_f, ustr_f, uinc_b, ustr_b, np.zeros((128, 128))], axis=1).astype(np.float32)
    m_f = (same & (sI <= tI)).astype(np.float32)
    m_b = (same & (sI >= tI)).astype(np.float32)
    msk = np.concatenate([m_f, m_b], axis=1).astype(np.float32)
    return cosT, sinT, cst, msk


def host_inputs(inp, S):
    c = np.ascontiguousarray
    w_in = np.asarray(inp["w_in"])[0]
    w1 = c(w_in[:, _w1_columns()])
    cosT, sinT, cst, msk = host_consts(S)
    wgk = np.concatenate([np.asarray(inp["w_gk2"])[0], np.asarray(inp["b_gk"])[0][:, None, :]], axis=1)
    shared = dict(
        w1=w1,
        nw_mixr=c(np.asarray(inp["norm_mix_w"])[0].reshape(1, D)),
        nw_ffnr=c(np.asarray(inp["norm_ffn_w"])[0].reshape(1, D)),
        nw_fin=c(np.asarray(inp["norm_final_w"]).reshape(1, D)),
        cosT=cosT, sinT=sinT, cst=cst, msk=msk,
        wgk=c(wgk.astype(np.float32)),
        gnw=c(np.tile(np.asarray(inp["gla_norm_w"])[0], 4).reshape(1, 1024)),
        sln=c(np.asarray(inp["diff_subln_w"])[0].reshape(1, 256)),
        lam4=c(np.concatenate([np.asarray(inp[k])[0] for k in ("lambda_q1", "lambda_k1", "lambda_q2", "lambda_k2")]).reshape(1, 512)),
        w_out=c(np.asarray(inp["w_out"])[0]),
        w_fi=c(np.asarray(inp["w_ffn_in"])[0]),
        w_fo=c(np.asarray(inp["w_ffn_out"])[0]),
    )
    return {k: np.asarray(v, dtype=np.float32) for k, v in shared.items()}


def kernel(**inputs):
    x = np.asarray(inputs["x"], dtype=np.float32)
    B, S, _ = x.shape
    nseq = B // N_CORES
    b = Builder(S, nseq)
    nc = b.build()
    shared = host_inputs(inputs, S)
    in_maps = [dict(shared, x=np.ascontiguousarray(x[i * nseq:(i + 1) * nseq])) for i in range(N_CORES)]
    res = run_bass_kernel_spmd(nc, in_maps, core_ids=list(range(N_CORES)))
    return np.concatenate([np.asarray(r["y"]) for r in res.results], axis=0).astype(np.float32)
```

```python
from contextlib import ExitStack
import math
import numpy as np
import ml_dtypes
import concourse.bass as bass
import concourse.mybir as mybir
from concourse.bass_utils import run_bass_kernel_spmd

F32 = mybir.dt.float32
BF16 = mybir.dt.bfloat16
AF = mybir.ActivationFunctionType
ALU = mybir.AluOpType

D = 1024
KC = D // 128
GLA_H, GLA_DK, GLA_DV, GLA_RANK = 4, 128, 256, 16
DIFF_H, DIFF_HD = 4, 128
FFN = 2816
FC = FFN // 128
NORM_EPS = 1e-6
SUBLN_EPS = 1e-5
LAMBDA_INIT = 0.8 - 0.6 * math.exp(-0.3 * 0)
N_CORES = 8

OFF_GQ, OFF_GK, OFF_GV, OFF_GR, OFF_GLR = 0, 512, 1024, 2048, 3072
OFF_DQ, OFF_DK, OFF_DV, OFF_GA, OFF_GB = 3104, 4128, 5152, 6176, 7200


def _w1_columns():
    cols = []
    cols += list(range(OFF_GQ, OFF_GQ + 512))
    cols += list(range(OFF_GK, OFF_GK + 512))
    cols += list(range(OFF_DQ, OFF_DQ + 1024))
    cols += list(range(OFF_DK, OFF_DK + 1024))
    cols += list(range(OFF_GLR, OFF_GLR + 32))
    cols += list(range(OFF_GK, OFF_GK + 512))
    for off in (OFF_GV, OFF_GR, OFF_DV, OFF_GA, OFF_GB):
        cols += list(range(off, off + 1024))
    return np.array(cols, dtype=np.int64)


W1_COLS = 512 * 6 + 32 + 512 * 11
FM_BLOCKS = [("gq", 0), ("gk", 512)] + [("dq", 1024 + 512 * i) for i in range(2)] + \
            [("dk", 2048 + 512 * i) for i in range(2)]
GLR_OFF = 3072
TM_OFF = 3104
TM_BLOCKS = [("gk", 0)] + [(n, j) for n in ("gv", "gr", "dv", "ga", "gb") for j in range(2)]


class Sched:
    CE = ("pe", "act", "dve", "pool")
    ALL = ("pe", "act", "dve", "pool", "sp")

    def __init__(self, nc, es):
        self.nc = nc
        self.prog = {e: es.enter_context(nc.semaphore("prog_" + e)) for e in self.CE}
        self.cum = {e: 0 for e in self.CE}
        self.dsem = {}
        self.es = es
        self.res_w = {}
        self.res_r = {}
        self.ops = {e: [] for e in self.ALL}
        self.waited = {e: {} for e in self.ALL}
        self.sigval = {}
        self.uid = 0

    def _slot(self, slot):
        if slot not in self.dsem:
            self.dsem[slot] = [self.es.enter_context(self.nc.semaphore("d_" + slot)), 0]
        return self.dsem[slot]

    def _collect(self, eng, reads, writes, is_dma):
        deps = []
        for k in reads:
            for ev in self.res_w.get(k, ()):
                deps.append((ev, True))
        for k in writes:
            for ev in self.res_w.get(k, ()):
                deps.append((ev, False))
            for ev in self.res_r.get(k, ()):
                deps.append((ev, False))
        out = []
        for ev, raw in deps:
            if ev[0] == "eng" and ev[1] == eng and not is_dma:
                if eng == "pe":
                    continue
            out.append(ev)
        return out

    def _update(self, ev, reads, writes):
        for k in writes:
            self.res_w[k] = [ev]
            self.res_r[k] = []
        for k in reads:
            self.res_r.setdefault(k, []).append(ev)

    def op(self, eng, fn, reads=(), writes=()):
        self.uid += 1
        ev = ("eng", eng, self.uid)
        deps = self._collect(eng, reads, writes, False)
        self.ops[eng].append(dict(fn=fn, deps=deps, ev=ev, dma=None))
        self._update(ev, reads, writes)
        return ev

    def dma(self, queue, fn, slot, reads=(), writes=()):
        s = self._slot(slot)
        deps = self._collect(queue, reads, writes, True)
        if s[1] > 0:
            deps.append(("dma", slot, s[1]))
        s[1] += 1
        ev = ("dma", slot, s[1])
        self.ops[queue].append(dict(fn=fn, deps=deps, ev=ev, dma=slot))
        self._update(ev, reads, writes)
        return ev

    def flush_wait_all(self):
        deps = [("dma", slot, s[1]) for slot, s in self.dsem.items() if s[1] > 0]
        self.ops["sp"].append(dict(fn=None, deps=deps, ev=None, dma=None))

    def emit(self, block):
        done = getattr(self, "done_uid", 0)
        for e in self.ALL:
            for o in self.ops[e]:
                o["deps"] = [d for d in o["deps"] if not (d[0] == "eng" and d[2] <= done)]
        self.done_uid = self.uid
        need = set()
        for e in self.ALL:
            for o in self.ops[e]:
                for d in o["deps"]:
                    if d[0] == "eng":
                        need.add(d[2])
        for e in self.CE:
            c = self.cum[e]
            for o in self.ops[e]:
                if o["dma"] is None and o["ev"] is not None and o["ev"][2] in need:
                    c += 1
                    self.sigval[o["ev"][2]] = c
            self.cum[e] = c
        starters = dict(pe=block.tensor, act=block.scalar, dve=block.vector, pool=block.gpsimd, sp=block.sync)
        for e in self.ALL:
            ops = self.ops[e]
            if not ops:
                continue
            waited = self.waited[e]

            def body(eng, ops=ops, waited=waited, e=e):
                for o in ops:
                    for d in o["deps"]:
                        if d[0] == "eng":
                            key, sem, val = d[1], self.prog[d[1]], self.sigval[d[2]]
                        else:
                            key, sem, val = "d_" + d[1], self.dsem[d[1]][0], 16 * d[2]
                        if waited.get(key, 0) >= val:
                            continue
                        eng.wait_ge(sem, val)
                        waited[key] = val
                    if o["fn"] is None:
                        continue
                    ins = o["fn"](eng)
                    if o["dma"] is not None:
                        ins.then_inc(self.dsem[o["dma"]][0], 16)
                    elif o["ev"][2] in self.sigval:
                        ins.then_inc(self.prog[e], 1)
            starters[e](body)
        self.ops = {e: [] for e in self.ALL}


class Builder:
    def __init__(self, S, NSEQ, debug=False, phases=("p1", "gla", "attn", "ffn"), scratch_in=False, cut=99):
        self.S, self.NSEQ, self.debug, self.phases = S, NSEQ, debug, phases
        self.scratch_in, self.cut = scratch_in, cut
        self.T = S * NSEQ
        self.NT = S // 128
        self.NG = S // 512
        self.nc = bass.Bass("TRN2", target_bir_lowering=False)

    def dram(self, name, shape, dt, kind):
        return self.nc.dram_tensor(name, list(shape), dt, kind=kind).ap()

    def declare(self):
        S, T, NSEQ = self.S, self.T, self.NSEQ
        I = "ExternalInput"
        self.x = self.dram("x", [NSEQ, S, D], F32, I)
        self.w1 = self.dram("w1", [D, W1_COLS], F32, I)
        self.nw_mixr = self.dram("nw_mixr", [1, D], F32, I)
        self.nw_ffnr = self.dram("nw_ffnr", [1, D], F32, I)
        self.nw_fin = self.dram("nw_fin", [1, D], F32, I)
        self.cosT = self.dram("cosT", [128, S], F32, I)
        self.sinT = self.dram("sinT", [128, S], F32, I)
        self.wgk = self.dram("wgk", [2, 17, 512], F32, I)
        self.gnw = self.dram("gnw", [1, 1024], F32, I)
        self.sln = self.dram("sln", [1, 256], F32, I)
        self.lam4 = self.dram("lam4", [1, 512], F32, I)
        self.w_out = self.dram("w_out", [D, D], F32, I)
        self.w_fi = self.dram("w_fi", [D, 2 * FFN], F32, I)
        self.w_fo = self.dram("w_fo", [FFN, D], F32, I)
        self.cst = self.dram("cst", [128, 6 * 128], F32, I)
        self.msk = self.dram("msk", [128, 2 * 128], F32, I)
        self.y = self.dram("y", [NSEQ, S, D], F32, "ExternalOutput")
        K = "ExternalOutput" if self.debug else "Internal"
        K1 = K
        if self.scratch_in:
            K = "ExternalInput"
        self.gqT = self.dram("gqT", [4, 128, T], BF16, K)
        self.gkT = self.dram("gkT", [4, 128, T], BF16, K)
        self.dqT = self.dram("dqT", [8, 128, T], BF16, K)
        self.dkT = self.dram("dkT", [8, 128, T], BF16, K)
        self.glrT = self.dram("glrT", [32, T], BF16, K)
        self.tm = {n: self.dram("tm_" + n, [T, (512 if n == "gk" else 1024)], BF16, K)
                   for n in ("gk", "gv", "gr", "dv", "ga", "gb")}
        K = K1
        self.ob = self.dram("ob", [T, 1024], F32, K)
        self.ya = self.dram("ya", [T, 1024], F32, K)
        self.yb = self.dram("yb", [T, 1024], F32, K)
        self.wfib = self.dram("wfib", [FC, 128, 2, KC, 128], BF16, "Internal")

    def _uniq(self, name):
        self._n = getattr(self, "_n", 0) + 1
        return f"{name}_{self._n}"

    def sb(self, es, name, shape, dt):
        return es.enter_context(self.nc.sbuf_tensor(self._uniq(name), list(shape), dt))

    def ps(self, es, name, shape, dt=F32):
        return es.enter_context(self.nc.psum_tensor(self._uniq(name), list(shape), dt))

    def phase1(self, sc, s):
        nc, S, NT, NG = self.nc, self.S, self.NT, self.NG
        t0 = s * S
        with ExitStack() as es, nc.Block() as block:
            uT = self.sb(es, "uT", [128, KC, S], BF16)
            cosT = self.sb(es, "cosT_sb", [128, S], F32)
            sinT = self.sb(es, "sinT_sb", [128, S], F32)
            nwb = self.sb(es, "nwb", [128, D], F32)
            ident = self.sb(es, "ident", [128, 128], BF16)
            identf = self.sb(es, "identf", [128, 128], F32)
            xt = [self.sb(es, f"xt{i}", [128, D], F32) for i in range(4)]
            xn = [self.sb(es, f"xn{i}", [128, D], BF16) for i in range(2)]
            st = [self.sb(es, f"st{i}", [128, 4], F32) for i in range(2)]
            wb = [self.sb(es, f"wb{i}", [128, KC, 512], BF16) for i in range(2)]
            so = [self.sb(es, f"so{i}", [128, 512], BF16) for i in range(4)]
            sof = [self.sb(es, f"sof{i}", [32, 512], BF16) for i in range(2)]
            r1 = [self.sb(es, f"r1_{i}", [128, 512], F32) for i in range(2)]
            r2 = [self.sb(es, f"r2_{i}", [128, 512], F32) for i in range(2)]
            psT = [self.ps(es, f"psT{i}", [128, 1024], BF16) for i in range(2)]
            pb = [self.ps(es, f"pb{i}", [128, 512]) for i in range(6)]
            sc.dma("sp", lambda e: e.dma_start(out=nwb[:], in_=self.nw_mixr.broadcast_to([128, D])), "c_nwm", writes=["nwb"])
            sc.dma("sp", lambda e: e.dma_start(out=identf[:], in_=self.cst[:, 0:128]), "c_id", writes=["identf"])
            sc.op("dve", lambda e: e.tensor_copy(out=ident[:], in_=identf[:]), reads=["identf"], writes=["ident"])

            def load_x(tt):
                j = tt % 4
                sc.dma("sp", lambda e: e.dma_start(out=xt[j][:], in_=self.x[s, tt * 128:(tt + 1) * 128, :]), f"xt{j}", writes=[f"xt{j}"])

            def stage_a(tt):
                i = tt % 2
                j = tt % 4
                if tt + 3 < NT:
                    load_x(tt + 3)
                sc.op("act", lambda e: e.activation(out=xn[i][:], in_=xt[j][:], func=AF.Square, accum_out=st[i][:, 0:1]),
                      reads=[f"xt{j}"], writes=[f"xn{i}", f"st{i}"])
                sc.op("act", lambda e: e.activation(out=st[i][:, 1:2], in_=st[i][:, 0:1], func=AF.Ln, scale=1.0 / D, bias=self.eps_t[:, 0:1]),
                      reads=[f"st{i}"], writes=[f"st{i}"])
                sc.op("act", lambda e: e.activation(out=st[i][:, 2:3], in_=st[i][:, 1:2], func=AF.Exp, scale=-0.5),
                      reads=[f"st{i}"], writes=[f"st{i}"])
                sc.op("dve", lambda e: e.scalar_tensor_tensor(out=xn[i][:], in0=xt[j][:], scalar=st[i][:, 2:3], in1=nwb[:], op0=ALU.mult, op1=ALU.mult),
                      reads=[f"xt{j}", f"st{i}", "nwb"], writes=[f"xn{i}"])

            def stage_b(tt):
                i = tt % 2
                for kc in range(KC):
                    sc.op("pe", lambda e, kc=kc: e.transpose(psT[i][:, kc * 128:(kc + 1) * 128], xn[i][:, kc * 128:(kc + 1) * 128], ident[:]),
                          reads=[f"xn{i}", "ident"], writes=[f"psT{i}"])
                sc.op("dve", lambda e: e.tensor_copy(out=uT[:, :, tt * 128:(tt + 1) * 128], in_=psT[i][:].rearrange("p (k t) -> p k t", k=KC)),
                      reads=[f"psT{i}"], writes=[("uT", tt)])

            for tt in range(min(3, NT)):
                load_x(tt)
            stage_a(0)
            for tt in range(NT):
                if tt + 1 < NT:
                    stage_a(tt + 1)
                stage_b(tt)
            for q_ in range((S + 1023) // 1024):
                c0_, c1_ = q_ * 1024, min(S, (q_ + 1) * 1024)
                sc.dma("sp", lambda e, c0_=c0_, c1_=c1_: e.dma_start(out=cosT[:, c0_:c1_], in_=self.cosT[:, c0_:c1_]), f"c_cos{q_ % 2}", writes=[("cosT", q_)])
                sc.dma("sp", lambda e, c0_=c0_, c1_=c1_: e.dma_start(out=sinT[:, c0_:c1_], in_=self.sinT[:, c0_:c1_]), f"c_sin{q_ % 2}", writes=[("sinT", q_)])
            uT_all = [("uT", tt) for tt in range(NT)]

            state = dict(wbi=0, pbi=0, soi=0, ri=0, sofi=0)

            def load_w(col0, ncols):
                i = state["wbi"] % 2
                state["wbi"] += 1
                src = self.w1.rearrange("(k p) c -> p k c", p=128)[:, :, col0:col0 + ncols]
                sc.dma("pool", lambda e, i=i: e.dma_start(out=wb[i][:, :, 0:ncols], in_=src), f"wb{i}", writes=[f"wb{i}"])
                return i

            def next_pb():
                j = state["pbi"] % 6
                state["pbi"] += 1
                return j

            def store(dst_ap, src_tile_key, src_ap, slotname):
                sc.dma("sp", lambda e: e.dma_start(out=dst_ap, in_=src_ap), slotname, reads=[src_tile_key])

            def fm_group(wi, c, tg, j, M=128):
                for kc in range(KC):
                    sc.op("pe", lambda e, kc=kc: e.matmul(pb[j][0:M, :], lhsT=wb[wi][:, kc, c * 128:c * 128 + M],
                                                            rhs=uT[:, kc, tg * 512:(tg + 1) * 512], start=(kc == 0), stop=(kc == KC - 1)),
                          reads=[f"wb{wi}"] + uT_all[tg * 4:(tg + 1) * 4], writes=[f"pb{j}"])

            for name, col0 in FM_BLOCKS:
                wi = load_w(col0, 512)
                bidx = (col0 - {"gq": 0, "gk": 512, "dq": 1024, "dk": 2048}[name]) // 512
                if name in ("gq", "gk"):
                    dst = self.gqT if name == "gq" else self.gkT
                    for c in range(4):
                        for tg in range(NG):
                            j = next_pb()
                            fm_group(wi, c, tg, j)
                            k = state["soi"] % 4
                            state["soi"] += 1
                            if (c + tg) % 2 == 0:
                                sc.op("act", lambda e, j=j, k=k: e.activation(out=so[k][:], in_=pb[j][:], func=AF.Copy),
                                      reads=[f"pb{j}"], writes=[f"so{k}"])
                            else:
                                sc.op("dve", lambda e, j=j, k=k: e.tensor_copy(out=so[k][:], in_=pb[j][:]),
                                      reads=[f"pb{j}"], writes=[f"so{k}"])
                            store(dst[c, :, t0 + tg * 512:t0 + (tg + 1) * 512], f"so{k}", so[k][:], f"so{k}")
                else:
                    dst = self.dqT if name == "dq" else self.dkT
                    for c in range(4):
                        hc = bidx * 4 + c
                        for tg in range(NG):
                            ja = next_pb()
                            fm_group(wi, c, tg, ja)
                            ri = state["ri"] % 2
                            state["ri"] += 1
                            k = state["soi"] % 4
                            state["soi"] += 1
                            tsl = slice(tg * 512, (tg + 1) * 512)
                            sc.op("dve", lambda e, ja=ja, ri=ri, tsl=tsl: e.tensor_tensor(out=r1[ri][:], in0=pb[ja][:], in1=cosT[:, tsl], op=ALU.mult),
                                  reads=[f"pb{ja}", ("cosT", tg // 2)], writes=[f"r1_{ri}"])
                            sc.op("dve", lambda e, ja=ja, ri=ri, tsl=tsl: e.tensor_tensor(out=r2[ri][0:64, :], in0=pb[ja][64:128, :], in1=sinT[0:64, tsl], op=ALU.mult),
                                  reads=[f"pb{ja}", ("sinT", tg // 2)], writes=[(f"r2_{ri}", 0)])
                            sc.op("dve", lambda e, ja=ja, ri=ri, tsl=tsl: e.tensor_tensor(out=r2[ri][64:128, :], in0=pb[ja][0:64, :], in1=sinT[64:128, tsl], op=ALU.mult),
                                  reads=[f"pb{ja}", ("sinT", tg // 2)], writes=[(f"r2_{ri}", 1)])
                            sc.op("pool", lambda e, ri=ri, k=k: e.tensor_tensor(out=so[k][:], in0=r1[ri][:], in1=r2[ri][:], op=ALU.add),
                                  reads=[f"r1_{ri}", (f"r2_{ri}", 0), (f"r2_{ri}", 1)], writes=[f"so{k}"])
                            store(dst[hc, :, t0 + tg * 512:t0 + (tg + 1) * 512], f"so{k}", so[k][:], f"so{k}")
            wi = load_w(GLR_OFF, 32)
            for tg in range(NG):
                j = next_pb()
                fm_group(wi, 0, tg, j, M=32)
                k = state["sofi"] % 2
                state["sofi"] += 1
                sc.op("dve", lambda e, j=j, k=k: e.tensor_copy(out=sof[k][:], in_=pb[j][0:32, :]), reads=[f"pb{j}"], writes=[f"sof{k}"])
                store(self.glrT[:, t0 + tg * 512:t0 + (tg + 1) * 512], f"sof{k}", sof[k][:], f"sof{k}")
            for bi, (name, jj) in enumerate(TM_BLOCKS):
                wi = load_w(TM_OFF + bi * 512, 512)
                dst = self.tm[name]
                for tt in range(NT):
                    j = next_pb()
                    for kc in range(KC):
                        sc.op("pe", lambda e, kc=kc, j=j, tt=tt, wi=wi: e.matmul(pb[j][:], lhsT=uT[:, kc, tt * 128:(tt + 1) * 128], rhs=wb[wi][:, kc, :],
                                                                        start=(kc == 0), stop=(kc == KC - 1)),
                              reads=[f"wb{wi}", ("uT", tt)], writes=[f"pb{j}"])
                    k = state["soi"] % 4
                    state["soi"] += 1
                    if name == "gr":
                        sc.op("act", lambda e, j=j, k=k: e.activation(out=so[k][:], in_=pb[j][:], func=AF.Silu), reads=[f"pb{j}"], writes=[f"so{k}"])
                    elif name in ("ga", "gb"):
                        sc.op("act", lambda e, j=j, k=k: e.activation(out=so[k][:], in_=pb[j][:], func=AF.Sigmoid), reads=[f"pb{j}"], writes=[f"so{k}"])
                    elif tt % 2 == 0:
                        sc.op("act", lambda e, j=j, k=k: e.activation(out=so[k][:], in_=pb[j][:], func=AF.Copy), reads=[f"pb{j}"], writes=[f"so{k}"])
                    else:
                        sc.op("dve", lambda e, j=j, k=k: e.tensor_copy(out=so[k][:], in_=pb[j][:]), reads=[f"pb{j}"], writes=[f"so{k}"])
                    store(dst[t0 + tt * 128:t0 + (tt + 1) * 128, jj * 512:(jj + 1) * 512], f"so{k}", so[k][:], f"so{k}")
            sc.flush_wait_all()
            sc.emit(block)


    def phase_gla(self, sc, s, dirn):
        nc, S = self.nc, self.S
        t0 = s * S
        NP, NSS = S // 128, S // 512
        fwd = dirn == 0
        with ExitStack() as es, nc.Block() as block:
            Uinc = self.sb(es, "Uinc", [128, 128], BF16)
            Ustr = self.sb(es, "Ustr", [128, 128], BF16)
            Uf = self.sb(es, "Uf", [128, 256], F32)
            wgkf = self.sb(es, "wgkf", [17, 512], F32)
            mask4 = self.sb(es, "mask4", [128, 4, 128], F32)
            wgk = self.sb(es, "wgk_sb", [17, 512], BF16)
            gnwb = self.sb(es, "gnwb", [128, 1024], F32)
            glra = [self.sb(es, f"glra{i}", [17, 512], BF16) for i in range(2)]
            qT = [self.sb(es, f"gqT{i}", [128, 4, 512], BF16) for i in range(2)]
            kT = [self.sb(es, f"gkT{i}", [128, 4, 512], BF16) for i in range(2)]
            ktm = [self.sb(es, f"gktm{i}", [128, 4, 512], BF16) for i in range(2)]
            vv = [self.sb(es, f"gv{i}", [128, 4, 1024], BF16) for i in range(2)]
            obt = [self.sb(es, f"obt{i}", [128, 1024], F32) for i in range(2)]
            grt = [self.sb(es, f"grt{i}", [128, 1024], BF16) for i in range(2)]
            gat = [self.sb(es, f"gat{i}", [128, 1024], BF16) for i in range(2)]
            e1 = self.sb(es, "g_e1", [128, 512], F32)
            sp = self.sb(es, "g_sp", [128, 512], BF16)
            ebl = self.sb(es, "g_ebl", [128, 512], F32)
            ks = self.sb(es, "g_ks", [128, 512], BF16)
            eb = self.sb(es, "g_eb", [128, 4, 128], F32)
            enb = self.sb(es, "g_enb", [128, 4, 128], F32)
            qt = self.sb(es, "g_qt", [128, 4, 128], BF16)
            kt = self.sb(es, "g_kt", [128, 4, 128], BF16)
            attm = self.sb(es, "g_attm", [128, 4, 128], BF16)
            S32 = [self.sb(es, f"S32_{h}", [128, 256], F32) for h in range(4)]
            Sbf = [self.sb(es, f"Sbf_{h}", [128, 256], BF16) for h in range(4)]
            osb = [self.sb(es, f"osb{i}", [128, 1024], F32) for i in range(2)]
            ysb = [self.sb(es, f"ysb{i}", [128, 1024], F32) for i in range(2)]
            g32 = self.sb(es, "g_g32", [128, 1024], F32)
            junk = self.sb(es, "g_junk", [128, 4, 256], F32)
            stt = [self.sb(es, f"g_st{i}", [128, 12], F32) for i in range(2)]
            bA = self.ps(es, "bA", [128, 512])
            bB = self.ps(es, "bB", [128, 512])
            bO = [self.ps(es, f"bO{h}", [128, 512]) for h in range(4)]
            bKV = [self.ps(es, f"bKV{i}", [128, 512]) for i in range(2)]

            co = 128 * (1 + 2 * dirn)
            sc.dma("sp", lambda e: e.dma_start(out=Uf[:], in_=self.cst[:, co:co + 256]), "c_a", writes=["Uf"])
            sc.op("dve", lambda e: e.tensor_copy(out=Uinc[:], in_=Uf[:, 0:128]), reads=["Uf"], writes=["Uinc"])
            sc.op("dve", lambda e: e.tensor_copy(out=Ustr[:], in_=Uf[:, 128:256]), reads=["Uf"], writes=["Ustr"])
            for h in range(4):
                sc.dma("sp", lambda e, h=h: e.dma_start(out=mask4[:, h, :], in_=self.msk[:, dirn * 128:(dirn + 1) * 128]), "c_c", writes=["mask4"])
            sc.dma("sp", lambda e: e.dma_start(out=wgkf[:], in_=self.wgk[dirn, :, :]), "c_d", writes=["wgkf"])
            sc.op("dve", lambda e: e.tensor_copy(out=wgk[:], in_=wgkf[:]), reads=["wgkf"], writes=["wgk"])
            sc.dma("sp", lambda e: e.dma_start(out=gnwb[:], in_=self.gnw.broadcast_to([128, 1024])), "c_e", writes=["gnwb"])
            for i in range(2):
                sc.op("dve", lambda e, i=i: e.memset(glra[i][:], 1.0), writes=[f"glra{i}"])
            for h in range(4):
                sc.op("dve", lambda e, h=h: e.memset(S32[h][:], 0.0), writes=[f"S32_{h}"])
                sc.op("dve", lambda e, h=h: e.memset(Sbf[h][:], 0.0), writes=[f"Sbf_{h}"])

            if fwd:
                r1, c1, r2, c2 = slice(0, 64), 63, slice(64, 128), 127
            else:
                r1, c1, r2, c2 = slice(64, 128), 64, slice(0, 64), 0
            qscale = float(GLA_DK ** -0.5)

            def load_ss(ss, i):
                tb = t0 + ss * 512
                sc.dma("sp", lambda e: e.dma_start(out=glra[i][0:16, :], in_=self.glrT[dirn * 16:(dirn + 1) * 16, tb:tb + 512]), f"glra{i}", writes=[f"glra{i}"])
                sc.dma("sp", lambda e: e.dma_start(out=qT[i][:], in_=self.gqT[:, :, tb:tb + 512].rearrange("h p t -> p h t")), f"gqT{i}", writes=[f"gqT{i}"])
                sc.dma("sp", lambda e: e.dma_start(out=kT[i][:], in_=self.gkT[:, :, tb:tb + 512].rearrange("h p t -> p h t")), f"gkT{i}", writes=[f"gkT{i}"])
                sc.dma("sp", lambda e: e.dma_start(out=ktm[i][:], in_=self.tm["gk"][tb:tb + 512, :].rearrange("(a p) c -> p a c", p=128)), f"gktm{i}", writes=[f"gktm{i}"])
                sc.dma("sp", lambda e: e.dma_start(out=vv[i][:], in_=self.tm["gv"][tb:tb + 512, :].rearrange("(a p) c -> p a c", p=128)), f"gv{i}", writes=[f"gv{i}"])

            ksA = [ks, self.sb(es, "g_ks_b", [128, 512], BF16)]
            ksB = [self.sb(es, f"g_ksB{i}", [128, 512], BF16) for i in range(2)]
            for q_ in range(2):
                sc.op("pool", lambda e, q_=q_: e.memset(ksA[q_][:], 0.0), writes=[f"ksA{q_}"])
                sc.op("pool", lambda e, q_=q_: e.memset(ksB[q_][:], 0.0), writes=[f"ksB{q_}"])
            eb2 = [eb, self.sb(es, "g_eb_b", [128, 4, 128], F32)]
            qt2 = [qt, self.sb(es, "g_qt_b", [128, 4, 128], BF16)]
            attm2 = [attm, self.sb(es, "g_attm_b", [128, 4, 128], BF16)]

            order = list(range(NSS)) if fwd else list(range(NSS - 1, -1, -1))
            steps = []
            for n, ss in enumerate(order):
                for m, sub in enumerate([0, 1, 2, 3] if fwd else [3, 2, 1, 0]):
                    steps.append(dict(n=n, ss=ss, i=n % 2, sub=sub, pi=(n * 4 + m) % 2, q=(n * 4 + m) % 2,
                                      tok=t0 + ss * 512 + sub * 128, tsl=slice(sub * 128, (sub + 1) * 128), first=(m == 0)))

            def loads(st):
                if fwd:
                    pi, tok = st["pi"], st["tok"]
                    sc.dma("sp", lambda e: e.dma_start(out=obt[pi][:], in_=self.ob[tok:tok + 128, :]), f"obt{pi}", writes=[f"obt{pi}"])
                    sc.dma("sp", lambda e: e.dma_start(out=grt[pi][:], in_=self.tm["gr"][tok:tok + 128, :]), f"grt{pi}", writes=[f"grt{pi}"])
                    sc.dma("sp", lambda e: e.dma_start(out=gat[pi][:], in_=self.tm["ga"][tok:tok + 128, :]), f"gat{pi}", writes=[f"gat{pi}"])

            def pre_a(st):
                i, tsl = st["i"], st["tsl"]
                sc.op("pe", lambda e: e.matmul(bA[:], lhsT=glra[i][0:17, tsl], rhs=wgk[0:17, :], start=True, stop=True),
                      reads=[f"glra{i}", "wgk"], writes=["bA"])
                sc.op("act", lambda e: e.activation(out=e1[:], in_=bA[:], func=AF.Exp, scale=-1.0), reads=["bA"], writes=["e1"])
                sc.op("act", lambda e: e.activation(out=sp[:], in_=e1[:], func=AF.Ln, bias=self.eps_t[:, 2:3]), reads=["e1"], writes=["sp"])

            def pre_b(st):
                i, sub, q = st["i"], st["sub"], st["q"]
                sc.op("pe", lambda e: e.matmul(bA[:], lhsT=Ustr[:], rhs=sp[:], start=True, stop=True), reads=["Ustr", "sp"], writes=["bA"])
                for h in range(4):
                    sc.op("pe", lambda e, h=h: e.matmul(bB[:, h * 128:(h + 1) * 128], lhsT=sp[:, h * 128:(h + 1) * 128], rhs=Uinc[:], start=True, stop=True),
                          reads=["sp", "Uinc"], writes=["bB"])
                sc.op("act", lambda e: e.activation(out=ebl[:], in_=bA[:], func=AF.Exp), reads=["bA"], writes=["ebl"])
                sc.op("act", lambda e: e.activation(out=eb2[q][:].rearrange("p h t -> p (h t)"), in_=bB[:], func=AF.Exp), reads=["bB"], writes=[f"eb{q}"])
                sc.op("act", lambda e: e.activation(out=enb[:].rearrange("p h t -> p (h t)"), in_=bB[:], func=AF.Exp, scale=-1.0), reads=["bB"], writes=["enb"])

            def pre_c(st):
                i, sub, q, tsl = st["i"], st["sub"], st["q"], st["tsl"]
                sc.op("pool", lambda e: e.tensor_tensor(out=ksA[q][r1, :], in0=ktm[i][r1, sub, :], in1=ebl[r1, :], op=ALU.mult),
                      reads=[f"gktm{i}", "ebl"], writes=[f"ksA{q}"])
                sc.op("pool", lambda e: e.tensor_tensor(out=ksB[q][r2, :], in0=ktm[i][r2, sub, :], in1=ebl[r2, :], op=ALU.mult),
                      reads=[f"gktm{i}", "ebl"], writes=[f"ksB{q}"])
                sc.op("dve", lambda e: e.scalar_tensor_tensor(out=qt2[q][:], in0=eb2[q][:], scalar=qscale, in1=qT[i][:, :, tsl], op0=ALU.mult, op1=ALU.mult),
                      reads=[f"eb{q}", f"gqT{i}"], writes=[f"qt{q}"])
                sc.op("pool", lambda e: e.tensor_tensor(out=kt[:], in0=enb[:], in1=kT[i][:, :, tsl], op=ALU.mult),
                      reads=["enb", f"gkT{i}"], writes=["kt"])

            def pre_d(st):
                q = st["q"]
                for h in range(4):
                    sc.op("pe", lambda e, h=h: e.matmul(bA[:, h * 128:(h + 1) * 128], lhsT=kt[:, h, :], rhs=qt2[q][:, h, :], start=True, stop=True),
                          reads=["kt", f"qt{q}"], writes=["bA"])
                sc.op("dve", lambda e: e.tensor_tensor(out=attm2[q][:].rearrange("p h t -> p (h t)"), in0=bA[:], in1=mask4[:].rearrange("p h t -> p (h t)"), op=ALU.mult),
                      reads=["bA", "mask4"], writes=[f"attm{q}"])

            def scan_open(st):
                i, sub, q = st["i"], st["sub"], st["q"]
                for h in range(4):
                    vs = slice(h * 256, (h + 1) * 256)
                    sc.op("pe", lambda e, h=h, vs=vs: e.matmul(bO[h][:, 0:256], lhsT=attm2[q][:, h, :], rhs=vv[i][:, sub, vs], start=True, stop=False),
                          reads=[f"attm{q}", f"gv{i}"], writes=[f"bO{h}"])
                    sc.op("pe", lambda e, h=h: e.matmul(bO[h][r1, 0:256], lhsT=qt2[q][:, h, r1], rhs=Sbf[h][:], start=False, stop=True),
                          reads=[f"qt{q}", f"Sbf_{h}"], writes=[f"bO{h}"])

            def scan_kv(st, h):
                i, sub, q = st["i"], st["sub"], st["q"]
                vs = slice(h * 256, (h + 1) * 256)
                kb = h % 2
                hs = slice(h * 128, (h + 1) * 128)
                sc.op("pe", lambda e: e.matmul(bKV[kb][:, 0:256], lhsT=ksA[q][:, hs], rhs=vv[i][:, sub, vs], start=True, stop=True),
                      reads=[f"ksA{q}", f"gv{i}"], writes=[f"bKV{kb}"])
                sc.op("pe", lambda e: e.matmul(bKV[kb][:, 256:512], lhsT=ksB[q][:, hs], rhs=vv[i][:, sub, vs], start=True, stop=True),
                      reads=[f"ksB{q}", f"gv{i}"], writes=[f"bKV{kb}"])

            def scan_head(st, h):
                i, sub, q, pi = st["i"], st["sub"], st["q"], st["pi"]
                vs = slice(h * 256, (h + 1) * 256)
                kb = h % 2
                sc.op("dve", lambda e: e.scalar_tensor_tensor(out=Sbf[h][:], in0=S32[h][:], scalar=eb2[q][:, h, c1:c1 + 1], in1=bKV[kb][:, 0:256], op0=ALU.mult, op1=ALU.add),
                      reads=[f"S32_{h}", f"eb{q}", f"bKV{kb}"], writes=[f"Sbf_{h}"])
                sc.op("dve", lambda e: e.scalar_tensor_tensor(out=S32[h][:], in0=S32[h][:], scalar=eb2[q][:, h, c1:c1 + 1], in1=bKV[kb][:, 0:256], op0=ALU.mult, op1=ALU.add),
                      reads=[f"S32_{h}", f"eb{q}", f"bKV{kb}"], writes=[f"S32_{h}"])
                sc.op("pe", lambda e: e.matmul(bO[h][r2, 0:256], lhsT=qt2[q][:, h, r2], rhs=Sbf[h][:], start=False, stop=True),
                      reads=[f"qt{q}", f"Sbf_{h}"], writes=[f"bO{h}"])
                sc.op("dve", lambda e: e.scalar_tensor_tensor(out=S32[h][:], in0=S32[h][:], scalar=eb2[q][:, h, c2:c2 + 1], in1=bKV[kb][:, 256:512], op0=ALU.mult, op1=ALU.add),
                      reads=[f"S32_{h}", f"eb{q}", f"bKV{kb}"], writes=[f"S32_{h}"])
                sc.op("act", lambda e: e.activation(out=Sbf[h][:], in_=S32[h][:], func=AF.Copy), reads=[f"S32_{h}"], writes=[f"Sbf_{h}"])
                if fwd:
                    sc.op("dve", lambda e: e.tensor_tensor(out=osb[pi][:, vs], in0=bO[h][:, 0:256], in1=obt[pi][:, vs], op=ALU.add),
                          reads=[f"bO{h}", f"obt{pi}"], writes=[(f"osb{pi}", h)])
                else:
                    sc.op("act", lambda e: e.activation(out=osb[pi][:, vs], in_=bO[h][:, 0:256], func=AF.Copy),
                          reads=[f"bO{h}"], writes=[(f"osb{pi}", h)])

            def epilogue(st):
                pi, tok = st["pi"], st["tok"]
                okeys = [(f"osb{pi}", h) for h in range(4)]
                if not fwd:
                    sc.dma("act", lambda e: e.dma_start(out=self.ob[tok:tok + 128, :], in_=osb[pi][:]), f"osb{pi}", reads=okeys, writes=[("ob", tok)])
                    return
                for h in range(4):
                    vs = slice(h * 256, (h + 1) * 256)
                    sc.op("act", lambda e, h=h, vs=vs: e.activation(out=junk[:, h, :], in_=osb[pi][:, vs], func=AF.Square, accum_out=stt[pi][:, h:h + 1]),
                          reads=[(f"osb{pi}", h)], writes=[f"junk{h}", (f"gst{pi}", h)])
                sc.op("act", lambda e: e.activation(out=stt[pi][:, 4:8], in_=stt[pi][:, 0:4], func=AF.Ln, scale=1.0 / GLA_DV, bias=self.eps_t[:, 0:1]),
                      reads=[(f"gst{pi}", h) for h in range(4)], writes=[f"gst{pi}"])
                sc.op("act", lambda e: e.activation(out=stt[pi][:, 8:12], in_=stt[pi][:, 4:8], func=AF.Exp, scale=-0.5), reads=[f"gst{pi}"], writes=[f"gst{pi}"])
                sc.op("pool", lambda e: e.tensor_tensor(out=g32[:], in0=grt[pi][:], in1=gat[pi][:], op=ALU.mult), reads=[f"grt{pi}", f"gat{pi}"], writes=["g32"])
                sc.op("pool", lambda e: e.tensor_tensor(out=g32[:], in0=g32[:], in1=gnwb[:], op=ALU.mult), reads=["g32", "gnwb"], writes=["g32"])
                for h in range(4):
                    vs = slice(h * 256, (h + 1) * 256)
                    sc.op("dve", lambda e, h=h, vs=vs: e.scalar_tensor_tensor(out=ysb[pi][:, vs], in0=osb[pi][:, vs], scalar=stt[pi][:, 8 + h:9 + h], in1=g32[:, vs], op0=ALU.mult, op1=ALU.mult),
                          reads=[(f"osb{pi}", h), f"gst{pi}", "g32"], writes=[(f"ysb{pi}", h)])
                sc.dma("act", lambda e: e.dma_start(out=self.ya[tok:tok + 128, :], in_=ysb[pi][:]), f"ysb{pi}",
                       reads=[(f"ysb{pi}", h) for h in range(4)], writes=[("ya", tok)])

            load_ss(order[0], 0)
            if NSS > 1:
                load_ss(order[1], 1)
            loads(steps[0])
            for f_ in (pre_a, pre_b, pre_c, pre_d):
                f_(steps[0])
            for p, st in enumerate(steps):
                nx = steps[p + 1] if p + 1 < len(steps) else None
                if nx is not None:
                    loads(nx)
                scan_kv(st, 0)
                scan_kv(st, 1)
                scan_open(st)
                parts = (pre_a, pre_b, pre_c)
                for h in range(4):
                    if nx is not None and h < 3:
                        parts[h](nx)
                    scan_head(st, h)
                    if h + 2 < 4:
                        scan_kv(st, h + 2)
                if nx is not None:
                    pre_d(nx)
                epilogue(st)
                if nx is not None and nx["first"] and nx["n"] + 1 < NSS:
                    load_ss(order[nx["n"] + 1], (nx["n"] + 1) % 2)
            sc.flush_wait_all()
            sc.emit(block)

    def phase_attn(self, sc, s, wprep=False):
        nc, S, NT, NG = self.nc, self.S, self.NT, self.NG
        t0 = s * S
        LAG = 2
        with ExitStack() as es, nc.Block() as block:
            if wprep:
                wt = self.sb(es, "wprep", [128, KC, FFN], BF16)
                wsrc = self.w_fi.rearrange("(k p) c -> p k c", p=128)

                def wprep_load(g):
                    for kc in range(KC):
                        sc.dma("pool", lambda e, kc=kc: e.dma_start(out=wt[:, kc, :], in_=wsrc[:, kc, g * FFN:(g + 1) * FFN]), f"wp{kc % 4}", writes=[("wprep", kc)])

                def wprep_store(g):
                    allk = [("wprep", kc) for kc in range(KC)]
                    for f in range(FC):
                        sc.dma("sp", lambda e, f=f: e.dma_start(out=self.wfib[f, :, g, :, :], in_=wt[:, :, f * 128:(f + 1) * 128]), f"wps{f % 4}",
                               reads=allk, writes=[("wfib", f, g)])
                wprep_load(0)
            KT = [self.sb(es, f"aKT{i}", [128, 2, S], BF16) for i in range(2)]
            VA = [self.sb(es, f"aVA{i}", [128, NT, 258], BF16) for i in range(2)]
            QT = [self.sb(es, f"aQT{i}", [128, 2, 512], BF16) for i in range(2)]
            PT = [self.sb(es, f"aPT{i}", [128, 1024], BF16) for i in range(2)]
            A0 = self.sb(es, "aA0", [128, 4, 256], F32)
            o2 = self.sb(es, "ao2", [128, 4, 256], F32)
            gbt = [self.sb(es, f"agb{i}", [128, 4, 256], BF16) for i in range(2)]
            g32 = self.sb(es, "ag32", [128, 4, 256], F32)
            ybs = [self.sb(es, f"aybs{i}", [128, 4, 256], F32) for i in range(2)]
            slnb = self.sb(es, "aslnb", [128, 256], F32)
            lam = self.sb(es, "alam", [1, 512], F32)
            lt = self.sb(es, "alt", [1, 16], F32)
            ones1 = self.sb(es, "aones1", [1, 128], F32)
            nlam = self.sb(es, "anlam", [128, 1], F32)
            rr = self.sb(es, "arr", [128, 16], F32)
            stt = self.sb(es, "ast", [128, 12], F32)
            junk = self.sb(es, "ajunk", [128, 4, 256], F32)
            ACCall = self.ps(es, "aACC", [128, 4, 512])
            ACC = [ACCall[:, i, :] for i in range(4)]
            raw = [self.sb(es, f"araw{i}", [128, 4, 258], F32) for i in range(2)]
            SB = [self.ps(es, f"aSB{i}", [128, 1024]) for i in range(2)]

            sc.dma("sp", lambda e: e.dma_start(out=lam[:], in_=self.lam4[:, :]), "c_a", writes=["lam"])
            sc.dma("sp", lambda e: e.dma_start(out=slnb[:], in_=self.sln.broadcast_to([128, 256])), "c_b", writes=["slnb"])
            sc.op("dve", lambda e: e.memset(ones1[:], 1.0), writes=["ones1"])
            sc.op("dve", lambda e: e.tensor_tensor(out=lam[:, 0:128], in0=lam[:, 0:128], in1=lam[:, 128:256], op=ALU.mult), reads=["lam"], writes=["lam"])
            sc.op("dve", lambda e: e.tensor_tensor(out=lam[:, 256:384], in0=lam[:, 256:384], in1=lam[:, 384:512], op=ALU.mult), reads=["lam"], writes=["lam"])
            sc.op("act", lambda e: e.activation(out=lam[:, 128:256], in_=lam[:, 0:128], func=AF.Copy, accum_out=lt[:, 0:1]), reads=["lam"], writes=["lam", "lt"])
            sc.op("act", lambda e: e.activation(out=lam[:, 384:512], in_=lam[:, 256:384], func=AF.Copy, accum_out=lt[:, 1:2]), reads=["lam"], writes=["lam", "lt"])
            sc.op("act", lambda e: e.activation(out=lt[:, 2:4], in_=lt[:, 0:2], func=AF.Exp), reads=["lt"], writes=["lt"])
            sc.op("dve", lambda e: e.tensor_tensor(out=lt[:, 4:5], in0=lt[:, 3:4], in1=lt[:, 2:3], op=ALU.subtract), reads=["lt"], writes=["lt"])
            sc.op("dve", lambda e: e.tensor_scalar(out=lt[:, 5:6], in0=lt[:, 4:5], scalar1=-LAMBDA_INIT, scalar2=None, op0=ALU.add), reads=["lt"], writes=["lt"])
            sc.op("pe", lambda e: e.matmul(ACC[0][:, 0:1], lhsT=ones1[:], rhs=lt[:, 5:6], start=True, stop=True), reads=["ones1", "lt"], writes=["aACC0"])
            sc.op("dve", lambda e: e.tensor_copy(out=nlam[:], in_=ACC[0][:, 0:1]), reads=["aACC0"], writes=["nlam"])
            sc.op("dve", lambda e: e.tensor_scalar(out=slnb[:], in0=slnb[:], scalar1=float(1.0 - LAMBDA_INIT), scalar2=None, op0=ALU.mult), reads=["slnb"], writes=["slnb"])
            for i in range(2):
                sc.op("pool", lambda e, i=i: e.memset(VA[i][:, :, 256:258], 1.0), writes=[f"aVA{i}"])

            def load_head(h, i):
                for c in range(2):
                    sc.dma("sp", lambda e, c=c: e.dma_start(out=KT[i][:, c, :], in_=self.dkT[2 * h + c, :, t0:t0 + S]), f"aKT{i}", writes=[f"aKT{i}"])
                for a0 in range(0, NT, 8):
                    a1 = min(NT, a0 + 8)
                    sc.dma("sp", lambda e, a0=a0, a1=a1: e.dma_start(
                        out=VA[i][:, a0:a1, 0:256],
                        in_=self.tm["dv"][t0 + a0 * 128:t0 + a1 * 128, h * 256:(h + 1) * 256].rearrange("(a p) c -> p a c", p=128)),
                        f"aVA{i}", writes=[f"aVA{i}"])

            scale = float(DIFF_HD ** -0.5)
            load_head(0, 0)
            work = [(h, qg) for h in range(DIFF_H) for qg in range(NG)]

            def load_q(widx):
                h, qg = work[widx]
                qi, tb = widx % 2, t0 + qg * 512
                for c in range(2):
                    sc.dma("sp", lambda e, c=c: e.dma_start(out=QT[qi][:, c, :], in_=self.dqT[2 * h + c, :, tb:tb + 512]), f"aQT{qi}", writes=[f"aQT{qi}"])
                sc.dma("sp", lambda e: e.dma_start(out=gbt[qi][:], in_=self.tm["gb"][tb:tb + 512, h * 256:(h + 1) * 256].rearrange("(a p) c -> p a c", p=128)),
                       f"agb{qi}", writes=[f"agb{qi}"])
            load_q(0)
            NP2 = NT // 2
            pairs = [(widx, h, qg, c, p) for widx, (h, qg) in enumerate(work) for c in range(2) for p in range(NP2)]

            def emit_scores(g):
                widx, h, qg, c, p = pairs[g]
                b_, qi, hi = g % 2, widx % 2, h % 2
                for u in range(2):
                    k_ = 2 * p + u
                    sc.op("pe", lambda e, u=u, k_=k_: e.matmul(SB[b_][:, u * 512:(u + 1) * 512], lhsT=KT[hi][:, c, k_ * 128:(k_ + 1) * 128], rhs=QT[qi][:, c, :], start=True, stop=True),
                          reads=[f"aKT{hi}", f"aQT{qi}"], writes=[f"aSB{b_}"])
                sc.op("act", lambda e: e.activation(out=PT[b_][:], in_=SB[b_][:], func=AF.Exp, scale=scale, bias=self.eps_t[:, 3:4]),
                      reads=[f"aSB{b_}"], writes=[f"aPT{b_}"])

            def emit_pv(g):
                widx, h, qg, c, p = pairs[g]
                b_, hi = g % 2, h % 2
                for u in range(2):
                    k_ = 2 * p + u
                    for qs in range(4):
                        sc.op("pe", lambda e, u=u, k_=k_, qs=qs: e.matmul(ACC[qs][:, 0:257], lhsT=PT[b_][:, u * 512 + qs * 128:u * 512 + (qs + 1) * 128], rhs=VA[hi][:, k_, 0:257],
                                                                   start=(k_ == 0), stop=(k_ == NT - 1)),
                              reads=[f"aPT{b_}", f"aVA{hi}"], writes=[f"aACC{qs}"])

            def emit_drain(g):
                widx, h, qg, c, p = pairs[g]
                qi, tb = widx % 2, t0 + qg * 512
                sc.op("dve", lambda e: e.tensor_copy(out=raw[c][:, :, 0:257], in_=ACCall[:, :, 0:257]),
                      reads=[f"aACC{q_}" for q_ in range(4)], writes=[f"araw{c}"])
                sc.op("dve", lambda e: e.reciprocal(out=rr[:, c * 4:c * 4 + 4], in_=raw[c][:, :, 256]), reads=[f"araw{c}"], writes=[("rr", c)])
                if c == 1:
                    sc.op("dve", lambda e: e.tensor_scalar(out=rr[:, 12:16], in0=rr[:, 4:8], scalar1=nlam[:, 0:1], scalar2=None, op0=ALU.mult),
                          reads=[("rr", 1), "nlam"], writes=[("rr", 2)])
                for qs in range(4):
                    if c == 0:
                        sc.op("dve", lambda e, qs=qs: e.tensor_scalar(out=A0[:, qs, :], in0=raw[0][:, qs, 0:256], scalar1=rr[:, qs:qs + 1], scalar2=None, op0=ALU.mult),
                              reads=["araw0", ("rr", 0)], writes=[("A0", qs)])
                    else:
                        sc.op("dve", lambda e, qs=qs: e.scalar_tensor_tensor(out=o2[:, qs, :], in0=raw[1][:, qs, 0:256], scalar=rr[:, 12 + qs:13 + qs], in1=A0[:, qs, :], op0=ALU.mult, op1=ALU.add),
                              reads=["araw1", ("rr", 2), ("A0", qs)], writes=[("o2", qs)])
                if c == 0:
                    return
                for qs in range(4):
                    sc.op("act", lambda e, qs=qs: e.activation(out=junk[:, qs, :], in_=o2[:, qs, :], func=AF.Square, accum_out=stt[:, qs:qs + 1]), reads=[("o2", qs)], writes=[f"ajunk{qs}", ("ast", qs)])
                sc.op("act", lambda e: e.activation(out=stt[:, 4:8], in_=stt[:, 0:4], func=AF.Ln, scale=1.0 / 256, bias=self.eps_t[:, 1:2]), reads=[("ast", q_) for q_ in range(4)], writes=["ast"])
                sc.op("act", lambda e: e.activation(out=stt[:, 8:12], in_=stt[:, 4:8], func=AF.Exp, scale=-0.5), reads=["ast"], writes=["ast"])
                for qs in range(4):
                    sc.op("pool", lambda e, qs=qs: e.tensor_tensor(out=g32[:, qs, :], in0=gbt[qi][:, qs, :], in1=slnb[:], op=ALU.mult), reads=[f"agb{qi}", "slnb"], writes=[("ag32", qs)])
                for qs in range(4):
                    sc.op("dve", lambda e, qs=qs: e.scalar_tensor_tensor(out=ybs[qi][:, qs, :], in0=o2[:, qs, :], scalar=stt[:, 8 + qs:9 + qs], in1=g32[:, qs, :], op0=ALU.mult, op1=ALU.mult),
                          reads=[("o2", qs), "ast", ("ag32", qs)], writes=[(f"aybs{qi}", qs)])
                sc.dma("pool", lambda e: e.dma_start(out=self.yb[tb:tb + 512, h * 256:(h + 1) * 256].rearrange("(a p) c -> p a c", p=128), in_=ybs[qi][:]),
                       f"aybs{qi}", reads=[(f"aybs{qi}", q_) for q_ in range(4)], writes=[("yb", tb, h)])

            total = len(pairs)
            for g in range(total + 1):
                deferred = []
                if g < total:
                    widx, h, qg, c, p = pairs[g]
                    if p == 0 and c == 0:
                        if widx + 1 < len(work):
                            deferred.append(lambda widx=widx: load_q(widx + 1))
                        if qg == 0:
                            if h + 1 < DIFF_H:
                                deferred.append(lambda h=h: load_head(h + 1, (h + 1) % 2))
                            if wprep and h == 1:
                                deferred.append(lambda: (wprep_store(0), wprep_load(1)))
                            if wprep and h == 3:
                                deferred.append(lambda: wprep_store(1))
                    emit_scores(g)
                if g >= 1:
                    emit_pv(g - 1)
                    if pairs[g - 1][4] == NP2 - 1:
                        emit_drain(g - 1)
                for d in deferred:
                    d()
            sc.flush_wait_all()
            sc.emit(block)

    def phase_wprep(self, sc):
        nc = self.nc
        with ExitStack() as es, nc.Block() as block:
            wt = self.sb(es, "wprep", [128, KC, 2 * FFN], BF16)
            src = self.w_fi.rearrange("(k p) c -> p k c", p=128)
            for kc in range(KC):
                sc.dma("pool", lambda e, kc=kc: e.dma_start(out=wt[:, kc, :], in_=src[:, kc, :]), f"wp{kc % 4}", writes=[("wprep", kc)])
            allk = [("wprep", kc) for kc in range(KC)]
            for f in range(FC):
                for g in range(2):
                    c0 = g * FFN + f * 128
                    sc.dma("sp", lambda e, f=f, g=g, c0=c0: e.dma_start(out=self.wfib[f, :, g, :, :], in_=wt[:, :, c0:c0 + 128]), f"wps{(2 * f + g) % 4}",
                           reads=allk, writes=[("wfib", f, g)])
            sc.flush_wait_all()
            sc.emit(block)

    def phase_ffn(self, sc):
        nc, S, T = self.nc, self.S, self.T
        NGT = T // 512
        with ExitStack() as es, nc.Block() as block:
            wo = self.sb(es, "f_wo", [128, KC, D], BF16)
            wfo = self.sb(es, "f_wfo", [128, FC, D], BF16)
            wfi = [self.sb(es, f"f_wfi{i}", [128, 2, KC, 128], BF16) for i in range(4)]
            ident = self.sb(es, "f_ident", [128, 128], BF16)
            identf = self.sb(es, "f_identf", [128, 128], F32)
            nwfb = self.sb(es, "f_nwfb", [128, D], F32)
            nwfin = self.sb(es, "f_nwfin", [128, D], F32)
            la = [self.sb(es, f"f_la{i}", [128, D], F32) for i in range(2)]
            lb = [self.sb(es, f"f_lb{i}", [128, D], F32) for i in range(2)]
            lx = [self.sb(es, f"f_lx{i}", [128, D], F32) for i in range(2)]
            mb = [self.sb(es, f"f_mb{i}", [128, D], BF16) for i in range(2)]
            mT = self.sb(es, "f_mT", [128, KC, 512], BF16)
            hT = self.sb(es, "f_hT", [128, KC, 512], BF16)
            hsb = self.sb(es, "f_h", [128, 4, D], F32)
            aT = self.sb(es, "f_aT", [128, FC, 512], BF16)
            sg = [self.sb(es, f"f_sg{i}", [128, 512], F32) for i in range(2)]
            ot = [self.sb(es, f"f_ot{i}", [128, D], F32) for i in range(2)]
            stt = [self.sb(es, f"f_st{i}", [128, 4], F32) for i in range(2)]
            psT = [self.ps(es, f"f_psT{i}", [128, 1024], BF16) for i in range(2)]
            pb = [self.ps(es, f"f_pb{i}", [128, 512]) for i in range(6)]

            sc.dma("pool", lambda e: e.dma_start(out=wo[:], in_=self.w_out.rearrange("(k p) c -> p k c", p=128)), "pw_a", writes=["wo"])
            for f0 in range(0, FC, 2):
                sc.dma("pool", lambda e, f0=f0: e.dma_start(out=wfo[:, f0:f0 + 2, :], in_=self.w_fo[f0 * 128:(f0 + 2) * 128, :].rearrange("(k p) c -> p k c", p=128)),
                       f"pw_b{(f0 // 2) % 2}", writes=[("wfo", f0)])
            wfo_all = [("wfo", f0) for f0 in range(0, FC, 2)]
            sc.dma("sp", lambda e: e.dma_start(out=identf[:], in_=self.cst[:, 0:128]), "c_c", writes=["identf"])
            sc.op("dve", lambda e: e.tensor_copy(out=ident[:], in_=identf[:]), reads=["identf"], writes=["ident"])
            sc.dma("sp", lambda e: e.dma_start(out=nwfb[:], in_=self.nw_ffnr.broadcast_to([128, D])), "c_d", writes=["nwfb"])
            sc.dma("sp", lambda e: e.dma_start(out=nwfin[:], in_=self.nw_fin.broadcast_to([128, D])), "c_e", writes=["nwfin"])

            xf = self.x.rearrange("n s d -> (n s) d")
            yf = self.y.rearrange("n s d -> (n s) d")
            st8 = dict(pbi=0, li=0, wi=0, sgi=0, oti=0, pti=0)

            def next_pb():
                j = st8["pbi"] % 6
                st8["pbi"] += 1
                return j

            def norm_rows(src_ap, src_keys, i):
                sc.op("act", lambda e: e.activation(out=mb[i][:], in_=src_ap, func=AF.Square, accum_out=stt[i][:, 0:1]), reads=src_keys, writes=[f"mb{i}", f"fst{i}"])
                sc.op("act", lambda e: e.activation(out=stt[i][:, 1:2], in_=stt[i][:, 0:1], func=AF.Ln, scale=1.0 / D, bias=self.eps_t[:, 0:1]), reads=[f"fst{i}"], writes=[f"fst{i}"])
                sc.op("act", lambda e: e.activation(out=stt[i][:, 2:3], in_=stt[i][:, 1:2], func=AF.Exp, scale=-0.5), reads=[f"fst{i}"], writes=[f"fst{i}"])

            def transpose_to(dstT, dkey, i, tt_, on_act):
                pt = st8["pti"] % 2
                st8["pti"] += 1
                for kc in range(KC):
                    sc.op("pe", lambda e, kc=kc: e.transpose(psT[pt][:, kc * 128:(kc + 1) * 128], mb[i][:, kc * 128:(kc + 1) * 128], ident[:]),
                          reads=[f"mb{i}", "ident"], writes=[f"f_psT{pt}"])
                src = psT[pt][:].rearrange("p (k t) -> p k t", k=KC)
                if on_act:
                    sc.op("act", lambda e: e.activation(out=dstT[:, :, tt_ * 128:(tt_ + 1) * 128], in_=src, func=AF.Copy), reads=[f"f_psT{pt}"], writes=[(dkey, tt_)])
                else:
                    sc.op("dve", lambda e: e.tensor_copy(out=dstT[:, :, tt_ * 128:(tt_ + 1) * 128], in_=src), reads=[f"f_psT{pt}"], writes=[(dkey, tt_)])

            for g in range(NGT):
                tb = g * 512

                def s1a(tt_):
                    i = tt_ % 2
                    r0 = tb + tt_ * 128
                    sc.dma("sp", lambda e: e.dma_start(out=la[i][:], in_=self.ya[r0:r0 + 128, :]), f"f_la{i}", writes=[f"la{i}"])
                    sc.dma("sp", lambda e: e.dma_start(out=lb[i][:], in_=self.yb[r0:r0 + 128, :]), f"f_lb{i}", writes=[f"lb{i}"])
                    sc.op("dve", lambda e: e.tensor_tensor(out=mb[i][:], in0=la[i][:], in1=lb[i][:], op=ALU.add), reads=[f"la{i}", f"lb{i}"], writes=[f"mb{i}"])

                s1a(0)
                for tt_ in range(4):
                    if tt_ + 1 < 4:
                        s1a(tt_ + 1)
                    transpose_to(mT, "mT", tt_ % 2, tt_, on_act=True)

                def s2a(tt_):
                    i = tt_ % 2
                    r0 = tb + tt_ * 128
                    sc.dma("sp", lambda e: e.dma_start(out=lx[i][:], in_=xf[r0:r0 + 128, :]), f"f_lx{i}", writes=[f"lx{i}"])
                    for nh in range(2):
                        j = next_pb()
                        for kc in range(KC):
                            sc.op("pe", lambda e, kc=kc, j=j, nh=nh: e.matmul(pb[j][:], lhsT=mT[:, kc, tt_ * 128:(tt_ + 1) * 128], rhs=wo[:, kc, nh * 512:(nh + 1) * 512],
                                                                          start=(kc == 0), stop=(kc == KC - 1)),
                                  reads=[("mT", tt_), "wo"], writes=[f"f_pb{j}"])
                        sc.op("dve", lambda e, j=j, nh=nh: e.tensor_tensor(out=hsb[:, tt_, nh * 512:(nh + 1) * 512], in0=pb[j][:], in1=lx[i][:, nh * 512:(nh + 1) * 512], op=ALU.add),
                              reads=[f"f_pb{j}", f"lx{i}"], writes=[("h", tt_)])
                    norm_rows(hsb[:, tt_, :], [("h", tt_)], i)
                    sc.op("dve", lambda e: e.scalar_tensor_tensor(out=mb[i][:], in0=hsb[:, tt_, :], scalar=stt[i][:, 2:3], in1=nwfb[:], op0=ALU.mult, op1=ALU.mult),
                          reads=[("h", tt_), f"fst{i}", "nwfb"], writes=[f"mb{i}"])

                s2a(0)
                for tt_ in range(4):
                    if tt_ + 1 < 4:
                        s2a(tt_ + 1)
                    transpose_to(hT, "hT", tt_ % 2, tt_, on_act=False)
                hT_all = [("hT", k) for k in range(4)]
                for f in range(FC):
                    wi = st8["wi"] % 4
                    st8["wi"] += 1
                    sc.dma("sp", lambda e, wi=wi, f=f: e.dma_start(out=wfi[wi][:], in_=self.wfib[f, :, :, :, :]), f"f_wfi{wi}", reads=[("wfib", f, 0), ("wfib", f, 1)], writes=[f"wfi{wi}"])
                    jg, ju = next_pb(), next_pb()
                    for gi, j in ((0, jg), (1, ju)):
                        for kc in range(KC):
                            sc.op("pe", lambda e, kc=kc, j=j, gi=gi, wi=wi: e.matmul(pb[j][:], lhsT=wfi[wi][:, gi, kc, :], rhs=hT[:, kc, :], start=(kc == 0), stop=(kc == KC - 1)),
                                  reads=[f"wfi{wi}"] + hT_all, writes=[f"f_pb{j}"])
                    si = st8["sgi"] % 2
                    st8["sgi"] += 1
                    sc.op("act", lambda e, jg=jg, si=si: e.activation(out=sg[si][:], in_=pb[jg][:], func=AF.Silu), reads=[f"f_pb{jg}"], writes=[f"sg{si}"])
                    sc.op("dve", lambda e, ju=ju, si=si, f=f: e.tensor_tensor(out=aT[:, f, :], in0=pb[ju][:], in1=sg[si][:], op=ALU.mult),
                          reads=[f"f_pb{ju}", f"sg{si}"], writes=[("aT", f)])
                aT_all = [("aT", f) for f in range(FC)]
                for tt_ in range(4):
                    oi = st8["oti"] % 2
                    st8["oti"] += 1
                    r0 = tb + tt_ * 128
                    for nh in range(2):
                        j = next_pb()
                        for f in range(FC):
                            sc.op("pe", lambda e, f=f, j=j, nh=nh, tt_=tt_: e.matmul(pb[j][:], lhsT=aT[:, f, tt_ * 128:(tt_ + 1) * 128], rhs=wfo[:, f, nh * 512:(nh + 1) * 512],
                                                                                  start=(f == 0), stop=(f == FC - 1)),
                                  reads=aT_all + wfo_all, writes=[f"f_pb{j}"])
                        sc.op("dve", lambda e, j=j, nh=nh, tt_=tt_: e.tensor_tensor(out=hsb[:, tt_, nh * 512:(nh + 1) * 512], in0=pb[j][:], in1=hsb[:, tt_, nh * 512:(nh + 1) * 512], op=ALU.add),
                              reads=[f"f_pb{j}", ("h", tt_)], writes=[("h", tt_)])
                    norm_rows(hsb[:, tt_, :], [("h", tt_)], oi)
                    sc.op("dve", lambda e, oi=oi, tt_=tt_: e.scalar_tensor_tensor(out=ot[oi][:], in0=hsb[:, tt_, :], scalar=stt[oi][:, 2:3], in1=nwfin[:], op0=ALU.mult, op1=ALU.mult),
                          reads=[("h", tt_), f"fst{oi}", "nwfin"], writes=[f"ot{oi}"])
                    sc.dma("pool", lambda e, oi=oi, r0=r0: e.dma_start(out=yf[r0:r0 + 128, :], in_=ot[oi][:]), f"f_ot{oi}", reads=[f"ot{oi}"], writes=[("y", r0)])
            sc.flush_wait_all()
            sc.emit(block)

    def build(self):
        nc = self.nc
        self.declare()
        with ExitStack() as es:
            sc = Sched(nc, es)
            self.eps_t = self.sb(es, "eps_t", [128, 4], F32)
            with nc.Block() as block:
                sc.op("dve", lambda e: e.memset(self.eps_t[:, 0:1], NORM_EPS), writes=["eps"])
                sc.op("dve", lambda e: e.memset(self.eps_t[:, 1:2], SUBLN_EPS), writes=["eps"])
                sc.op("dve", lambda e: e.memset(self.eps_t[:, 2:3], 1.0), writes=["eps"])
                sc.op("dve", lambda e: e.memset(self.eps_t[:, 3:4], 0.0), writes=["eps"])
                sc.emit(block)
            for s in range(self.NSEQ):
                if "p1" in self.phases:
                    self.phase1(sc, s)
                if "gla" in self.phases:
                    self.phase_gla(sc, s, 1)
                    self.phase_gla(sc, s, 0)
                if "attn" in self.phases:
                    self.phase_attn(sc, s, wprep=("ffn" in self.phases and s == self.NSEQ - 1))
            if "ffn" in self.phases:
                if "attn" not in self.phases:
                    self.phase_wprep(sc)
                self.phase_ffn(sc)
        return nc


def host_consts(S):
    inv_freq = 1.0 / (10000.0 ** (np.arange(0, DIFF_HD, 2, dtype=np.float32) / DIFF_HD))
    pos = np.arange(S, dtype=np.float32)
    fr = pos[None, :] * np.concatenate([inv_freq, inv_freq])[:, None].astype(np.float32)
    cosT = np.cos(fr).astype(np.float32)
    sgn = np.where(np.arange(128) < 64, -1.0, 1.0).astype(np.float32)[:, None]
    sinT = (np.sin(fr) * sgn).astype(np.float32)
    p = np.arange(128)
    same = (p[:, None] // 64) == (p[None, :] // 64)
    sI, tI = p[:, None], p[None, :]
    g = -1.0 / 16.0
    ident = np.eye(128, dtype=np.float32)
    uinc_f = (same & (sI <= tI)) * g
    ustr_f = (same & (sI > tI)) * g
    uinc_b = (same & (sI >= tI)) * g
    ustr_b = (same & (sI < tI)) * g
    cst = np.concatenate([ident, uinc_f, ustr_f, uinc_b, ustr_b, np.zeros((128, 128))], axis=1).astype(np.float32)
    m_f = (same & (sI <= tI)).astype(np.float32)
    m_b = (same & (sI >= tI)).astype(np.float32)
    msk = np.concatenate([m_f, m_b], axis=1).astype(np.float32)
    return cosT, sinT, cst, msk


def host_inputs(inp, S):
    c = np.ascontiguousarray
    w_in = np.asarray(inp["w_in"])[0]
    w1 = c(w_in[:, _w1_columns()])
    cosT, sinT, cst, msk = host_consts(S)
    wgk = np.concatenate([np.asarray(inp["w_gk2"])[0], np.asarray(inp["b_gk"])[0][:, None, :]], axis=1)
    shared = dict(
        w1=w1,
        nw_mixr=c(np.asarray(inp["norm_mix_w"])[0].reshape(1, D)),
        nw_ffnr=c(np.asarray(inp["norm_ffn_w"])[0].reshape(1, D)),
        nw_fin=c(np.asarray(inp["norm_final_w"]).reshape(1, D)),
        cosT=cosT, sinT=sinT, cst=cst, msk=msk,
        wgk=c(wgk.astype(np.float32)),
        gnw=c(np.tile(np.asarray(inp["gla_norm_w"])[0], 4).reshape(1, 1024)),
        sln=c(np.asarray(inp["diff_subln_w"])[0].reshape(1, 256)),
        lam4=c(np.concatenate([np.asarray(inp[k])[0] for k in ("lambda_q1", "lambda_k1", "lambda_q2", "lambda_k2")]).reshape(1, 512)),
        w_out=c(np.asarray(inp["w_out"])[0]),
        w_fi=c(np.asarray(inp["w_ffn_in"])[0]),
        w_fo=c(np.asarray(inp["w_ffn_out"])[0]),
    )
    return {k: np.asarray(v, dtype=np.float32) for k, v in shared.items()}


def kernel(**inputs):
    x = np.asarray(inputs["x"], dtype=np.float32)
    B, S, _ = x.shape
    nseq = B // N_CORES
    b = Builder(S, nseq)
    nc = b.build()
    shared = host_inputs(inputs, S)
    in_maps = [dict(shared, x=np.ascontiguousarray(x[i * nseq:(i + 1) * nseq])) for i in range(N_CORES)]
    res = run_bass_kernel_spmd(nc, in_maps, core_ids=list(range(N_CORES)))
    return np.concatenate([np.asarray(r["y"]) for r in res.results], axis=0).astype(np.float32)
```

```python
from contextlib import ExitStack
import math
import numpy as np
import ml_dtypes
import concourse.bass as bass
import concourse.mybir as mybir
from concourse.bass_utils import run_bass_kernel_spmd

F32 = mybir.dt.float32
BF16 = mybir.dt.bfloat16
AF = mybir.ActivationFunctionType
ALU = mybir.AluOpType

D = 1024
KC = D // 128
GLA_H, GLA_DK, GLA_DV, GLA_RANK = 4, 128, 256, 16
DIFF_H, DIFF_HD = 4, 128
FFN = 2816
FC = FFN // 128
NORM_EPS = 1e-6
SUBLN_EPS = 1e-5
LAMBDA_INIT = 0.8 - 0.6 * math.exp(-0.3 * 0)
N_CORES = 8

OFF_GQ, OFF_GK, OFF_GV, OFF_GR, OFF_GLR = 0, 512, 1024, 2048, 3072
OFF_DQ, OFF_DK, OFF_DV, OFF_GA, OFF_GB = 3104, 4128, 5152, 6176, 7200


def _w1_columns():
    cols = []
    cols += list(range(OFF_GQ, OFF_GQ + 512))
    cols += list(range(OFF_GK, OFF_GK + 512))
    cols += list(range(OFF_DQ, OFF_DQ + 1024))
    cols += list(range(OFF_DK, OFF_DK + 1024))
    cols += list(range(OFF_GLR, OFF_GLR + 32))
    cols += list(range(OFF_GK, OFF_GK + 512))
    for off in (OFF_GV, OFF_GR, OFF_DV, OFF_GA, OFF_GB):
        cols += list(range(off, off + 1024))
    return np.array(cols, dtype=np.int64)


W1_COLS = 512 * 6 + 32 + 512 * 11
FM_BLOCKS = [("gq", 0), ("gk", 512)] + [("dq", 1024 + 512 * i) for i in range(2)] + \
            [("dk", 2048 + 512 * i) for i in range(2)]
GLR_OFF = 3072
TM_OFF = 3104
TM_BLOCKS = [("gk", 0)] + [(n, j) for n in ("gv", "gr", "dv", "ga", "gb") for j in range(2)]


class Sched:
    CE = ("pe", "act", "dve", "pool")
    ALL = ("pe", "act", "dve", "pool", "sp")

    def __init__(self, nc, es):
        self.nc = nc
        self.prog = {e: es.enter_context(nc.semaphore("prog_" + e)) for e in self.CE}
        self.cum = {e: 0 for e in self.CE}
        self.dsem = {}
        self.es = es
        self.res_w = {}
        self.res_r = {}
        self.ops = {e: [] for e in self.ALL}
        self.waited = {e: {} for e in self.ALL}
        self.sigval = {}
        self.uid = 0

    def _slot(self, slot):
        if slot not in self.dsem:
            self.dsem[slot] = [self.es.enter_context(self.nc.semaphore("d_" + slot)), 0]
        return self.dsem[slot]

    def _collect(self, eng, reads, writes, is_dma):
        deps = []
        for k in reads:
            for ev in self.res_w.get(k, ()):
                deps.append((ev, True))
        for k in writes:
            for ev in self.res_w.get(k, ()):
                deps.append((ev, False))
            for ev in self.res_r.get(k, ()):
                deps.append((ev, False))
        out = []
        for ev, raw in deps:
            if ev[0] == "eng" and ev[1] == eng and not is_dma:
                if eng == "pe":
                    continue
            out.append(ev)
        return out

    def _update(self, ev, reads, writes):
        for k in writes:
            self.res_w[k] = [ev]
            self.res_r[k] = []
        for k in reads:
            self.res_r.setdefault(k, []).append(ev)

    def op(self, eng, fn, reads=(), writes=()):
        self.uid += 1
        ev = ("eng", eng, self.uid)
        deps = self._collect(eng, reads, writes, False)
        self.ops[eng].append(dict(fn=fn, deps=deps, ev=ev, dma=None))
        self._update(ev, reads, writes)
        return ev

    def dma(self, queue, fn, slot, reads=(), writes=()):
        s = self._slot(slot)
        deps = self._collect(queue, reads, writes, True)
        if s[1] > 0:
            deps.append(("dma", slot, s[1]))
        s[1] += 1
        ev = ("dma", slot, s[1])
        self.ops[queue].append(dict(fn=fn, deps=deps, ev=ev, dma=slot))
        self._update(ev, reads, writes)
        return ev

    def flush_wait_all(self):
        deps = [("dma", slot, s[1]) for slot, s in self.dsem.items() if s[1] > 0]
        self.ops["sp"].append(dict(fn=None, deps=deps, ev=None, dma=None))

    def emit(self, block):
        done = getattr(self, "done_uid", 0)
        for e in self.ALL:
            for o in self.ops[e]:
                o["deps"] = [d for d in o["deps"] if not (d[0] == "eng" and d[2] <= done)]
        self.done_uid = self.uid
        need = set()
        for e in self.ALL:
            for o in self.ops[e]:
                for d in o["deps"]:
                    if d[0] == "eng":
                        need.add(d[2])
        for e in self.CE:
            c = self.cum[e]
            for o in self.ops[e]:
                if o["dma"] is None and o["ev"] is not None and o["ev"][2] in need:
                    c += 1
                    self.sigval[o["ev"][2]] = c
            self.cum[e] = c
        starters = dict(pe=block.tensor, act=block.scalar, dve=block.vector, pool=block.gpsimd, sp=block.sync)
        for e in self.ALL:
            ops = self.ops[e]
            if not ops:
                continue
            waited = self.waited[e]

            def body(eng, ops=ops, waited=waited, e=e):
                for o in ops:
                    for d in o["deps"]:
                        if d[0] == "eng":
                            key, sem, val = d[1], self.prog[d[1]], self.sigval[d[2]]
                        else:
                            key, sem, val = "d_" + d[1], self.dsem[d[1]][0], 16 * d[2]
                        if waited.get(key, 0) >= val:
                            continue
                        eng.wait_ge(sem, val)
                        waited[key] = val
                    if o["fn"] is None:
                        continue
                    ins = o["fn"](eng)
                    if o["dma"] is not None:
                        ins.then_inc(self.dsem[o["dma"]][0], 16)
                    elif o["ev"][2] in self.sigval:
                        ins.then_inc(self.prog[e], 1)
            starters[e](body)
        self.ops = {e: [] for e in self.ALL}


class Builder:
    def __init__(self, S, NSEQ, debug=False, phases=("p1", "gla", "attn", "ffn"), scratch_in=False, cut=99):
        self.S, self.NSEQ, self.debug, self.phases = S, NSEQ, debug, phases
        self.scratch_in, self.cut = scratch_in, cut
        self.T = S * NSEQ
        self.NT = S // 128
        self.NG = S // 512
        self.nc = bass.Bass("TRN2", target_bir_lowering=False)

    def dram(self, name, shape, dt, kind):
        return self.nc.dram_tensor(name, list(shape), dt, kind=kind).ap()

    def declare(self):
        S, T, NSEQ = self.S, self.T, self.NSEQ
        I = "ExternalInput"
        self.x = self.dram("x", [NSEQ, S, D], F32, I)
        self.w1 = self.dram("w1", [D, W1_COLS], F32, I)
        self.nw_mixr = self.dram("nw_mixr", [1, D], F32, I)
        self.nw_ffnr = self.dram("nw_ffnr", [1, D], F32, I)
        self.nw_fin = self.dram("nw_fin", [1, D], F32, I)
        self.cosT = self.dram("cosT", [128, S], F32, I)
        self.sinT = self.dram("sinT", [128, S], F32, I)
        self.wgk = self.dram("wgk", [2, 17, 512], F32, I)
        self.gnw = self.dram("gnw", [1, 1024], F32, I)
        self.sln = self.dram("sln", [1, 256], F32, I)
        self.lam4 = self.dram("lam4", [1, 512], F32, I)
        self.w_out = self.dram("w_out", [D, D], F32, I)
        self.w_fi = self.dram("w_fi", [D, 2 * FFN], F32, I)
        self.w_fo = self.dram("w_fo", [FFN, D], F32, I)
        self.cst = self.dram("cst", [128, 6 * 128], F32, I)
        self.msk = self.dram("msk", [128, 2 * 128], F32, I)
        self.y = self.dram("y", [NSEQ, S, D], F32, "ExternalOutput")
        K = "ExternalOutput" if self.debug else "Internal"
        K1 = K
        if self.scratch_in:
            K = "ExternalInput"
        self.gqT = self.dram("gqT", [4, 128, T], BF16, K)
        self.gkT = self.dram("gkT", [4, 128, T], BF16, K)
        self.dqT = self.dram("dqT", [8, 128, T], BF16, K)
        self.dkT = self.dram("dkT", [8, 128, T], BF16, K)
        self.glrT = self.dram("glrT", [32, T], BF16, K)
        self.tm = {n: self.dram("tm_" + n, [T, (512 if n == "gk" else 1024)], BF16, K)
                   for n in ("gk", "gv", "gr", "dv", "ga", "gb")}
        K = K1
        self.ob = self.dram("ob", [T, 1024], F32, K)
        self.ya = self.dram("ya", [T, 1024], F32, K)
        self.yb = self.dram("yb", [T, 1024], F32, K)
        self.wfib = self.dram("wfib", [FC, 128, 2, KC, 128], BF16, "Internal")

    def _uniq(self, name):
        self._n = getattr(self, "_n", 0) + 1
        return f"{name}_{self._n}"

    def sb(self, es, name, shape, dt):
        return es.enter_context(self.nc.sbuf_tensor(self._uniq(name), list(shape), dt))

    def ps(self, es, name, shape, dt=F32):
        return es.enter_context(self.nc.psum_tensor(self._uniq(name), list(shape), dt))

    def phase1(self, sc, s):
        nc, S, NT, NG = self.nc, self.S, self.NT, self.NG
        t0 = s * S
        with ExitStack() as es, nc.Block() as block:
            uT = self.sb(es, "uT", [128, KC, S], BF16)
            cosT = self.sb(es, "cosT_sb", [128, S], F32)
            sinT = self.sb(es, "sinT_sb", [128, S], F32)
            nwb = self.sb(es, "nwb", [128, D], F32)
            ident = self.sb(es, "ident", [128, 128], BF16)
            identf = self.sb(es, "identf", [128, 128], F32)
            xt = [self.sb(es, f"xt{i}", [128, D], F32) for i in range(4)]
            xn = [self.sb(es, f"xn{i}", [128, D], BF16) for i in range(2)]
            st = [self.sb(es, f"st{i}", [128, 4], F32) for i in range(2)]
            wb = [self.sb(es, f"wb{i}", [128, KC, 512], BF16) for i in range(2)]
            so = [self.sb(es, f"so{i}", [128, 512], BF16) for i in range(4)]
            sof = [self.sb(es, f"sof{i}", [32, 512], BF16) for i in range(2)]
            r1 = [self.sb(es, f"r1_{i}", [128, 512], F32) for i in range(2)]
            r2 = [self.sb(es, f"r2_{i}", [128, 512], F32) for i in range(2)]
            psT = [self.ps(es, f"psT{i}", [128, 1024], BF16) for i in range(2)]
            pb = [self.ps(es, f"pb{i}", [128, 512]) for i in range(6)]
            sc.dma("sp", lambda e: e.dma_start(out=nwb[:], in_=self.nw_mixr.broadcast_to([128, D])), "c_nwm", writes=["nwb"])
            sc.dma("sp", lambda e: e.dma_start(out=identf[:], in_=self.cst[:, 0:128]), "c_id", writes=["identf"])
            sc.op("dve", lambda e: e.tensor_copy(out=ident[:], in_=identf[:]), reads=["identf"], writes=["ident"])

            def load_x(tt):
                j = tt % 4
                sc.dma("sp", lambda e: e.dma_start(out=xt[j][:], in_=self.x[s, tt * 128:(tt + 1) * 128, :]), f"xt{j}", writes=[f"xt{j}"])

            def stage_a(tt):
                i = tt % 2
                j = tt % 4
                if tt + 3 < NT:
                    load_x(tt + 3)
                sc.op("act", lambda e: e.activation(out=xn[i][:], in_=xt[j][:], func=AF.Square, accum_out=st[i][:, 0:1]),
                      reads=[f"xt{j}"], writes=[f"xn{i}", f"st{i}"])
                sc.op("act", lambda e: e.activation(out=st[i][:, 1:2], in_=st[i][:, 0:1], func=AF.Ln, scale=1.0 / D, bias=self.eps_t[:, 0:1]),
                      reads=[f"st{i}"], writes=[f"st{i}"])
                sc.op("act", lambda e: e.activation(out=st[i][:, 2:3], in_=st[i][:, 1:2], func=AF.Exp, scale=-0.5),
                      reads=[f"st{i}"], writes=[f"st{i}"])
                sc.op("dve", lambda e: e.scalar_tensor_tensor(out=xn[i][:], in0=xt[j][:], scalar=st[i][:, 2:3], in1=nwb[:], op0=ALU.mult, op1=ALU.mult),
                      reads=[f"xt{j}", f"st{i}", "nwb"], writes=[f"xn{i}"])

            def stage_b(tt):
                i = tt % 2
                for kc in range(KC):
                    sc.op("pe", lambda e, kc=kc: e.transpose(psT[i][:, kc * 128:(kc + 1) * 128], xn[i][:, kc * 128:(kc + 1) * 128], ident[:]),
                          reads=[f"xn{i}", "ident"], writes=[f"psT{i}"])
                sc.op("dve", lambda e: e.tensor_copy(out=uT[:, :, tt * 128:(tt + 1) * 128], in_=psT[i][:].rearrange("p (k t) -> p k t", k=KC)),
                      reads=[f"psT{i}"], writes=[("uT", tt)])

            for tt in range(min(3, NT)):
                load_x(tt)
            stage_a(0)
            for tt in range(NT):
                if tt + 1 < NT:
                    stage_a(tt + 1)
                stage_b(tt)
            for q_ in range((S + 1023) // 1024):
                c0_, c1_ = q_ * 1024, min(S, (q_ + 1) * 1024)
                sc.dma("sp", lambda e, c0_=c0_, c1_=c1_: e.dma_start(out=cosT[:, c0_:c1_], in_=self.cosT[:, c0_:c1_]), f"c_cos{q_ % 2}", writes=[("cosT", q_)])
                sc.dma("sp", lambda e, c0_=c0_, c1_=c1_: e.dma_start(out=sinT[:, c0_:c1_], in_=self.sinT[:, c0_:c1_]), f"c_sin{q_ % 2}", writes=[("sinT", q_)])
            uT_all = [("uT", tt) for tt in range(NT)]

            state = dict(wbi=0, pbi=0, soi=0, ri=0, sofi=0)

            def load_w(col0, ncols):
                i = state["wbi"] % 2
                state["wbi"] += 1
                src = self.w1.rearrange("(k p) c -> p k c", p=128)[:, :, col0:col0 + ncols]
                sc.dma("pool", lambda e, i=i: e.dma_start(out=wb[i][:, :, 0:ncols], in_=src), f"wb{i}", writes=[f"wb{i}"])
                return i

            def next_pb():
                j = state["pbi"] % 6
                state["pbi"] += 1
                return j

            def store(dst_ap, src_tile_key, src_ap, slotname):
                sc.dma("sp", lambda e: e.dma_start(out=dst_ap, in_=src_ap), slotname, reads=[src_tile_key])

            def fm_group(wi, c, tg, j, M=128):
                for kc in range(KC):
                    sc.op("pe", lambda e, kc=kc: e.matmul(pb[j][0:M, :], lhsT=wb[wi][:, kc, c * 128:c * 128 + M],
                                                            rhs=uT[:, kc, tg * 512:(tg + 1) * 512], start=(kc == 0), stop=(kc == KC - 1)),
                          reads=[f"wb{wi}"] + uT_all[tg * 4:(tg + 1) * 4], writes=[f"pb{j}"])

            for name, col0 in FM_BLOCKS:
                wi = load_w(col0, 512)
                bidx = (col0 - {"gq": 0, "gk": 512, "dq": 1024, "dk": 2048}[name]) // 512
                if name in ("gq", "gk"):
                    dst = self.gqT if name == "gq" else self.gkT
                    for c in range(4):
                        for tg in range(NG):
                            j = next_pb()
                            fm_group(wi, c, tg, j)
                            k = state["soi"] % 4
                            state["soi"] += 1
                            if (c + tg) % 2 == 0:
                                sc.op("act", lambda e, j=j, k=k: e.activation(out=so[k][:], in_=pb[j][:], func=AF.Copy),
                                      reads=[f"pb{j}"], writes=[f"so{k}"])
                            else:
                                sc.op("dve", lambda e, j=j, k=k: e.tensor_copy(out=so[k][:], in_=pb[j][:]),
                                      reads=[f"pb{j}"], writes=[f"so{k}"])
                            store(dst[c, :, t0 + tg * 512:t0 + (tg + 1) * 512], f"so{k}", so[k][:], f"so{k}")
                else:
                    dst = self.dqT if name == "dq" else self.dkT
                    for c in range(4):
                        hc = bidx * 4 + c
                        for tg in range(NG):
                            ja = next_pb()
                            fm_group(wi, c, tg, ja)
                            ri = state["ri"] % 2
                            state["ri"] += 1
                            k = state["soi"] % 4
                            state["soi"] += 1
                            tsl = slice(tg * 512, (tg + 1) * 512)
                            sc.op("dve", lambda e, ja=ja, ri=ri, tsl=tsl: e.tensor_tensor(out=r1[ri][:], in0=pb[ja][:], in1=cosT[:, tsl], op=ALU.mult),
                                  reads=[f"pb{ja}", ("cosT", tg // 2)], writes=[f"r1_{ri}"])
                            sc.op("dve", lambda e, ja=ja, ri=ri, tsl=tsl: e.tensor_tensor(out=r2[ri][0:64, :], in0=pb[ja][64:128, :], in1=sinT[0:64, tsl], op=ALU.mult),
                                  reads=[f"pb{ja}", ("sinT", tg // 2)], writes=[(f"r2_{ri}", 0)])
                            sc.op("dve", lambda e, ja=ja, ri=ri, tsl=tsl: e.tensor_tensor(out=r2[ri][64:128, :], in0=pb[ja][0:64, :], in1=sinT[64:128, tsl], op=ALU.mult),
                                  reads=[f"pb{ja}", ("sinT", tg // 2)], writes=[(f"r2_{ri}", 1)])
                            sc.op("pool", lambda e, ri=ri, k=k: e.tensor_tensor(out=so[k][:], in0=r1[ri][:], in1=r2[ri][:], op=ALU.add),
                                  reads=[f"r1_{ri}", (f"r2_{ri}", 0), (f"r2_{ri}", 1)], writes=[f"so{k}"])
                            store(dst[hc, :, t0 + tg * 512:t0 + (tg + 1) * 512], f"so{k}", so[k][:], f"so{k}")
            wi = load_w(GLR_OFF, 32)
            for tg in range(NG):
                j = next_pb()
                fm_group(wi, 0, tg, j, M=32)
                k = state["sofi"] % 2
                state["sofi"] += 1
                sc.op("dve", lambda e, j=j, k=k: e.tensor_copy(out=sof[k][:], in_=pb[j][0:32, :]), reads=[f"pb{j}"], writes=[f"sof{k}"])
                store(self.glrT[:, t0 + tg * 512:t0 + (tg + 1) * 512], f"sof{k}", sof[k][:], f"sof{k}")
            for bi, (name, jj) in enumerate(TM_BLOCKS):
                wi = load_w(TM_OFF + bi * 512, 512)
                dst = self.tm[name]
                for tt in range(NT):
                    j = next_pb()
                    for kc in range(KC):
                        sc.op("pe", lambda e, kc=kc, j=j, tt=tt, wi=wi: e.matmul(pb[j][:], lhsT=uT[:, kc, tt * 128:(tt + 1) * 128], rhs=wb[wi][:, kc, :],
                                                                        start=(kc == 0), stop=(kc == KC - 1)),
                              reads=[f"wb{wi}", ("uT", tt)], writes=[f"pb{j}"])
                    k = state["soi"] % 4
                    state["soi"] += 1
                    if name == "gr":
                        sc.op("act", lambda e, j=j, k=k: e.activation(out=so[k][:], in_=pb[j][:], func=AF.Silu), reads=[f"pb{j}"], writes=[f"so{k}"])
                    elif name in ("ga", "gb"):
                        sc.op("act", lambda e, j=j, k=k: e.activation(out=so[k][:], in_=pb[j][:], func=AF.Sigmoid), reads=[f"pb{j}"], writes=[f"so{k}"])
                    elif tt % 2 == 0:
                        sc.op("act", lambda e, j=j, k=k: e.activation(out=so[k][:], in_=pb[j][:], func=AF.Copy), reads=[f"pb{j}"], writes=[f"so{k}"])
                    else:
                        sc.op("dve", lambda e, j=j, k=k: e.tensor_copy(out=so[k][:], in_=pb[j][:]), reads=[f"pb{j}"], writes=[f"so{k}"])
                    store(dst[t0 + tt * 128:t0 + (tt + 1) * 128, jj * 512:(jj + 1) * 512], f"so{k}", so[k][:], f"so{k}")
            sc.flush_wait_all()
            sc.emit(block)


    def phase_gla(self, sc, s, dirn):
        nc, S = self.nc, self.S
        t0 = s * S
        NP, NSS = S // 128, S // 512
        fwd = dirn == 0
        with ExitStack() as es, nc.Block() as block:
            Uinc = self.sb(es, "Uinc", [128, 128], BF16)
            Ustr = self.sb(es, "Ustr", [128, 128], BF16)
            Uf = self.sb(es, "Uf", [128, 256], F32)
            wgkf = self.sb(es, "wgkf", [17, 512], F32)
            mask4 = self.sb(es, "mask4", [128, 4, 128], F32)
            wgk = self.sb(es, "wgk_sb", [17, 512], BF16)
            gnwb = self.sb(es, "gnwb", [128, 1024], F32)
            glra = [self.sb(es, f"glra{i}", [17, 512], BF16) for i in range(2)]
            qT = [self.sb(es, f"gqT{i}", [128, 4, 512], BF16) for i in range(2)]
            kT = [self.sb(es, f"gkT{i}", [128, 4, 512], BF16) for i in range(2)]
            ktm = [self.sb(es, f"gktm{i}", [128, 4, 512], BF16) for i in range(2)]
            vv = [self.sb(es, f"gv{i}", [128, 4, 1024], BF16) for i in range(2)]
            obt = [self.sb(es, f"obt{i}", [128, 1024], F32) for i in range(2)]
            grt = [self.sb(es, f"grt{i}", [128, 1024], BF16) for i in range(2)]
            gat = [self.sb(es, f"gat{i}", [128, 1024], BF16) for i in range(2)]
            e1 = self.sb(es, "g_e1", [128, 512], F32)
            sp = self.sb(es, "g_sp", [128, 512], BF16)
            ebl = self.sb(es, "g_ebl", [128, 512], F32)
            ks = self.sb(es, "g_ks", [128, 512], BF16)
            eb = self.sb(es, "g_eb", [128, 4, 128], F32)
            enb = self.sb(es, "g_enb", [128, 4, 128], F32)
            qt = self.sb(es, "g_qt", [128, 4, 128], BF16)
            kt = self.sb(es, "g_kt", [128, 4, 128], BF16)
            attm = self.sb(es, "g_attm", [128, 4, 128], BF16)
            S32 = [self.sb(es, f"S32_{h}", [128, 256], F32) for h in range(4)]
            Sbf = [self.sb(es, f"Sbf_{h}", [128, 256], BF16) for h in range(4)]
            osb = [self.sb(es, f"osb{i}", [128, 1024], F32) for i in range(2)]
            ysb = [self.sb(es, f"ysb{i}", [128, 1024], F32) for i in range(2)]
            g32 = self.sb(es, "g_g32", [128, 1024], F32)
            junk = self.sb(es, "g_junk", [128, 4, 256], F32)
            stt = [self.sb(es, f"g_st{i}", [128, 12], F32) for i in range(2)]
            bA = self.ps(es, "bA", [128, 512])
            bB = self.ps(es, "bB", [128, 512])
            bO = [self.ps(es, f"bO{h}", [128, 512]) for h in range(4)]
            bKV = [self.ps(es, f"bKV{i}", [128, 512]) for i in range(2)]

            co = 128 * (1 + 2 * dirn)
            sc.dma("sp", lambda e: e.dma_start(out=Uf[:], in_=self.cst[:, co:co + 256]), "c_a", writes=["Uf"])
            sc.op("dve", lambda e: e.tensor_copy(out=Uinc[:], in_=Uf[:, 0:128]), reads=["Uf"], writes=["Uinc"])
            sc.op("dve", lambda e: e.tensor_copy(out=Ustr[:], in_=Uf[:, 128:256]), reads=["Uf"], writes=["Ustr"])
            for h in range(4):
                sc.dma("sp", lambda e, h=h: e.dma_start(out=mask4[:, h, :], in_=self.msk[:, dirn * 128:(dirn + 1) * 128]), "c_c", writes=["mask4"])
            sc.dma("sp", lambda e: e.dma_start(out=wgkf[:], in_=self.wgk[dirn, :, :]), "c_d", writes=["wgkf"])
            sc.op("dve", lambda e: e.tensor_copy(out=wgk[:], in_=wgkf[:]), reads=["wgkf"], writes=["wgk"])
            sc.dma("sp", lambda e: e.dma_start(out=gnwb[:], in_=self.gnw.broadcast_to([128, 1024])), "c_e", writes=["gnwb"])
            for i in range(2):
                sc.op("dve", lambda e, i=i: e.memset(glra[i][:], 1.0), writes=[f"glra{i}"])
            for h in range(4):
                sc.op("dve", lambda e, h=h: e.memset(S32[h][:], 0.0), writes=[f"S32_{h}"])
                sc.op("dve", lambda e, h=h: e.memset(Sbf[h][:], 0.0), writes=[f"Sbf_{h}"])

            if fwd:
                r1, c1, r2, c2 = slice(0, 64), 63, slice(64, 128), 127
            else:
                r1, c1, r2, c2 = slice(64, 128), 64, slice(0, 64), 0
            qscale = float(GLA_DK ** -0.5)

            def load_ss(ss, i):
                tb = t0 + ss * 512
                sc.dma("sp", lambda e: e.dma_start(out=glra[i][0:16, :], in_=self.glrT[dirn * 16:(dirn + 1) * 16, tb:tb + 512]), f"glra{i}", writes=[f"glra{i}"])
                sc.dma("sp", lambda e: e.dma_start(out=qT[i][:], in_=self.gqT[:, :, tb:tb + 512].rearrange("h p t -> p h t")), f"gqT{i}", writes=[f"gqT{i}"])
                sc.dma("sp", lambda e: e.dma_start(out=kT[i][:], in_=self.gkT[:, :, tb:tb + 512].rearrange("h p t -> p h t")), f"gkT{i}", writes=[f"gkT{i}"])
                sc.dma("sp", lambda e: e.dma_start(out=ktm[i][:], in_=self.tm["gk"][tb:tb + 512, :].rearrange("(a p) c -> p a c", p=128)), f"gktm{i}", writes=[f"gktm{i}"])
                sc.dma("sp", lambda e: e.dma_start(out=vv[i][:], in_=self.tm["gv"][tb:tb + 512, :].rearrange("(a p) c -> p a c", p=128)), f"gv{i}", writes=[f"gv{i}"])

            ksA = [ks, self.sb(es, "g_ks_b", [128, 512], BF16)]
            ksB = [self.sb(es, f"g_ksB{i}", [128, 512], BF16) for i in range(2)]
            for q_ in range(2):
                sc.op("pool", lambda e, q_=q_: e.memset(ksA[q_][:], 0.0), writes=[f"ksA{q_}"])
                sc.op("pool", lambda e, q_=q_: e.memset(ksB[q_][:], 0.0), writes=[f"ksB{q_}"])
            eb2 = [eb, self.sb(es, "g_eb_b", [128, 4, 128], F32)]
            qt2 = [qt, self.sb(es, "g_qt_b", [128, 4, 128], BF16)]
            attm2 = [attm, self.sb(es, "g_attm_b", [128, 4, 128], BF16)]

            order = list(range(NSS)) if fwd else list(range(NSS - 1, -1, -1))
            steps = []
            for n, ss in enumerate(order):
                for m, sub in enumerate([0, 1, 2, 3] if fwd else [3, 2, 1, 0]):
                    steps.append(dict(n=n, ss=ss, i=n % 2, sub=sub, pi=(n * 4 + m) % 2, q=(n * 4 + m) % 2,
                                      tok=t0 + ss * 512 + sub * 128, tsl=slice(sub * 128, (sub + 1) * 128), first=(m == 0)))

            def loads(st):
                if fwd:
                    pi, tok = st["pi"], st["tok"]
                    sc.dma("sp", lambda e: e.dma_start(out=obt[pi][:], in_=self.ob[tok:tok + 128, :]), f"obt{pi}", writes=[f"obt{pi}"])
                    sc.dma("sp", lambda e: e.dma_start(out=grt[pi][:], in_=self.tm["gr"][tok:tok + 128, :]), f"grt{pi}", writes=[f"grt{pi}"])
                    sc.dma("sp", lambda e: e.dma_start(out=gat[pi][:], in_=self.tm["ga"][tok:tok + 128, :]), f"gat{pi}", writes=[f"gat{pi}"])

            def pre_a(st):
                i, tsl = st["i"], st["tsl"]
                sc.op("pe", lambda e: e.matmul(bA[:], lhsT=glra[i][0:17, tsl], rhs=wgk[0:17, :], start=True, stop=True),
                      reads=[f"glra{i}", "wgk"], writes=["bA"])
                sc.op("act", lambda e: e.activation(out=e1[:], in_=bA[:], func=AF.Exp, scale=-1.0), reads=["bA"], writes=["e1"])
                sc.op("act", lambda e: e.activation(out=sp[:], in_=e1[:], func=AF.Ln, bias=self.eps_t[:, 2:3]), reads=["e1"], writes=["sp"])

            def pre_b(st):
                i, sub, q = st["i"], st["sub"], st["q"]
                sc.op("pe", lambda e: e.matmul(bA[:], lhsT=Ustr[:], rhs=sp[:], start=True, stop=True), reads=["Ustr", "sp"], writes=["bA"])
                for h in range(4):
                    sc.op("pe", lambda e, h=h: e.matmul(bB[:, h * 128:(h + 1) * 128], lhsT=sp[:, h * 128:(h + 1) * 128], rhs=Uinc[:], start=True, stop=True),
                          reads=["sp", "Uinc"], writes=["bB"])
                sc.op("act", lambda e: e.activation(out=ebl[:], in_=bA[:], func=AF.Exp), reads=["bA"], writes=["ebl"])
                sc.op("act", lambda e: e.activation(out=eb2[q][:].rearrange("p h t -> p (h t)"), in_=bB[:], func=AF.Exp), reads=["bB"], writes=[f"eb{q}"])
                sc.op("act", lambda e: e.activation(out=enb[:].rearrange("p h t -> p (h t)"), in_=bB[:], func=AF.Exp, scale=-1.0), reads=["bB"], writes=["enb"])

            def pre_c(st):
                i, sub, q, tsl = st["i"], st["sub"], st["q"], st["tsl"]
                sc.op("pool", lambda e: e.tensor_tensor(out=ksA[q][r1, :], in0=ktm[i][r1, sub, :], in1=ebl[r1, :], op=ALU.mult),
                      reads=[f"gktm{i}", "ebl"], writes=[f"ksA{q}"])
                sc.op("pool", lambda e: e.tensor_tensor(out=ksB[q][r2, :], in0=ktm[i][r2, sub, :], in1=ebl[r2, :], op=ALU.mult),
                      reads=[f"gktm{i}", "ebl"], writes=[f"ksB{q}"])
                sc.op("dve", lambda e: e.scalar_tensor_tensor(out=qt2[q][:], in0=eb2[q][:], scalar=qscale, in1=qT[i][:, :, tsl], op0=ALU.mult, op1=ALU.mult),
                      reads=[f"eb{q}", f"gqT{i}"], writes=[f"qt{q}"])
                sc.op("pool", lambda e: e.tensor_tensor(out=kt[:], in0=enb[:], in1=kT[i][:, :, tsl], op=ALU.mult),
                      reads=["enb", f"gkT{i}"], writes=["kt"])

            def pre_d(st):
                q = st["q"]
                for h in range(4):
                    sc.op("pe", lambda e, h=h: e.matmul(bA[:, h * 128:(h + 1) * 128], lhsT=kt[:, h, :], rhs=qt2[q][:, h, :], start=True, stop=True),
                          reads=["kt", f"qt{q}"], writes=["bA"])
                sc.op("dve", lambda e: e.tensor_tensor(out=attm2[q][:].rearrange("p h t -> p (h t)"), in0=bA[:], in1=mask4[:].rearrange("p h t -> p (h t)"), op=ALU.mult),
                      reads=["bA", "mask4"], writes=[f"attm{q}"])

            def scan_open(st):
                i, sub, q = st["i"], st["sub"], st["q"]
                for h in range(4):
                    vs = slice(h * 256, (h + 1) * 256)
                    sc.op("pe", lambda e, h=h, vs=vs: e.matmul(bO[h][:, 0:256], lhsT=attm2[q][:, h, :], rhs=vv[i][:, sub, vs], start=True, stop=False),
                          reads=[f"attm{q}", f"gv{i}"], writes=[f"bO{h}"])
                    sc.op("pe", lambda e, h=h: e.matmul(bO[h][r1, 0:256], lhsT=qt2[q][:, h, r1], rhs=Sbf[h][:], start=False, stop=True),
                          reads=[f"qt{q}", f"Sbf_{h}"], writes=[f"bO{h}"])

            def scan_kv(st, h):
                i, sub, q = st["i"], st["sub"], st["q"]
                vs = slice(h * 256, (h + 1) * 256)
                kb = h % 2
                hs = slice(h * 128, (h + 1) * 128)
                sc.op("pe", lambda e: e.matmul(bKV[kb][:, 0:256], lhsT=ksA[q][:, hs], rhs=vv[i][:, sub, vs], start=True, stop=True),
                      reads=[f"ksA{q}", f"gv{i}"], writes=[f"bKV{kb}"])
                sc.op("pe", lambda e: e.matmul(bKV[kb][:, 256:512], lhsT=ksB[q][:, hs], rhs=vv[i][:, sub, vs], start=True, stop=True),
                      reads=[f"ksB{q}", f"gv{i}"], writes=[f"bKV{kb}"])

            def scan_head(st, h):
                i, sub, q, pi = st["i"], st["sub"], st["q"], st["pi"]
                vs = slice(h * 256, (h + 1) * 256)
                kb = h % 2
                sc.op("dve", lambda e: e.scalar_tensor_tensor(out=Sbf[h][:], in0=S32[h][:], scalar=eb2[q][:, h, c1:c1 + 1], in1=bKV[kb][:, 0:256], op0=ALU.mult, op1=ALU.add),
                      reads=[f"S32_{h}", f"eb{q}", f"bKV{kb}"], writes=[f"Sbf_{h}"])
                sc.op("dve", lambda e: e.scalar_tensor_tensor(out=S32[h][:], in0=S32[h][:], scalar=eb2[q][:, h, c1:c1 + 1], in1=bKV[kb][:, 0:256], op0=ALU.mult, op1=ALU.add),
                      reads=[f"S32_{h}", f"eb{q}", f"bKV{kb}"], writes=[f"S32_{h}"])
                sc.op("pe", lambda e: e.matmul(bO[h][r2, 0:256], lhsT=qt2[q][:, h, r2], rhs=Sbf[h][:], start=False, stop=True),
                      reads=[f"qt{q}", f"Sbf_{h}"], writes=[f"bO{h}"])
                sc.op("dve", lambda e: e.scalar_tensor_tensor(out=S32[h][:], in0=S32[h][:], scalar=eb2[q][:, h, c2:c2 + 1], in1=bKV[kb][:, 256:512], op0=ALU.mult, op1=ALU.add),
                      reads=[f"S32_{h}", f"eb{q}", f"bKV{kb}"], writes=[f"S32_{h}"])
                sc.op("act", lambda e: e.activation(out=Sbf[h][:], in_=S32[h][:], func=AF.Copy), reads=[f"S32_{h}"], writes=[f"Sbf_{h}"])
                if fwd:
                    sc.op("dve", lambda e: e.tensor_tensor(out=osb[pi][:, vs], in0=bO[h][:, 0:256], in1=obt[pi][:, vs], op=ALU.add),
                          reads=[f"bO{h}", f"obt{pi}"], writes=[(f"osb{pi}", h)])
                else:
                    sc.op("act", lambda e: e.activation(out=osb[pi][:, vs], in_=bO[h][:, 0:256], func=AF.Copy),
                          reads=[f"bO{h}"], writes=[(f"osb{pi}", h)])

            def epilogue(st):
                pi, tok = st["pi"], st["tok"]
                okeys = [(f"osb{pi}", h) for h in range(4)]
                if not fwd:
                    sc.dma("act", lambda e: e.dma_start(out=self.ob[tok:tok + 128, :], in_=osb[pi][:]), f"osb{pi}", reads=okeys, writes=[("ob", tok)])
                    return
                for h in range(4):
                    vs = slice(h * 256, (h + 1) * 256)
                    sc.op("act", lambda e, h=h, vs=vs: e.activation(out=junk[:, h, :], in_=osb[pi][:, vs], func=AF.Square, accum_out=stt[pi][:, h:h + 1]),
                          reads=[(f"osb{pi}", h)], writes=[f"junk{h}", (f"gst{pi}", h)])
                sc.op("act", lambda e: e.activation(out=stt[pi][:, 4:8], in_=stt[pi][:, 0:4], func=AF.Ln, scale=1.0 / GLA_DV, bias=self.eps_t[:, 0:1]),
                      reads=[(f"gst{pi}", h) for h in range(4)], writes=[f"gst{pi}"])
                sc.op("act", lambda e: e.activation(out=stt[pi][:, 8:12], in_=stt[pi][:, 4:8], func=AF.Exp, scale=-0.5), reads=[f"gst{pi}"], writes=[f"gst{pi}"])
                for h in range(4):
                    vs = slice(h * 256, (h + 1) * 256)
                    sc.op("dve", lambda e, h=h, vs=vs: e.scalar_tensor_tensor(out=ysb[pi][:, vs], in0=osb[pi][:, vs], scalar=stt[pi][:, 8 + h:9 + h], in1=g32[:, vs], op0=ALU.mult, op1=ALU.mult),
                          reads=[(f"osb{pi}", h), f"gst{pi}", "g32"], writes=[(f"ysb{pi}", h)])
                sc.dma("act", lambda e: e.dma_start(out=self.ya[tok:tok + 128, :], in_=ysb[pi][:]), f"ysb{pi}",
                       reads=[(f"ysb{pi}", h) for h in range(4)], writes=[("ya", tok)])

            def gates(st):
                if not fwd:
                    return
                pi = st["pi"]
                sc.op("pool", lambda e: e.tensor_tensor(out=g32[:], in0=grt[pi][:], in1=gat[pi][:], op=ALU.mult), reads=[f"grt{pi}", f"gat{pi}"], writes=["g32"])
                sc.op("pool", lambda e: e.tensor_tensor(out=g32[:], in0=g32[:], in1=gnwb[:], op=ALU.mult), reads=["g32", "gnwb"], writes=["g32"])

            load_ss(order[0], 0)
            if NSS > 1:
                load_ss(order[1], 1)
            loads(steps[0])
            for f_ in (pre_a, pre_b, pre_c, pre_d):
                f_(steps[0])
            for p, st in enumerate(steps):
                nx = steps[p + 1] if p + 1 < len(steps) else None
                if nx is not None:
                    loads(nx)
                scan_kv(st, 0)
                scan_kv(st, 1)
                scan_open(st)
                gates(st)
                parts = (pre_a, pre_b, pre_c)
                for h in range(4):
                    if nx is not None and h < 3:
                        parts[h](nx)
                    scan_head(st, h)
                    if h + 2 < 4:
                        scan_kv(st, h + 2)
                if nx is not None:
                    pre_d(nx)
                epilogue(st)
                if nx is not None and nx["first"] and nx["n"] + 1 < NSS:
                    load_ss(order[nx["n"] + 1], (nx["n"] + 1) % 2)
            sc.flush_wait_all()
            sc.emit(block)

    def phase_attn(self, sc, s, wprep=False):
        nc, S, NT, NG = self.nc, self.S, self.NT, self.NG
        t0 = s * S
        LAG = 2
        with ExitStack() as es, nc.Block() as block:
            if wprep:
                wt = self.sb(es, "wprep", [128, KC, FFN], BF16)
                wsrc = self.w_fi.rearrange("(k p) c -> p k c", p=128)

                def wprep_load(g):
                    for kc in range(KC):
                        sc.dma("pool", lambda e, kc=kc: e.dma_start(out=wt[:, kc, :], in_=wsrc[:, kc, g * FFN:(g + 1) * FFN]), f"wp{kc % 4}", writes=[("wprep", kc)])

                def wprep_store(g):
                    allk = [("wprep", kc) for kc in range(KC)]
                    for f in range(FC):
                        sc.dma("sp", lambda e, f=f: e.dma_start(out=self.wfib[f, :, g, :, :], in_=wt[:, :, f * 128:(f + 1) * 128]), f"wps{f % 4}",
                               reads=allk, writes=[("wfib", f, g)])
                wprep_load(0)
            KT = [self.sb(es, f"aKT{i}", [128, 2, S], BF16) for i in range(2)]
            VA = [self.sb(es, f"aVA{i}", [128, NT, 258], BF16) for i in range(2)]
            QT = [self.sb(es, f"aQT{i}", [128, 2, 512], BF16) for i in range(2)]
            PT = [self.sb(es, f"aPT{i}", [128, 1024], BF16) for i in range(2)]
            A0 = self.sb(es, "aA0", [128, 4, 256], F32)
            o2 = self.sb(es, "ao2", [128, 4, 256], F32)
            gbt = [self.sb(es, f"agb{i}", [128, 4, 256], BF16) for i in range(2)]
            g32 = self.sb(es, "ag32", [128, 4, 256], F32)
            ybs = [self.sb(es, f"aybs{i}", [128, 4, 256], F32) for i in range(2)]
            slnb = self.sb(es, "aslnb", [128, 256], F32)
            lam = self.sb(es, "alam", [1, 512], F32)
            lt = self.sb(es, "alt", [1, 16], F32)
            ones1 = self.sb(es, "aones1", [1, 128], F32)
            nlam = self.sb(es, "anlam", [128, 1], F32)
            rr = self.sb(es, "arr", [128, 16], F32)
            stt = self.sb(es, "ast", [128, 12], F32)
            junk = self.sb(es, "ajunk", [128, 4, 256], F32)
            ACCall = self.ps(es, "aACC", [128, 4, 512])
            ACC = [ACCall[:, i, :] for i in range(4)]
            raw = [self.sb(es, f"araw{i}", [128, 4, 258], F32) for i in range(2)]
            SB = [self.ps(es, f"aSB{i}", [128, 1024]) for i in range(2)]

            sc.dma("sp", lambda e: e.dma_start(out=lam[:], in_=self.lam4[:, :]), "c_a", writes=["lam"])
            sc.dma("sp", lambda e: e.dma_start(out=slnb[:], in_=self.sln.broadcast_to([128, 256])), "c_b", writes=["slnb"])
            sc.op("dve", lambda e: e.memset(ones1[:], 1.0), writes=["ones1"])
            sc.op("dve", lambda e: e.tensor_tensor(out=lam[:, 0:128], in0=lam[:, 0:128], in1=lam[:, 128:256], op=ALU.mult), reads=["lam"], writes=["lam"])
            sc.op("dve", lambda e: e.tensor_tensor(out=lam[:, 256:384], in0=lam[:, 256:384], in1=lam[:, 384:512], op=ALU.mult), reads=["lam"], writes=["lam"])
            sc.op("act", lambda e: e.activation(out=lam[:, 128:256], in_=lam[:, 0:128], func=AF.Copy, accum_out=lt[:, 0:1]), reads=["lam"], writes=["lam", "lt"])
            sc.op("act", lambda e: e.activation(out=lam[:, 384:512], in_=lam[:, 256:384], func=AF.Copy, accum_out=lt[:, 1:2]), reads=["lam"], writes=["lam", "lt"])
            sc.op("act", lambda e: e.activation(out=lt[:, 2:4], in_=lt[:, 0:2], func=AF.Exp), reads=["lt"], writes=["lt"])
            sc.op("dve", lambda e: e.tensor_tensor(out=lt[:, 4:5], in0=lt[:, 3:4], in1=lt[:, 2:3], op=ALU.subtract), reads=["lt"], writes=["lt"])
            sc.op("dve", lambda e: e.tensor_scalar(out=lt[:, 5:6], in0=lt[:, 4:5], scalar1=-LAMBDA_INIT, scalar2=None, op0=ALU.add), reads=["lt"], writes=["lt"])
            sc.op("pe", lambda e: e.matmul(ACC[0][:, 0:1], lhsT=ones1[:], rhs=lt[:, 5:6], start=True, stop=True), reads=["ones1", "lt"], writes=["aACC0"])
            sc.op("dve", lambda e: e.tensor_copy(out=nlam[:], in_=ACC[0][:, 0:1]), reads=["aACC0"], writes=["nlam"])
            sc.op("dve", lambda e: e.tensor_scalar(out=slnb[:], in0=slnb[:], scalar1=float(1.0 - LAMBDA_INIT), scalar2=None, op0=ALU.mult), reads=["slnb"], writes=["slnb"])
            for i in range(2):
                sc.op("pool", lambda e, i=i: e.memset(VA[i][:, :, 256:258], 1.0), writes=[f"aVA{i}"])

            def load_head(h, i):
                for c in range(2):
                    sc.dma("sp", lambda e, c=c: e.dma_start(out=KT[i][:, c, :], in_=self.dkT[2 * h + c, :, t0:t0 + S]), f"aKT{i}", writes=[f"aKT{i}"])
                for a0 in range(0, NT, 8):
                    a1 = min(NT, a0 + 8)
                    sc.dma("sp", lambda e, a0=a0, a1=a1: e.dma_start(
                        out=VA[i][:, a0:a1, 0:256],
                        in_=self.tm["dv"][t0 + a0 * 128:t0 + a1 * 128, h * 256:(h + 1) * 256].rearrange("(a p) c -> p a c", p=128)),
                        f"aVA{i}", writes=[f"aVA{i}"])

            scale = float(DIFF_HD ** -0.5)
            load_head(0, 0)
            work = [(h, qg) for h in range(DIFF_H) for qg in range(NG)]

            def load_q(widx):
                h, qg = work[widx]
                qi, tb = widx % 2, t0 + qg * 512
                for c in range(2):
                    sc.dma("sp", lambda e, c=c: e.dma_start(out=QT[qi][:, c, :], in_=self.dqT[2 * h + c, :, tb:tb + 512]), f"aQT{qi}", writes=[f"aQT{qi}"])
                sc.dma("sp", lambda e: e.dma_start(out=gbt[qi][:], in_=self.tm["gb"][tb:tb + 512, h * 256:(h + 1) * 256].rearrange("(a p) c -> p a c", p=128)),
                       f"agb{qi}", writes=[f"agb{qi}"])
            load_q(0)
            NP2 = NT // 2
            pairs = [(widx, h, qg, c, p) for widx, (h, qg) in enumerate(work) for c in range(2) for p in range(NP2)]

            def emit_scores(g):
                widx, h, qg, c, p = pairs[g]
                b_, qi, hi = g % 2, widx % 2, h % 2
                for u in range(2):
                    k_ = 2 * p + u
                    sc.op("pe", lambda e, u=u, k_=k_: e.matmul(SB[b_][:, u * 512:(u + 1) * 512], lhsT=KT[hi][:, c, k_ * 128:(k_ + 1) * 128], rhs=QT[qi][:, c, :], start=True, stop=True),
                          reads=[f"aKT{hi}", f"aQT{qi}"], writes=[f"aSB{b_}"])
                sc.op("act", lambda e: e.activation(out=PT[b_][:], in_=SB[b_][:], func=AF.Exp, scale=scale, bias=self.eps_t[:, 3:4]),
                      reads=[f"aSB{b_}"], writes=[f"aPT{b_}"])

            def emit_pv(g):
                widx, h, qg, c, p = pairs[g]
                b_, hi = g % 2, h % 2
                for u in range(2):
                    k_ = 2 * p + u
                    for qs in range(4):
                        sc.op("pe", lambda e, u=u, k_=k_, qs=qs: e.matmul(ACC[qs][:, 0:257], lhsT=PT[b_][:, u * 512 + qs * 128:u * 512 + (qs + 1) * 128], rhs=VA[hi][:, k_, 0:257],
                                                                   start=(k_ == 0), stop=(k_ == NT - 1)),
                              reads=[f"aPT{b_}", f"aVA{hi}"], writes=[f"aACC{qs}"])

            def emit_drain(g):
                widx, h, qg, c, p = pairs[g]
                qi, tb = widx % 2, t0 + qg * 512
                sc.op("dve", lambda e: e.tensor_copy(out=raw[c][:, :, 0:257], in_=ACCall[:, :, 0:257]),
                      reads=[f"aACC{q_}" for q_ in range(4)], writes=[f"araw{c}"])
                sc.op("dve", lambda e: e.reciprocal(out=rr[:, c * 4:c * 4 + 4], in_=raw[c][:, :, 256]), reads=[f"araw{c}"], writes=[("rr", c)])
                if c == 1:
                    sc.op("dve", lambda e: e.tensor_scalar(out=rr[:, 12:16], in0=rr[:, 4:8], scalar1=nlam[:, 0:1], scalar2=None, op0=ALU.mult),
                          reads=[("rr", 1), "nlam"], writes=[("rr", 2)])
                for qs in range(4):
                    if c == 0:
                        sc.op("dve", lambda e, qs=qs: e.tensor_scalar(out=A0[:, qs, :], in0=raw[0][:, qs, 0:256], scalar1=rr[:, qs:qs + 1], scalar2=None, op0=ALU.mult),
                              reads=["araw0", ("rr", 0)], writes=[("A0", qs)])
                    else:
                        sc.op("dve", lambda e, qs=qs: e.scalar_tensor_tensor(out=o2[:, qs, :], in0=raw[1][:, qs, 0:256], scalar=rr[:, 12 + qs:13 + qs], in1=A0[:, qs, :], op0=ALU.mult, op1=ALU.add),
                              reads=["araw1", ("rr", 2), ("A0", qs)], writes=[("o2", qs)])
                if c == 0:
                    return
                for qs in range(4):
                    sc.op("act", lambda e, qs=qs: e.activation(out=junk[:, qs, :], in_=o2[:, qs, :], func=AF.Square, accum_out=stt[:, qs:qs + 1]), reads=[("o2", qs)], writes=[f"ajunk{qs}", ("ast", qs)])
                sc.op("act", lambda e: e.activation(out=stt[:, 4:8], in_=stt[:, 0:4], func=AF.Ln, scale=1.0 / 256, bias=self.eps_t[:, 1:2]), reads=[("ast", q_) for q_ in range(4)], writes=["ast"])
                sc.op("act", lambda e: e.activation(out=stt[:, 8:12], in_=stt[:, 4:8], func=AF.Exp, scale=-0.5), reads=["ast"], writes=["ast"])
                for qs in range(4):
                    sc.op("pool", lambda e, qs=qs: e.tensor_tensor(out=g32[:, qs, :], in0=gbt[qi][:, qs, :], in1=slnb[:], op=ALU.mult), reads=[f"agb{qi}", "slnb"], writes=[("ag32", qs)])
                for qs in range(4):
                    sc.op("dve", lambda e, qs=qs: e.scalar_tensor_tensor(out=ybs[qi][:, qs, :], in0=o2[:, qs, :], scalar=stt[:, 8 + qs:9 + qs], in1=g32[:, qs, :], op0=ALU.mult, op1=ALU.mult),
                          reads=[("o2", qs), "ast", ("ag32", qs)], writes=[(f"aybs{qi}", qs)])
                sc.dma("pool", lambda e: e.dma_start(out=self.yb[tb:tb + 512, h * 256:(h + 1) * 256].rearrange("(a p) c -> p a c", p=128), in_=ybs[qi][:]),
                       f"aybs{qi}", reads=[(f"aybs{qi}", q_) for q_ in range(4)], writes=[("yb", tb, h)])

            total = len(pairs)
            for g in range(total + 1):
                deferred = []
                if g < total:
                    widx, h, qg, c, p = pairs[g]
                    if p == 0 and c == 0:
                        if widx + 1 < len(work):
                            deferred.append(lambda widx=widx: load_q(widx + 1))
                        if qg == 0:
                            if h + 1 < DIFF_H:
                                deferred.append(lambda h=h: load_head(h + 1, (h + 1) % 2))
                            if wprep and h == 1:
                                deferred.append(lambda: (wprep_store(0), wprep_load(1)))
                            if wprep and h == 3:
                                deferred.append(lambda: wprep_store(1))
                    emit_scores(g)
                if g >= 1:
                    emit_pv(g - 1)
                    if pairs[g - 1][4] == NP2 - 1:
                        emit_drain(g - 1)
                for d in deferred:
                    d()
            sc.flush_wait_all()
            sc.emit(block)

    def phase_wprep(self, sc):
        nc = self.nc
        with ExitStack() as es, nc.Block() as block:
            wt = self.sb(es, "wprep", [128, KC, 2 * FFN], BF16)
            src = self.w_fi.rearrange("(k p) c -> p k c", p=128)
            for kc in range(KC):
                sc.dma("pool", lambda e, kc=kc: e.dma_start(out=wt[:, kc, :], in_=src[:, kc, :]), f"wp{kc % 4}", writes=[("wprep", kc)])
            allk = [("wprep", kc) for kc in range(KC)]
            for f in range(FC):
                for g in range(2):
                    c0 = g * FFN + f * 128
                    sc.dma("sp", lambda e, f=f, g=g, c0=c0: e.dma_start(out=self.wfib[f, :, g, :, :], in_=wt[:, :, c0:c0 + 128]), f"wps{(2 * f + g) % 4}",
                           reads=allk, writes=[("wfib", f, g)])
            sc.flush_wait_all()
            sc.emit(block)

    def phase_ffn(self, sc):
        nc, S, T = self.nc, self.S, self.T
        NGT = T // 512
        with ExitStack() as es, nc.Block() as block:
            wo = self.sb(es, "f_wo", [128, KC, D], BF16)
            wfo = self.sb(es, "f_wfo", [128, FC, D], BF16)
            wfi = [self.sb(es, f"f_wfi{i}", [128, 2, KC, 128], BF16) for i in range(4)]
            ident = self.sb(es, "f_ident", [128, 128], BF16)
            identf = self.sb(es, "f_identf", [128, 128], F32)
            nwfb = self.sb(es, "f_nwfb", [128, D], F32)
            nwfin = self.sb(es, "f_nwfin", [128, D], F32)
            la = [self.sb(es, f"f_la{i}", [128, D], F32) for i in range(2)]
            lb = [self.sb(es, f"f_lb{i}", [128, D], F32) for i in range(2)]
            lx = [self.sb(es, f"f_lx{i}", [128, D], F32) for i in range(2)]
            mb = [self.sb(es, f"f_mb{i}", [128, D], BF16) for i in range(2)]
            mT = self.sb(es, "f_mT", [128, KC, 512], BF16)
            hT = self.sb(es, "f_hT", [128, KC, 512], BF16)
            hsb = self.sb(es, "f_h", [128, 4, D], F32)
            aT = self.sb(es, "f_aT", [128, FC, 512], BF16)
            sg = [self.sb(es, f"f_sg{i}", [128, 512], F32) for i in range(2)]
            ot = [self.sb(es, f"f_ot{i}", [128, D], F32) for i in range(2)]
            stt = [self.sb(es, f"f_st{i}", [128, 4], F32) for i in range(2)]
            psT = [self.ps(es, f"f_psT{i}", [128, 1024], BF16) for i in range(2)]
            pb = [self.ps(es, f"f_pb{i}", [128, 512]) for i in range(6)]

            sc.dma("pool", lambda e: e.dma_start(out=wo[:], in_=self.w_out.rearrange("(k p) c -> p k c", p=128)), "pw_a", writes=["wo"])
            for f0 in range(0, FC, 2):
                sc.dma("pool", lambda e, f0=f0: e.dma_start(out=wfo[:, f0:f0 + 2, :], in_=self.w_fo[f0 * 128:(f0 + 2) * 128, :].rearrange("(k p) c -> p k c", p=128)),
                       f"pw_b{(f0 // 2) % 2}", writes=[("wfo", f0)])
            wfo_all = [("wfo", f0) for f0 in range(0, FC, 2)]
            sc.dma("sp", lambda e: e.dma_start(out=identf[:], in_=self.cst[:, 0:128]), "c_c", writes=["identf"])
            sc.op("dve", lambda e: e.tensor_copy(out=ident[:], in_=identf[:]), reads=["identf"], writes=["ident"])
            sc.dma("sp", lambda e: e.dma_start(out=nwfb[:], in_=self.nw_ffnr.broadcast_to([128, D])), "c_d", writes=["nwfb"])
            sc.dma("sp", lambda e: e.dma_start(out=nwfin[:], in_=self.nw_fin.broadcast_to([128, D])), "c_e", writes=["nwfin"])

            xf = self.x.rearrange("n s d -> (n s) d")
            yf = self.y.rearrange("n s d -> (n s) d")
            st8 = dict(pbi=0, li=0, wi=0, sgi=0, oti=0, pti=0)

            def next_pb():
                j = st8["pbi"] % 6
                st8["pbi"] += 1
                return j

            def norm_rows(src_ap, src_keys, i):
                sc.op("act", lambda e: e.activation(out=mb[i][:], in_=src_ap, func=AF.Square, accum_out=stt[i][:, 0:1]), reads=src_keys, writes=[f"mb{i}", f"fst{i}"])
                sc.op("act", lambda e: e.activation(out=stt[i][:, 1:2], in_=stt[i][:, 0:1], func=AF.Ln, scale=1.0 / D, bias=self.eps_t[:, 0:1]), reads=[f"fst{i}"], writes=[f"fst{i}"])
                sc.op("act", lambda e: e.activation(out=stt[i][:, 2:3], in_=stt[i][:, 1:2], func=AF.Exp, scale=-0.5), reads=[f"fst{i}"], writes=[f"fst{i}"])

            def transpose_to(dstT, dkey, i, tt_, on_act):
                pt = st8["pti"] % 2
                st8["pti"] += 1
                for kc in range(KC):
                    sc.op("pe", lambda e, kc=kc: e.transpose(psT[pt][:, kc * 128:(kc + 1) * 128], mb[i][:, kc * 128:(kc + 1) * 128], ident[:]),
                          reads=[f"mb{i}", "ident"], writes=[f"f_psT{pt}"])
                src = psT[pt][:].rearrange("p (k t) -> p k t", k=KC)
                if on_act:
                    sc.op("act", lambda e: e.activation(out=dstT[:, :, tt_ * 128:(tt_ + 1) * 128], in_=src, func=AF.Copy), reads=[f"f_psT{pt}"], writes=[(dkey, tt_)])
                else:
                    sc.op("dve", lambda e: e.tensor_copy(out=dstT[:, :, tt_ * 128:(tt_ + 1) * 128], in_=src), reads=[f"f_psT{pt}"], writes=[(dkey, tt_)])

            for g in range(NGT):
                tb = g * 512

                def s1a(tt_):
                    i = tt_ % 2
                    r0 = tb + tt_ * 128
                    sc.dma("sp", lambda e: e.dma_start(out=la[i][:], in_=self.ya[r0:r0 + 128, :]), f"f_la{i}", writes=[f"la{i}"])
                    sc.dma("sp", lambda e: e.dma_start(out=lb[i][:], in_=self.yb[r0:r0 + 128, :]), f"f_lb{i}", writes=[f"lb{i}"])
                    sc.op("dve", lambda e: e.tensor_tensor(out=mb[i][:], in0=la[i][:], in1=lb[i][:], op=ALU.add), reads=[f"la{i}", f"lb{i}"], writes=[f"mb{i}"])

                s1a(0)
                for tt_ in range(4):
                    if tt_ + 1 < 4:
                        s1a(tt_ + 1)
                    transpose_to(mT, "mT", tt_ % 2, tt_, on_act=True)

                def s2a(tt_):
                    i = tt_ % 2
                    r0 = tb + tt_ * 128
                    sc.dma("sp", lambda e: e.dma_start(out=lx[i][:], in_=xf[r0:r0 + 128, :]), f"f_lx{i}", writes=[f"lx{i}"])
                    for nh in range(2):
                        j = next_pb()
                        for kc in range(KC):
                            sc.op("pe", lambda e, kc=kc, j=j, nh=nh: e.matmul(pb[j][:], lhsT=mT[:, kc, tt_ * 128:(tt_ + 1) * 128], rhs=wo[:, kc, nh * 512:(nh + 1) * 512],
                                                                          start=(kc == 0), stop=(kc == KC - 1)),
                                  reads=[("mT", tt_), "wo"], writes=[f"f_pb{j}"])
                        sc.op("dve", lambda e, j=j, nh=nh: e.tensor_tensor(out=hsb[:, tt_, nh * 512:(nh + 1) * 512], in0=pb[j][:], in1=lx[i][:, nh * 512:(nh + 1) * 512], op=ALU.add),
                              reads=[f"f_pb{j}", f"lx{i}"], writes=[("h", tt_)])
                    norm_rows(hsb[:, tt_, :], [("h", tt_)], i)
                    sc.op("dve", lambda e: e.scalar_tensor_tensor(out=mb[i][:], in0=hsb[:, tt_, :], scalar=stt[i][:, 2:3], in1=nwfb[:], op0=ALU.mult, op1=ALU.mult),
                          reads=[("h", tt_), f"fst{i}", "nwfb"], writes=[f"mb{i}"])

                s2a(0)
                for tt_ in range(4):
                    if tt_ + 1 < 4:
                        s2a(tt_ + 1)
                    transpose_to(hT, "hT", tt_ % 2, tt_, on_act=False)
                hT_all = [("hT", k) for k in range(4)]
                for f in range(FC):
                    wi = st8["wi"] % 4
                    st8["wi"] += 1
                    sc.dma("sp", lambda e, wi=wi, f=f: e.dma_start(out=wfi[wi][:], in_=self.wfib[f, :, :, :, :]), f"f_wfi{wi}", reads=[("wfib", f, 0), ("wfib", f, 1)], writes=[f"wfi{wi}"])
                    jg, ju = next_pb(), next_pb()
                    for gi, j in ((0, jg), (1, ju)):
                        for kc in range(KC):
                            sc.op("pe", lambda e, kc=kc, j=j, gi=gi, wi=wi: e.matmul(pb[j][:], lhsT=wfi[wi][:, gi, kc, :], rhs=hT[:, kc, :], start=(kc == 0), stop=(kc == KC - 1)),
                                  reads=[f"wfi{wi}"] + hT_all, writes=[f"f_pb{j}"])
                    si = st8["sgi"] % 2
                    st8["sgi"] += 1
                    sc.op("act", lambda e, jg=jg, si=si: e.activation(out=sg[si][:], in_=pb[jg][:], func=AF.Silu), reads=[f"f_pb{jg}"], writes=[f"sg{si}"])
                    sc.op("dve", lambda e, ju=ju, si=si, f=f: e.tensor_tensor(out=aT[:, f, :], in0=pb[ju][:], in1=sg[si][:], op=ALU.mult),
                          reads=[f"f_pb{ju}", f"sg{si}"], writes=[("aT", f)])
                aT_all = [("aT", f) for f in range(FC)]
                for tt_ in range(4):
                    oi = st8["oti"] % 2
                    st8["oti"] += 1
                    r0 = tb + tt_ * 128
                    for nh in range(2):
                        j = next_pb()
                        for f in range(FC):
                            sc.op("pe", lambda e, f=f, j=j, nh=nh, tt_=tt_: e.matmul(pb[j][:], lhsT=aT[:, f, tt_ * 128:(tt_ + 1) * 128], rhs=wfo[:, f, nh * 512:(nh + 1) * 512],
                                                                                  start=(f == 0), stop=(f == FC - 1)),
                                  reads=aT_all + wfo_all, writes=[f"f_pb{j}"])
                        sc.op("dve", lambda e, j=j, nh=nh, tt_=tt_: e.tensor_tensor(out=hsb[:, tt_, nh * 512:(nh + 1) * 512], in0=pb[j][:], in1=hsb[:, tt_, nh * 512:(nh + 1) * 512], op=ALU.add),
                              reads=[f"f_pb{j}", ("h", tt_)], writes=[("h", tt_)])
                    norm_rows(hsb[:, tt_, :], [("h", tt_)], oi)
                    sc.op("dve", lambda e, oi=oi, tt_=tt_: e.scalar_tensor_tensor(out=ot[oi][:], in0=hsb[:, tt_, :], scalar=stt[oi][:, 2:3], in1=nwfin[:], op0=ALU.mult, op1=ALU.mult),
                          reads=[("h", tt_), f"fst{oi}", "nwfin"], writes=[f"ot{oi}"])
                    sc.dma("pool", lambda e, oi=oi, r0=r0: e.dma_start(out=yf[r0:r0 + 128, :], in_=ot[oi][:]), f"f_ot{oi}", reads=[f"ot{oi}"], writes=[("y", r0)])
            sc.flush_wait_all()
            sc.emit(block)

    def build(self):
        nc = self.nc
        self.declare()
        with ExitStack() as es:
            sc = Sched(nc, es)
            self.eps_t = self.sb(es, "eps_t", [128, 4], F32)
            with nc.Block() as block:
                sc.op("dve", lambda e: e.memset(self.eps_t[:, 0:1], NORM_EPS), writes=["eps"])
                sc.op("dve", lambda e: e.memset(self.eps_t[:, 1:2], SUBLN_EPS), writes=["eps"])
                sc.op("dve", lambda e: e.memset(self.eps_t[:, 2:3], 1.0), writes=["eps"])
                sc.op("dve", lambda e: e.memset(self.eps_t[:, 3:4], 0.0), writes=["eps"])
                sc.emit(block)
            for s in range(self.NSEQ):
                if "p1" in self.phases:
                    self.phase1(sc, s)
                if "gla" in self.phases:
                    self.phase_gla(sc, s, 1)
                    self.phase_gla(sc, s, 0)
                if "attn" in self.phases:
                    self.phase_attn(sc, s, wprep=("ffn" in self.phases and s == self.NSEQ - 1))
            if "ffn" in self.phases:
                if "attn" not in self.phases:
                    self.phase_wprep(sc)
                self.phase_ffn(sc)
        return nc


def host_consts(S):
    inv_freq = 1.0 / (10000.0 ** (np.arange(0, DIFF_HD, 2, dtype=np.float32) / DIFF_HD))
    pos = np.arange(S, dtype=np.float32)
    fr = pos[None, :] * np.concatenate([inv_freq, inv_freq])[:, None].astype(np.float32)
    cosT = np.cos(fr).astype(np.float32)
    sgn = np.where(np.arange(128) < 64, -1.0, 1.0).astype(np.float32)[:, None]
    sinT = (np.sin(fr) * sgn).astype(np.float32)
    p = np.arange(128)
    same = (p[:, None] // 64) == (p[None, :] // 64)
    sI, tI = p[:, None], p[None, :]
    g = -1.0 / 16.0
    ident = np.eye(128, dtype=np.float32)
    uinc_f = (same & (sI <= tI)) * g
    ustr_f = (same & (sI > tI)) * g
    uinc_b = (same & (sI >= tI)) * g
    ustr_b = (same & (sI < tI)) * g
    cst = np.concatenate([ident, uinc_f, ustr_f, uinc_b, ustr_b, np.zeros((128, 128))], axis=1).astype(np.float32)
    m_f = (same & (sI <= tI)).astype(np.float32)
    m_b = (same & (sI >= tI)).astype(np.float32)
    msk = np.concatenate([m_f, m_b], axis=1).astype(np.float32)
    return cosT, sinT, cst, msk


def host_inputs(inp, S):
    c = np.ascontiguousarray
    w_in = np.asarray(inp["w_in"])[0]
    w1 = c(w_in[:, _w1_columns()])
    cosT, sinT, cst, msk = host_consts(S)
    wgk = np.concatenate([np.asarray(inp["w_gk2"])[0], np.asarray(inp["b_gk"])[0][:, None, :]], axis=1)
    shared = dict(
        w1=w1,
        nw_mixr=c(np.asarray(inp["norm_mix_w"])[0].reshape(1, D)),
        nw_ffnr=c(np.asarray(inp["norm_ffn_w"])[0].reshape(1, D)),
        nw_fin=c(np.asarray(inp["norm_final_w"]).reshape(1, D)),
        cosT=cosT, sinT=sinT, cst=cst, msk=msk,
        wgk=c(wgk.astype(np.float32)),
        gnw=c(np.tile(np.asarray(inp["gla_norm_w"])[0], 4).reshape(1, 1024)),
        sln=c(np.asarray(inp["diff_subln_w"])[0].reshape(1, 256)),
        lam4=c(np.concatenate([np.asarray(inp[k])[0] for k in ("lambda_q1", "lambda_k1", "lambda_q2", "lambda_k2")]).reshape(1, 512)),
        w_out=c(np.asarray(inp["w_out"])[0]),
        w_fi=c(np.asarray(inp["w_ffn_in"])[0]),
        w_fo=c(np.asarray(inp["w_ffn_out"])[0]),
    )
    return {k: np.asarray(v, dtype=np.float32) for k, v in shared.items()}


def kernel(**inputs):
    x = np.asarray(inputs["x"], dtype=np.float32)
    B, S, _ = x.shape
    nseq = B // N_CORES
    b = Builder(S, nseq)
    nc = b.build()
    shared = host_inputs(inputs, S)
    in_maps = [dict(shared, x=np.ascontiguousarray(x[i * nseq:(i + 1) * nseq])) for i in range(N_CORES)]
    res = run_bass_kernel_spmd(nc, in_maps, core_ids=list(range(N_CORES)))
    return np.concatenate([np.asarray(r["y"]) for r in res.results], axis=0).astype(np.float32)
```

```python
from contextlib import ExitStack
import math
import numpy as np
import ml_dtypes
import concourse.bass as bass
import concourse.mybir as mybir
from concourse.bass_utils import run_bass_kernel_spmd

F32 = mybir.dt.float32
BF16 = mybir.dt.bfloat16
AF = mybir.ActivationFunctionType
ALU = mybir.AluOpType

D = 1024
KC = D // 128
GLA_H, GLA_DK, GLA_DV, GLA_RANK = 4, 128, 256, 16
DIFF_H, DIFF_HD = 4, 128
FFN = 2816
FC = FFN // 128
NORM_EPS = 1e-6
SUBLN_EPS = 1e-5
LAMBDA_INIT = 0.8 - 0.6 * math.exp(-0.3 * 0)
N_CORES = 8

OFF_GQ, OFF_GK, OFF_GV, OFF_GR, OFF_GLR = 0, 512, 1024, 2048, 3072
OFF_DQ, OFF_DK, OFF_DV, OFF_GA, OFF_GB = 3104, 4128, 5152, 6176, 7200


def _w1_columns():
    cols = []
    cols += list(range(OFF_GQ, OFF_GQ + 512))
    cols += list(range(OFF_GK, OFF_GK + 512))
    cols += list(range(OFF_DQ, OFF_DQ + 1024))
    cols += list(range(OFF_DK, OFF_DK + 1024))
    cols += list(range(OFF_GLR, OFF_GLR + 32))
    cols += list(range(OFF_GK, OFF_GK + 512))
    for off in (OFF_GV, OFF_GR, OFF_DV, OFF_GA, OFF_GB):
        cols += list(range(off, off + 1024))
    return np.array(cols, dtype=np.int64)


W1_COLS = 512 * 6 + 32 + 512 * 11
FM_BLOCKS = [("gq", 0), ("gk", 512)] + [("dq", 1024 + 512 * i) for i in range(2)] + \
            [("dk", 2048 + 512 * i) for i in range(2)]
GLR_OFF = 3072
TM_OFF = 3104
TM_BLOCKS = [("gk", 0)] + [(n, j) for n in ("gv", "gr", "dv", "ga", "gb") for j in range(2)]


class Sched:
    CE = ("pe", "act", "dve", "pool")
    ALL = ("pe", "act", "dve", "pool", "sp")

    def __init__(self, nc, es):
        self.nc = nc
        self.prog = {e: es.enter_context(nc.semaphore("prog_" + e)) for e in self.CE}
        self.cum = {e: 0 for e in self.CE}
        self.dsem = {}
        self.es = es
        self.res_w = {}
        self.res_r = {}
        self.ops = {e: [] for e in self.ALL}
        self.waited = {e: {} for e in self.ALL}
        self.sigval = {}
        self.uid = 0

    def _slot(self, slot):
        if slot not in self.dsem:
            self.dsem[slot] = [self.es.enter_context(self.nc.semaphore("d_" + slot)), 0]
        return self.dsem[slot]

    def _collect(self, eng, reads, writes, is_dma):
        deps = []
        for k in reads:
            for ev in self.res_w.get(k, ()):
                deps.append((ev, True))
        for k in writes:
            for ev in self.res_w.get(k, ()):
                deps.append((ev, False))
            for ev in self.res_r.get(k, ()):
                deps.append((ev, False))
        out = []
        for ev, raw in deps:
            if ev[0] == "eng" and ev[1] == eng and not is_dma:
                if eng == "pe":
                    continue
            out.append(ev)
        return out

    def _update(self, ev, reads, writes):
        for k in writes:
            self.res_w[k] = [ev]
            self.res_r[k] = []
        for k in reads:
            self.res_r.setdefault(k, []).append(ev)

    def op(self, eng, fn, reads=(), writes=()):
        self.uid += 1
        ev = ("eng", eng, self.uid)
        deps = self._collect(eng, reads, writes, False)
        self.ops[eng].append(dict(fn=fn, deps=deps, ev=ev, dma=None))
        self._update(ev, reads, writes)
        return ev

    def dma(self, queue, fn, slot, reads=(), writes=()):
        s = self._slot(slot)
        deps = self._collect(queue, reads, writes, True)
        if s[1] > 0:
            deps.append(("dma", slot, s[1]))
        s[1] += 1
        ev = ("dma", slot, s[1])
        self.ops[queue].append(dict(fn=fn, deps=deps, ev=ev, dma=slot))
        self._update(ev, reads, writes)
        return ev

    def flush_wait_all(self):
        deps = [("dma", slot, s[1]) for slot, s in self.dsem.items() if s[1] > 0]
        self.ops["sp"].append(dict(fn=None, deps=deps, ev=None, dma=None))

    def emit(self, block):
        done = getattr(self, "done_uid", 0)
        for e in self.ALL:
            for o in self.ops[e]:
                o["deps"] = [d for d in o["deps"] if not (d[0] == "eng" and d[2] <= done)]
        self.done_uid = self.uid
        need = set()
        for e in self.ALL:
            for o in self.ops[e]:
                for d in o["deps"]:
                    if d[0] == "eng":
                        need.add(d[2])
        for e in self.CE:
            c = self.cum[e]
            for o in self.ops[e]:
                if o["dma"] is None and o["ev"] is not None and o["ev"][2] in need:
                    c += 1
                    self.sigval[o["ev"][2]] = c
            self.cum[e] = c
        starters = dict(pe=block.tensor, act=block.scalar, dve=block.vector, pool=block.gpsimd, sp=block.sync)
        for e in self.ALL:
            ops = self.ops[e]
            if not ops:
                continue
            waited = self.waited[e]

            def body(eng, ops=ops, waited=waited, e=e):
                for o in ops:
                    for d in o["deps"]:
                        if d[0] == "eng":
                            key, sem, val = d[1], self.prog[d[1]], self.sigval[d[2]]
                        else:
                            key, sem, val = "d_" + d[1], self.dsem[d[1]][0], 16 * d[2]
                        if waited.get(key, 0) >= val:
                            continue
                        eng.wait_ge(sem, val)
                        waited[key] = val
                    if o["fn"] is None:
                        continue
                    ins = o["fn"](eng)
                    if o["dma"] is not None:
                        ins.then_inc(self.dsem[o["dma"]][0], 16)
                    elif o["ev"][2] in self.sigval:
                        ins.then_inc(self.prog[e], 1)
            starters[e](body)
        self.ops = {e: [] for e in self.ALL}


class Builder:
    def __init__(self, S, NSEQ, debug=False, phases=("p1", "gla", "attn", "ffn"), scratch_in=False, cut=99):
        self.S, self.NSEQ, self.debug, self.phases = S, NSEQ, debug, phases
        self.scratch_in, self.cut = scratch_in, cut
        self.T = S * NSEQ
        self.NT = S // 128
        self.NG = S // 512
        self.nc = bass.Bass("TRN2", target_bir_lowering=False)

    def dram(self, name, shape, dt, kind):
        return self.nc.dram_tensor(name, list(shape), dt, kind=kind).ap()

    def declare(self):
        S, T, NSEQ = self.S, self.T, self.NSEQ
        I = "ExternalInput"
        self.x = self.dram("x", [NSEQ, S, D], F32, I)
        self.w1 = self.dram("w1", [D, W1_COLS], F32, I)
        self.nw_mixr = self.dram("nw_mixr", [1, D], F32, I)
        self.nw_ffnr = self.dram("nw_ffnr", [1, D], F32, I)
        self.nw_fin = self.dram("nw_fin", [1, D], F32, I)
        self.cosT = self.dram("cosT", [128, S], F32, I)
        self.sinT = self.dram("sinT", [128, S], F32, I)
        self.wgk = self.dram("wgk", [2, 17, 512], F32, I)
        self.gnw = self.dram("gnw", [1, 1024], F32, I)
        self.sln = self.dram("sln", [1, 256], F32, I)
        self.lam4 = self.dram("lam4", [1, 512], F32, I)
        self.w_out = self.dram("w_out", [D, D], F32, I)
        self.w_fi = self.dram("w_fi", [D, 2 * FFN], F32, I)
        self.w_fo = self.dram("w_fo", [FFN, D], F32, I)
        self.cst = self.dram("cst", [128, 6 * 128], F32, I)
        self.msk = self.dram("msk", [128, 2 * 128], F32, I)
        self.y = self.dram("y", [NSEQ, S, D], F32, "ExternalOutput")
        K = "ExternalOutput" if self.debug else "Internal"
        K1 = K
        if self.scratch_in:
            K = "ExternalInput"
        self.gqT = self.dram("gqT", [4, 128, T], BF16, K)
        self.gkT = self.dram("gkT", [4, 128, T], BF16, K)
        self.dqT = self.dram("dqT", [8, 128, T], BF16, K)
        self.dkT = self.dram("dkT", [8, 128, T], BF16, K)
        self.glrT = self.dram("glrT", [32, T], BF16, K)
        self.tm = {n: self.dram("tm_" + n, [T, (512 if n == "gk" else 1024)], BF16, K)
                   for n in ("gk", "gv", "gr", "dv", "ga", "gb")}
        K = K1
        self.ob = self.dram("ob", [T, 1024], F32, K)
        self.ya = self.dram("ya", [T, 1024], F32, K)
        self.yb = self.dram("yb", [T, 1024], F32, K)
        self.wfib = self.dram("wfib", [FC, 128, 2, KC, 128], BF16, "Internal")

    def _uniq(self, name):
        self._n = getattr(self, "_n", 0) + 1
        return f"{name}_{self._n}"

    def sb(self, es, name, shape, dt):
        return es.enter_context(self.nc.sbuf_tensor(self._uniq(name), list(shape), dt))

    def ps(self, es, name, shape, dt=F32):
        return es.enter_context(self.nc.psum_tensor(self._uniq(name), list(shape), dt))

    def phase1(self, sc, s):
        nc, S, NT, NG = self.nc, self.S, self.NT, self.NG
        t0 = s * S
        with ExitStack() as es, nc.Block() as block:
            uT = self.sb(es, "uT", [128, KC, S], BF16)
            cosT = self.sb(es, "cosT_sb", [128, S], F32)
            sinT = self.sb(es, "sinT_sb", [128, S], F32)
            nwb = self.sb(es, "nwb", [128, D], F32)
            ident = self.sb(es, "ident", [128, 128], BF16)
            identf = self.sb(es, "identf", [128, 128], F32)
            xt = [self.sb(es, f"xt{i}", [128, D], F32) for i in range(4)]
            xn = [self.sb(es, f"xn{i}", [128, D], BF16) for i in range(2)]
            st = [self.sb(es, f"st{i}", [128, 4], F32) for i in range(2)]
            wb = [self.sb(es, f"wb{i}", [128, KC, 512], BF16) for i in range(2)]
            so = [self.sb(es, f"so{i}", [128, 512], BF16) for i in range(4)]
            sof = [self.sb(es, f"sof{i}", [32, 512], BF16) for i in range(2)]
            r1 = [self.sb(es, f"r1_{i}", [128, 512], F32) for i in range(2)]
            r2 = [self.sb(es, f"r2_{i}", [128, 512], F32) for i in range(2)]
            psT = [self.ps(es, f"psT{i}", [128, 1024], BF16) for i in range(2)]
            pb = [self.ps(es, f"pb{i}", [128, 512]) for i in range(6)]
            sc.dma("sp", lambda e: e.dma_start(out=nwb[:], in_=self.nw_mixr.broadcast_to([128, D])), "c_nwm", writes=["nwb"])
            sc.dma("sp", lambda e: e.dma_start(out=identf[:], in_=self.cst[:, 0:128]), "c_id", writes=["identf"])
            sc.op("dve", lambda e: e.tensor_copy(out=ident[:], in_=identf[:]), reads=["identf"], writes=["ident"])

            def load_x(tt):
                j = tt % 4
                sc.dma("sp", lambda e: e.dma_start(out=xt[j][:], in_=self.x[s, tt * 128:(tt + 1) * 128, :]), f"xt{j}", writes=[f"xt{j}"])

            def stage_a(tt):
                i = tt % 2
                j = tt % 4
                if tt + 3 < NT:
                    load_x(tt + 3)
                sc.op("act", lambda e: e.activation(out=xn[i][:], in_=xt[j][:], func=AF.Square, accum_out=st[i][:, 0:1]),
                      reads=[f"xt{j}"], writes=[f"xn{i}", f"st{i}"])
                sc.op("act", lambda e: e.activation(out=st[i][:, 1:2], in_=st[i][:, 0:1], func=AF.Ln, scale=1.0 / D, bias=self.eps_t[:, 0:1]),
                      reads=[f"st{i}"], writes=[f"st{i}"])
                sc.op("act", lambda e: e.activation(out=st[i][:, 2:3], in_=st[i][:, 1:2], func=AF.Exp, scale=-0.5),
                      reads=[f"st{i}"], writes=[f"st{i}"])
                sc.op("dve", lambda e: e.scalar_tensor_tensor(out=xn[i][:], in0=xt[j][:], scalar=st[i][:, 2:3], in1=nwb[:], op0=ALU.mult, op1=ALU.mult),
                      reads=[f"xt{j}", f"st{i}", "nwb"], writes=[f"xn{i}"])

            def stage_b(tt):
                i = tt % 2
                for kc in range(KC):
                    sc.op("pe", lambda e, kc=kc: e.transpose(psT[i][:, kc * 128:(kc + 1) * 128], xn[i][:, kc * 128:(kc + 1) * 128], ident[:]),
                          reads=[f"xn{i}", "ident"], writes=[f"psT{i}"])
                sc.op("dve", lambda e: e.tensor_copy(out=uT[:, :, tt * 128:(tt + 1) * 128], in_=psT[i][:].rearrange("p (k t) -> p k t", k=KC)),
                      reads=[f"psT{i}"], writes=[("uT", tt)])

            for tt in range(min(3, NT)):
                load_x(tt)
            stage_a(0)
            for tt in range(NT):
                if tt + 1 < NT:
                    stage_a(tt + 1)
                stage_b(tt)
            for q_ in range((S + 1023) // 1024):
                c0_, c1_ = q_ * 1024, min(S, (q_ + 1) * 1024)
                sc.dma("sp", lambda e, c0_=c0_, c1_=c1_: e.dma_start(out=cosT[:, c0_:c1_], in_=self.cosT[:, c0_:c1_]), f"c_cos{q_ % 2}", writes=[("cosT", q_)])
                sc.dma("sp", lambda e, c0_=c0_, c1_=c1_: e.dma_start(out=sinT[:, c0_:c1_], in_=self.sinT[:, c0_:c1_]), f"c_sin{q_ % 2}", writes=[("sinT", q_)])
            uT_all = [("uT", tt) for tt in range(NT)]

            state = dict(wbi=0, pbi=0, soi=0, ri=0, sofi=0)

            def load_w(col0, ncols):
                i = state["wbi"] % 2
                state["wbi"] += 1
                src = self.w1.rearrange("(k p) c -> p k c", p=128)[:, :, col0:col0 + ncols]
                sc.dma("pool", lambda e, i=i: e.dma_start(out=wb[i][:, :, 0:ncols], in_=src), f"wb{i}", writes=[f"wb{i}"])
                return i

            def next_pb():
                j = state["pbi"] % 6
                state["pbi"] += 1
                return j

            def store(dst_ap, src_tile_key, src_ap, slotname):
                sc.dma("sp", lambda e: e.dma_start(out=dst_ap, in_=src_ap), slotname, reads=[src_tile_key])

            def fm_group(wi, c, tg, j, M=128):
                for kc in range(KC):
                    sc.op("pe", lambda e, kc=kc: e.matmul(pb[j][0:M, :], lhsT=wb[wi][:, kc, c * 128:c * 128 + M],
                                                            rhs=uT[:, kc, tg * 512:(tg + 1) * 512], start=(kc == 0), stop=(kc == KC - 1)),
                          reads=[f"wb{wi}"] + uT_all[tg * 4:(tg + 1) * 4], writes=[f"pb{j}"])

            for name, col0 in FM_BLOCKS:
                wi = load_w(col0, 512)
                bidx = (col0 - {"gq": 0, "gk": 512, "dq": 1024, "dk": 2048}[name]) // 512
                if name in ("gq", "gk"):
                    dst = self.gqT if name == "gq" else self.gkT
                    for c in range(4):
                        for tg in range(NG):
                            j = next_pb()
                            fm_group(wi, c, tg, j)
                            k = state["soi"] % 4
                            state["soi"] += 1
                            if (c + tg) % 2 == 0:
                                sc.op("act", lambda e, j=j, k=k: e.activation(out=so[k][:], in_=pb[j][:], func=AF.Copy),
                                      reads=[f"pb{j}"], writes=[f"so{k}"])
                            else:
                                sc.op("dve", lambda e, j=j, k=k: e.tensor_copy(out=so[k][:], in_=pb[j][:]),
                                      reads=[f"pb{j}"], writes=[f"so{k}"])
                            store(dst[c, :, t0 + tg * 512:t0 + (tg + 1) * 512], f"so{k}", so[k][:], f"so{k}")
                else:
                    dst = self.dqT if name == "dq" else self.dkT
                    for c in range(4):
                        hc = bidx * 4 + c
                        for tg in range(NG):
                            ja = next_pb()
                            fm_group(wi, c, tg, ja)
                            ri = state["ri"] % 2
                            state["ri"] += 1
                            k = state["soi"] % 4
                            state["soi"] += 1
                            tsl = slice(tg * 512, (tg + 1) * 512)
                            sc.op("dve", lambda e, ja=ja, ri=ri, tsl=tsl: e.tensor_tensor(out=r1[ri][:], in0=pb[ja][:], in1=cosT[:, tsl], op=ALU.mult),
                                  reads=[f"pb{ja}", ("cosT", tg // 2)], writes=[f"r1_{ri}"])
                            sc.op("dve", lambda e, ja=ja, ri=ri, tsl=tsl: e.tensor_tensor(out=r2[ri][0:64, :], in0=pb[ja][64:128, :], in1=sinT[0:64, tsl], op=ALU.mult),
                                  reads=[f"pb{ja}", ("sinT", tg // 2)], writes=[(f"r2_{ri}", 0)])
                            sc.op("dve", lambda e, ja=ja, ri=ri, tsl=tsl: e.tensor_tensor(out=r2[ri][64:128, :], in0=pb[ja][0:64, :], in1=sinT[64:128, tsl], op=ALU.mult),
                                  reads=[f"pb{ja}", ("sinT", tg // 2)], writes=[(f"r2_{ri}", 1)])
                            sc.op("pool", lambda e, ri=ri, k=k: e.tensor_tensor(out=so[k][:], in0=r1[ri][:], in1=r2[ri][:], op=ALU.add),
                                  reads=[f"r1_{ri}", (f"r2_{ri}", 0), (f"r2_{ri}", 1)], writes=[f"so{k}"])
                            store(dst[hc, :, t0 + tg * 512:t0 + (tg + 1) * 512], f"so{k}", so[k][:], f"so{k}")
            wi = load_w(GLR_OFF, 32)
            for tg in range(NG):
                j = next_pb()
                fm_group(wi, 0, tg, j, M=32)
                k = state["sofi"] % 2
                state["sofi"] += 1
                sc.op("dve", lambda e, j=j, k=k: e.tensor_copy(out=sof[k][:], in_=pb[j][0:32, :]), reads=[f"pb{j}"], writes=[f"sof{k}"])
                store(self.glrT[:, t0 + tg * 512:t0 + (tg + 1) * 512], f"sof{k}", sof[k][:], f"sof{k}")
            for bi, (name, jj) in enumerate(TM_BLOCKS):
                wi = load_w(TM_OFF + bi * 512, 512)
                dst = self.tm[name]
                for tt in range(NT):
                    j = next_pb()
                    for kc in range(KC):
                        sc.op("pe", lambda e, kc=kc, j=j, tt=tt, wi=wi: e.matmul(pb[j][:], lhsT=uT[:, kc, tt * 128:(tt + 1) * 128], rhs=wb[wi][:, kc, :],
                                                                        start=(kc == 0), stop=(kc == KC - 1)),
                              reads=[f"wb{wi}", ("uT", tt)], writes=[f"pb{j}"])
                    k = state["soi"] % 4
                    state["soi"] += 1
                    if name == "gr":
                        sc.op("act", lambda e, j=j, k=k: e.activation(out=so[k][:], in_=pb[j][:], func=AF.Silu), reads=[f"pb{j}"], writes=[f"so{k}"])
                    elif name in ("ga", "gb"):
                        sc.op("act", lambda e, j=j, k=k: e.activation(out=so[k][:], in_=pb[j][:], func=AF.Sigmoid), reads=[f"pb{j}"], writes=[f"so{k}"])
                    elif tt % 2 == 0:
                        sc.op("act", lambda e, j=j, k=k: e.activation(out=so[k][:], in_=pb[j][:], func=AF.Copy), reads=[f"pb{j}"], writes=[f"so{k}"])
                    else:
                        sc.op("dve", lambda e, j=j, k=k: e.tensor_copy(out=so[k][:], in_=pb[j][:]), reads=[f"pb{j}"], writes=[f"so{k}"])
                    store(dst[t0 + tt * 128:t0 + (tt + 1) * 128, jj * 512:(jj + 1) * 512], f"so{k}", so[k][:], f"so{k}")
            sc.flush_wait_all()
            sc.emit(block)


    def phase_gla(self, sc, s, dirn):
        nc, S = self.nc, self.S
        t0 = s * S
        NP, NSS = S // 128, S // 512
        fwd = dirn == 0
        with ExitStack() as es, nc.Block() as block:
            Uinc = self.sb(es, "Uinc", [128, 128], BF16)
            Ustr = self.sb(es, "Ustr", [128, 128], BF16)
            Uf = self.sb(es, "Uf", [128, 256], F32)
            wgkf = self.sb(es, "wgkf", [17, 512], F32)
            mask4 = self.sb(es, "mask4", [128, 4, 128], F32)
            wgk = self.sb(es, "wgk_sb", [17, 512], BF16)
            gnwb = self.sb(es, "gnwb", [128, 1024], F32)
            glra = [self.sb(es, f"glra{i}", [17, 512], BF16) for i in range(2)]
            qT = [self.sb(es, f"gqT{i}", [128, 4, 512], BF16) for i in range(2)]
            kT = [self.sb(es, f"gkT{i}", [128, 4, 512], BF16) for i in range(2)]
            ktm = [self.sb(es, f"gktm{i}", [128, 4, 512], BF16) for i in range(2)]
            vv = [self.sb(es, f"gv{i}", [128, 4, 1024], BF16) for i in range(2)]
            obt = [self.sb(es, f"obt{i}", [128, 1024], F32) for i in range(2)]
            grt = [self.sb(es, f"grt{i}", [128, 1024], BF16) for i in range(2)]
            gat = [self.sb(es, f"gat{i}", [128, 1024], BF16) for i in range(2)]
            e1 = self.sb(es, "g_e1", [128, 512], F32)
            sp = self.sb(es, "g_sp", [128, 512], BF16)
            ebl = self.sb(es, "g_ebl", [128, 512], F32)
            ks = self.sb(es, "g_ks", [128, 512], BF16)
            eb = self.sb(es, "g_eb", [128, 4, 128], F32)
            enb = self.sb(es, "g_enb", [128, 4, 128], F32)
            qt = self.sb(es, "g_qt", [128, 4, 128], BF16)
            kt = self.sb(es, "g_kt", [128, 4, 128], BF16)
            attm = self.sb(es, "g_attm", [128, 4, 128], BF16)
            S32 = [self.sb(es, f"S32_{h}", [128, 256], F32) for h in range(4)]
            Sbf = [self.sb(es, f"Sbf_{h}", [128, 256], BF16) for h in range(4)]
            osb = [self.sb(es, f"osb{i}", [128, 1024], F32) for i in range(2)]
            ysb = [self.sb(es, f"ysb{i}", [128, 1024], F32) for i in range(2)]
            g32 = self.sb(es, "g_g32", [128, 1024], F32)
            junk = self.sb(es, "g_junk", [128, 4, 256], F32)
            stt = [self.sb(es, f"g_st{i}", [128, 12], F32) for i in range(2)]
            bA = self.ps(es, "bA", [128, 512])
            bB = self.ps(es, "bB", [128, 512])
            bO = [self.ps(es, f"bO{h}", [128, 512]) for h in range(4)]
            bKV = [self.ps(es, f"bKV{i}", [128, 512]) for i in range(2)]

            co = 128 * (1 + 2 * dirn)
            sc.dma("sp", lambda e: e.dma_start(out=Uf[:], in_=self.cst[:, co:co + 256]), "c_a", writes=["Uf"])
            sc.op("dve", lambda e: e.tensor_copy(out=Uinc[:], in_=Uf[:, 0:128]), reads=["Uf"], writes=["Uinc"])
            sc.op("dve", lambda e: e.tensor_copy(out=Ustr[:], in_=Uf[:, 128:256]), reads=["Uf"], writes=["Ustr"])
            for h in range(4):
                sc.dma("sp", lambda e, h=h: e.dma_start(out=mask4[:, h, :], in_=self.msk[:, dirn * 128:(dirn + 1) * 128]), "c_c", writes=["mask4"])
            sc.dma("sp", lambda e: e.dma_start(out=wgkf[:], in_=self.wgk[dirn, :, :]), "c_d", writes=["wgkf"])
            sc.op("dve", lambda e: e.tensor_copy(out=wgk[:], in_=wgkf[:]), reads=["wgkf"], writes=["wgk"])
            sc.dma("sp", lambda e: e.dma_start(out=gnwb[:], in_=self.gnw.broadcast_to([128, 1024])), "c_e", writes=["gnwb"])
            for i in range(2):
                sc.op("dve", lambda e, i=i: e.memset(glra[i][:], 1.0), writes=[f"glra{i}"])
            for h in range(4):
                sc.op("dve", lambda e, h=h: e.memset(S32[h][:], 0.0), writes=[f"S32_{h}"])
                sc.op("dve", lambda e, h=h: e.memset(Sbf[h][:], 0.0), writes=[f"Sbf_{h}"])

            if fwd:
                r1, c1, r2, c2 = slice(0, 64), 63, slice(64, 128), 127
            else:
                r1, c1, r2, c2 = slice(64, 128), 64, slice(0, 64), 0
            qscale = float(GLA_DK ** -0.5)

            def load_ss(ss, i):
                tb = t0 + ss * 512
                sc.dma("sp", lambda e: e.dma_start(out=glra[i][0:16, :], in_=self.glrT[dirn * 16:(dirn + 1) * 16, tb:tb + 512]), f"glra{i}", writes=[f"glra{i}"])
                sc.dma("sp", lambda e: e.dma_start(out=qT[i][:], in_=self.gqT[:, :, tb:tb + 512].rearrange("h p t -> p h t")), f"gqT{i}", writes=[f"gqT{i}"])
                sc.dma("sp", lambda e: e.dma_start(out=kT[i][:], in_=self.gkT[:, :, tb:tb + 512].rearrange("h p t -> p h t")), f"gkT{i}", writes=[f"gkT{i}"])
                sc.dma("sp", lambda e: e.dma_start(out=ktm[i][:], in_=self.tm["gk"][tb:tb + 512, :].rearrange("(a p) c -> p a c", p=128)), f"gktm{i}", writes=[f"gktm{i}"])
                sc.dma("sp", lambda e: e.dma_start(out=vv[i][:], in_=self.tm["gv"][tb:tb + 512, :].rearrange("(a p) c -> p a c", p=128)), f"gv{i}", writes=[f"gv{i}"])

            ksA = [ks, self.sb(es, "g_ks_b", [128, 512], BF16)]
            ksB = [self.sb(es, f"g_ksB{i}", [128, 512], BF16) for i in range(2)]
            for q_ in range(2):
                sc.op("pool", lambda e, q_=q_: e.memset(ksA[q_][:], 0.0), writes=[f"ksA{q_}"])
                sc.op("pool", lambda e, q_=q_: e.memset(ksB[q_][:], 0.0), writes=[f"ksB{q_}"])
            eb2 = [eb, self.sb(es, "g_eb_b", [128, 4, 128], F32)]
            qt2 = [qt, self.sb(es, "g_qt_b", [128, 4, 128], BF16)]
            attm2 = [attm, self.sb(es, "g_attm_b", [128, 4, 128], BF16)]

            order = list(range(NSS)) if fwd else list(range(NSS - 1, -1, -1))
            steps = []
            for n, ss in enumerate(order):
                for m, sub in enumerate([0, 1, 2, 3] if fwd else [3, 2, 1, 0]):
                    steps.append(dict(n=n, ss=ss, i=n % 2, sub=sub, pi=(n * 4 + m) % 2, q=(n * 4 + m) % 2,
                                      tok=t0 + ss * 512 + sub * 128, tsl=slice(sub * 128, (sub + 1) * 128), first=(m == 0)))

            def loads(st):
                if fwd:
                    pi, tok = st["pi"], st["tok"]
                    sc.dma("sp", lambda e: e.dma_start(out=obt[pi][:], in_=self.ob[tok:tok + 128, :]), f"obt{pi}", writes=[f"obt{pi}"])
                    sc.dma("sp", lambda e: e.dma_start(out=grt[pi][:], in_=self.tm["gr"][tok:tok + 128, :]), f"grt{pi}", writes=[f"grt{pi}"])
                    sc.dma("sp", lambda e: e.dma_start(out=gat[pi][:], in_=self.tm["ga"][tok:tok + 128, :]), f"gat{pi}", writes=[f"gat{pi}"])

            def pre_a(st):
                i, tsl = st["i"], st["tsl"]
                sc.op("pe", lambda e: e.matmul(bA[:], lhsT=glra[i][0:17, tsl], rhs=wgk[0:17, :], start=True, stop=True),
                      reads=[f"glra{i}", "wgk"], writes=["bA"])
                sc.op("act", lambda e: e.activation(out=e1[:], in_=bA[:], func=AF.Exp, scale=-1.0), reads=["bA"], writes=["e1"])
                sc.op("act", lambda e: e.activation(out=sp[:], in_=e1[:], func=AF.Ln, bias=self.eps_t[:, 2:3]), reads=["e1"], writes=["sp"])

            def pre_b(st):
                i, sub, q = st["i"], st["sub"], st["q"]
                sc.op("pe", lambda e: e.matmul(bA[:], lhsT=Ustr[:], rhs=sp[:], start=True, stop=True), reads=["Ustr", "sp"], writes=["bA"])
                for h in range(4):
                    sc.op("pe", lambda e, h=h: e.matmul(bB[:, h * 128:(h + 1) * 128], lhsT=sp[:, h * 128:(h + 1) * 128], rhs=Uinc[:], start=True, stop=True),
                          reads=["sp", "Uinc"], writes=["bB"])
                sc.op("act", lambda e: e.activation(out=ebl[:], in_=bA[:], func=AF.Exp), reads=["bA"], writes=["ebl"])
                sc.op("act", lambda e: e.activation(out=eb2[q][:].rearrange("p h t -> p (h t)"), in_=bB[:], func=AF.Exp), reads=["bB"], writes=[f"eb{q}"])
                sc.op("act", lambda e: e.activation(out=enb[:].rearrange("p h t -> p (h t)"), in_=bB[:], func=AF.Exp, scale=-1.0), reads=["bB"], writes=["enb"])

            def pre_c(st):
                i, sub, q, tsl = st["i"], st["sub"], st["q"], st["tsl"]
                sc.op("pool", lambda e: e.tensor_tensor(out=ksA[q][r1, :], in0=ktm[i][r1, sub, :], in1=ebl[r1, :], op=ALU.mult),
                      reads=[f"gktm{i}", "ebl"], writes=[f"ksA{q}"])
                sc.op("pool", lambda e: e.tensor_tensor(out=ksB[q][r2, :], in0=ktm[i][r2, sub, :], in1=ebl[r2, :], op=ALU.mult),
                      reads=[f"gktm{i}", "ebl"], writes=[f"ksB{q}"])
                sc.op("dve", lambda e: e.scalar_tensor_tensor(out=qt2[q][:], in0=eb2[q][:], scalar=qscale, in1=qT[i][:, :, tsl], op0=ALU.mult, op1=ALU.mult),
                      reads=[f"eb{q}", f"gqT{i}"], writes=[f"qt{q}"])
                sc.op("pool", lambda e: e.tensor_tensor(out=kt[:], in0=enb[:], in1=kT[i][:, :, tsl], op=ALU.mult),
                      reads=["enb", f"gkT{i}"], writes=["kt"])

            def pre_d(st):
                q = st["q"]
                for h in range(4):
                    sc.op("pe", lambda e, h=h: e.matmul(bA[:, h * 128:(h + 1) * 128], lhsT=kt[:, h, :], rhs=qt2[q][:, h, :], start=True, stop=True),
                          reads=["kt", f"qt{q}"], writes=["bA"])
                sc.op("dve", lambda e: e.tensor_tensor(out=attm2[q][:].rearrange("p h t -> p (h t)"), in0=bA[:], in1=mask4[:].rearrange("p h t -> p (h t)"), op=ALU.mult),
                      reads=["bA", "mask4"], writes=[f"attm{q}"])

            def scan_open(st):
                i, sub, q = st["i"], st["sub"], st["q"]
                for h in range(4):
                    vs = slice(h * 256, (h + 1) * 256)
                    sc.op("pe", lambda e, h=h, vs=vs: e.matmul(bO[h][:, 0:256], lhsT=attm2[q][:, h, :], rhs=vv[i][:, sub, vs], start=True, stop=False),
                          reads=[f"attm{q}", f"gv{i}"], writes=[f"bO{h}"])
                    sc.op("pe", lambda e, h=h: e.matmul(bO[h][r1, 0:256], lhsT=qt2[q][:, h, r1], rhs=Sbf[h][:], start=False, stop=True),
                          reads=[f"qt{q}", f"Sbf_{h}"], writes=[f"bO{h}"])

            def scan_kv(st, h):
                i, sub, q = st["i"], st["sub"], st["q"]
                vs = slice(h * 256, (h + 1) * 256)
                kb = h % 2
                hs = slice(h * 128, (h + 1) * 128)
                sc.op("pe", lambda e: e.matmul(bKV[kb][:, 0:256], lhsT=ksA[q][:, hs], rhs=vv[i][:, sub, vs], start=True, stop=True),
                      reads=[f"ksA{q}", f"gv{i}"], writes=[f"bKV{kb}"])
                sc.op("pe", lambda e: e.matmul(bKV[kb][:, 256:512], lhsT=ksB[q][:, hs], rhs=vv[i][:, sub, vs], start=True, stop=True),
                      reads=[f"ksB{q}", f"gv{i}"], writes=[f"bKV{kb}"])

            def scan_head(st, h):
                i, sub, q, pi = st["i"], st["sub"], st["q"], st["pi"]
                vs = slice(h * 256, (h + 1) * 256)
                kb = h % 2
                sc.op("dve", lambda e: e.scalar_tensor_tensor(out=Sbf[h][:], in0=S32[h][:], scalar=eb2[q][:, h, c1:c1 + 1], in1=bKV[kb][:, 0:256], op0=ALU.mult, op1=ALU.add),
                      reads=[f"S32_{h}", f"eb{q}", f"bKV{kb}"], writes=[f"Sbf_{h}"])
                sc.op("dve", lambda e: e.scalar_tensor_tensor(out=S32[h][:], in0=S32[h][:], scalar=eb2[q][:, h, c1:c1 + 1], in1=bKV[kb][:, 0:256], op0=ALU.mult, op1=ALU.add),
                      reads=[f"S32_{h}", f"eb{q}", f"bKV{kb}"], writes=[f"S32_{h}"])
                sc.op("pe", lambda e: e.matmul(bO[h][r2, 0:256], lhsT=qt2[q][:, h, r2], rhs=Sbf[h][:], start=False, stop=True),
                      reads=[f"qt{q}", f"Sbf_{h}"], writes=[f"bO{h}"])
                sc.op("dve", lambda e: e.scalar_tensor_tensor(out=S32[h][:], in0=S32[h][:], scalar=eb2[q][:, h, c2:c2 + 1], in1=bKV[kb][:, 256:512], op0=ALU.mult, op1=ALU.add),
                      reads=[f"S32_{h}", f"eb{q}", f"bKV{kb}"], writes=[f"S32_{h}"])
                sc.op("act", lambda e: e.activation(out=Sbf[h][:], in_=S32[h][:], func=AF.Copy), reads=[f"S32_{h}"], writes=[f"Sbf_{h}"])
                if fwd:
                    sc.op("dve", lambda e: e.tensor_tensor(out=osb[pi][:, vs], in0=bO[h][:, 0:256], in1=obt[pi][:, vs], op=ALU.add),
                          reads=[f"bO{h}", f"obt{pi}"], writes=[(f"osb{pi}", h)])
                else:
                    sc.op("act", lambda e: e.activation(out=osb[pi][:, vs], in_=bO[h][:, 0:256], func=AF.Copy),
                          reads=[f"bO{h}"], writes=[(f"osb{pi}", h)])

            def epilogue(st):
                pi, tok = st["pi"], st["tok"]
                okeys = [(f"osb{pi}", h) for h in range(4)]
                if not fwd:
                    sc.dma("act", lambda e: e.dma_start(out=self.ob[tok:tok + 128, :], in_=osb[pi][:]), f"osb{pi}", reads=okeys, writes=[("ob", tok)])
                    return
                for h in range(4):
                    vs = slice(h * 256, (h + 1) * 256)
                    sc.op("act", lambda e, h=h, vs=vs: e.activation(out=junk[:, h, :], in_=osb[pi][:, vs], func=AF.Square, accum_out=stt[pi][:, h:h + 1]),
                          reads=[(f"osb{pi}", h)], writes=[f"junk{h}", (f"gst{pi}", h)])
                sc.op("act", lambda e: e.activation(out=stt[pi][:, 4:8], in_=stt[pi][:, 0:4], func=AF.Ln, scale=1.0 / GLA_DV, bias=self.eps_t[:, 0:1]),
                      reads=[(f"gst{pi}", h) for h in range(4)], writes=[f"gst{pi}"])
                sc.op("act", lambda e: e.activation(out=stt[pi][:, 8:12], in_=stt[pi][:, 4:8], func=AF.Exp, scale=-0.5), reads=[f"gst{pi}"], writes=[f"gst{pi}"])
                for h in range(4):
                    vs = slice(h * 256, (h + 1) * 256)
                    sc.op("dve", lambda e, h=h, vs=vs: e.scalar_tensor_tensor(out=ysb[pi][:, vs], in0=osb[pi][:, vs], scalar=stt[pi][:, 8 + h:9 + h], in1=g32[:, vs], op0=ALU.mult, op1=ALU.mult),
                          reads=[(f"osb{pi}", h), f"gst{pi}", "g32"], writes=[(f"ysb{pi}", h)])
                sc.dma("act", lambda e: e.dma_start(out=self.ya[tok:tok + 128, :], in_=ysb[pi][:]), f"ysb{pi}",
                       reads=[(f"ysb{pi}", h) for h in range(4)], writes=[("ya", tok)])

            def gates(st):
                if not fwd:
                    return
                pi = st["pi"]
                sc.op("pool", lambda e: e.tensor_tensor(out=g32[:], in0=grt[pi][:], in1=gat[pi][:], op=ALU.mult), reads=[f"grt{pi}", f"gat{pi}"], writes=["g32"])
                sc.op("pool", lambda e: e.tensor_tensor(out=g32[:], in0=g32[:], in1=gnwb[:], op=ALU.mult), reads=["g32", "gnwb"], writes=["g32"])

            load_ss(order[0], 0)
            if NSS > 1:
                load_ss(order[1], 1)
            loads(steps[0])
            for f_ in (pre_a, pre_b, pre_c, pre_d):
                f_(steps[0])
            for p, st in enumerate(steps):
                nx = steps[p + 1] if p + 1 < len(steps) else None
                if nx is not None:
                    loads(nx)
                scan_kv(st, 0)
                scan_kv(st, 1)
                scan_open(st)
                gates(st)
                parts = (pre_a, pre_b, pre_c)
                for h in range(4):
                    if nx is not None and h < 3:
                        parts[h](nx)
                    scan_head(st, h)
                    if h + 2 < 4:
                        scan_kv(st, h + 2)
                if nx is not None:
                    pre_d(nx)
                epilogue(st)
                if nx is not None and nx["first"] and nx["n"] + 1 < NSS:
                    load_ss(order[nx["n"] + 1], (nx["n"] + 1) % 2)
            sc.flush_wait_all()
            sc.emit(block)

    def phase_attn(self, sc, s, wprep=False):
        nc, S, NT, NG = self.nc, self.S, self.NT, self.NG
        t0 = s * S
        LAG = 2
        with ExitStack() as es, nc.Block() as block:
            if wprep:
                wt = self.sb(es, "wprep", [128, KC, FFN], BF16)
                wsrc = self.w_fi.rearrange("(k p) c -> p k c", p=128)

                def wprep_load(g):
                    for kc in range(KC):
                        sc.dma("pool", lambda e, kc=kc: e.dma_start(out=wt[:, kc, :], in_=wsrc[:, kc, g * FFN:(g + 1) * FFN]), f"wp{kc % 4}", writes=[("wprep", kc)])

                def wprep_store(g):
                    allk = [("wprep", kc) for kc in range(KC)]
                    for f in range(FC):
                        sc.dma("sp", lambda e, f=f: e.dma_start(out=self.wfib[f, :, g, :, :], in_=wt[:, :, f * 128:(f + 1) * 128]), f"wps{f % 4}",
                               reads=allk, writes=[("wfib", f, g)])
                wprep_load(0)
            KT = [self.sb(es, f"aKT{i}", [128, 2, S], BF16) for i in range(2)]
            VA = [self.sb(es, f"aVA{i}", [128, NT, 258], BF16) for i in range(2)]
            QT = [self.sb(es, f"aQT{i}", [128, 2, 512], BF16) for i in range(2)]
            PT = [self.sb(es, f"aPT{i}", [128, 1024], BF16) for i in range(2)]
            A0 = self.sb(es, "aA0", [128, 4, 256], F32)
            o2 = self.sb(es, "ao2", [128, 4, 256], F32)
            gbt = [self.sb(es, f"agb{i}", [128, 4, 256], BF16) for i in range(2)]
            g32 = self.sb(es, "ag32", [128, 4, 256], F32)
            ybs = [self.sb(es, f"aybs{i}", [128, 4, 256], F32) for i in range(2)]
            slnb = self.sb(es, "aslnb", [128, 256], F32)
            lam = self.sb(es, "alam", [1, 512], F32)
            lt = self.sb(es, "alt", [1, 16], F32)
            ones1 = self.sb(es, "aones1", [1, 128], F32)
            nlam = self.sb(es, "anlam", [128, 1], F32)
            rr = self.sb(es, "arr", [128, 16], F32)
            stt = self.sb(es, "ast", [128, 12], F32)
            junk = self.sb(es, "ajunk", [128, 4, 256], F32)
            ACCall = self.ps(es, "aACC", [128, 4, 512])
            ACC = [ACCall[:, i, :] for i in range(4)]
            raw = [self.sb(es, f"araw{i}", [128, 4, 258], F32) for i in range(2)]
            SB = [self.ps(es, f"aSB{i}", [128, 1024]) for i in range(2)]

            sc.dma("sp", lambda e: e.dma_start(out=lam[:], in_=self.lam4[:, :]), "c_a", writes=["lam"])
            sc.dma("sp", lambda e: e.dma_start(out=slnb[:], in_=self.sln.broadcast_to([128, 256])), "c_b", writes=["slnb"])
            sc.op("dve", lambda e: e.memset(ones1[:], 1.0), writes=["ones1"])
            sc.op("dve", lambda e: e.tensor_tensor(out=lam[:, 0:128], in0=lam[:, 0:128], in1=lam[:, 128:256], op=ALU.mult), reads=["lam"], writes=["lam"])
            sc.op("dve", lambda e: e.tensor_tensor(out=lam[:, 256:384], in0=lam[:, 256:384], in1=lam[:, 384:512], op=ALU.mult), reads=["lam"], writes=["lam"])
            sc.op("act", lambda e: e.activation(out=lam[:, 128:256], in_=lam[:, 0:128], func=AF.Copy, accum_out=lt[:, 0:1]), reads=["lam"], writes=["lam", "lt"])
            sc.op("act", lambda e: e.activation(out=lam[:, 384:512], in_=lam[:, 256:384], func=AF.Copy, accum_out=lt[:, 1:2]), reads=["lam"], writes=["lam", "lt"])
            sc.op("act", lambda e: e.activation(out=lt[:, 2:4], in_=lt[:, 0:2], func=AF.Exp), reads=["lt"], writes=["lt"])
            sc.op("dve", lambda e: e.tensor_tensor(out=lt[:, 4:5], in0=lt[:, 3:4], in1=lt[:, 2:3], op=ALU.subtract), reads=["lt"], writes=["lt"])
            sc.op("dve", lambda e: e.tensor_scalar(out=lt[:, 5:6], in0=lt[:, 4:5], scalar1=-LAMBDA_INIT, scalar2=None, op0=ALU.add), reads=["lt"], writes=["lt"])
            sc.op("pe", lambda e: e.matmul(ACC[0][:, 0:1], lhsT=ones1[:], rhs=lt[:, 5:6], start=True, stop=True), reads=["ones1", "lt"], writes=["aACC0"])
            sc.op("dve", lambda e: e.tensor_copy(out=nlam[:], in_=ACC[0][:, 0:1]), reads=["aACC0"], writes=["nlam"])
            sc.op("dve", lambda e: e.tensor_scalar(out=slnb[:], in0=slnb[:], scalar1=float(1.0 - LAMBDA_INIT), scalar2=None, op0=ALU.mult), reads=["slnb"], writes=["slnb"])
            for i in range(2):
                sc.op("pool", lambda e, i=i: e.memset(VA[i][:, :, 256:258], 1.0), writes=[f"aVA{i}"])

            def load_head(h, i):
                for c in range(2):
                    sc.dma("sp", lambda e, c=c: e.dma_start(out=KT[i][:, c, :], in_=self.dkT[2 * h + c, :, t0:t0 + S]), f"aKT{i}", writes=[f"aKT{i}"])
                for a0 in range(0, NT, 8):
                    a1 = min(NT, a0 + 8)
                    sc.dma("sp", lambda e, a0=a0, a1=a1: e.dma_start(
                        out=VA[i][:, a0:a1, 0:256],
                        in_=self.tm["dv"][t0 + a0 * 128:t0 + a1 * 128, h * 256:(h + 1) * 256].rearrange("(a p) c -> p a c", p=128)),
                        f"aVA{i}", writes=[f"aVA{i}"])

            scale = float(DIFF_HD ** -0.5)
            load_head(0, 0)
            work = [(h, qg) for h in range(DIFF_H) for qg in range(NG)]

            def load_q(widx):
                h, qg = work[widx]
                qi, tb = widx % 2, t0 + qg * 512
                for c in range(2):
                    sc.dma("sp", lambda e, c=c: e.dma_start(out=QT[qi][:, c, :], in_=self.dqT[2 * h + c, :, tb:tb + 512]), f"aQT{qi}", writes=[f"aQT{qi}"])
                sc.dma("sp", lambda e: e.dma_start(out=gbt[qi][:], in_=self.tm["gb"][tb:tb + 512, h * 256:(h + 1) * 256].rearrange("(a p) c -> p a c", p=128)),
                       f"agb{qi}", writes=[f"agb{qi}"])
            load_q(0)
            NP2 = NT // 2
            pairs = [(widx, h, qg, c, p) for widx, (h, qg) in enumerate(work) for c in range(2) for p in range(NP2)]

            def emit_scores(g):
                widx, h, qg, c, p = pairs[g]
                b_, qi, hi = g % 2, widx % 2, h % 2
                for u in range(2):
                    k_ = 2 * p + u
                    sc.op("pe", lambda e, u=u, k_=k_: e.matmul(SB[b_][:, u * 512:(u + 1) * 512], lhsT=KT[hi][:, c, k_ * 128:(k_ + 1) * 128], rhs=QT[qi][:, c, :], start=True, stop=True),
                          reads=[f"aKT{hi}", f"aQT{qi}"], writes=[f"aSB{b_}"])
                sc.op("act", lambda e: e.activation(out=PT[b_][:], in_=SB[b_][:], func=AF.Exp, scale=scale, bias=self.eps_t[:, 3:4]),
                      reads=[f"aSB{b_}"], writes=[f"aPT{b_}"])

            def emit_pv(g):
                widx, h, qg, c, p = pairs[g]
                b_, hi = g % 2, h % 2
                for u in range(2):
                    k_ = 2 * p + u
                    for qs in range(4):
                        sc.op("pe", lambda e, u=u, k_=k_, qs=qs: e.matmul(ACC[qs][:, 0:257], lhsT=PT[b_][:, u * 512 + qs * 128:u * 512 + (qs + 1) * 128], rhs=VA[hi][:, k_, 0:257],
                                                                   start=(k_ == 0), stop=(k_ == NT - 1)),
                              reads=[f"aPT{b_}", f"aVA{hi}"], writes=[f"aACC{qs}"])

            def emit_drain(g):
                widx, h, qg, c, p = pairs[g]
                qi, tb = widx % 2, t0 + qg * 512
                sc.op("dve", lambda e: e.tensor_copy(out=raw[c][:, :, 0:257], in_=ACCall[:, :, 0:257]),
                      reads=[f"aACC{q_}" for q_ in range(4)], writes=[f"araw{c}"])
                sc.op("dve", lambda e: e.reciprocal(out=rr[:, c * 4:c * 4 + 4], in_=raw[c][:, :, 256]), reads=[f"araw{c}"], writes=[("rr", c)])
                if c == 1:
                    sc.op("dve", lambda e: e.tensor_scalar(out=rr[:, 12:16], in0=rr[:, 4:8], scalar1=nlam[:, 0:1], scalar2=None, op0=ALU.mult),
                          reads=[("rr", 1), "nlam"], writes=[("rr", 2)])
                for qs in range(4):
                    if c == 0:
                        sc.op("dve", lambda e, qs=qs: e.tensor_scalar(out=A0[:, qs, :], in0=raw[0][:, qs, 0:256], scalar1=rr[:, qs:qs + 1], scalar2=None, op0=ALU.mult),
                              reads=["araw0", ("rr", 0)], writes=[("A0", qs)])
                    else:
                        sc.op("dve", lambda e, qs=qs: e.scalar_tensor_tensor(out=o2[:, qs, :], in0=raw[1][:, qs, 0:256], scalar=rr[:, 12 + qs:13 + qs], in1=A0[:, qs, :], op0=ALU.mult, op1=ALU.add),
                              reads=["araw1", ("rr", 2), ("A0", qs)], writes=[("o2", qs)])

            def emit_epilogue(g):
                widx, h, qg, c, p = pairs[g]
                qi, tb = widx % 2, t0 + qg * 512
                for qs in range(4):
                    sc.op("act", lambda e, qs=qs: e.activation(out=junk[:, qs, :], in_=o2[:, qs, :], func=AF.Square, accum_out=stt[:, qs:qs + 1]), reads=[("o2", qs)], writes=[f"ajunk{qs}", ("ast", qs)])
                sc.op("act", lambda e: e.activation(out=stt[:, 4:8], in_=stt[:, 0:4], func=AF.Ln, scale=1.0 / 256, bias=self.eps_t[:, 1:2]), reads=[("ast", q_) for q_ in range(4)], writes=["ast"])
                sc.op("act", lambda e: e.activation(out=stt[:, 8:12], in_=stt[:, 4:8], func=AF.Exp, scale=-0.5), reads=["ast"], writes=["ast"])
                for qs in range(4):
                    sc.op("pool", lambda e, qs=qs: e.tensor_tensor(out=g32[:, qs, :], in0=gbt[qi][:, qs, :], in1=slnb[:], op=ALU.mult), reads=[f"agb{qi}", "slnb"], writes=[("ag32", qs)])
                for qs in range(4):
                    sc.op("dve", lambda e, qs=qs: e.scalar_tensor_tensor(out=ybs[qi][:, qs, :], in0=o2[:, qs, :], scalar=stt[:, 8 + qs:9 + qs], in1=g32[:, qs, :], op0=ALU.mult, op1=ALU.mult),
                          reads=[("o2", qs), "ast", ("ag32", qs)], writes=[(f"aybs{qi}", qs)])
                sc.dma("pool", lambda e: e.dma_start(out=self.yb[tb:tb + 512, h * 256:(h + 1) * 256].rearrange("(a p) c -> p a c", p=128), in_=ybs[qi][:]),
                       f"aybs{qi}", reads=[(f"aybs{qi}", q_) for q_ in range(4)], writes=[("yb", tb, h)])

            total = len(pairs)
            DEFER = 2
            later = {}

            def at(step, fn):
                later.setdefault(min(step, total), []).append(fn)

            for g in range(total + 1):
                if g < total:
                    widx, h, qg, c, p = pairs[g]
                    if p == 0 and c == 0:
                        if widx + 1 < len(work):
                            at(g + DEFER, lambda widx=widx: load_q(widx + 1))
                        if qg == 0:
                            if h + 1 < DIFF_H:
                                at(g, lambda h=h: load_head(h + 1, (h + 1) % 2))
                            if wprep and h == 1:
                                at(g, lambda: (wprep_store(0), wprep_load(1)))
                            if wprep and h == 3:
                                at(g, lambda: wprep_store(1))
                    emit_scores(g)
                if g >= 1:
                    emit_pv(g - 1)
                    if pairs[g - 1][4] == NP2 - 1:
                        emit_drain(g - 1)
                        if pairs[g - 1][3] == 1:
                            gd = g - 1
                            later.setdefault(min(g + DEFER, total), []).insert(0, lambda gd=gd: emit_epilogue(gd))
                for d in later.pop(g, []):
                    d()
            sc.flush_wait_all()
            sc.emit(block)

    def phase_wprep(self, sc):
        nc = self.nc
        with ExitStack() as es, nc.Block() as block:
            wt = self.sb(es, "wprep", [128, KC, 2 * FFN], BF16)
            src = self.w_fi.rearrange("(k p) c -> p k c", p=128)
            for kc in range(KC):
                sc.dma("pool", lambda e, kc=kc: e.dma_start(out=wt[:, kc, :], in_=src[:, kc, :]), f"wp{kc % 4}", writes=[("wprep", kc)])
            allk = [("wprep", kc) for kc in range(KC)]
            for f in range(FC):
                for g in range(2):
                    c0 = g * FFN + f * 128
                    sc.dma("sp", lambda e, f=f, g=g, c0=c0: e.dma_start(out=self.wfib[f, :, g, :, :], in_=wt[:, :, c0:c0 + 128]), f"wps{(2 * f + g) % 4}",
                           reads=allk, writes=[("wfib", f, g)])
            sc.flush_wait_all()
            sc.emit(block)

    def phase_ffn(self, sc):
        nc, S, T = self.nc, self.S, self.T
        NGT = T // 512
        with ExitStack() as es, nc.Block() as block:
            wo = self.sb(es, "f_wo", [128, KC, D], BF16)
            wfo = self.sb(es, "f_wfo", [128, FC, D], BF16)
            wfi = [self.sb(es, f"f_wfi{i}", [128, 2, KC, 128], BF16) for i in range(4)]
            ident = self.sb(es, "f_ident", [128, 128], BF16)
            identf = self.sb(es, "f_identf", [128, 128], F32)
            nwfb = self.sb(es, "f_nwfb", [128, D], F32)
            nwfin = self.sb(es, "f_nwfin", [128, D], F32)
            la = [self.sb(es, f"f_la{i}", [128, D], F32) for i in range(2)]
            lb = [self.sb(es, f"f_lb{i}", [128, D], F32) for i in range(2)]
            lx = [self.sb(es, f"f_lx{i}", [128, D], F32) for i in range(2)]
            mb = [self.sb(es, f"f_mb{i}", [128, D], BF16) for i in range(2)]
            mT = self.sb(es, "f_mT", [128, KC, 512], BF16)
            hT = self.sb(es, "f_hT", [128, KC, 512], BF16)
            hsb = self.sb(es, "f_h", [128, 4, D], F32)
            aT = self.sb(es, "f_aT", [128, FC, 512], BF16)
            sg = [self.sb(es, f"f_sg{i}", [128, 512], F32) for i in range(2)]
            ot = [self.sb(es, f"f_ot{i}", [128, D], F32) for i in range(2)]
            stt = [self.sb(es, f"f_st{i}", [128, 4], F32) for i in range(2)]
            psT = [self.ps(es, f"f_psT{i}", [128, 1024], BF16) for i in range(2)]
            pb = [self.ps(es, f"f_pb{i}", [128, 512]) for i in range(6)]

            sc.dma("pool", lambda e: e.dma_start(out=wo[:], in_=self.w_out.rearrange("(k p) c -> p k c", p=128)), "pw_a", writes=["wo"])
            for f0 in range(0, FC, 2):
                sc.dma("pool", lambda e, f0=f0: e.dma_start(out=wfo[:, f0:f0 + 2, :], in_=self.w_fo[f0 * 128:(f0 + 2) * 128, :].rearrange("(k p) c -> p k c", p=128)),
                       f"pw_b{(f0 // 2) % 2}", writes=[("wfo", f0)])
            wfo_all = [("wfo", f0) for f0 in range(0, FC, 2)]
            sc.dma("sp", lambda e: e.dma_start(out=identf[:], in_=self.cst[:, 0:128]), "c_c", writes=["identf"])
            sc.op("dve", lambda e: e.tensor_copy(out=ident[:], in_=identf[:]), reads=["identf"], writes=["ident"])
            sc.dma("sp", lambda e: e.dma_start(out=nwfb[:], in_=self.nw_ffnr.broadcast_to([128, D])), "c_d", writes=["nwfb"])
            sc.dma("sp", lambda e: e.dma_start(out=nwfin[:], in_=self.nw_fin.broadcast_to([128, D])), "c_e", writes=["nwfin"])

            xf = self.x.rearrange("n s d -> (n s) d")
            yf = self.y.rearrange("n s d -> (n s) d")
            st8 = dict(pbi=0, li=0, wi=0, sgi=0, oti=0, pti=0)

            def next_pb():
                j = st8["pbi"] % 6
                st8["pbi"] += 1
                return j

            def norm_rows(src_ap, src_keys, i):
                sc.op("act", lambda e: e.activation(out=mb[i][:], in_=src_ap, func=AF.Square, accum_out=stt[i][:, 0:1]), reads=src_keys, writes=[f"mb{i}", f"fst{i}"])
                sc.op("act", lambda e: e.activation(out=stt[i][:, 1:2], in_=stt[i][:, 0:1], func=AF.Ln, scale=1.0 / D, bias=self.eps_t[:, 0:1]), reads=[f"fst{i}"], writes=[f"fst{i}"])
                sc.op("act", lambda e: e.activation(out=stt[i][:, 2:3], in_=stt[i][:, 1:2], func=AF.Exp, scale=-0.5), reads=[f"fst{i}"], writes=[f"fst{i}"])

            def transpose_to(dstT, dkey, i, tt_, on_act):
                pt = st8["pti"] % 2
                st8["pti"] += 1
                for kc in range(KC):
                    sc.op("pe", lambda e, kc=kc: e.transpose(psT[pt][:, kc * 128:(kc + 1) * 128], mb[i][:, kc * 128:(kc + 1) * 128], ident[:]),
                          reads=[f"mb{i}", "ident"], writes=[f"f_psT{pt}"])
                src = psT[pt][:].rearrange("p (k t) -> p k t", k=KC)
                if on_act:
                    sc.op("act", lambda e: e.activation(out=dstT[:, :, tt_ * 128:(tt_ + 1) * 128], in_=src, func=AF.Copy), reads=[f"f_psT{pt}"], writes=[(dkey, tt_)])
                else:
                    sc.op("dve", lambda e: e.tensor_copy(out=dstT[:, :, tt_ * 128:(tt_ + 1) * 128], in_=src), reads=[f"f_psT{pt}"], writes=[(dkey, tt_)])

            for g in range(NGT):
                tb = g * 512

                def s1a(tt_):
                    i = tt_ % 2
                    r0 = tb + tt_ * 128
                    sc.dma("sp", lambda e: e.dma_start(out=la[i][:], in_=self.ya[r0:r0 + 128, :]), f"f_la{i}", writes=[f"la{i}"])
                    sc.dma("sp", lambda e: e.dma_start(out=lb[i][:], in_=self.yb[r0:r0 + 128, :]), f"f_lb{i}", writes=[f"lb{i}"])
                    sc.op("dve", lambda e: e.tensor_tensor(out=mb[i][:], in0=la[i][:], in1=lb[i][:], op=ALU.add), reads=[f"la{i}", f"lb{i}"], writes=[f"mb{i}"])

                s1a(0)
                for tt_ in range(4):
                    if tt_ + 1 < 4:
                        s1a(tt_ + 1)
                    transpose_to(mT, "mT", tt_ % 2, tt_, on_act=True)

                def s2a(tt_):
                    i = tt_ % 2
                    r0 = tb + tt_ * 128
                    sc.dma("sp", lambda e: e.dma_start(out=lx[i][:], in_=xf[r0:r0 + 128, :]), f"f_lx{i}", writes=[f"lx{i}"])
                    for nh in range(2):
                        j = next_pb()
                        for kc in range(KC):
                            sc.op("pe", lambda e, kc=kc, j=j, nh=nh: e.matmul(pb[j][:], lhsT=mT[:, kc, tt_ * 128:(tt_ + 1) * 128], rhs=wo[:, kc, nh * 512:(nh + 1) * 512],
                                                                          start=(kc == 0), stop=(kc == KC - 1)),
                                  reads=[("mT", tt_), "wo"], writes=[f"f_pb{j}"])
                        sc.op("dve", lambda e, j=j, nh=nh: e.tensor_tensor(out=hsb[:, tt_, nh * 512:(nh + 1) * 512], in0=pb[j][:], in1=lx[i][:, nh * 512:(nh + 1) * 512], op=ALU.add),
                              reads=[f"f_pb{j}", f"lx{i}"], writes=[("h", tt_)])
                    norm_rows(hsb[:, tt_, :], [("h", tt_)], i)
                    sc.op("dve", lambda e: e.scalar_tensor_tensor(out=mb[i][:], in0=hsb[:, tt_, :], scalar=stt[i][:, 2:3], in1=nwfb[:], op0=ALU.mult, op1=ALU.mult),
                          reads=[("h", tt_), f"fst{i}", "nwfb"], writes=[f"mb{i}"])

                s2a(0)
                for tt_ in range(4):
                    if tt_ + 1 < 4:
                        s2a(tt_ + 1)
                    transpose_to(hT, "hT", tt_ % 2, tt_, on_act=False)
                hT_all = [("hT", k) for k in range(4)]
                for f in range(FC):
                    wi = st8["wi"] % 4
                    st8["wi"] += 1
                    sc.dma("sp", lambda e, wi=wi, f=f: e.dma_start(out=wfi[wi][:], in_=self.wfib[f, :, :, :, :]), f"f_wfi{wi}", reads=[("wfib", f, 0), ("wfib", f, 1)], writes=[f"wfi{wi}"])
                    jg, ju = next_pb(), next_pb()
                    for gi, j in ((0, jg), (1, ju)):
                        for kc in range(KC):
                            sc.op("pe", lambda e, kc=kc, j=j, gi=gi, wi=wi: e.matmul(pb[j][:], lhsT=wfi[wi][:, gi, kc, :], rhs=hT[:, kc, :], start=(kc == 0), stop=(kc == KC - 1)),
                                  reads=[f"wfi{wi}"] + hT_all, writes=[f"f_pb{j}"])
                    si = st8["sgi"] % 2
                    st8["sgi"] += 1
                    sc.op("act", lambda e, jg=jg, si=si: e.activation(out=sg[si][:], in_=pb[jg][:], func=AF.Silu), reads=[f"f_pb{jg}"], writes=[f"sg{si}"])
                    sc.op("dve", lambda e, ju=ju, si=si, f=f: e.tensor_tensor(out=aT[:, f, :], in0=pb[ju][:], in1=sg[si][:], op=ALU.mult),
                          reads=[f"f_pb{ju}", f"sg{si}"], writes=[("aT", f)])
                aT_all = [("aT", f) for f in range(FC)]
                for tt_ in range(4):
                    oi = st8["oti"] % 2
                    st8["oti"] += 1
                    r0 = tb + tt_ * 128
                    for nh in range(2):
                        j = next_pb()
                        for f in range(FC):
                            sc.op("pe", lambda e, f=f, j=j, nh=nh, tt_=tt_: e.matmul(pb[j][:], lhsT=aT[:, f, tt_ * 128:(tt_ + 1) * 128], rhs=wfo[:, f, nh * 512:(nh + 1) * 512],
                                                                                  start=(f == 0), stop=(f == FC - 1)),
                                  reads=aT_all + wfo_all, writes=[f"f_pb{j}"])
                        sc.op("dve", lambda e, j=j, nh=nh, tt_=tt_: e.tensor_tensor(out=hsb[:, tt_, nh * 512:(nh + 1) * 512], in0=pb[j][:], in1=hsb[:, tt_, nh * 512:(nh + 1) * 512], op=ALU.add),
                              reads=[f"f_pb{j}", ("h", tt_)], writes=[("h", tt_)])
                    norm_rows(hsb[:, tt_, :], [("h", tt_)], oi)
                    sc.op("dve", lambda e, oi=oi, tt_=tt_: e.scalar_tensor_tensor(out=ot[oi][:], in0=hsb[:, tt_, :], scalar=stt[oi][:, 2:3], in1=nwfin[:], op0=ALU.mult, op1=ALU.mult),
                          reads=[("h", tt_), f"fst{oi}", "nwfin"], writes=[f"ot{oi}"])
                    sc.dma("pool", lambda e, oi=oi, r0=r0: e.dma_start(out=yf[r0:r0 + 128, :], in_=ot[oi][:]), f"f_ot{oi}", reads=[f"ot{oi}"], writes=[("y", r0)])
            sc.flush_wait_all()
            sc.emit(block)

    def build(self):
        nc = self.nc
        self.declare()
        with ExitStack() as es:
            sc = Sched(nc, es)
            self.eps_t = self.sb(es, "eps_t", [128, 4], F32)
            with nc.Block() as block:
                sc.op("dve", lambda e: e.memset(self.eps_t[:, 0:1], NORM_EPS), writes=["eps"])
                sc.op("dve", lambda e: e.memset(self.eps_t[:, 1:2], SUBLN_EPS), writes=["eps"])
                sc.op("dve", lambda e: e.memset(self.eps_t[:, 2:3], 1.0), writes=["eps"])
                sc.op("dve", lambda e: e.memset(self.eps_t[:, 3:4], 0.0), writes=["eps"])
                sc.emit(block)
            for s in range(self.NSEQ):
                if "p1" in self.phases:
                    self.phase1(sc, s)
                if "gla" in self.phases:
                    self.phase_gla(sc, s, 1)
                    self.phase_gla(sc, s, 0)
                if "attn" in self.phases:
                    self.phase_attn(sc, s, wprep=("ffn" in self.phases and s == self.NSEQ - 1))
            if "ffn" in self.phases:
                if "attn" not in self.phases:
                    self.phase_wprep(sc)
                self.phase_ffn(sc)
        return nc


def host_consts(S):
    inv_freq = 1.0 / (10000.0 ** (np.arange(0, DIFF_HD, 2, dtype=np.float32) / DIFF_HD))
    pos = np.arange(S, dtype=np.float32)
    fr = pos[None, :] * np.concatenate([inv_freq, inv_freq])[:, None].astype(np.float32)
    cosT = np.cos(fr).astype(np.float32)
    sgn = np.where(np.arange(128) < 64, -1.0, 1.0).astype(np.float32)[:, None]
    sinT = (np.sin(fr) * sgn).astype(np.float32)
    p = np.arange(128)
    same = (p[:, None] // 64) == (p[None, :] // 64)
    sI, tI = p[:, None], p[None, :]
    g = -1.0 / 16.0
    ident = np.eye(128, dtype=np.float32)
    uinc_f = (same & (sI <= tI)) * g
    ustr_f = (same & (sI > tI)) * g
    uinc_b = (same & (sI >= tI)) * g
    ustr_b = (same & (sI < tI)) * g
    cst = np.concatenate([ident, uinc_f, ustr_f, uinc_b, ustr_b, np.zeros((128, 128))], axis=1).astype(np.float32)
    m_f = (same & (sI <= tI)).astype(np.float32)
    m_b = (same & (sI >= tI)).astype(np.float32)
    msk = np.concatenate([m_f, m_b], axis=1).astype(np.float32)
    return cosT, sinT, cst, msk


def host_inputs(inp, S):
    c = np.ascontiguousarray
    w_in = np.asarray(inp["w_in"])[0]
    w1 = c(w_in[:, _w1_columns()])
    cosT, sinT, cst, msk = host_consts(S)
    wgk = np.concatenate([np.asarray(inp["w_gk2"])[0], np.asarray(inp["b_gk"])[0][:, None, :]], axis=1)
    shared = dict(
        w1=w1,
        nw_mixr=c(np.asarray(inp["norm_mix_w"])[0].reshape(1, D)),
        nw_ffnr=c(np.asarray(inp["norm_ffn_w"])[0].reshape(1, D)),
        nw_fin=c(np.asarray(inp["norm_final_w"]).reshape(1, D)),
        cosT=cosT, sinT=sinT, cst=cst, msk=msk,
        wgk=c(wgk.astype(np.float32)),
        gnw=c(np.tile(np.asarray(inp["gla_norm_w"])[0], 4).reshape(1, 1024)),
        sln=c(np.asarray(inp["diff_subln_w"])[0].reshape(1, 256)),
        lam4=c(np.concatenate([np.asarray(inp[k])[0] for k in ("lambda_q1", "lambda_k1", "lambda_q2", "lambda_k2")]).reshape(1, 512)),
        w_out=c(np.asarray(inp["w_out"])[0]),
        w_fi=c(np.asarray(inp["w_ffn_in"])[0]),
        w_fo=c(np.asarray(inp["w_ffn_out"])[0]),
    )
    return {k: np.asarray(v, dtype=np.float32) for k, v in shared.items()}


def kernel(**inputs):
    x = np.asarray(inputs["x"], dtype=np.float32)
    B, S, _ = x.shape
    nseq = B // N_CORES
    b = Builder(S, nseq)
    nc = b.build()
    shared = host_inputs(inputs, S)
    in_maps = [dict(shared, x=np.ascontiguousarray(x[i * nseq:(i + 1) * nseq])) for i in range(N_CORES)]
    res = run_bass_kernel_spmd(nc, in_maps, core_ids=list(range(N_CORES)))
    return np.concatenate([np.asarray(r["y"]) for r in res.results], axis=0).astype(np.float32)
```
